# Optimizing a Trainium2 kernel written in Bass

```python
import jax
import jax.numpy as jnp
from jax import lax
import numpy as np

D_MODEL = 1024
BATCH = 4
SEQ = 4096
DEPTH = 1

GRID_W = 64
CTX_LEN = 256
N_MOD = 6
H_ATTN = 8
KV_ATTN = 2
HD_ATTN = 64
GQA_GROUP = H_ATTN // KV_ATTN
ATTN_WIDTH = H_ATTN * HD_ATTN
KV_WIDTH = KV_ATTN * HD_ATTN
ROPE_THETA = 10000.0
ROPE_PAIRS_AXIS = HD_ATTN // 4
Q_BLOCK = 128
H_GDN = 4
DK_GDN = 128
DV_GDN = 128
GDN_QK = H_GDN * DK_GDN
GDN_QKV_WIDTH = H_GDN * (2 * DK_GDN + DV_GDN)
GDN_WIDTH = H_GDN * DV_GDN
GDN_CHUNK = 64
SHORT_CONV = 3
MIX_WIDTH = ATTN_WIDTH + GDN_WIDTH
IN_SPLITS = (ATTN_WIDTH, KV_WIDTH, KV_WIDTH, GDN_QKV_WIDTH, GDN_WIDTH, H_GDN, H_GDN, H_GDN, H_GDN)
IN_WIDTH = ATTN_WIDTH + 2 * KV_WIDTH + GDN_QKV_WIDTH + GDN_WIDTH + 4 * H_GDN
D_FF = 2816
FFN_CONV = 3
EPS = 1e-6

kernel_name = "hymba_style_attn_gdn_dit_block"


def rms_norm(x, g):
    xf = x.astype(jnp.float32)
    y = xf * lax.rsqrt(jnp.mean(xf * xf, axis=-1, keepdims=True) + EPS)
    return (y * g.astype(jnp.float32)).astype(x.dtype)


def l2_norm(x):
    return x * lax.rsqrt(jnp.sum(x * x, axis=-1, keepdims=True) + EPS)


def ada_terms(cvec, w_mod, b_mod):
    m = jax.nn.silu(cvec) @ w_mod + b_mod
    m = m.reshape(m.shape[:-1] + (N_MOD, D_MODEL))
    return [m[..., i, None, :] for i in range(N_MOD)]


def modulate(x, g, shift, scale):
    return rms_norm(x, g) * (1.0 + scale) + shift


def depthwise_conv(x, w):
    pad = w.shape[0] // 2
    return lax.conv_general_dilated(
        x, w[:, None, :].astype(x.dtype), window_strides=(1,), padding=((pad, pad),),
        dimension_numbers=("NWC", "WIO", "NWC"), feature_group_count=x.shape[-1])


def axial_rope_tables(n_tokens):
    rows = n_tokens // GRID_W
    row = jnp.repeat(jnp.arange(rows, dtype=jnp.float32), GRID_W)
    col = jnp.tile(jnp.arange(GRID_W, dtype=jnp.float32), rows)
    inv_freq = ROPE_THETA ** (-jnp.arange(ROPE_PAIRS_AXIS, dtype=jnp.float32) / ROPE_PAIRS_AXIS)
    ang = jnp.concatenate([row[:, None] * inv_freq, col[:, None] * inv_freq], axis=-1)
    return jnp.cos(ang), jnp.sin(ang)


def apply_rope(x, cos, sin):
    half = x.shape[-1] // 2
    x1, x2 = x[..., :half], x[..., half:]
    c, s = cos[None, :, None, :], sin[None, :, None, :]
    return jnp.concatenate([x1 * c - x2 * s, x2 * c + x1 * s], axis=-1)


def attn_q(q, q_norm_g, rope):
    B, T = q.shape[:2]
    q = rms_norm(q.reshape(B, T, H_ATTN, HD_ATTN), q_norm_g).astype(jnp.float32)
    if rope is not None:
        q = apply_rope(q, *rope)
    return q.reshape(B, T, KV_ATTN, GQA_GROUP, HD_ATTN) * (HD_ATTN ** -0.5)


def attn_kv(k, v, k_norm_g, rope):
    B, T = k.shape[:2]
    k = rms_norm(k.reshape(B, T, KV_ATTN, HD_ATTN), k_norm_g).astype(jnp.float32)
    if rope is not None:
        k = apply_rope(k, *rope)
    return k, v.reshape(B, T, KV_ATTN, HD_ATTN).astype(jnp.float32)


def block_attention(q, k, v):
    B, T = q.shape[:2]
    qb = jnp.swapaxes(q.reshape((B, T // Q_BLOCK, Q_BLOCK) + q.shape[2:]), 0, 1)

    def one_block(q_blk):
        s = jnp.einsum("bqkgd,bskd->bkgqs", q_blk, k)
        p = jax.nn.softmax(s, axis=-1)
        return jnp.einsum("bkgqs,bskd->bqkgd", p, v)

    o = lax.map(one_block, qb)
    return jnp.swapaxes(o, 0, 1).reshape(B, T, -1)


def gdn_inputs(qkv_raw, conv_w, with_q):
    if not with_q:
        qkv_raw, conv_w = qkv_raw[..., GDN_QK:], conv_w[:, GDN_QK:]
    y = jax.nn.silu(depthwise_conv(qkv_raw, conv_w).astype(jnp.float32))
    B, T = y.shape[:2]
    q = None
    if with_q:
        q = l2_norm(y[..., :GDN_QK].reshape(B, T, H_GDN, DK_GDN)) * (DK_GDN ** -0.5)
        y = y[..., GDN_QK:]
    k = l2_norm(y[..., :GDN_QK].reshape(B, T, H_GDN, DK_GDN))
    v = y[..., GDN_QK:].reshape(B, T, H_GDN, DV_GDN)
    return q, k, v


def gdn_decay(a, a_log, dt_bias):
    return -jnp.exp(a_log.astype(jnp.float32)) * jax.nn.softplus(a.astype(jnp.float32) + dt_bias.astype(jnp.float32))


def chunk_gated_delta(q, k, v, g, beta, s0):
    with_output = q is not None
    B, T, H, dk = k.shape
    dv = v.shape[-1]
    C = GDN_CHUNK
    N = T // C

    def to_chunks(a):
        a = jnp.moveaxis(a, 2, 1)
        return a.reshape((B, H, N, C) + a.shape[3:])

    k, v, g, beta = to_chunks(k), to_chunks(v), to_chunks(g), to_chunks(beta)
    g_cum = jnp.cumsum(g, axis=-1)
    g_last = g_cum[..., -1]
    idx = jnp.arange(C)
    incl = idx[:, None] >= idx[None, :]
    strict = idx[:, None] > idx[None, :]
    diff = g_cum[..., :, None] - g_cum[..., None, :]
    decay_incl = jnp.exp(jnp.where(incl, diff, -jnp.inf))
    decay_strict = jnp.where(strict, decay_incl, 0.0)
    k_beta = k * beta[..., None]
    lower = jnp.einsum("bhnid,bhnjd->bhnij", k_beta, k) * decay_strict
    eye = jnp.eye(C, dtype=jnp.float32)
    t_inv = lax.linalg.triangular_solve(lower + eye, jnp.broadcast_to(eye, lower.shape),
                                        left_side=True, lower=True, unit_diagonal=True)
    u = t_inv @ (v * beta[..., None])
    w = t_inv @ (k_beta * jnp.exp(g_cum)[..., None])
    k_dec = k * jnp.exp(g_last[..., None] - g_cum)[..., None]
    decay_last = jnp.exp(g_last)
    xs = (k_dec, w, u, decay_last)
    if with_output:
        q = to_chunks(q)
        q_dec = q * jnp.exp(g_cum)[..., None]
        intra = jnp.einsum("bhnid,bhnjd->bhnij", q, k) * decay_incl
        xs = xs + (q_dec, intra)
    xs = tuple(jnp.moveaxis(a, 2, 0) for a in xs)

    def step(state, xs_n):
        k_n, w_n, u_n, d_n = xs_n[:4]
        v_new = u_n - jnp.einsum("bhcd,bhdv->bhcv", w_n, state)
        new_state = state * d_n[..., None, None] + jnp.einsum("bhcd,bhcv->bhdv", k_n, v_new)
        if not with_output:
            return new_state, None
        q_n, a_n = xs_n[4:]
        o_n = jnp.einsum("bhcd,bhdv->bhcv", q_n, state) + jnp.einsum("bhij,bhjv->bhiv", a_n, v_new)
        return new_state, o_n

    state, o = lax.scan(step, s0, xs)
    if not with_output:
        return None, state
    o = jnp.moveaxis(o, 0, 2).reshape(B, H, T, dv)
    return jnp.moveaxis(o, 1, 2), state


def flip(a):
    return None if a is None else a[:, ::-1]


def gated_rms(o, z, g):
    B, T = z.shape[:2]
    zz = z.reshape(B, T, H_GDN, DV_GDN).astype(jnp.float32)
    return (rms_norm(o, g) * jax.nn.silu(zz)).reshape(B, T, GDN_WIDTH).astype(z.dtype)


def hybrid_mixer(h, hc, rope, w_in, q_norm_g, k_norm_g, attn_out_g, conv_qkv_w,
                 a_log_f, a_log_b, dt_bias_f, dt_bias_b, gdn_norm_g, w_out, with_ctx_out):
    B = h.shape[0]
    offs = np.cumsum(IN_SPLITS)[:-1].tolist()
    q_a, k_a, v_a, qkv_d, z_d, a_f, a_b, b_f, b_b = jnp.split(h @ w_in, offs, axis=-1)
    cq_a, ck_a, cv_a, cqkv_d, cz_d, ca_f, ca_b, cb_f, cb_b = jnp.split(hc @ w_in, offs, axis=-1)

    kc, vc = attn_kv(ck_a, cv_a, k_norm_g, None)
    kl, vl = attn_kv(k_a, v_a, k_norm_g, rope)
    ql = attn_q(q_a, q_norm_g, rope)
    o_attn = block_attention(ql, jnp.concatenate([kc, kl], axis=1), jnp.concatenate([vc, vl], axis=1))
    o_attn = rms_norm(o_attn.astype(h.dtype), attn_out_g)

    qd, kd, vd = gdn_inputs(qkv_d, conv_qkv_w, True)
    qdc, kdc, vdc = gdn_inputs(cqkv_d, conv_qkv_w, with_ctx_out)
    g_f, g_b = gdn_decay(a_f, a_log_f, dt_bias_f), gdn_decay(a_b, a_log_b, dt_bias_b)
    gc_f, gc_b = gdn_decay(ca_f, a_log_f, dt_bias_f), gdn_decay(ca_b, a_log_b, dt_bias_b)
    be_f, be_b = jax.nn.sigmoid(b_f.astype(jnp.float32)), jax.nn.sigmoid(b_b.astype(jnp.float32))
    bec_f, bec_b = jax.nn.sigmoid(cb_f.astype(jnp.float32)), jax.nn.sigmoid(cb_b.astype(jnp.float32))
    s0 = jnp.zeros((B, H_GDN, DK_GDN, DV_GDN), jnp.float32)
    oc_f, s_ctx_f = chunk_gated_delta(qdc, kdc, vdc, gc_f, bec_f, s0)
    oc_b, s_ctx_b = chunk_gated_delta(flip(qdc), flip(kdc), flip(vdc), flip(gc_b), flip(bec_b), s0)
    ol_f, _ = chunk_gated_delta(qd, kd, vd, g_f, be_f, s_ctx_f)
    ol_b, _ = chunk_gated_delta(flip(qd), flip(kd), flip(vd), flip(g_b), flip(be_b), s_ctx_b)
    o_gdn = gated_rms(ol_f + flip(ol_b), z_d, gdn_norm_g)

    lat_out = jnp.concatenate([o_attn, o_gdn], axis=-1) @ w_out
    if not with_ctx_out:
        return lat_out, None
    qc = attn_q(cq_a, q_norm_g, None)
    oc_attn = rms_norm(block_attention(qc, kc, vc).astype(hc.dtype), attn_out_g)
    oc_gdn = gated_rms(oc_f + flip(oc_b), cz_d, gdn_norm_g)
    ctx_out = jnp.concatenate([oc_attn, oc_gdn], axis=-1) @ w_out
    return lat_out, ctx_out


def conv_ffn(h, w_up, conv_w, conv_b, w_down):
    u = depthwise_conv(h @ w_up, conv_w) + conv_b
    gate, val = jnp.split(u, 2, axis=-1)
    return (jax.nn.silu(gate) * val) @ w_down


def setup_inputs(seed: int = 0) -> dict:
    key = jax.random.key(seed)
    ks = jax.random.split(key, 24)
    f32 = jnp.float32
    L = DEPTH

    def normal(k, shape, scale):
        return scale * jax.random.normal(k, shape, f32)

    def gain(k, shape):
        return 1.0 + 0.05 * jax.random.normal(k, shape, f32)

    a_log = jnp.log(jax.random.uniform(ks[12], (2, L, H_GDN), f32, 1.0, 16.0))
    dt = jnp.exp(jax.random.uniform(ks[13], (2, L, H_GDN), f32, float(np.log(1e-3)), float(np.log(1e-1))))
    dt_bias = dt + jnp.log(-jnp.expm1(-dt))
    return {
        "x": normal(ks[0], (BATCH, SEQ, D_MODEL), 1.0),
        "c": normal(ks[1], (BATCH, D_MODEL), 1.0),
        "ctx": normal(ks[2], (BATCH, CTX_LEN, D_MODEL), 1.0),
        "c_ctx": normal(ks[3], (D_MODEL,), 1.0),
        "w_mod": normal(ks[4], (L, D_MODEL, N_MOD * D_MODEL), 0.5 * D_MODEL ** -0.5),
        "b_mod": normal(ks[5], (L, N_MOD * D_MODEL), 0.02),
        "norm1_g": gain(ks[6], (L, D_MODEL)),
        "w_in": normal(ks[7], (L, D_MODEL, IN_WIDTH), D_MODEL ** -0.5),
        "q_norm_g": gain(ks[8], (L, HD_ATTN)),
        "k_norm_g": gain(ks[9], (L, HD_ATTN)),
        "attn_out_g": gain(ks[10], (L, ATTN_WIDTH)),
        "conv_qkv_w": normal(ks[11], (L, SHORT_CONV, GDN_QKV_WIDTH), SHORT_CONV ** -0.5),
        "a_log_f": a_log[0],
        "a_log_b": a_log[1],
        "dt_bias_f": dt_bias[0],
        "dt_bias_b": dt_bias[1],
        "gdn_norm_g": gain(ks[14], (L, DV_GDN)),
        "w_out": normal(ks[15], (L, MIX_WIDTH, D_MODEL), MIX_WIDTH ** -0.5),
        "norm2_g": gain(ks[16], (L, D_MODEL)),
        "w_up": normal(ks[17], (L, D_MODEL, 2 * D_FF), D_MODEL ** -0.5),
        "ffn_conv_w": normal(ks[18], (L, FFN_CONV, 2 * D_FF), FFN_CONV ** -0.5),
        "ffn_conv_b": normal(ks[19], (L, 2 * D_FF), 0.02),
        "w_down": normal(ks[20], (L, D_FF, D_MODEL), D_FF ** -0.5),
        "final_norm_g": gain(ks[21], (D_MODEL,)),
    }


def reference(x, c, ctx, c_ctx, w_mod, b_mod, norm1_g, w_in, q_norm_g, k_norm_g, attn_out_g,
              conv_qkv_w, a_log_f, a_log_b, dt_bias_f, dt_bias_b, gdn_norm_g, w_out, norm2_g,
              w_up, ffn_conv_w, ffn_conv_b, w_down, final_norm_g):
    rope = axial_rope_tables(x.shape[1])
    xc = ctx
    for l in range(DEPTH):
        last = l == DEPTH - 1
        sh1, sc1, g1, sh2, sc2, g2 = ada_terms(c, w_mod[l], b_mod[l])
        cmod = ada_terms(c_ctx, w_mod[l], b_mod[l])
        h = modulate(x, norm1_g[l], sh1, sc1)
        hc = modulate(xc, norm1_g[l], cmod[0], cmod[1])
        lat_out, ctx_out = hybrid_mixer(
            h, hc, rope, w_in[l], q_norm_g[l], k_norm_g[l], attn_out_g[l], conv_qkv_w[l],
            a_log_f[l], a_log_b[l], dt_bias_f[l], dt_bias_b[l], gdn_norm_g[l], w_out[l],
            not last)
        x = x + g1 * lat_out
        x = x + g2 * conv_ffn(modulate(x, norm2_g[l], sh2, sc2), w_up[l], ffn_conv_w[l],
                              ffn_conv_b[l], w_down[l])
        if not last:
            xc = xc + cmod[2] * ctx_out
            xc = xc + cmod[5] * conv_ffn(modulate(xc, norm2_g[l], cmod[3], cmod[4]), w_up[l],
                                         ffn_conv_w[l], ffn_conv_b[l], w_down[l])
    return rms_norm(x, final_norm_g)
```

```python
import os
from contextlib import ExitStack
import numpy as np
import concourse.bass as bass
import concourse.mybir as mybir
from concourse.bass_utils import run_bass_kernel_spmd

F32 = mybir.dt.float32
BF16 = mybir.dt.bfloat16
AF = mybir.ActivationFunctionType
ALU = mybir.AluOpType
AX = mybir.AxisListType

D = 1024
SEQ = 4096
CTX = 256
NT_LAT = 32
NT_ALL = 34
NT_EXT = 17
NT_OWN = 16
TOK_ALL = NT_ALL * 128
TOK_EXT = NT_EXT * 128
DFF = 2816
NFC = 44
EPS = 1e-6
BIG = 30000.0


class Buf:
    __slots__ = ("name", "last_w", "readers", "dsem", "dcount", "excl")

    def __init__(self, name="b", excl=False):
        self.name = name
        self.excl = excl
        self.last_w = None
        self.readers = []
        self.dsem = None
        self.dcount = 0


class _Eng:
    def __init__(self, name):
        self.name = name
        self.count = 0
        self.waited = {}
        self.ops = []
        self.is_pe = name == "pe"


class FW:
    def __init__(self, nc, stack):
        self.nc = nc
        self.stack = stack
        self.engs = {n: _Eng(n) for n in ("pe", "act", "dve", "pool", "sp")}
        self.sems = {}
        for n in self.engs:
            self.sems[n] = stack.enter_context(nc.semaphore("s_" + n))
        self.nd = 0
        self._dma_tot = {}
        self.free_dsems = []
        self.rr = 0

    def _dma_sem(self, b):
        if b.dsem is None:
            key = "d%d" % self.nd
            self.nd += 1
            self.sems[key] = self.stack.enter_context(self.nc.semaphore("s_" + key))
            b.dsem = key
        return b.dsem

    def _deps(self, eng, reads, writes):
        deps = {}

        def add(ev):
            if ev is None:
                return
            k, v = ev
            if eng.is_pe and k == "pe":
                return
            if deps.get(k, 0) < v:
                deps[k] = v
        for b in reads:
            add(b.last_w)
            if b.excl:
                for r in b.readers:
                    if r[0] != eng.name:
                        add(r)
        for b in writes:
            add(b.last_w)
            for r in b.readers:
                add(r)
        waits = []
        for k, v in deps.items():
            if eng.waited.get(k, 0) < v:
                eng.waited[k] = v
                waits.append((k, v))
        return waits

    def op(self, engname, fn, reads=(), writes=()):
        eng = self.engs[engname]
        waits = self._deps(eng, reads, writes)
        eng.count += 1
        ev = (engname, eng.count)
        eng.ops.append((waits, fn, (engname, 1)))
        for b in reads:
            b.readers.append(ev)
        for b in writes:
            b.last_w = ev
            b.readers = []
        return ev

    def dma(self, fn, reads=(), writes=(), q="sp", track=None):
        eng = self.engs[q]
        waits = self._deps(eng, reads, writes)
        tb = track if track is not None else (writes[0] if writes else reads[0])
        key = self._dma_sem(tb)
        tb.dcount += 16
        ev = (key, tb.dcount)
        self._dma_tot[key] = tb.dcount
        eng.ops.append((waits, fn, (key, 16)))
        for b in reads:
            b.readers.append(ev)
        for b in writes:
            b.last_w = ev
            b.readers = []
        return ev

    def barrier(self):
        targets = {n: e.count for n, e in self.engs.items() if e.count > 0}
        for n, e in self.engs.items():
            waits = []
            for k, v in list(targets.items()) + list(self._dma_tot.items()):
                if k == n and e.is_pe:
                    continue
                if e.waited.get(k, 0) < v:
                    e.waited[k] = v
                    waits.append((k, v))
            if waits:
                e.ops.append((waits, None, None))

    def finish(self):
        self.barrier()
        nc = self.nc
        sems = self.sems

        def run(e, obj):
            for waits, fn, inc in e.ops:
                for k, v in waits:
                    obj.wait_ge(sems[k], v)
                if fn is not None:
                    ins = fn(obj)
                    ins.then_inc(sems[inc[0]], inc[1])

        with nc.Block() as block:
            @block.tensor
            def _(o):
                run(self.engs["pe"], o)

            @block.scalar
            def _(o):
                run(self.engs["act"], o)

            @block.vector
            def _(o):
                run(self.engs["dve"], o)

            @block.gpsimd
            def _(o):
                run(self.engs["pool"], o)

            @block.sync
            def _(o):
                run(self.engs["sp"], o)

    def mm(self, out, lhsT, rhs, start=True, stop=True, r=(), w=()):
        return self.op("pe", lambda e: e.matmul(out, lhsT=lhsT, rhs=rhs, start=start, stop=stop), r, w)

    def tr(self, out, in_, ident, r=(), w=()):
        return self.op("pe", lambda e: e.transpose(out=out, in_=in_, identity=ident), r, w)

    def act(self, out, in_, func, r=(), w=(), bias=None, scale=None, accum=None):
        kw = {}
        if bias is not None:
            kw["bias"] = bias
        if scale is not None:
            kw["scale"] = scale
        if accum is not None:
            kw["accum_out"] = accum
        return self.op("act", lambda e: e.activation(out=out, in_=in_, func=func, **kw), r, w)

    def ts(self, eng, out, in0, s1, s2, op0, op1=None, r=(), w=()):
        if op1 is None:
            return self.op(eng, lambda e: e.tensor_scalar(out=out, in0=in0, scalar1=s1, scalar2=None, op0=op0), r, w)
        return self.op(eng, lambda e: e.tensor_scalar(out=out, in0=in0, scalar1=s1, scalar2=s2, op0=op0, op1=op1), r, w)

    def tt(self, eng, out, in0, in1, op, r=(), w=()):
        return self.op(eng, lambda e: e.tensor_tensor(out=out, in0=in0, in1=in1, op=op), r, w)

    def stt(self, eng, out, in0, scalar, in1, op0, op1, r=(), w=()):
        return self.op(eng, lambda e: e.scalar_tensor_tensor(out=out, in0=in0, scalar=scalar, in1=in1, op0=op0, op1=op1), r, w)

    def cp(self, eng, out, in_, r=(), w=()):
        if eng == "act":
            return self.op("act", lambda e: e.copy(out=out, in_=in_), r, w)
        return self.op(eng, lambda e: e.tensor_copy(out=out, in_=in_), r, w)

    def cpa(self, out, in_, r=(), w=()):
        self.rr += 1
        return self.cp("dve" if self.rr % 2 else "act", out, in_, r, w)

    def ld(self, out, in_, w, q="sp", r=()):
        return self.dma(lambda e: e.dma_start(out=out, in_=in_), reads=r, writes=w, q=q)

    def stor(self, out, in_, r, track=None):
        return self.dma(lambda e: e.dma_start(out=out, in_=in_), reads=r, writes=(), track=track)


class _Stop(Exception):
    pass


def _build(dbg=False, stop_after=None):
    nc = bass.Bass("TRN2", target_bir_lowering=False)
    dbg_outs = {}
    try:
        return _build_inner(nc, dbg, stop_after, dbg_outs)
    except _Stop:
        return nc, dbg_outs


def _build_inner(nc, dbg, stop_after, dbg_outs):

    def din(name, shape):
        return nc.dram_tensor(name, list(shape), F32, kind="ExternalInput").ap()

    xs = din("xs", [SEQ, D])
    cs = din("cs", [CTX, D])
    ccol = din("ccol", [128, 16])
    w_mod = din("w_mod", [D, 6 * D])
    bmod_col = din("bmod_col", [128, 48])
    bmod_row = din("bmod_row", [1, 6 * D])
    ng_col = din("ng_col", [128, 16])
    w_in_g = din("w_in_g", [D, 1552])
    w_in_a = din("w_in_a", [D, 1280])
    convw = din("convw", [128, 36])
    rope_cs = din("rope_cs", [SEQ, 64])
    vecs = din("vecs", [1, 128 + 512 + 128 + 16 + 1024])
    w_out = din("w_out", [D, D])
    w_up = din("w_up", [D, 2 * DFF])
    fcw = din("fcw", [128, NFC * 4])
    w_down = din("w_down", [DFF, D])
    cmat = din("cmat", [128, 8 * 128 + 4])
    if stop_after is None:
        y = nc.dram_tensor("y", [NT_OWN * 128, D], F32, kind="ExternalOutput").ap()
        x1s = nc.dram_tensor("x1s", [NT_OWN * 128, D], F32, kind="Internal").ap()

    with ExitStack() as st:
        fw = FW(nc, st)

        def chk(tag):
            if os.environ.get("DBGSTOP") == tag:
                fw.finish()
                raise _Stop()

        def sb(stack, name, shape, dt):
            return stack.enter_context(nc.sbuf_tensor(name, list(shape), dt))

        psb = [st.enter_context(nc.psum_tensor("psb%d" % i, [128, 512], F32)) for i in range(8)]
        PSB = [Buf("psb%d" % i, excl=True) for i in range(8)]

        def psbf(i):
            return psb[i][:, :].bitcast(BF16)
        pst_i = [0]

        def next_pst():
            pst_i[0] += 1
            i = 6 + pst_i[0] % 2
            return psbf(i), PSB[i]

        cm = sb(st, "cm", [128, 8 * 128 + 4], F32); CM = Buf("cm")
        fw.ld(cm[:], cmat[:, :], [CM])
        ident_f = cm[:, 0:128]
        mcum = [cm[:, 128:258], cm[:, 258:388]]
        blk = cm[:, 388:516]
        negm = [cm[:, 516:644], cm[:, 772:900]]
        posm = [cm[:, 644:772], cm[:, 900:1028]]
        ident_b = sb(st, "ident_b", [128, 128], BF16); IDB = Buf("idb")
        fw.cp("dve", ident_b[:], ident_f, [CM], [IDB])
        maskb = sb(st, "maskb", [128, 4, 128], BF16); MASKB = Buf("maskb")
        fw.cp("dve", maskb[:].rearrange("p a b -> p (a b)"), cm[:, 516:1028], [CM], [MASKB])
        ones_b = sb(st, "ones_b", [128, 128], BF16); ONB = Buf("onb")
        fw.op("pool", lambda e: e.memset(ones_b[:], 1.0), [], [ONB])
        ones_f = sb(st, "ones_f", [128, 128], F32); ONF = Buf("onf")
        fw.op("pool", lambda e: e.memset(ones_f[:], 1.0), [], [ONF])
        cst = sb(st, "cst", [128, 8], F32); CST = Buf("cst")
        fw.op("pool", lambda e: e.memset(cst[:, 0:1], EPS), [], [CST])
        fw.op("pool", lambda e: e.memset(cst[:, 1:2], 1.0), [], [CST])
        fw.op("pool", lambda e: e.memset(cst[:, 2:3], EPS * 128.0), [], [CST])
        eps_ap = cst[:, 0:1]

        vb_ = sb(st, "vecs_bc", [128, 1808], F32); VEC = Buf("vecs")
        fw.ld(vb_[:], vecs[0:1, :].broadcast_to([128, 1808]), [VEC])
        qg_bc = vb_[:, 0:64]
        kg_bc = vb_[:, 64:128]
        aog_bc = vb_[:, 128:640]
        gng_bc = vb_[:, 640:768]
        alogdt_bc = vb_[:, 768:784]
        fng_bc = vb_[:, 784:1808]
        gq8 = sb(st, "gq8", [128, 512], F32); GQ8 = Buf("gq8")
        for hh in range(8):
            fw.ts("dve", gq8[:, hh * 64:(hh + 1) * 64], qg_bc, 0.125, None, ALU.mult, None, [VEC], [GQ8])
        cw = sb(st, "convw_sb", [128, 36], F32); CW = Buf("cw")
        fw.ld(cw[:], convw[:, :], [CW])
        fcw_sb = sb(st, "fcw_sb", [128, NFC * 4], F32); FCW = Buf("fcw")
        fw.ld(fcw_sb[:], fcw[:, :], [FCW])
        ngc = sb(st, "ngc", [128, 16], F32); NGC = Buf("ngc")
        fw.ld(ngc[:], ng_col[:, :], [NGC])
        G12 = sb(st, "G12", [128, 2, 1024], F32); G12B = Buf("G12")
        modv = sb(st, "modv", [128, 48], F32); MODV = Buf("modv")

        def dump(name, ap, shape, buf, dt=F32):
            if not dbg:
                return
            t = nc.dram_tensor("dbg_" + name, list(shape), dt, kind="ExternalOutput").ap()
            dbg_outs[name] = t
            fw.stor(t, ap, [buf])

        if stop_after == -1:
            dump("gq8", gq8[:], [128, 512], GQ8)
            fw.finish()
            return nc, dbg_outs
        with ExitStack() as p0:
            sc = sb(p0, "sc", [128, 16], F32); SC = Buf("sc")
            fw.ld(sc[:], ccol[:, :], [SC])
            fw.act(sc[:], sc[:], AF.Silu, [SC], [SC])
            sc2 = sb(p0, "sc2", [128, 8, 2], F32); SC2 = Buf("sc2")
            fw.cp("dve", sc2[:, :, 0], sc[:, 0:8], [SC], [SC2])
            fw.cp("dve", sc2[:, :, 1], sc[:, 8:16], [SC], [SC2])
            scbc = sb(p0, "scbc", [128, 8, 128], F32); SCBC = Buf("scbc")
            for k in range(8):
                fw.ts("dve", scbc[:, k, :], ones_f[:], sc[:, k:k + 1], None, ALU.mult, None, [SC, ONF], [SCBC])
            bmc = sb(p0, "bmc", [128, 48], F32); BMC = Buf("bmc")
            fw.ld(bmc[:], bmod_col[:, :], [BMC])
            bg = sb(p0, "bgate", [128, 2, 1024], F32); BG = Buf("bgate")
            fw.ld(bg[:, 0, :], bmod_row[0:1, 2048:3072].broadcast_to([128, 1024]), [BG])
            fw.ld(bg[:, 1, :], bmod_row[0:1, 5120:6144].broadcast_to([128, 1024]), [BG])
            mcol = sb(p0, "mcol", [128, 48, 2], F32); MCOL = Buf("mcol")
            wm = [sb(p0, "wm%d" % i, [128, 8, 512], F32) for i in range(2)]
            WM = [Buf("wm0"), Buf("wm1")]
            wmv = w_mod.rearrange("(k p) n -> p k n", p=128)
            for jb in range(12):
                s = jb % 2
                fw.ld(wm[s][:], wmv[:, :, jb * 512:(jb + 1) * 512], [WM[s]])
                if jb in (4, 5, 10, 11):
                    gi = 0 if jb < 6 else 1
                    half = jb % 2 if jb < 6 else (jb - 10)
                    pb, PB = psb[0], PSB[0]
                    for k in range(8):
                        fw.mm(pb[:, :], scbc[:, k, :], wm[s][:, k, :], k == 0, k == 7, [SCBC, WM[s]], [PB])
                    fw.tt("dve", G12[:, gi, half * 512:(half + 1) * 512], pb[:, :], bg[:, gi, half * 512:(half + 1) * 512],
                          ALU.add, [PB, BG], [G12B])
                else:
                    pb, PB = psb[1], PSB[1]
                    for cc in range(4):
                        for k in range(8):
                            fw.mm(pb[:, cc * 2:cc * 2 + 2], wm[s][:, k, cc * 128:(cc + 1) * 128], sc2[:, k, :],
                                  k == 0, k == 7, [SC2, WM[s]], [PB])
                    for col in range(2):
                        fw.tt("dve", mcol[:, jb * 4:jb * 4 + 4, col], pb[:, col:8:2], bmc[:, jb * 4:jb * 4 + 4],
                              ALU.add, [PB, BMC], [MCOL])
            tmp8 = sb(p0, "tmp8", [128, 8], F32); T8 = Buf("t8")
            fw.ts("dve", tmp8[:], mcol[:, 8:16, 0], 1.0, None, ALU.add, None, [MCOL], [T8])
            fw.tt("dve", modv[:, 0:8], tmp8[:], ngc[:, 0:8], ALU.mult, [T8, NGC], [MODV])
            fw.cp("dve", modv[:, 8:16], mcol[:, 0:8, 0], [MCOL], [MODV])
            fw.ts("dve", tmp8[:], mcol[:, 8:16, 1], 1.0, None, ALU.add, None, [MCOL], [T8])
            fw.tt("dve", modv[:, 16:24], tmp8[:], ngc[:, 0:8], ALU.mult, [T8, NGC], [MODV])
            fw.cp("dve", modv[:, 24:32], mcol[:, 0:8, 1], [MCOL], [MODV])
            fw.ts("dve", tmp8[:], mcol[:, 32:40, 0], 1.0, None, ALU.add, None, [MCOL], [T8])
            fw.tt("dve", modv[:, 32:40], tmp8[:], ngc[:, 8:16], ALU.mult, [T8, NGC], [MODV])
            fw.cp("dve", modv[:, 40:48], mcol[:, 24:32, 0], [MCOL], [MODV])
            dump("modv", modv[:], [128, 48], MODV)
            dump("G12", G12[:], [128, 2, 1024], G12B)
            fw.barrier()
        if stop_after == 0:
            fw.finish()
            return nc, dbg_outs

        def tile_src(t):
            if t < NT_LAT:
                return xs[t * 128:(t + 1) * 128, :]
            return cs[(t - NT_LAT) * 128:(t - NT_LAT + 1) * 128, :]

        class NormCtx:
            pass

        def make_norm(stack, tag):
            n = NormCtx()
            n.xt = [sb(stack, "xt%s%d" % (tag, i), [128, D], F32) for i in range(2)]
            n.XT = [Buf("xt%d" % i) for i in range(2)]
            n.junk = sb(stack, "junk" + tag, [128, D], BF16); n.JUNK = Buf("junk")
            n.stt = sb(stack, "nst" + tag, [128, 4], F32); n.ST = Buf("nst")
            n.xn = [sb(stack, "xn%s%d" % (tag, i), [128, D], BF16) for i in range(4)]
            n.XN = [Buf("xn%d" % i) for i in range(4)]
            n.i = 0
            return n

        def norm_rows(n, src_ap, src_buf, slot):
            fw.act(n.junk[:], src_ap, AF.Square, [src_buf], [n.JUNK, n.ST], accum=n.stt[:, 0:1])
            fw.act(n.stt[:, 1:2], n.stt[:, 0:1], AF.Sqrt, [n.ST, CST], [n.ST], bias=eps_ap, scale=1.0 / D)
            fw.op("dve", lambda e: e.reciprocal(out=n.stt[:, 2:3], in_=n.stt[:, 1:2]), [n.ST], [n.ST])
            fw.ts("dve", n.xn[slot][:], src_ap, n.stt[:, 2:3], None, ALU.mult, None, [src_buf, n.ST], [n.XN[slot]])

        def transpose_mod(n, ntile, hT_ap_fn, HT, acol0, ncols_last=128):
            for kc in range(8):
                pt, PT = next_pst()
                for i in range(ntile):
                    fw.tr(pt[:, i * 128:(i + 1) * 128], n.xn[i][:, kc * 128:(kc + 1) * 128], ident_b[:],
                          [n.XN[i], IDB], [PT])
                T = (ntile - 1) * 128 + ncols_last
                chk("tm_tr")
                fw.ts("dve", hT_ap_fn(kc, T), pt[:, 0:T], modv[:, acol0 + kc:acol0 + kc + 1],
                      modv[:, acol0 + 8 + kc:acol0 + 9 + kc], ALU.mult, ALU.add, [PT, MODV], [HT])
                chk("tm_ev%d" % kc)

        def load_norm_group(n, tiles):
            for i, t in enumerate(tiles):
                s = n.i % 2
                n.i += 1
                fw.ld(n.xt[s][:], tile_src(t), [n.XT[s]])
                norm_rows(n, n.xt[s][:], n.XT[s], i)

        def load_weight_bf16(stack, name, src_view, ncols, dst, DST, piece=256):
            K = src_view.shape[1]
            with ExitStack() as ws:
                stg = [sb(ws, "%s_stg%d" % (name, i), [128, K, piece], F32) for i in range(2)]
                STG = [Buf("stg0"), Buf("stg1")]
                i = 0
                for c0 in range(0, ncols, piece):
                    c1 = min(ncols, c0 + piece)
                    s = i % 2
                    fw.ld(stg[s][:, :, 0:c1 - c0], src_view[:, :, c0:c1], [STG[s]])
                    eng = ("dve", "act")[i % 2]
                    fw.cp(eng, dst[:, :, c0:c1], stg[s][:, :, 0:c1 - c0], [STG[s]], [DST])
                    i += 1
                fw.barrier()

        groups_ext = [[0, 1, 2, 3], [4, 5, 6, 7], [8, 9, 10, 11], [12, 13, 14, 15], [16]]
        groups_oth = [[17, 18, 19], [20, 21, 22, 23], [24, 25, 26, 27], [28, 29, 30, 31], [32, 33]]

        mixer = ExitStack()
        st.enter_context(mixer)
        Oacc = sb(mixer, "Oacc", [128, NT_EXT, 512], BF16); OACC = [Buf("oacc%d" % t) for t in range(NT_EXT)]

        gdn = ExitStack()
        st.enter_context(gdn)
        rawK = sb(gdn, "rawK", [128, 4, TOK_ALL], BF16)
        rawV = sb(gdn, "rawV", [128, 4, TOK_ALL], BF16)
        rawQ = sb(gdn, "rawQ", [128, 4, TOK_EXT], BF16)
        RAW = {}
        ab = sb(gdn, "ab", [128, NT_ALL, 16], F32); AB = Buf("ab")

        def rawbuf(kind, h, grp):
            key = (kind, h, grp)
            if key not in RAW:
                RAW[key] = Buf("raw%s%d_%d" % (kind, h, grp))
            return RAW[key]

        def tok_group(t):
            return t // 4

        with ExitStack() as p1:
            wg = sb(p1, "wg", [128, 8, 1552], BF16); WG = Buf("wg")
            load_weight_bf16(p1, "wg", w_in_g.rearrange("(k p) n -> p k n", p=128), 1552, wg, WG)
            chk("wload")
            nctx = make_norm(p1, "a")
            hT = [sb(p1, "hT%d" % i, [128, 8, 512], BF16) for i in range(2)]
            HT = [Buf("hT0"), Buf("hT1")]
            SEG = 1024
            NSL = 2
            acc = [sb(p1, "cacc%d" % i, [128, SEG], F32) for i in range(NSL)]
            ACC = [Buf("cacc%d" % i) for i in range(NSL)]
            sq = [sb(p1, "csq%d" % i, [128, SEG], BF16) for i in range(NSL)]
            SQ = [Buf("csq%d" % i) for i in range(NSL)]
            rin = [sb(p1, "crin%d" % i, [128, 512], F32) for i in range(2)]
            RIN = [Buf("crin%d" % i) for i in range(2)]
            rci = [0]
            pending = []
            csi = [0]
            cprev = {}

            def conv_seg(ch, a, b, s0, s1):
                kind = "QKV"[ch // 4]
                h = ch % 4
                arr = (rawQ, rawK, rawV)[ch // 4]
                w0, w1, w2 = (cw[:, ch * 3 + j:ch * 3 + j + 1] for j in range(3))
                n = s1 - s0
                sl = csi[0] % NSL
                csi[0] += 1
                bufs = sorted({tok_group(t) for t in range(s0 // 128, (s1 + 127) // 128)} |
                              ({tok_group(s1 // 128)} if s1 < b else set()))
                RB = [rawbuf(kind, h, g) for g in bufs]
                fw.act(acc[sl][:, 0:n], arr[:, h, s0:s1], AF.Copy, RB + [CW], [ACC[sl]], scale=w1)
                if s0 > a:
                    pl = cprev[(ch, a)]
                    fw.stt("dve", acc[sl][:, 0:1], pl[0], w0, acc[sl][:, 0:1], ALU.mult, ALU.add,
                           [pl[1], CW, ACC[sl]], [ACC[sl]])
                fw.stt("dve", acc[sl][:, 1:n], arr[:, h, s0:s1 - 1], w0, acc[sl][:, 1:n], ALU.mult, ALU.add,
                       RB + [CW, ACC[sl]], [ACC[sl]])
                nr = n if s1 < b else n - 1
                fw.stt("dve", acc[sl][:, 0:nr], arr[:, h, s0 + 1:s0 + 1 + nr], w2, acc[sl][:, 0:nr], ALU.mult, ALU.add,
                       RB + [CW, ACC[sl]], [ACC[sl]])
                if s1 < b:
                    keep = sb(p1, "keep%d_%d" % (ch, s0), [128, 1], BF16)
                    KB = Buf("keep")
                    fw.cp("pool", keep[:], arr[:, h, s1 - 1:s1], RB, [KB])
                    cprev[(ch, a)] = (keep[:], KB)
                WB = [rawbuf(kind, h, g) for g in sorted({tok_group(t) for t in range(s0 // 128, (s1 + 127) // 128)})]

                def tail():
                    if kind == "V":
                        fw.act(arr[:, h, s0:s1], acc[sl][:, 0:n], AF.Silu, [ACC[sl]], WB)
                        return
                    fw.act(acc[sl][:, 0:n], acc[sl][:, 0:n], AF.Silu, [ACC[sl]], [ACC[sl]])
                    fw.tt("pool", sq[sl][:, 0:n], acc[sl][:, 0:n], acc[sl][:, 0:n], ALU.mult, [ACC[sl]], [SQ[sl]])
                    for c0 in range(0, n, 512):
                        c1 = min(n, c0 + 512)
                        rci[0] += 1
                        pb, PB = psb[4 + rci[0] % 2], PSB[4 + rci[0] % 2]
                        r_, RN = rin[rci[0] % 2], RIN[rci[0] % 2]
                        fw.mm(pb[:, 0:c1 - c0], ones_b[:], sq[sl][:, c0:c1], True, True, [ONB, SQ[sl]], [PB])
                        fw.act(r_[:, 0:c1 - c0], pb[:, 0:c1 - c0], AF.Ln, [PB, CST], [RN], bias=eps_ap, scale=1.0)
                        fw.act(r_[:, 0:c1 - c0], r_[:, 0:c1 - c0], AF.Exp, [RN], [RN], scale=-0.5)
                        fw.tt("dve", arr[:, h, s0 + c0:s0 + c1], acc[sl][:, c0:c1], r_[:, 0:c1 - c0], ALU.mult,
                              [ACC[sl], RN], WB)
                pending.append(tail)
                while len(pending) > 1:
                    pending.pop(0)()

            csegs = []
            for ch in range(12):
                rngs = [(0, TOK_EXT)] if ch < 4 else [(0, SEQ), (SEQ, TOK_ALL)]
                for (a_, b_) in rngs:
                    for s0 in range(a_, b_, SEG):
                        s1 = min(b_, s0 + SEG)
                        need = None if a_ == SEQ else min(b_, s1 + 1)
                        csegs.append((need, ch, a_, b_, s0, s1))
            cdone = set()

            def emit_ready_convs(avail_lat, ctx_done, limit=None):
                k = 0
                for i_, (need, ch, a_, b_, s0, s1) in enumerate(csegs):
                    if i_ in cdone:
                        continue
                    ok = ctx_done if need is None else need <= avail_lat
                    if ok:
                        conv_seg(ch, a_, b_, s0, s1)
                        cdone.add(i_)
                        k += 1
                        if limit is not None and k >= limit:
                            return

            allg = groups_ext + groups_oth

            def prep1(gi):
                grp = allg[gi]
                load_norm_group(nctx, grp)
                transpose_mod(nctx, len(grp), lambda kc, T, s=gi % 2: hT[s][:, kc, 0:T], HT[gi % 2],
                              16 if grp[0] >= NT_LAT else 0)
            prep1(0)
            for gi, grp in enumerate(allg):
                is_ext = grp[0] < NT_EXT
                is_ctx = grp[0] >= NT_LAT
                s = gi % 2
                T = len(grp) * 128
                tok0 = grp[0] * 128
                chunks = list(range(12)) if is_ext else list(range(4, 12))
                for ci, ch in enumerate(chunks):
                    if ci == 2 and gi + 1 < len(allg):
                        prep1(gi + 1)
                    pb, PB = psb[ci % 4], PSB[ci % 4]
                    for kc in range(8):
                        fw.mm(pb[:, 0:T], wg[:, kc, ch * 128:(ch + 1) * 128], hT[s][:, kc, 0:T], kc == 0, kc == 7,
                              [WG, HT[s]], [PB])
                    kind = "QKV"[ch // 4]
                    dst = (rawQ, rawK, rawV)[ch // 4]
                    fw.cpa(dst[:, ch % 4, tok0:tok0 + T], pb[:, 0:T], [PB], [rawbuf(kind, ch % 4, tok_group(grp[0]))])
                for i, t in enumerate(grp):
                    pb, PB = psb[4 + (i % 2)], PSB[4 + (i % 2)]
                    for kc in range(8):
                        fw.mm(pb[:, 0:16], hT[s][:, kc, i * 128:(i + 1) * 128], wg[:, kc, 1536:1552], kc == 0, kc == 7,
                              [WG, HT[s]], [PB])
                    fw.cp("act", ab[:, t, :], pb[:, 0:16], [PB], [AB])
                chk("grp0")
                avail_lat = (grp[-1] + 1) * 128 if grp[0] < NT_LAT else SEQ
                emit_ready_convs(avail_lat, grp[0] >= NT_LAT)
            emit_ready_convs(SEQ, True)
            while pending:
                pending.pop(0)()
            assert len(cdone) == len(csegs)
            fw.barrier()
        if stop_after == 1:
            dump("rawK", rawK[:], [128, 4, TOK_ALL], rawbuf("K", 0, 0), BF16)
            dump("ab", ab[:], [128, NT_ALL, 16], AB)
            fw.finish()
            return nc, dbg_outs

        if stop_after == 2:
            dump("KT", rawK[:], [128, 4, TOK_ALL], rawbuf("K", 0, 0), BF16)
            dump("QT", rawQ[:], [128, 4, TOK_EXT], rawbuf("Q", 0, 0), BF16)
            dump("VT", rawV[:], [128, 4, TOK_ALL], rawbuf("V", 0, 0), BF16)
            dump("ab", ab[:], [128, NT_ALL, 16], AB)
            fw.finish()
            return nc, dbg_outs

        def a3(name, n, stack=gdn):
            return sb(stack, name, [128, NT_ALL, n], F32)
        gg = a3("gg", 8); GG = Buf("gg")
        beta = a3("beta", 8); BETA = Buf("beta")
        egc = a3("egc", 8); EGC = Buf("egc")
        ekd = a3("ekd", 8); EKD = Buf("ekd")
        bgt = a3("bgt", 8); BGT = Buf("bgt")
        gcpl = a3("gcpl", 8); GCPL = Buf("gcpl")
        ngcn = a3("ngcn", 8); NGCN = Buf("ngcn")
        dl = a3("dl", 16); DL = Buf("dl")
        with ExitStack() as pg:
            t1 = a3("t1", 8, pg); T1 = Buf("t1")
            lnb = a3("lnb", 8, pg); LNB = Buf("lnb")
            gcs = a3("gcs", 32, pg); GCS = Buf("gcs")
            ealog = sb(pg, "ealog", [128, 8], F32); EAL = Buf("ealog")
            gI = [sb(pg, "gI%d" % i, [128, 8, 2], F32) for i in range(2)]
            GI = [Buf("gI0"), Buf("gI1")]
            fw.tt("dve", t1[:], ab[:, :, 0:8], alogdt_bc[:, 8:16].unsqueeze(1).to_broadcast([128, NT_ALL, 8]), ALU.add,
                  [AB, VEC], [T1])
            fw.act(t1[:], t1[:], AF.Exp, [T1], [T1])
            fw.act(t1[:], t1[:], AF.Ln, [T1, CST], [T1], bias=cst[:, 1:2], scale=1.0)
            fw.act(ealog[:], alogdt_bc[:, 0:8], AF.Exp, [VEC], [EAL])
            fw.stt("dve", gg[:], t1[:], -1.0, ealog[:].unsqueeze(1).to_broadcast([128, NT_ALL, 8]), ALU.mult, ALU.mult,
                   [T1, EAL], [GG])
            fw.act(beta[:], ab[:, :, 8:16], AF.Sigmoid, [AB], [BETA])
            fw.act(lnb[:], beta[:], AF.Ln, [BETA], [LNB])
            for t in range(NT_ALL):
                k = t % 2
                pb, PB = psb[k], PSB[k]
                for c in range(2):
                    fw.ts("dve", gI[k][:, :, c], gg[:, t, :], mcum[0][:, 128 + c:129 + c], None, ALU.mult, None,
                          [GG, CM], [GI[k]])
                fw.mm(pb[:, 0:4], mcum[0][:, 0:128], gg[:, t, 0:4], True, True, [CM, GG], [PB])
                fw.mm(pb[:, 4:8], mcum[1][:, 0:128], gg[:, t, 4:8], False, True, [CM, GG], [PB])
                fw.mm(pb[:, 8:16], blk, gg[:, t, 0:8], False, True, [CM, GG], [PB])
                fw.mm(pb[:, 16:32], ones_f[:], gI[k][:].rearrange("p a b -> p (a b)"), False, True, [ONF, GI[k]], [PB])
                fw.cp("act", gcs[:, t, :], pb[:, 0:32], [PB], [GCS])
            fw.act(egc[:], gcs[:, :, 0:8], AF.Exp, [GCS], [EGC])
            fw.tt("dve", t1[:], gcs[:, :, 8:16], gcs[:, :, 0:8], ALU.subtract, [GCS], [T1])
            fw.act(ekd[:], t1[:], AF.Exp, [T1], [EKD])
            fw.tt("dve", bgt[:], beta[:], egc[:], ALU.mult, [BETA, EGC], [BGT])
            fw.tt("dve", gcpl[:], gcs[:, :, 0:8], lnb[:], ALU.add, [GCS, LNB], [GCPL])
            fw.ts("dve", ngcn[:], gcs[:, :, 0:8], -1.0, None, ALU.mult, None, [GCS], [NGCN])
            fw.act(dl[:], gcs[:, :, 16:32], AF.Exp, [GCS], [DL])
            fw.barrier()
        if stop_after == 3:
            dump("gg", gg[:], [128, NT_ALL, 8], GG)
            dump("beta", beta[:], [128, NT_ALL, 8], BETA)
            fw.finish()
            return nc, dbg_outs

        fw.op("pool", lambda e: e.memset(Oacc[:], 0.0), [], OACC)
        with ExitStack() as ps_:
            maskb4 = sb(ps_, "maskb4", [128, 4, 512], BF16); MASKB4 = Buf("maskb4")
            for ty in range(4):
                for h in range(4):
                    fw.cp("pool", maskb4[:, ty, h * 128:(h + 1) * 128], maskb[:, ty, :], [MASKB], [MASKB4])
            identb4 = sb(ps_, "identb4", [128, 4, 128], BF16); IDB4 = Buf("idb4")
            for h in range(4):
                fw.cp("dve", identb4[:, h, :], ident_b[:], [IDB], [IDB4])

            class QS:
                pass
            DBL = ("kbg", "kdec", "vb", "AT", "wT", "u")
            sets = []
            for d in range(2):
                q = QS()
                for nm, dt_ in (("kbg", BF16), ("kdec", BF16), ("vb", BF16), ("gM", F32), ("Dstr", BF16), ("Dinc", BF16),
                                ("B0", BF16), ("B1", BF16), ("AT", BF16),
                                ("u", F32), ("wT", BF16), ("vnew", BF16), ("tmp", F32), ("S", F32), ("Sbf", BF16)):
                    if nm in DBL:
                        setattr(q, nm + "_2", [sb(ps_, "q%d_%s_%d" % (d, nm, i), [128, 4, 128], dt_) for i in range(2)])
                        setattr(q, nm.upper() + "_B2", [Buf("q%d_%s_%d" % (d, nm, i)) for i in range(2)])
                    else:
                        setattr(q, nm, sb(ps_, "q%d_%s" % (d, nm), [128, 4, 128], dt_))
                        setattr(q, nm.upper() + "_", Buf("q%d_%s" % (d, nm)))
                q.AP0 = sb(ps_, "q%d_AP0" % d, [128, 4, 256], BF16)
                q.AP1 = sb(ps_, "q%d_AP1" % d, [128, 4, 256], BF16)
                q.APA0_, q.APA1_, q.APP0_, q.APP1_ = Buf("apa0"), Buf("apa1"), Buf("app0"), Buf("app1")
                fw.op("pool", lambda e, q=q: e.memset(q.S[:], 0.0), [], [q.S_])
                fw.op("pool", lambda e, q=q: e.memset(q.Sbf[:], 0.0), [], [q.SBF_])
                q.banks = [0, 1, 2, 3] if d == 0 else [4, 5, 6, 7]
                q.bi = 0
                sets.append(q)

            class QView:
                def __init__(self, base, par):
                    object.__setattr__(self, "_b", base)
                    object.__setattr__(self, "_p", par)

                def __getattr__(self, name):
                    b_, p_ = self._b, self._p
                    if name in DBL:
                        return getattr(b_, name + "_2")[p_]
                    if name.endswith("_") and name[:-1].lower() in [x.lower() for x in DBL] and name[:-1].isupper():
                        for x in DBL:
                            if x.upper() == name[:-1]:
                                return getattr(b_, x.upper() + "_B2")[p_]
                    return getattr(b_, name)

                def __setattr__(self, name, val):
                    setattr(self._b, name, val)

            def qview(d, par):
                return QView(sets[d], par)

            def nb(q):
                q.bi += 1
                i = q.banks[q.bi % 4]
                return i

            def v4(ap512):
                return ap512.rearrange("p (a b) -> p a b", a=4)

            def bc4(ap_p4):
                return ap_p4.unsqueeze(2).to_broadcast([ap_p4.shape[0], 4, 128])

            def quad_pre(t, d, with_out, par):
                q = qview(d, par)
                c0 = d * 4
                tsl = slice(t * 128, (t + 1) * 128)
                grp = tok_group(t)
                KB = [rawbuf("K", h, grp) for h in range(4)]
                VB = [rawbuf("V", h, grp) for h in range(4)]
                QB = [rawbuf("Q", h, grp) for h in range(4)] if with_out else []
                i = nb(q)
                bv = psbf(i)
                for h in range(4):
                    fw.tr(bv[:, h * 128:(h + 1) * 128], rawK[:, h, tsl], ident_b[:], [KB[h], IDB], [PSB[i]])
                fw.tt("dve", q.kbg[:], v4(bv[:, 0:512]), bc4(bgt[:, t, c0:c0 + 4]), ALU.mult, [PSB[i], BGT], [q.KBG_])
                fw.tt("dve", q.kdec[:], v4(bv[:, 0:512]), bc4(ekd[:, t, c0:c0 + 4]), ALU.mult, [PSB[i], EKD], [q.KDEC_])
                yield
                i = nb(q)
                bv = psbf(i)
                for h in range(4):
                    fw.tr(bv[:, h * 128:(h + 1) * 128], rawV[:, h, tsl], ident_b[:], [VB[h], IDB], [PSB[i]])
                fw.tt("dve", q.vb[:], v4(bv[:, 0:512]), bc4(beta[:, t, c0:c0 + 4]), ALU.mult, [PSB[i], BETA], [q.VB_])
                yield
                fw.tt("dve", q.gM[:], mcum[d][:, 0:128].unsqueeze(1).to_broadcast([128, 4, 128]), bc4(gg[:, t, c0:c0 + 4]),
                      ALU.mult, [CM, GG], [q.GM_])
                gMf = q.gM[:].rearrange("p a b -> p (a b)")
                i = nb(q)
                fw.mm(psb[i][:, :], ones_f[:], gMf, True, False, [ONF, q.GM_], [PSB[i]])
                fw.mm(psb[i][:, :], ident_b[:], maskb4[:, 2 * d + 1, :], False, True, [IDB, MASKB4], [PSB[i]])
                for h in range(4):
                    fw.act(q.Dstr[:, h, :], psb[i][:, h * 128:(h + 1) * 128], AF.Exp, [PSB[i], GCPL], [q.DSTR_],
                           bias=gcpl[:, t, c0 + h:c0 + h + 1], scale=-1.0)
                yield
                if with_out:
                    i = nb(q)
                    fw.mm(psb[i][:, :], ones_f[:], gMf, True, False, [ONF, q.GM_], [PSB[i]])
                    fw.mm(psb[i][:, :], ident_b[:], maskb4[:, 2 * d, :], False, True, [IDB, MASKB4], [PSB[i]])
                    for h in range(4):
                        fw.act(q.Dinc[:, h, :], psb[i][:, h * 128:(h + 1) * 128], AF.Exp, [PSB[i], NGCN], [q.DINC_],
                               bias=ngcn[:, t, c0 + h:c0 + h + 1], scale=1.0)
                    yield
                i = nb(q)
                for h in range(4):
                    fw.mm(psb[i][:, h * 128:(h + 1) * 128], rawK[:, h, tsl], rawK[:, h, tsl], h == 0, True, [KB[h]], [PSB[i]])
                fw.stt("dve", q.B0[:], v4(psb[i][:, :]), -1.0, q.Dstr[:], ALU.mult, ALU.mult, [PSB[i], q.DSTR_], [q.B0_])
                yield
                if with_out:
                    i = nb(q)
                    for h in range(4):
                        fw.mm(psb[i][:, h * 128:(h + 1) * 128], rawK[:, h, tsl], rawQ[:, h, tsl], h == 0, True,
                              [KB[h], QB[h]], [PSB[i]])
                    fw.tt("dve", q.AT[:], v4(psb[i][:, :]), q.Dinc[:], ALU.mult, [PSB[i], q.DINC_], [q.AT_])
                    yield
                AP = [q.AP0, q.AP1]
                APA = [q.APA0_, q.APA1_]
                APP = [q.APP0_, q.APP1_]
                Bb = [(q.B0, q.B0_), (q.B1, q.B1_)]
                i = nb(q)
                bv = psbf(i)
                for h in range(4):
                    fw.tr(bv[:, h * 128:(h + 1) * 128], q.B0[:, h, :], ident_b[:], [q.B0_, IDB], [PSB[i]])
                fw.cp("act", AP[0][:, :, 0:128], v4(bv[:, 0:512]), [PSB[i]], [APA[0]])
                fw.tt("dve", AP[1][:, :, 128:256], v4(bv[:, 0:512]), identb4[:], ALU.add, [PSB[i], IDB4], [APP[1]])
                yield
                for j in range(1, 6):
                    cur, nxt = (j - 1) % 2, j % 2
                    Bc, BcB = Bb[(j - 1) % 2]
                    Bn, BnB = Bb[j % 2]
                    if j == 1:
                        i = nb(q)
                        for h in range(4):
                            fw.mm(psb[i][:, h * 128:(h + 1) * 128], Bc[:, h, :], AP[cur][:, h, 0:128], h == 0, True,
                                  [BcB, APA[cur]], [PSB[i]])
                        i2 = nb(q)
                        for h in range(4):
                            fw.mm(psb[i2][:, h * 128:(h + 1) * 128], AP[cur][:, h, 0:128], Bc[:, h, :], h == 0, True,
                                  [BcB, APA[cur]], [PSB[i2]])
                        fw.cp("act", AP[nxt][:, :, 0:128], v4(psb[i][:, :]), [PSB[i]], [APA[nxt]])
                        fw.cp("dve", Bn[:], v4(psb[i2][:, :]), [PSB[i2]], [BnB])
                        yield
                    elif j < 5:
                        ia, ib = nb(q), nb(q)
                        for h in range(4):
                            bk = ia if h < 2 else ib
                            hh = h % 2
                            fw.mm(psb[bk][:, hh * 256:(hh + 1) * 256], Bc[:, h, :], AP[cur][:, h, :], hh == 0, True,
                                  [BcB, APA[cur], APP[cur]], [PSB[bk]])
                        i2 = nb(q)
                        for h in range(4):
                            fw.mm(psb[i2][:, h * 128:(h + 1) * 128], AP[cur][:, h, 0:128], Bc[:, h, :], h == 0, True,
                                  [BcB, APA[cur]], [PSB[i2]])
                        for bk, h0 in ((ia, 0), (ib, 2)):
                            pv_ = psb[bk][:, :].rearrange("p (a b) -> p a b", a=2)
                            fw.cp("act", AP[nxt][:, h0:h0 + 2, 0:128], pv_[:, :, 0:128], [PSB[bk]], [APA[nxt]])
                            fw.tt("dve", AP[nxt][:, h0:h0 + 2, 128:256], AP[cur][:, h0:h0 + 2, 128:256], pv_[:, :, 128:256],
                                  ALU.add, [PSB[bk], APP[cur]], [APP[nxt]])
                        fw.cp("dve", Bn[:], v4(psb[i2][:, :]), [PSB[i2]], [BnB])
                        yield
                    else:
                        i = nb(q)
                        for h in range(4):
                            fw.mm(psb[i][:, h * 128:(h + 1) * 128], Bc[:, h, :], AP[cur][:, h, 128:256], h == 0, True,
                                  [BcB, APP[cur]], [PSB[i]])
                        i2 = nb(q)
                        for h in range(4):
                            fw.mm(psb[i2][:, h * 128:(h + 1) * 128], AP[cur][:, h, 0:128], Bc[:, h, :], h == 0, True,
                                  [BcB, APA[cur]], [PSB[i2]])
                        fw.tt("dve", AP[nxt][:, :, 128:256], AP[cur][:, :, 128:256], v4(psb[i][:, :]), ALU.add,
                              [PSB[i], APP[cur]], [APP[nxt]])
                        fw.cp("act", Bn[:], v4(psb[i2][:, :]), [PSB[i2]], [BnB])
                        yield
                B5, B5B = Bb[1]
                i = nb(q)
                for h in range(4):
                    fw.mm(psb[i][:, h * 128:(h + 1) * 128], B5[:, h, :], AP[1][:, h, 128:256], h == 0, True, [B5B, APP[1]], [PSB[i]])
                fw.tt("dve", AP[0][:, :, 128:256], AP[1][:, :, 128:256], v4(psb[i][:, :]), ALU.add, [PSB[i], APP[1]], [APP[0]])
                yield
                Ptf = AP[0]
                PTF_ = APP[0]
                i = nb(q)
                for h in range(4):
                    fw.mm(psb[i][:, h * 128:(h + 1) * 128], Ptf[:, h, 128:256], q.vb[:, h, :], h == 0, True, [PTF_, q.VB_], [PSB[i]])
                fw.cp("act", q.u[:], v4(psb[i][:, :]), [PSB[i]], [q.U_])
                i = nb(q)
                for h in range(4):
                    fw.mm(psb[i][:, h * 128:(h + 1) * 128], q.kbg[:, h, :], Ptf[:, h, 128:256], h == 0, True, [PTF_, q.KBG_], [PSB[i]])
                fw.cp("dve", q.wT[:], v4(psb[i][:, :]), [PSB[i]], [q.WT_])
                yield

            def quad_steps(t, d, with_out, par):
                q = qview(d, par)
                c0 = d * 4
                tsl = slice(t * 128, (t + 1) * 128)
                grp = tok_group(t)
                KB = [rawbuf("K", h, grp) for h in range(4)]
                VB = [rawbuf("V", h, grp) for h in range(4)]
                QB = [rawbuf("Q", h, grp) for h in range(4)] if with_out else []
                for c in ((0, 1) if d == 0 else (1, 0)):
                    R = slice(64 * c, 64 * c + 64)
                    i = nb(q)
                    for h in range(4):
                        fw.mm(psb[i][:, h * 128:(h + 1) * 128], q.wT[:, h, :], q.Sbf[:, h, :], h == 0, True, [q.WT_, q.SBF_], [PSB[i]])
                    fw.tt("dve", q.vnew[R, :, :], q.u[R, :, :], v4(psb[i][R, :]), ALU.subtract, [PSB[i], q.U_], [q.VNEW_])
                    yield
                    if with_out:
                        i = nb(q)
                        for h in range(4):
                            fw.mm(psb[i][:, h * 128:(h + 1) * 128], rawQ[:, h, tsl], q.Sbf[:, h, :], h == 0, True,
                                  [QB[h], q.SBF_], [PSB[i]])
                        fw.tt("dve", q.tmp[R, :, :], v4(psb[i][R, :]), bc4(egc[R, t, c0:c0 + 4]), ALU.mult, [PSB[i], EGC], [q.TMP_])
                        i = nb(q)
                        for h in range(4):
                            fw.mm(psb[i][:, h * 128:(h + 1) * 128], q.AT[R, h, :], q.vnew[R, h, :], h == 0, True,
                                  [q.AT_, q.VNEW_], [PSB[i]])
                        fw.tt("dve", q.tmp[R, :, :], q.tmp[R, :, :], v4(psb[i][R, :]), ALU.add, [PSB[i], q.TMP_], [q.TMP_])
                        fw.tt("pool", Oacc[R, t, :], Oacc[R, t, :], q.tmp[R, :, :].rearrange("p a b -> p (a b)"), ALU.add,
                              [q.TMP_, OACC[t]], [OACC[t]])
                        yield
                    i = nb(q)
                    for h in range(4):
                        fw.mm(psb[i][:, h * 128:(h + 1) * 128], q.kdec[R, h, :], q.vnew[R, h, :], h == 0, True,
                              [q.KDEC_, q.VNEW_], [PSB[i]])
                    dlv = dl[:, t, :].rearrange("p (a b) -> p a b", b=2)[:, c0:c0 + 4, c]
                    fw.tt("dve", q.S[:], q.S[:], bc4(dlv), ALU.mult, [q.S_, DL], [q.S_])
                    fw.tt("dve", q.S[:], q.S[:], v4(psb[i][:, :]), ALU.add, [PSB[i], q.S_], [q.S_])
                    fw.cp("act", q.Sbf[:], q.S[:], [q.S_], [q.SBF_])
                    yield

            def chain(tiles, d):
                pre = quad_pre(tiles[0], d, tiles[0] < NT_EXT, 0)
                yield from pre
                for k, t in enumerate(tiles):
                    gens = [quad_steps(t, d, t < NT_EXT, k % 2)]
                    if k + 1 < len(tiles):
                        gens.append(quad_pre(tiles[k + 1], d, tiles[k + 1] < NT_EXT, (k + 1) % 2))
                    while gens:
                        for g_ in list(gens):
                            try:
                                next(g_)
                                yield
                            except StopIteration:
                                gens.remove(g_)

            nq = int(os.environ.get("GDN_NQ", "999"))
            chF = chain(([32, 33] + list(range(0, NT_EXT)))[:nq], 0)
            chB = chain(([33, 32] + list(range(31, -1, -1)))[:nq], 1)
            alive = [chF, chB]
            if os.environ.get("GDN_ONLY"):
                alive = [chF] if os.environ["GDN_ONLY"] == "F" else [chB]
            nst = 0
            while alive:
                for g_ in list(alive):
                    try:
                        next(g_)
                        nst += 1
                        chk("qs%d" % nst)
                    except StopIteration:
                        alive.remove(g_)
            fw.barrier()
            if stop_after == 4:
                dump("Oacc", Oacc[:], [128, NT_EXT, 512], OACC[0], BF16)
                dump("S0", sets[0].S[:], [128, 4, 128], sets[0].S_)
                dump("S1", sets[1].S[:], [128, 4, 128], sets[1].S_)
                fw.finish()
                return nc, dbg_outs
        gdn.close()

        att = ExitStack()
        st.enter_context(att)
        KTa = sb(att, "KTa", [128, TOK_ALL], BF16); KTA = [Buf("kta%d" % g) for g in range(9)]
        Va = sb(att, "Va", [128, NT_ALL, 2, 65], BF16); VA = [Buf("va%d" % t) for t in range(NT_ALL)]
        QTa = sb(att, "QTa", [128, 4, TOK_EXT], BF16); QTA = [Buf("qta%d" % t) for t in range(NT_EXT)]
        zs = sb(att, "zs", [128, NT_EXT, 512], BF16); ZS = [Buf("zs%d" % t) for t in range(NT_EXT)]
        fw.op("pool", lambda e: e.memset(Va[:], 1.0), [], VA)
        with ExitStack() as p2:
            rope_sb = sb(p2, "rope_sb", [128, NT_LAT, 64], F32); ROPE = Buf("rope")
            fw.ld(rope_sb[:], rope_cs.rearrange("(t p) c -> p t c", p=128), [ROPE])
            wa = sb(p2, "wa", [128, 8, 1280], BF16); WA = Buf("wa")
            load_weight_bf16(p2, "wa", w_in_a.rearrange("(k p) n -> p k n", p=128), 1280, wa, WA)
            nctx = make_norm(p2, "b")
            hT = [sb(p2, "hTb%d" % i, [128, 8, 512], BF16) for i in range(2)]
            HT = [Buf("hTb0"), Buf("hTb1")]
            qsq_ = [sb(p2, "qsq%d" % i, [128, 640], F32) for i in range(2)]; QSQ_ = [Buf("qsq0"), Buf("qsq1")]
            qst_ = [sb(p2, "qst%d" % i, [128, 32], F32) for i in range(2)]; QST_ = [Buf("qst0"), Buf("qst1")]
            qn_ = [sb(p2, "qn%d" % i, [128, 640], F32) for i in range(2)]; QN_ = [Buf("qn0"), Buf("qn1")]
            rt_ = [[sb(p2, "rt%d_%d" % (k, i), [128, 10, 32], F32) for i in range(4)] for k in range(2)]
            RT_ = [[Buf("rt%d_%d" % (k, i)) for i in range(4)] for k in range(2)]
            qr_ = [sb(p2, "qr%d" % i, [128, 640], BF16) for i in range(2)]; QR_ = [Buf("qr0"), Buf("qr1")]
            kg2 = sb(p2, "kg2", [128, 640], F32); KG2 = Buf("kg2")
            fw.cp("dve", kg2[:, 0:512], gq8[:], [GQ8], [KG2])
            for hh in range(2):
                fw.cp("dve", kg2[:, 512 + hh * 64:576 + hh * 64], kg_bc, [VEC], [KG2])
            allg = groups_ext + groups_oth

            def prep2(gi):
                grp = allg[gi]
                load_norm_group(nctx, grp)
                transpose_mod(nctx, len(grp), lambda kc, T, s=gi % 2: hT[s][:, kc, 0:T], HT[gi % 2],
                              16 if grp[0] >= NT_LAT else 0)
            prep2(0)
            tiles2 = [(gi, i_, t) for gi, grp in enumerate(allg) for i_, t in enumerate(grp)]

            def mm2(k):
                gi, i_, t = tiles2[k]
                s = gi % 2
                bq, bkv, bz = (0, 1, 2) if k % 2 == 0 else (3, 4, 5)
                lt = hT[s][:, :, i_ * 128:(i_ + 1) * 128]
                is_ext = t < NT_EXT
                if is_ext:
                    for kc in range(8):
                        fw.mm(psb[bq][:, :], lt[:, kc, :], wa[:, kc, 0:512], kc == 0, kc == 7, [WA, HT[s]], [PSB[bq]])
                    for kc in range(8):
                        fw.mm(psb[bz][:, :], lt[:, kc, :], wa[:, kc, 768:1280], kc == 0, kc == 7, [WA, HT[s]], [PSB[bz]])
                for kc in range(8):
                    fw.mm(psb[bkv][:, 0:256], lt[:, kc, :], wa[:, kc, 512:768], kc == 0, kc == 7, [WA, HT[s]], [PSB[bkv]])

            def post2(k):
                gi, i_, t = tiles2[k]
                bq, bkv, bz = (0, 1, 2) if k % 2 == 0 else (3, 4, 5)
                is_ext = t < NT_EXT
                is_ctx = t >= NT_LAT
                kk = k % 2
                qsq, QSQ, qst, QST, qn, QN, rt, RT, qr, QR = (qsq_[kk], QSQ_[kk], qst_[kk], QST_[kk], qn_[kk], QN_[kk],
                                                              rt_[kk], RT_[kk], qr_[kk], QR_[kk])
                if True:
                    hoff = 0 if is_ext else 8
                    if is_ext:
                        fw.act(zs[:, t, :], psb[bz][:, :], AF.Silu, [PSB[bz]], [ZS[t]])
                    fw.cp("act", Va[:, t, :, 0:64], psb[bkv][:, 128:256].rearrange("p (a b) -> p a b", a=2), [PSB[bkv]], [VA[t]])
                    if is_ext:
                        fw.cp("dve", qn[:, 0:512], psb[bq][:, :], [PSB[bq]], [QN])
                    fw.cp("dve", qn[:, 512:640], psb[bkv][:, 0:128], [PSB[bkv]], [QN])
                    yield
                    c_lo, c_hi = hoff * 64, 640
                    fw.tt("dve", qsq[:, c_lo:c_hi], qn[:, c_lo:c_hi], qn[:, c_lo:c_hi], ALU.mult, [QN], [QSQ])
                    yield
                    fw.op("dve", lambda e, hoff=hoff: e.tensor_reduce(
                        out=qst[:, hoff:10], in_=qsq[:, hoff * 64:640].rearrange("p (a b) -> p a b", b=64),
                        axis=AX.X, op=ALU.add), [QSQ], [QST])
                    yield
                    fw.act(qst[:, 10 + hoff:20], qst[:, hoff:10], AF.Sqrt, [QST, CST], [QST], bias=eps_ap, scale=1.0 / 64)
                    yield
                    fw.op("dve", lambda e, hoff=hoff: e.reciprocal(out=qst[:, 20 + hoff:30], in_=qst[:, 10 + hoff:20]), [QST], [QST])
                    yield
                    n_h = 10 - hoff
                    qv = qn[:, c_lo:c_hi].rearrange("p (a b) -> p a b", b=64)
                    fw.tt("dve", qv, qv, qst[:, 20 + hoff:30].unsqueeze(2).to_broadcast([128, n_h, 64]), ALU.mult, [QN, QST], [QN])
                    yield
                    fw.tt("dve", qn[:, c_lo:c_hi], qn[:, c_lo:c_hi], kg2[:, c_lo:c_hi], ALU.mult, [QN, KG2], [QN])
                    yield
                    qrv = qr[:, c_lo:c_hi].rearrange("p (a b) -> p a b", b=64)
                    if not is_ctx:
                        cosb = rope_sb[:, t, 0:32].unsqueeze(1).to_broadcast([128, n_h, 32])
                        sinb = rope_sb[:, t, 32:64].unsqueeze(1).to_broadcast([128, n_h, 32])
                        x1_, x2_ = qv[:, :, 0:32], qv[:, :, 32:64]
                        r0, r1, r2, r3 = (rt[j][:, hoff:10, :] for j in range(4))
                        fw.tt("dve", r0, x1_, cosb, ALU.mult, [QN, ROPE], [RT[0]])
                        fw.tt("dve", r1, x2_, sinb, ALU.mult, [QN, ROPE], [RT[1]])
                        yield
                        fw.tt("dve", r2, x2_, cosb, ALU.mult, [QN, ROPE], [RT[2]])
                        fw.tt("dve", r3, x1_, sinb, ALU.mult, [QN, ROPE], [RT[3]])
                        yield
                        fw.tt("dve", qrv[:, :, 0:32], r0, r1, ALU.subtract, [RT[0], RT[1]], [QR])
                        fw.tt("dve", qrv[:, :, 32:64], r2, r3, ALU.add, [RT[2], RT[3]], [QR])
                        yield
                    else:
                        fw.cp("dve", qr[:, c_lo:c_hi], qn[:, c_lo:c_hi], [QN], [QR])
                    pt, PT = next_pst()
                    fw.tr(pt[:, 0:128], qr[:, 512:640], ident_b[:], [QR, IDB], [PT])
                    if is_ext:
                        for j in range(4):
                            fw.tr(pt[:, 128 + j * 128:256 + j * 128], qr[:, j * 128:(j + 1) * 128], ident_b[:], [QR, IDB], [PT])
                    fw.cp("act", KTa[:, t * 128:(t + 1) * 128], pt[:, 0:128], [PT], [KTA[t // 4]])
                    if is_ext:
                        fw.cp("dve", QTa[:, :, t * 128:(t + 1) * 128], pt[:, 128:640].rearrange("p (a b) -> p a b", a=4),
                              [PT], [QTA[t]])

            prepped = {0}

            def ensure_prep(k):
                gi = tiles2[k][0]
                for g_ in range(gi + 2):
                    if g_ < len(allg) and g_ not in prepped and g_ <= gi + 1:
                        prep2(g_)
                        prepped.add(g_)
            ensure_prep(0)
            mm2(0)
            mm2(1)
            nt2 = len(tiles2)
            for p_ in range(0, nt2, 2):
                gens = [post2(k) for k in (p_, p_ + 1) if k < nt2]
                for g_ in gens:
                    next(g_)
                for k in (p_ + 2, p_ + 3):
                    if k < nt2:
                        ensure_prep(k)
                        mm2(k)
                alive = list(gens)
                while alive:
                    for g_ in list(alive):
                        try:
                            next(g_)
                        except StopIteration:
                            alive.remove(g_)
            fw.barrier()
        wo = sb(att, "wo", [128, 8, 1024], BF16); WO = Buf("wo")
        with ExitStack() as p2w:
            load_weight_bf16(p2w, "wo", w_out.rearrange("(k p) n -> p k n", p=128), 1024, wo, WO)
        if stop_after == 5:
            dump("KTa", KTa[:], [128, TOK_ALL], KTA[0], BF16)
            dump("QTa", QTa[:], [128, 4, TOK_EXT], QTA[0], BF16)
            dump("Va", Va[:], [128, NT_ALL, 2, 65], VA[0], BF16)
            dump("zs", zs[:], [128, NT_EXT, 512], ZS[0], BF16)
            fw.finish()
            return nc, dbg_outs

        with ExitStack() as p3:
            pT = [sb(p3, "pT%d" % i, [128, 512], BF16) for i in range(4)]
            PTB = [Buf("pT%d" % i) for i in range(4)]
            oa = sb(p3, "oa", [128, 512], F32); OA = Buf("oa")
            oT = [sb(p3, "oT%d" % i, [65, 512], F32) for i in range(2)]
            OTB = [Buf("oT0"), Buf("oT1")]
            og = sb(p3, "og", [128, 512], F32); OG = Buf("og")
            ost = sb(p3, "ost", [128, 32], F32); OST = Buf("ost")
            ojunk = sb(p3, "ojunk", [128, 512], BF16); OJ = Buf("ojunk")
            mix = sb(p3, "mix", [128, 1024], BF16); MIX = Buf("mix")
            mixT = sb(p3, "mixT", [128, 8, 128], BF16); MIXT = Buf("mixT")
            xres = [sb(p3, "xres%d" % i, [128, D], F32) for i in range(2)]
            XRES = [Buf("xres0"), Buf("xres1")]
            x1t = [sb(p3, "x1t%d" % i, [128, D], F32) for i in range(2)]
            X1T = [Buf("x1t0"), Buf("x1t1")]
            nctx = make_norm(p3, "c")
            h2stage = sb(p3, "h2stage", [128, 8, 128], BF16); H2S = Buf("h2stage")
            h2d = nc.dram_tensor("h2d", [128, 8, NT_OWN * 128 + 128], BF16, kind="Internal").ap()
            ob = (6, 7)
            obT = (4, 5)
            items = [(qt, kt, g) for qt in range(NT_EXT) for kt in range(NT_ALL) for g in range(2)]
            LOOK = 3

            def emit_qk(n):
                qt, kt, g = items[n]
                bi = n % 4
                P0, P1 = 64 * g, 64 * g + 64
                fw.mm(psb[bi][:, :], KTa[P0:P1, kt * 128:(kt + 1) * 128], QTa[P0:P1, :, qt * 128:(qt + 1) * 128],
                      True, True, [KTA[kt // 4], QTA[qt]], [PSB[bi]])

            def emit_pv(n):
                qt, kt, g = items[n]
                bi = n % 4
                pi = n % 4
                fw.act(pT[pi][:], psb[bi][:, :], AF.Exp, [PSB[bi]], [PTB[pi]])
                fw.mm(psb[obT[g]][0:65, :], Va[:, kt, g, :], pT[pi][:], kt == 0, kt == NT_ALL - 1,
                      [PTB[pi], VA[kt]], [PSB[obT[g]]])
            def epilogue(qt):
                for g in range(2):
                    fw.cpa(oT[g][:], psb[obT[g]][0:65, :], [PSB[obT[g]]], [OTB[g]])
                yield
                yield
                for g in range(2):
                    for j in range(4):
                        fw.tr(psb[ob[g]][:, j * 65:(j + 1) * 65], oT[g][:, j * 128:(j + 1) * 128], ident_f[0:65, 0:65],
                              [OTB[g], CM], [PSB[ob[g]]])
                    yield
                for g in range(2):
                    ov = psb[ob[g]][:, 0:260].rearrange("p (a b) -> p a b", b=65)
                    fw.op("dve", lambda e, ov=ov, g=g: e.reciprocal(out=ost[:, g * 4:g * 4 + 4], in_=ov[:, :, 64]), [PSB[ob[g]]], [OST])
                    fw.tt("dve", oa[:, g * 256:(g + 1) * 256].rearrange("p (a b) -> p a b", b=64), ov[:, :, 0:64],
                          ost[:, g * 4:g * 4 + 4].unsqueeze(2).to_broadcast([128, 4, 64]), ALU.mult, [PSB[ob[g]], OST], [OA])
                yield
                fw.act(ojunk[:], oa[:], AF.Square, [OA], [OJ, OST], accum=ost[:, 8:9])
                yield
                fw.act(ost[:, 9:10], ost[:, 8:9], AF.Sqrt, [OST, CST], [OST], bias=eps_ap, scale=1.0 / 512)
                yield
                fw.op("dve", lambda e: e.reciprocal(out=ost[:, 10:11], in_=ost[:, 9:10]), [OST], [OST])
                yield
                fw.stt("dve", mix[:, 0:512], oa[:], ost[:, 10:11], aog_bc, ALU.mult, ALU.mult, [OA, OST, VEC], [MIX])
                yield
                fw.tt("pool", og[:], Oacc[:, qt, :], Oacc[:, qt, :], ALU.mult, [OACC[qt]], [OG])
                yield
                fw.op("dve", lambda e: e.tensor_reduce(out=ost[:, 12:16], in_=og[:].rearrange("p (a b) -> p a b", b=128),
                                                       axis=AX.X, op=ALU.add), [OG], [OST])
                yield
                fw.act(ost[:, 16:20], ost[:, 12:16], AF.Sqrt, [OST, CST], [OST], bias=cst[:, 2:3], scale=1.0 / 128)
                yield
                fw.op("dve", lambda e: e.reciprocal(out=ost[:, 20:24], in_=ost[:, 16:20]), [OST], [OST])
                yield
                ogv = og[:].rearrange("p (a b) -> p a b", b=128)
                fw.tt("dve", ogv, Oacc[:, qt, :].rearrange("p (a b) -> p a b", b=128),
                      ost[:, 20:24].unsqueeze(2).to_broadcast([128, 4, 128]), ALU.mult, [OACC[qt], OST], [OG])
                yield
                fw.tt("pool", ogv, ogv, gng_bc.unsqueeze(1).to_broadcast([128, 4, 128]), ALU.mult, [OG, VEC], [OG])
                yield
                fw.tt("dve", mix[:, 512:1024], og[:], zs[:, qt, :], ALU.mult, [OG, ZS[qt]], [MIX])
                yield
                pt, PT = next_pst()
                for kc in range(8):
                    fw.tr(pt[:, kc * 128:(kc + 1) * 128], mix[:, kc * 128:(kc + 1) * 128], ident_b[:], [MIX, IDB], [PT])
                yield
                fw.cpa(mixT[:], pt[:, 0:1024].rearrange("p (a b) -> p a b", a=8), [PT], [MIXT])
                yield
                s_ = qt % 2
                fw.ld(xres[s_][:], xs[qt * 128:(qt + 1) * 128, :], [XRES[s_]])
                for half in range(2):
                    bi = 6 + half
                    for kc in range(8):
                        fw.mm(psb[bi][:, :], mixT[:, kc, :], wo[:, kc, half * 512:(half + 1) * 512], kc == 0, kc == 7,
                              [MIXT, WO], [PSB[bi]])
                    yield
                    fw.tt("dve", x1t[s_][:, half * 512:(half + 1) * 512], psb[bi][:, :], G12[:, 0, half * 512:(half + 1) * 512],
                          ALU.mult, [PSB[bi], G12B], [X1T[s_]])
                    yield
                    fw.tt("pool", x1t[s_][:, half * 512:(half + 1) * 512], x1t[s_][:, half * 512:(half + 1) * 512],
                          xres[s_][:, half * 512:(half + 1) * 512], ALU.add, [X1T[s_], XRES[s_]], [X1T[s_]])
                if qt < NT_OWN:
                    fw.stor(x1s[qt * 128:(qt + 1) * 128, :], x1t[s_][:], [X1T[s_]])
                yield
                norm_rows(nctx, x1t[s_][:], X1T[s_], 0)
                yield
                transpose_mod(nctx, 1, lambda kc, T: h2stage[:, kc, 0:T], H2S, 32)
                fw.stor(h2d[:, :, qt * 128:(qt + 1) * 128], h2stage[:], [H2S])

            pend = [None]

            def step_pending():
                if pend[0] is not None:
                    try:
                        next(pend[0])
                    except StopIteration:
                        pend[0] = None

            def flush_pending():
                while pend[0] is not None:
                    step_pending()
            emit_qk(0)
            emit_qk(1)
            for n in range(len(items)):
                if n % 2 == 0 and n + 2 < len(items):
                    emit_qk(n + 2)
                    emit_qk(n + 3)
                emit_pv(n)
                step_pending()
                qt, kt, g = items[n]
                if kt == NT_ALL - 1 and g == 1:
                    flush_pending()
                    pend[0] = epilogue(qt)
                    step_pending()
            flush_pending()
            fw.barrier()
        att.close()
        mixer.close()

        with ExitStack() as p4:
            NTOK = NT_OWN * 128
            actT = sb(p4, "actT", [128, 22, NTOK], BF16); ACTT = [Buf("actT%d" % c) for c in range(22)]
            with ExitStack() as p4a:
                h2T = sb(p4a, "h2T", [128, 8, NTOK + 128], BF16); H2T = Buf("h2T")
                for kc in range(8):
                    fw.ld(h2T[:, kc, :], h2d[:, kc, :], [H2T])
                HALF = NTOK // 2
                ub = [sb(p4a, "ubuf%d" % i, [128, HALF + 4], F32) for i in range(2)]
                UB = [Buf("ubuf0"), Buf("ubuf1")]
                ucb = [sb(p4a, "uc%d" % i, [128, HALF], F32) for i in range(2)]
                UC = [Buf("uc0"), Buf("uc1")]
                sg = sb(p4a, "sg", [128, NTOK], BF16); SG = [Buf("sg0"), Buf("sg1")]
                wst = [sb(p4a, "wust%d" % i, [128, 8, 256], F32) for i in range(2)]
                WST = [Buf("wust0"), Buf("wust1")]
                wu = [sb(p4a, "wu%d" % i, [128, 8, 256], BF16) for i in range(2)]
                WU = [Buf("wu0"), Buf("wu1")]
                fw.op("pool", lambda e: e.memset(ub[0][:, 0:1], 0.0), [], [UB[0]])
                wupv = w_up.rearrange("(k p) n -> p k n", p=128)
                bcnt = 0

                def load_w(c):
                    s_ = c % 2
                    fw.ld(wst[s_][:, :, 0:128], wupv[:, :, c * 128:(c + 1) * 128], [WST[s_]])
                    fw.ld(wst[s_][:, :, 128:256], wupv[:, :, (22 + c) * 128:(23 + c) * 128], [WST[s_]])
                    fw.cp("pool", wu[s_][:], wst[s_][:], [WST[s_]], [WU[s_]])
                load_w(0)
                for c in range(22):
                    s_ = c % 2
                    if c + 1 < 22:
                        load_w(c + 1)
                    for part in range(2):
                        ch = c + 22 * part
                        w0, w1, w2, bb = (fcw_sb[:, ch * 4 + j:ch * 4 + j + 1] for j in range(4))
                        for hf in range(2):
                            tok_lo, col_lo, ntk = (0, 1, HALF + 1) if hf == 0 else (HALF - 1, 0, HALF + 2)
                            for o in range(0, ntk, 512):
                                n = min(512, ntk - o)
                                bi = bcnt % 6
                                bcnt += 1
                                for kc in range(8):
                                    fw.mm(psb[bi][:, 0:n], wu[s_][:, kc, part * 128:(part + 1) * 128],
                                          h2T[:, kc, tok_lo + o:tok_lo + o + n], kc == 0, kc == 7, [WU[s_], H2T], [PSB[bi]])
                                fw.cp("act", ub[hf][:, col_lo + o:col_lo + o + n], psb[bi][:, 0:n], [PSB[bi]], [UB[hf]])
                                j0 = max(0, 1 - (col_lo + o))
                                j1 = min(n, HALF + 1 - (col_lo + o))
                                if j1 > j0:
                                    fw.act(ucb[hf][:, col_lo + o + j0 - 1:col_lo + o + j1 - 1], psb[bi][:, j0:j1], AF.Identity,
                                           [PSB[bi], FCW], [UC[hf]], bias=bb, scale=w1)
                            fw.stt("dve", ucb[hf][:], ub[hf][:, 0:HALF], w0, ucb[hf][:], ALU.mult, ALU.add, [UB[hf], FCW, UC[hf]], [UC[hf]])
                            fw.stt("dve", ucb[hf][:], ub[hf][:, 2:HALF + 2], w2, ucb[hf][:], ALU.mult, ALU.add, [UB[hf], FCW, UC[hf]], [UC[hf]])
                            hs = slice(hf * HALF, (hf + 1) * HALF)
                            if part == 0:
                                fw.act(sg[:, hs], ucb[hf][:], AF.Silu, [UC[hf]], [SG[hf]])
                            else:
                                fw.tt("pool", actT[:, c, hs], ucb[hf][:], sg[:, hs], ALU.mult, [UC[hf], SG[hf]], [ACTT[c]])
                fw.barrier()
            if stop_after == 7:
                dump("actT", actT[:], [128, 22, NTOK], ACTT[0], BF16)
                fw.finish()
                return nc, dbg_outs
            wd = sb(p4, "wd", [128, 22, 1024], BF16); WD = Buf("wd")
            load_weight_bf16(p4, "wd", w_down.rearrange("(c p) n -> p c n", p=128), 1024, wd, WD, piece=128)
            x1r = [sb(p4, "x1r%d" % i, [128, D], F32) for i in range(2)]
            X1R = [Buf("x1r0"), Buf("x1r1")]
            x2 = [sb(p4, "x2_%d" % i, [128, D], F32) for i in range(2)]
            X2 = [Buf("x2_0"), Buf("x2_1")]
            fj = sb(p4, "fjunk", [128, D], BF16); FJ = Buf("fjunk")
            fst = sb(p4, "fst", [128, 8], F32); FST = Buf("fst")
            def mm_down(t):
                s_ = t % 2
                fw.ld(x1r[s_][:], x1s[t * 128:(t + 1) * 128, :], [X1R[s_]])
                for half in range(2):
                    bi = 2 * s_ + half
                    for c in range(22):
                        fw.mm(psb[bi][:, :], actT[:, c, t * 128:(t + 1) * 128], wd[:, c, half * 512:(half + 1) * 512],
                              c == 0, c == 21, [ACTT[c], WD], [PSB[bi]])

            def post_down(t):
                s_ = t % 2
                for half in range(2):
                    bi = 2 * s_ + half
                    hs = slice(half * 512, (half + 1) * 512)
                    fw.tt("dve", x2[s_][:, hs], psb[bi][:, :], G12[:, 1, hs], ALU.mult, [PSB[bi], G12B], [X2[s_]])
                    fw.tt("pool", x2[s_][:, hs], x2[s_][:, hs], x1r[s_][:, hs], ALU.add, [X2[s_], X1R[s_]], [X2[s_]])
                fw.act(fj[:], x2[s_][:], AF.Square, [X2[s_]], [FJ, FST], accum=fst[:, 4 * s_:4 * s_ + 1])
                fw.act(fst[:, 4 * s_ + 1:4 * s_ + 2], fst[:, 4 * s_:4 * s_ + 1], AF.Sqrt, [FST, CST], [FST], bias=eps_ap, scale=1.0 / D)
                fw.op("dve", lambda e: e.reciprocal(out=fst[:, 4 * s_ + 2:4 * s_ + 3], in_=fst[:, 4 * s_ + 1:4 * s_ + 2]), [FST], [FST])
                fw.stt("dve", x2[s_][:], x2[s_][:], fst[:, 4 * s_ + 2:4 * s_ + 3], fng_bc, ALU.mult, ALU.mult, [X2[s_], FST, VEC], [X2[s_]])
                fw.stor(y[t * 128:(t + 1) * 128, :], x2[s_][:], [X2[s_]])
            mm_down(0)
            for t in range(NT_OWN):
                if t + 1 < NT_OWN:
                    mm_down(t + 1)
                post_down(t)
            fw.barrier()
        fw.finish()
        return nc, dbg_outs


def _prep_inputs(inputs):
    f = np.float32
    x = np.asarray(inputs["x"], f)
    c = np.asarray(inputs["c"], f)
    ctx = np.asarray(inputs["ctx"], f)
    c_ctx = np.asarray(inputs["c_ctx"], f)
    w_in = np.asarray(inputs["w_in"], f)[0]

    def col(v):
        return np.ascontiguousarray(v.reshape(-1, 128).T)

    q_a = w_in[:, 0:512].reshape(D, 8, 64)
    order = [0, 4, 1, 5, 2, 6, 3, 7]
    q_perm = q_a[:, order, :].reshape(D, 512)
    k_a = w_in[:, 512:640]
    v_a = w_in[:, 640:768]
    gq = w_in[:, 768:1280]
    gk = w_in[:, 1280:1792]
    gv = w_in[:, 1792:2304]
    z = w_in[:, 2304:2816]
    a_f, a_b, b_f, b_b = (w_in[:, 2816 + 4 * i:2820 + 4 * i] for i in range(4))
    w_in_a = np.ascontiguousarray(np.concatenate([q_perm, k_a, v_a, z], axis=1))
    convq = np.asarray(inputs["conv_qkv_w"], f)[0]
    ffw = np.asarray(inputs["ffn_conv_w"], f)[0]
    ffb = np.asarray(inputs["ffn_conv_b"], f)[0]

    idx = np.arange(128)
    same = (idx[:, None] // 64) == (idx[None, :] // 64)
    ident = np.eye(128, dtype=f)
    ind = np.stack([(idx < 64), (idx >= 64)], axis=1).astype(f)
    mF = np.concatenate([(same & (idx[:, None] <= idx[None, :])).astype(f), ind], axis=1)
    mB = np.concatenate([(same & (idx[:, None] >= idx[None, :])).astype(f), ind], axis=1)
    blk = same.astype(f)
    negF = np.where(same & (idx[:, None] <= idx[None, :]), 0.0, -BIG).astype(f)
    posF = np.where(same & (idx[None, :] < idx[:, None]), 0.0, BIG).astype(f)
    negB = np.where(same & (idx[:, None] >= idx[None, :]), 0.0, -BIG).astype(f)
    posB = np.where(same & (idx[None, :] > idx[:, None]), 0.0, BIG).astype(f)
    cmat = np.ascontiguousarray(np.concatenate([ident, mF, mB, blk, negF, posF, negB, posB], axis=1))

    rows = SEQ // 64
    row = np.repeat(np.arange(rows, dtype=f), 64)
    colp = np.tile(np.arange(64, dtype=f), rows)
    inv_freq = (10000.0 ** (-np.arange(16, dtype=f) / 16)).astype(f)
    ang = np.concatenate([row[:, None] * inv_freq, colp[:, None] * inv_freq], axis=-1).astype(f)
    rope = np.concatenate([np.cos(ang), np.sin(ang)], axis=1).astype(f)

    shared = dict(
        w_mod=np.ascontiguousarray(np.asarray(inputs["w_mod"], f)[0]),
        bmod_col=col(np.asarray(inputs["b_mod"], f)[0]),
        bmod_row=np.ascontiguousarray(np.asarray(inputs["b_mod"], f)[0][None, :]),
        ng_col=np.ascontiguousarray(np.concatenate([col(np.asarray(inputs["norm1_g"], f)[0]),
                                                    col(np.asarray(inputs["norm2_g"], f)[0])], axis=1)),
        w_in_a=w_in_a,
        w_out=np.ascontiguousarray(np.asarray(inputs["w_out"], f)[0]),
        w_up=np.ascontiguousarray(np.asarray(inputs["w_up"], f)[0]),
        w_down=np.ascontiguousarray(np.asarray(inputs["w_down"], f)[0]),
        cmat=cmat,
    )
    in_maps = []
    for r in range(8):
        b, flip = r // 2, (r % 2 == 1)
        m = dict(shared)
        xb, cb, rp = x[b], ctx[b], rope
        if flip:
            xb, cb, rp = xb[::-1], cb[::-1], rp[::-1]
        m["xs"] = np.ascontiguousarray(xb)
        m["cs"] = np.ascontiguousarray(cb)
        m["rope_cs"] = np.ascontiguousarray(rp)
        m["ccol"] = np.ascontiguousarray(np.concatenate([col(c[b]), col(c_ctx)], axis=1))
        aF, aB, bF, bB = (a_b, a_f, b_b, b_f) if flip else (a_f, a_b, b_f, b_b)
        m["w_in_g"] = np.ascontiguousarray(np.concatenate([gq, gk, gv, aF, aB, bF, bB], axis=1))
        taps = [2, 1, 0] if flip else [0, 1, 2]
        cwq = convq[taps]
        m["convw"] = np.ascontiguousarray(cwq.reshape(3, 12, 128).transpose(2, 1, 0).reshape(128, 36))
        fw_ = ffw[taps].reshape(3, NFC, 128)
        fb_ = ffb.reshape(1, NFC, 128)
        m["fcw"] = np.ascontiguousarray(np.concatenate([fw_, fb_], axis=0).transpose(2, 1, 0).reshape(128, NFC * 4))
        al = [np.asarray(inputs[k], f)[0] for k in ("a_log_f", "a_log_b", "dt_bias_f", "dt_bias_b")]
        if flip:
            al = [al[1], al[0], al[3], al[2]]
        m["vecs"] = np.ascontiguousarray(np.concatenate([
            np.asarray(inputs["q_norm_g"], f)[0], np.asarray(inputs["k_norm_g"], f)[0],
            np.asarray(inputs["attn_out_g"], f)[0], np.asarray(inputs["gdn_norm_g"], f)[0],
            al[0], al[1], al[2], al[3], np.asarray(inputs["final_norm_g"], f)])[None, :])
        in_maps.append(m)
    return in_maps


def kernel(**inputs):
    in_maps = _prep_inputs(inputs)
    if os.environ.get("KSTOP"):
        nc, _ = _build(dbg=True, stop_after=int(os.environ["KSTOP"]))
        run_bass_kernel_spmd(nc, in_maps, core_ids=list(range(8)))
        return np.zeros((4, SEQ, D), np.float32)
    nc, _ = _build()
    res = run_bass_kernel_spmd(nc, in_maps, core_ids=list(range(8)))
    out = np.empty((4, SEQ, D), np.float32)
    for r in range(8):
        yb = np.asarray(res.results[r]["y"], np.float32)
        b = r // 2
        if r % 2 == 0:
            out[b, 0:2048] = yb
        else:
            out[b, 2048:4096] = yb[::-1]
    return out
```

```python
import os
from contextlib import ExitStack
import numpy as np
import concourse.bass as bass
import concourse.mybir as mybir
from concourse.bass_utils import run_bass_kernel_spmd

F32 = mybir.dt.float32
BF16 = mybir.dt.bfloat16
AF = mybir.ActivationFunctionType
ALU = mybir.AluOpType
AX = mybir.AxisListType

D = 1024
SEQ = 4096
CTX = 256
NT_LAT = 32
NT_ALL = 34
NT_EXT = 17
NT_OWN = 16
TOK_ALL = NT_ALL * 128
TOK_EXT = NT_EXT * 128
DFF = 2816
NFC = 44
EPS = 1e-6
BIG = 30000.0


class Buf:
    __slots__ = ("name", "last_w", "readers", "dsem", "dcount", "excl")

    def __init__(self, name="b", excl=False):
        self.name = name
        self.excl = excl
        self.last_w = None
        self.readers = []
        self.dsem = None
        self.dcount = 0


class _Eng:
    def __init__(self, name):
        self.name = name
        self.count = 0
        self.waited = {}
        self.ops = []
        self.is_pe = name == "pe"


class FW:
    def __init__(self, nc, stack):
        self.nc = nc
        self.stack = stack
        self.engs = {n: _Eng(n) for n in ("pe", "act", "dve", "pool", "sp")}
        self.sems = {}
        for n in self.engs:
            self.sems[n] = stack.enter_context(nc.semaphore("s_" + n))
        self.nd = 0
        self._dma_tot = {}
        self.free_dsems = []
        self.rr = 0

    def _dma_sem(self, b):
        if b.dsem is None:
            key = "d%d" % self.nd
            self.nd += 1
            self.sems[key] = self.stack.enter_context(self.nc.semaphore("s_" + key))
            b.dsem = key
        return b.dsem

    def _deps(self, eng, reads, writes):
        deps = {}

        def add(ev):
            if ev is None:
                return
            k, v = ev
            if eng.is_pe and k == "pe":
                return
            if deps.get(k, 0) < v:
                deps[k] = v
        for b in reads:
            add(b.last_w)
            if b.excl:
                for r in b.readers:
                    if r[0] != eng.name:
                        add(r)
        for b in writes:
            add(b.last_w)
            for r in b.readers:
                add(r)
        waits = []
        for k, v in deps.items():
            if eng.waited.get(k, 0) < v:
                eng.waited[k] = v
                waits.append((k, v))
        return waits

    def op(self, engname, fn, reads=(), writes=()):
        eng = self.engs[engname]
        waits = self._deps(eng, reads, writes)
        eng.count += 1
        ev = (engname, eng.count)
        eng.ops.append((waits, fn, (engname, 1)))
        for b in reads:
            b.readers.append(ev)
        for b in writes:
            b.last_w = ev
            b.readers = []
        return ev

    def dma(self, fn, reads=(), writes=(), q="sp", track=None):
        eng = self.engs[q]
        waits = self._deps(eng, reads, writes)
        tb = track if track is not None else (writes[0] if writes else reads[0])
        key = self._dma_sem(tb)
        tb.dcount += 16
        ev = (key, tb.dcount)
        self._dma_tot[key] = tb.dcount
        eng.ops.append((waits, fn, (key, 16)))
        for b in reads:
            b.readers.append(ev)
        for b in writes:
            b.last_w = ev
            b.readers = []
        return ev

    def barrier(self):
        targets = {n: e.count for n, e in self.engs.items() if e.count > 0}
        for n, e in self.engs.items():
            waits = []
            for k, v in list(targets.items()) + list(self._dma_tot.items()):
                if k == n and e.is_pe:
                    continue
                if e.waited.get(k, 0) < v:
                    e.waited[k] = v
                    waits.append((k, v))
            if waits:
                e.ops.append((waits, None, None))

    def finish(self):
        self.barrier()
        nc = self.nc
        sems = self.sems

        def run(e, obj):
            for waits, fn, inc in e.ops:
                for k, v in waits:
                    obj.wait_ge(sems[k], v)
                if fn is not None:
                    ins = fn(obj)
                    ins.then_inc(sems[inc[0]], inc[1])

        with nc.Block() as block:
            @block.tensor
            def _(o):
                run(self.engs["pe"], o)

            @block.scalar
            def _(o):
                run(self.engs["act"], o)

            @block.vector
            def _(o):
                run(self.engs["dve"], o)

            @block.gpsimd
            def _(o):
                run(self.engs["pool"], o)

            @block.sync
            def _(o):
                run(self.engs["sp"], o)

    def mm(self, out, lhsT, rhs, start=True, stop=True, r=(), w=()):
        return self.op("pe", lambda e: e.matmul(out, lhsT=lhsT, rhs=rhs, start=start, stop=stop), r, w)

    def tr(self, out, in_, ident, r=(), w=()):
        return self.op("pe", lambda e: e.transpose(out=out, in_=in_, identity=ident), r, w)

    def act(self, out, in_, func, r=(), w=(), bias=None, scale=None, accum=None):
        kw = {}
        if bias is not None:
            kw["bias"] = bias
        if scale is not None:
            kw["scale"] = scale
        if accum is not None:
            kw["accum_out"] = accum
        return self.op("act", lambda e: e.activation(out=out, in_=in_, func=func, **kw), r, w)

    def ts(self, eng, out, in0, s1, s2, op0, op1=None, r=(), w=()):
        if op1 is None:
            return self.op(eng, lambda e: e.tensor_scalar(out=out, in0=in0, scalar1=s1, scalar2=None, op0=op0), r, w)
        return self.op(eng, lambda e: e.tensor_scalar(out=out, in0=in0, scalar1=s1, scalar2=s2, op0=op0, op1=op1), r, w)

    def tt(self, eng, out, in0, in1, op, r=(), w=()):
        return self.op(eng, lambda e: e.tensor_tensor(out=out, in0=in0, in1=in1, op=op), r, w)

    def stt(self, eng, out, in0, scalar, in1, op0, op1, r=(), w=()):
        return self.op(eng, lambda e: e.scalar_tensor_tensor(out=out, in0=in0, scalar=scalar, in1=in1, op0=op0, op1=op1), r, w)

    def cp(self, eng, out, in_, r=(), w=()):
        if eng == "act":
            return self.op("act", lambda e: e.copy(out=out, in_=in_), r, w)
        return self.op(eng, lambda e: e.tensor_copy(out=out, in_=in_), r, w)

    def cpa(self, out, in_, r=(), w=()):
        self.rr += 1
        return self.cp("dve" if self.rr % 2 else "act", out, in_, r, w)

    def ld(self, out, in_, w, q="sp", r=()):
        return self.dma(lambda e: e.dma_start(out=out, in_=in_), reads=r, writes=w, q=q)

    def stor(self, out, in_, r, track=None):
        return self.dma(lambda e: e.dma_start(out=out, in_=in_), reads=r, writes=(), track=track)


class _Stop(Exception):
    pass


def _build(dbg=False, stop_after=None):
    nc = bass.Bass("TRN2", target_bir_lowering=False)
    dbg_outs = {}
    try:
        return _build_inner(nc, dbg, stop_after, dbg_outs)
    except _Stop:
        return nc, dbg_outs


def _build_inner(nc, dbg, stop_after, dbg_outs):

    def din(name, shape):
        return nc.dram_tensor(name, list(shape), F32, kind="ExternalInput").ap()

    xs = din("xs", [SEQ, D])
    cs = din("cs", [CTX, D])
    ccol = din("ccol", [128, 16])
    w_mod = din("w_mod", [D, 6 * D])
    bmod_col = din("bmod_col", [128, 48])
    bmod_row = din("bmod_row", [1, 6 * D])
    ng_col = din("ng_col", [128, 16])
    w_in_g = din("w_in_g", [D, 1552])
    w_in_a = din("w_in_a", [D, 1280])
    convw = din("convw", [128, 36])
    rope_cs = din("rope_cs", [SEQ, 64])
    vecs = din("vecs", [1, 128 + 512 + 128 + 16 + 1024])
    w_out = din("w_out", [D, D])
    w_up = din("w_up", [D, 2 * DFF])
    fcw = din("fcw", [128, NFC * 4])
    w_down = din("w_down", [DFF, D])
    cmat = din("cmat", [128, 8 * 128 + 4])
    if stop_after is None:
        y = nc.dram_tensor("y", [NT_OWN * 128, D], F32, kind="ExternalOutput").ap()
        x1s = nc.dram_tensor("x1s", [NT_OWN * 128, D], F32, kind="Internal").ap()

    with ExitStack() as st:
        fw = FW(nc, st)

        def chk(tag):
            if os.environ.get("DBGSTOP") == tag:
                fw.finish()
                raise _Stop()

        def sb(stack, name, shape, dt):
            return stack.enter_context(nc.sbuf_tensor(name, list(shape), dt))

        psb = [st.enter_context(nc.psum_tensor("psb%d" % i, [128, 512], F32)) for i in range(8)]
        PSB = [Buf("psb%d" % i, excl=True) for i in range(8)]

        def psbf(i):
            return psb[i][:, :].bitcast(BF16)
        pst_i = [0]

        def next_pst():
            pst_i[0] += 1
            i = 6 + pst_i[0] % 2
            return psbf(i), PSB[i]

        cm = sb(st, "cm", [128, 8 * 128 + 4], F32); CM = Buf("cm")
        fw.ld(cm[:], cmat[:, :], [CM])
        ident_f = cm[:, 0:128]
        mcum = [cm[:, 128:258], cm[:, 258:388]]
        blk = cm[:, 388:516]
        negm = [cm[:, 516:644], cm[:, 772:900]]
        posm = [cm[:, 644:772], cm[:, 900:1028]]
        ident_b = sb(st, "ident_b", [128, 128], BF16); IDB = Buf("idb")
        fw.cp("dve", ident_b[:], ident_f, [CM], [IDB])
        maskb = sb(st, "maskb", [128, 4, 128], BF16); MASKB = Buf("maskb")
        fw.cp("dve", maskb[:].rearrange("p a b -> p (a b)"), cm[:, 516:1028], [CM], [MASKB])
        ones_b = sb(st, "ones_b", [128, 128], BF16); ONB = Buf("onb")
        fw.op("pool", lambda e: e.memset(ones_b[:], 1.0), [], [ONB])
        ones_f = sb(st, "ones_f", [128, 128], F32); ONF = Buf("onf")
        fw.op("pool", lambda e: e.memset(ones_f[:], 1.0), [], [ONF])
        cst = sb(st, "cst", [128, 8], F32); CST = Buf("cst")
        fw.op("pool", lambda e: e.memset(cst[:, 0:1], EPS), [], [CST])
        fw.op("pool", lambda e: e.memset(cst[:, 1:2], 1.0), [], [CST])
        fw.op("pool", lambda e: e.memset(cst[:, 2:3], EPS * 128.0), [], [CST])
        eps_ap = cst[:, 0:1]

        vb_ = sb(st, "vecs_bc", [128, 1808], F32); VEC = Buf("vecs")
        fw.ld(vb_[:], vecs[0:1, :].broadcast_to([128, 1808]), [VEC])
        qg_bc = vb_[:, 0:64]
        kg_bc = vb_[:, 64:128]
        aog_bc = vb_[:, 128:640]
        gng_bc = vb_[:, 640:768]
        alogdt_bc = vb_[:, 768:784]
        fng_bc = vb_[:, 784:1808]
        gq8 = sb(st, "gq8", [128, 512], F32); GQ8 = Buf("gq8")
        for hh in range(8):
            fw.ts("dve", gq8[:, hh * 64:(hh + 1) * 64], qg_bc, 0.125, None, ALU.mult, None, [VEC], [GQ8])
        cw = sb(st, "convw_sb", [128, 36], F32); CW = Buf("cw")
        fw.ld(cw[:], convw[:, :], [CW])
        fcw_sb = sb(st, "fcw_sb", [128, NFC * 4], F32); FCW = Buf("fcw")
        fw.ld(fcw_sb[:], fcw[:, :], [FCW])
        ngc = sb(st, "ngc", [128, 16], F32); NGC = Buf("ngc")
        fw.ld(ngc[:], ng_col[:, :], [NGC])
        G12 = sb(st, "G12", [128, 2, 1024], F32); G12B = Buf("G12")
        modv = sb(st, "modv", [128, 48], F32); MODV = Buf("modv")

        def dump(name, ap, shape, buf, dt=F32):
            if not dbg:
                return
            t = nc.dram_tensor("dbg_" + name, list(shape), dt, kind="ExternalOutput").ap()
            dbg_outs[name] = t
            fw.stor(t, ap, [buf])

        if stop_after == -1:
            dump("gq8", gq8[:], [128, 512], GQ8)
            fw.finish()
            return nc, dbg_outs
        with ExitStack() as p0:
            sc = sb(p0, "sc", [128, 16], F32); SC = Buf("sc")
            fw.ld(sc[:], ccol[:, :], [SC])
            fw.act(sc[:], sc[:], AF.Silu, [SC], [SC])
            sc2 = sb(p0, "sc2", [128, 8, 2], F32); SC2 = Buf("sc2")
            fw.cp("dve", sc2[:, :, 0], sc[:, 0:8], [SC], [SC2])
            fw.cp("dve", sc2[:, :, 1], sc[:, 8:16], [SC], [SC2])
            scbc = sb(p0, "scbc", [128, 8, 128], F32); SCBC = Buf("scbc")
            for k in range(8):
                fw.ts("dve", scbc[:, k, :], ones_f[:], sc[:, k:k + 1], None, ALU.mult, None, [SC, ONF], [SCBC])
            bmc = sb(p0, "bmc", [128, 48], F32); BMC = Buf("bmc")
            fw.ld(bmc[:], bmod_col[:, :], [BMC])
            bg = sb(p0, "bgate", [128, 2, 1024], F32); BG = Buf("bgate")
            fw.ld(bg[:, 0, :], bmod_row[0:1, 2048:3072].broadcast_to([128, 1024]), [BG])
            fw.ld(bg[:, 1, :], bmod_row[0:1, 5120:6144].broadcast_to([128, 1024]), [BG])
            mcol = sb(p0, "mcol", [128, 48, 2], F32); MCOL = Buf("mcol")
            wm = [sb(p0, "wm%d" % i, [128, 8, 512], F32) for i in range(2)]
            WM = [Buf("wm0"), Buf("wm1")]
            wmv = w_mod.rearrange("(k p) n -> p k n", p=128)
            for jb in range(12):
                s = jb % 2
                fw.ld(wm[s][:], wmv[:, :, jb * 512:(jb + 1) * 512], [WM[s]])
                if jb in (4, 5, 10, 11):
                    gi = 0 if jb < 6 else 1
                    half = jb % 2 if jb < 6 else (jb - 10)
                    pb, PB = psb[0], PSB[0]
                    for k in range(8):
                        fw.mm(pb[:, :], scbc[:, k, :], wm[s][:, k, :], k == 0, k == 7, [SCBC, WM[s]], [PB])
                    fw.tt("dve", G12[:, gi, half * 512:(half + 1) * 512], pb[:, :], bg[:, gi, half * 512:(half + 1) * 512],
                          ALU.add, [PB, BG], [G12B])
                else:
                    pb, PB = psb[1], PSB[1]
                    for cc in range(4):
                        for k in range(8):
                            fw.mm(pb[:, cc * 2:cc * 2 + 2], wm[s][:, k, cc * 128:(cc + 1) * 128], sc2[:, k, :],
                                  k == 0, k == 7, [SC2, WM[s]], [PB])
                    for col in range(2):
                        fw.tt("dve", mcol[:, jb * 4:jb * 4 + 4, col], pb[:, col:8:2], bmc[:, jb * 4:jb * 4 + 4],
                              ALU.add, [PB, BMC], [MCOL])
            tmp8 = sb(p0, "tmp8", [128, 8], F32); T8 = Buf("t8")
            fw.ts("dve", tmp8[:], mcol[:, 8:16, 0], 1.0, None, ALU.add, None, [MCOL], [T8])
            fw.tt("dve", modv[:, 0:8], tmp8[:], ngc[:, 0:8], ALU.mult, [T8, NGC], [MODV])
            fw.cp("dve", modv[:, 8:16], mcol[:, 0:8, 0], [MCOL], [MODV])
            fw.ts("dve", tmp8[:], mcol[:, 8:16, 1], 1.0, None, ALU.add, None, [MCOL], [T8])
            fw.tt("dve", modv[:, 16:24], tmp8[:], ngc[:, 0:8], ALU.mult, [T8, NGC], [MODV])
            fw.cp("dve", modv[:, 24:32], mcol[:, 0:8, 1], [MCOL], [MODV])
            fw.ts("dve", tmp8[:], mcol[:, 32:40, 0], 1.0, None, ALU.add, None, [MCOL], [T8])
            fw.tt("dve", modv[:, 32:40], tmp8[:], ngc[:, 8:16], ALU.mult, [T8, NGC], [MODV])
            fw.cp("dve", modv[:, 40:48], mcol[:, 24:32, 0], [MCOL], [MODV])
            dump("modv", modv[:], [128, 48], MODV)
            dump("G12", G12[:], [128, 2, 1024], G12B)
            fw.barrier()
        if stop_after == 0:
            fw.finish()
            return nc, dbg_outs

        def tile_src(t):
            if t < NT_LAT:
                return xs[t * 128:(t + 1) * 128, :]
            return cs[(t - NT_LAT) * 128:(t - NT_LAT + 1) * 128, :]

        class NormCtx:
            pass

        def make_norm(stack, tag):
            n = NormCtx()
            n.xt = [sb(stack, "xt%s%d" % (tag, i), [128, D], F32) for i in range(2)]
            n.XT = [Buf("xt%d" % i) for i in range(2)]
            n.junk = sb(stack, "junk" + tag, [128, D], BF16); n.JUNK = Buf("junk")
            n.stt = sb(stack, "nst" + tag, [128, 4], F32); n.ST = Buf("nst")
            n.xn = [sb(stack, "xn%s%d" % (tag, i), [128, D], BF16) for i in range(4)]
            n.XN = [Buf("xn%d" % i) for i in range(4)]
            n.i = 0
            return n

        def norm_rows(n, src_ap, src_buf, slot):
            fw.act(n.junk[:], src_ap, AF.Square, [src_buf], [n.JUNK, n.ST], accum=n.stt[:, 0:1])
            fw.act(n.stt[:, 1:2], n.stt[:, 0:1], AF.Sqrt, [n.ST, CST], [n.ST], bias=eps_ap, scale=1.0 / D)
            fw.op("dve", lambda e: e.reciprocal(out=n.stt[:, 2:3], in_=n.stt[:, 1:2]), [n.ST], [n.ST])
            fw.ts("dve", n.xn[slot][:], src_ap, n.stt[:, 2:3], None, ALU.mult, None, [src_buf, n.ST], [n.XN[slot]])

        def transpose_mod(n, ntile, hT_ap_fn, HT, acol0, ncols_last=128):
            for kc in range(8):
                pt, PT = next_pst()
                for i in range(ntile):
                    fw.tr(pt[:, i * 128:(i + 1) * 128], n.xn[i][:, kc * 128:(kc + 1) * 128], ident_b[:],
                          [n.XN[i], IDB], [PT])
                T = (ntile - 1) * 128 + ncols_last
                chk("tm_tr")
                fw.ts("dve", hT_ap_fn(kc, T), pt[:, 0:T], modv[:, acol0 + kc:acol0 + kc + 1],
                      modv[:, acol0 + 8 + kc:acol0 + 9 + kc], ALU.mult, ALU.add, [PT, MODV], [HT])
                chk("tm_ev%d" % kc)

        def load_norm_group(n, tiles):
            for i, t in enumerate(tiles):
                s = n.i % 2
                n.i += 1
                fw.ld(n.xt[s][:], tile_src(t), [n.XT[s]])
                norm_rows(n, n.xt[s][:], n.XT[s], i)

        def load_weight_bf16(stack, name, src_view, ncols, dst, DST, piece=256):
            K = src_view.shape[1]
            with ExitStack() as ws:
                stg = [sb(ws, "%s_stg%d" % (name, i), [128, K, piece], F32) for i in range(2)]
                STG = [Buf("stg0"), Buf("stg1")]
                i = 0
                for c0 in range(0, ncols, piece):
                    c1 = min(ncols, c0 + piece)
                    s = i % 2
                    fw.ld(stg[s][:, :, 0:c1 - c0], src_view[:, :, c0:c1], [STG[s]])
                    eng = ("dve", "act")[i % 2]
                    fw.cp(eng, dst[:, :, c0:c1], stg[s][:, :, 0:c1 - c0], [STG[s]], [DST])
                    i += 1
                fw.barrier()

        groups_ext = [[0, 1, 2, 3], [4, 5, 6, 7], [8, 9, 10, 11], [12, 13, 14, 15], [16]]
        groups_oth = [[17, 18, 19], [20, 21, 22, 23], [24, 25, 26, 27], [28, 29, 30, 31], [32, 33]]

        mixer = ExitStack()
        st.enter_context(mixer)
        Oacc = sb(mixer, "Oacc", [128, NT_EXT, 512], BF16); OACC = [Buf("oacc%d" % t) for t in range(NT_EXT)]

        gdn = ExitStack()
        st.enter_context(gdn)
        rawK = sb(gdn, "rawK", [128, 4, TOK_ALL], BF16)
        rawV = sb(gdn, "rawV", [128, 4, TOK_ALL], BF16)
        rawQ = sb(gdn, "rawQ", [128, 4, TOK_EXT], BF16)
        RAW = {}
        ab = sb(gdn, "ab", [128, NT_ALL, 16], F32); AB = Buf("ab")

        def rawbuf(kind, h, grp):
            key = (kind, h, grp)
            if key not in RAW:
                RAW[key] = Buf("raw%s%d_%d" % (kind, h, grp))
            return RAW[key]

        def tok_group(t):
            return t // 4

        with ExitStack() as p1:
            wg = sb(p1, "wg", [128, 8, 1552], BF16); WG = Buf("wg")
            load_weight_bf16(p1, "wg", w_in_g.rearrange("(k p) n -> p k n", p=128), 1552, wg, WG)
            chk("wload")
            nctx = make_norm(p1, "a")
            hT = [sb(p1, "hT%d" % i, [128, 8, 512], BF16) for i in range(2)]
            HT = [Buf("hT0"), Buf("hT1")]
            SEG = 1024
            NSL = 2
            acc = [sb(p1, "cacc%d" % i, [128, SEG], F32) for i in range(NSL)]
            ACC = [Buf("cacc%d" % i) for i in range(NSL)]
            sq = [sb(p1, "csq%d" % i, [128, SEG], BF16) for i in range(NSL)]
            SQ = [Buf("csq%d" % i) for i in range(NSL)]
            rin = [sb(p1, "crin%d" % i, [128, 512], F32) for i in range(2)]
            RIN = [Buf("crin%d" % i) for i in range(2)]
            rci = [0]
            pending = []
            csi = [0]
            cprev = {}

            def conv_seg(ch, a, b, s0, s1):
                kind = "QKV"[ch // 4]
                h = ch % 4
                arr = (rawQ, rawK, rawV)[ch // 4]
                w0, w1, w2 = (cw[:, ch * 3 + j:ch * 3 + j + 1] for j in range(3))
                n = s1 - s0
                sl = csi[0] % NSL
                csi[0] += 1
                bufs = sorted({tok_group(t) for t in range(s0 // 128, (s1 + 127) // 128)} |
                              ({tok_group(s1 // 128)} if s1 < b else set()))
                RB = [rawbuf(kind, h, g) for g in bufs]
                fw.act(acc[sl][:, 0:n], arr[:, h, s0:s1], AF.Copy, RB + [CW], [ACC[sl]], scale=w1)
                if s0 > a:
                    pl = cprev[(ch, a)]
                    fw.stt("dve", acc[sl][:, 0:1], pl[0], w0, acc[sl][:, 0:1], ALU.mult, ALU.add,
                           [pl[1], CW, ACC[sl]], [ACC[sl]])
                fw.stt("dve", acc[sl][:, 1:n], arr[:, h, s0:s1 - 1], w0, acc[sl][:, 1:n], ALU.mult, ALU.add,
                       RB + [CW, ACC[sl]], [ACC[sl]])
                nr = n if s1 < b else n - 1
                fw.stt("dve", acc[sl][:, 0:nr], arr[:, h, s0 + 1:s0 + 1 + nr], w2, acc[sl][:, 0:nr], ALU.mult, ALU.add,
                       RB + [CW, ACC[sl]], [ACC[sl]])
                if s1 < b:
                    keep = sb(p1, "keep%d_%d" % (ch, s0), [128, 1], BF16)
                    KB = Buf("keep")
                    fw.cp("pool", keep[:], arr[:, h, s1 - 1:s1], RB, [KB])
                    cprev[(ch, a)] = (keep[:], KB)
                WB = [rawbuf(kind, h, g) for g in sorted({tok_group(t) for t in range(s0 // 128, (s1 + 127) // 128)})]

                def tail():
                    if kind == "V":
                        fw.act(arr[:, h, s0:s1], acc[sl][:, 0:n], AF.Silu, [ACC[sl]], WB)
                        return
                    fw.act(acc[sl][:, 0:n], acc[sl][:, 0:n], AF.Silu, [ACC[sl]], [ACC[sl]])
                    fw.tt("pool", sq[sl][:, 0:n], acc[sl][:, 0:n], acc[sl][:, 0:n], ALU.mult, [ACC[sl]], [SQ[sl]])
                    for c0 in range(0, n, 512):
                        c1 = min(n, c0 + 512)
                        rci[0] += 1
                        pb, PB = psb[4 + rci[0] % 2], PSB[4 + rci[0] % 2]
                        r_, RN = rin[rci[0] % 2], RIN[rci[0] % 2]
                        fw.mm(pb[:, 0:c1 - c0], ones_b[:], sq[sl][:, c0:c1], True, True, [ONB, SQ[sl]], [PB])
                        fw.act(r_[:, 0:c1 - c0], pb[:, 0:c1 - c0], AF.Ln, [PB, CST], [RN], bias=eps_ap, scale=1.0)
                        fw.act(r_[:, 0:c1 - c0], r_[:, 0:c1 - c0], AF.Exp, [RN], [RN], scale=-0.5)
                        fw.tt("dve", arr[:, h, s0 + c0:s0 + c1], acc[sl][:, c0:c1], r_[:, 0:c1 - c0], ALU.mult,
                              [ACC[sl], RN], WB)
                pending.append(tail)
                while len(pending) > 1:
                    pending.pop(0)()

            csegs = []
            for ch in range(12):
                rngs = [(0, TOK_EXT)] if ch < 4 else [(0, SEQ), (SEQ, TOK_ALL)]
                for (a_, b_) in rngs:
                    for s0 in range(a_, b_, SEG):
                        s1 = min(b_, s0 + SEG)
                        need = None if a_ == SEQ else min(b_, s1 + 1)
                        csegs.append((need, ch, a_, b_, s0, s1))
            cdone = set()
            cav = [0, False]

            def emit_ready_convs(avail_lat, ctx_done, limit=None):
                k = 0
                for i_, (need, ch, a_, b_, s0, s1) in enumerate(csegs):
                    if i_ in cdone:
                        continue
                    ok = ctx_done if need is None else need <= avail_lat
                    if ok:
                        conv_seg(ch, a_, b_, s0, s1)
                        cdone.add(i_)
                        k += 1
                        if limit is not None and k >= limit:
                            return

            allg = groups_ext + groups_oth

            def prep1(gi):
                grp = allg[gi]
                load_norm_group(nctx, grp)
                transpose_mod(nctx, len(grp), lambda kc, T, s=gi % 2: hT[s][:, kc, 0:T], HT[gi % 2],
                              16 if grp[0] >= NT_LAT else 0)
            prep1(0)
            for gi, grp in enumerate(allg):
                is_ext = grp[0] < NT_EXT
                is_ctx = grp[0] >= NT_LAT
                s = gi % 2
                T = len(grp) * 128
                tok0 = grp[0] * 128
                chunks = list(range(12)) if is_ext else list(range(4, 12))
                for ci, ch in enumerate(chunks):
                    if ci == 2 and gi + 1 < len(allg):
                        prep1(gi + 1)
                    if gi > 0:
                        emit_ready_convs(cav[0], cav[1], limit=2)
                    pb, PB = psb[ci % 4], PSB[ci % 4]
                    for kc in range(8):
                        fw.mm(pb[:, 0:T], wg[:, kc, ch * 128:(ch + 1) * 128], hT[s][:, kc, 0:T], kc == 0, kc == 7,
                              [WG, HT[s]], [PB])
                    kind = "QKV"[ch // 4]
                    dst = (rawQ, rawK, rawV)[ch // 4]
                    fw.cpa(dst[:, ch % 4, tok0:tok0 + T], pb[:, 0:T], [PB], [rawbuf(kind, ch % 4, tok_group(grp[0]))])
                for i, t in enumerate(grp):
                    pb, PB = psb[4 + (i % 2)], PSB[4 + (i % 2)]
                    for kc in range(8):
                        fw.mm(pb[:, 0:16], hT[s][:, kc, i * 128:(i + 1) * 128], wg[:, kc, 1536:1552], kc == 0, kc == 7,
                              [WG, HT[s]], [PB])
                    fw.cp("act", ab[:, t, :], pb[:, 0:16], [PB], [AB])
                chk("grp0")
                cav[0] = (grp[-1] + 1) * 128 if grp[0] < NT_LAT else SEQ
                cav[1] = grp[0] >= NT_LAT
            emit_ready_convs(SEQ, True)
            while pending:
                pending.pop(0)()
            assert len(cdone) == len(csegs)
            fw.barrier()
        if stop_after == 1:
            dump("rawK", rawK[:], [128, 4, TOK_ALL], rawbuf("K", 0, 0), BF16)
            dump("ab", ab[:], [128, NT_ALL, 16], AB)
            fw.finish()
            return nc, dbg_outs

        if stop_after == 2:
            dump("KT", rawK[:], [128, 4, TOK_ALL], rawbuf("K", 0, 0), BF16)
            dump("QT", rawQ[:], [128, 4, TOK_EXT], rawbuf("Q", 0, 0), BF16)
            dump("VT", rawV[:], [128, 4, TOK_ALL], rawbuf("V", 0, 0), BF16)
            dump("ab", ab[:], [128, NT_ALL, 16], AB)
            fw.finish()
            return nc, dbg_outs

        def a3(name, n, stack=gdn):
            return sb(stack, name, [128, NT_ALL, n], F32)
        gg = a3("gg", 8); GG = Buf("gg")
        beta = a3("beta", 8); BETA = Buf("beta")
        egc = a3("egc", 8); EGC = Buf("egc")
        ekd = a3("ekd", 8); EKD = Buf("ekd")
        bgt = a3("bgt", 8); BGT = Buf("bgt")
        gcpl = a3("gcpl", 8); GCPL = Buf("gcpl")
        ngcn = a3("ngcn", 8); NGCN = Buf("ngcn")
        dl = a3("dl", 16); DL = Buf("dl")
        with ExitStack() as pg:
            t1 = a3("t1", 8, pg); T1 = Buf("t1")
            lnb = a3("lnb", 8, pg); LNB = Buf("lnb")
            gcs = a3("gcs", 32, pg); GCS = Buf("gcs")
            ealog = sb(pg, "ealog", [128, 8], F32); EAL = Buf("ealog")
            gI = [sb(pg, "gI%d" % i, [128, 8, 2], F32) for i in range(2)]
            GI = [Buf("gI0"), Buf("gI1")]
            fw.tt("dve", t1[:], ab[:, :, 0:8], alogdt_bc[:, 8:16].unsqueeze(1).to_broadcast([128, NT_ALL, 8]), ALU.add,
                  [AB, VEC], [T1])
            fw.act(t1[:], t1[:], AF.Exp, [T1], [T1])
            fw.act(t1[:], t1[:], AF.Ln, [T1, CST], [T1], bias=cst[:, 1:2], scale=1.0)
            fw.act(ealog[:], alogdt_bc[:, 0:8], AF.Exp, [VEC], [EAL])
            fw.stt("dve", gg[:], t1[:], -1.0, ealog[:].unsqueeze(1).to_broadcast([128, NT_ALL, 8]), ALU.mult, ALU.mult,
                   [T1, EAL], [GG])
            fw.act(beta[:], ab[:, :, 8:16], AF.Sigmoid, [AB], [BETA])
            fw.act(lnb[:], beta[:], AF.Ln, [BETA], [LNB])
            for t in range(NT_ALL):
                k = t % 2
                pb, PB = psb[k], PSB[k]
                for c in range(2):
                    fw.ts("dve", gI[k][:, :, c], gg[:, t, :], mcum[0][:, 128 + c:129 + c], None, ALU.mult, None,
                          [GG, CM], [GI[k]])
                fw.mm(pb[:, 0:4], mcum[0][:, 0:128], gg[:, t, 0:4], True, True, [CM, GG], [PB])
                fw.mm(pb[:, 4:8], mcum[1][:, 0:128], gg[:, t, 4:8], False, True, [CM, GG], [PB])
                fw.mm(pb[:, 8:16], blk, gg[:, t, 0:8], False, True, [CM, GG], [PB])
                fw.mm(pb[:, 16:32], ones_f[:], gI[k][:].rearrange("p a b -> p (a b)"), False, True, [ONF, GI[k]], [PB])
                fw.cp("act", gcs[:, t, :], pb[:, 0:32], [PB], [GCS])
            fw.act(egc[:], gcs[:, :, 0:8], AF.Exp, [GCS], [EGC])
            fw.tt("dve", t1[:], gcs[:, :, 8:16], gcs[:, :, 0:8], ALU.subtract, [GCS], [T1])
            fw.act(ekd[:], t1[:], AF.Exp, [T1], [EKD])
            fw.tt("dve", bgt[:], beta[:], egc[:], ALU.mult, [BETA, EGC], [BGT])
            fw.tt("dve", gcpl[:], gcs[:, :, 0:8], lnb[:], ALU.add, [GCS, LNB], [GCPL])
            fw.ts("dve", ngcn[:], gcs[:, :, 0:8], -1.0, None, ALU.mult, None, [GCS], [NGCN])
            fw.act(dl[:], gcs[:, :, 16:32], AF.Exp, [GCS], [DL])
            fw.barrier()
        if stop_after == 3:
            dump("gg", gg[:], [128, NT_ALL, 8], GG)
            dump("beta", beta[:], [128, NT_ALL, 8], BETA)
            fw.finish()
            return nc, dbg_outs

        fw.op("pool", lambda e: e.memset(Oacc[:], 0.0), [], OACC)
        with ExitStack() as ps_:
            maskb4 = sb(ps_, "maskb4", [128, 4, 512], BF16); MASKB4 = Buf("maskb4")
            for ty in range(4):
                for h in range(4):
                    fw.cp("pool", maskb4[:, ty, h * 128:(h + 1) * 128], maskb[:, ty, :], [MASKB], [MASKB4])
            identb4 = sb(ps_, "identb4", [128, 4, 128], BF16); IDB4 = Buf("idb4")
            for h in range(4):
                fw.cp("dve", identb4[:, h, :], ident_b[:], [IDB], [IDB4])

            class QS:
                pass
            DBL = ("kbg", "kdec", "vb", "AT", "wT", "u")
            sets = []
            for d in range(2):
                q = QS()
                for nm, dt_ in (("kbg", BF16), ("kdec", BF16), ("vb", BF16), ("gM", F32), ("Dstr", BF16), ("Dinc", BF16),
                                ("B0", BF16), ("B1", BF16), ("AT", BF16),
                                ("u", F32), ("wT", BF16), ("vnew", BF16), ("tmp", F32), ("S", F32), ("Sbf", BF16)):
                    if nm in DBL:
                        setattr(q, nm + "_2", [sb(ps_, "q%d_%s_%d" % (d, nm, i), [128, 4, 128], dt_) for i in range(2)])
                        setattr(q, nm.upper() + "_B2", [Buf("q%d_%s_%d" % (d, nm, i)) for i in range(2)])
                    else:
                        setattr(q, nm, sb(ps_, "q%d_%s" % (d, nm), [128, 4, 128], dt_))
                        setattr(q, nm.upper() + "_", Buf("q%d_%s" % (d, nm)))
                q.AP0 = sb(ps_, "q%d_AP0" % d, [128, 4, 256], BF16)
                q.AP1 = sb(ps_, "q%d_AP1" % d, [128, 4, 256], BF16)
                q.APA0_, q.APA1_, q.APP0_, q.APP1_ = Buf("apa0"), Buf("apa1"), Buf("app0"), Buf("app1")
                fw.op("pool", lambda e, q=q: e.memset(q.S[:], 0.0), [], [q.S_])
                fw.op("pool", lambda e, q=q: e.memset(q.Sbf[:], 0.0), [], [q.SBF_])
                q.banks = [0, 1, 2, 3] if d == 0 else [4, 5, 6, 7]
                q.bi = 0
                sets.append(q)

            class QView:
                def __init__(self, base, par):
                    object.__setattr__(self, "_b", base)
                    object.__setattr__(self, "_p", par)

                def __getattr__(self, name):
                    b_, p_ = self._b, self._p
                    if name in DBL:
                        return getattr(b_, name + "_2")[p_]
                    if name.endswith("_") and name[:-1].lower() in [x.lower() for x in DBL] and name[:-1].isupper():
                        for x in DBL:
                            if x.upper() == name[:-1]:
                                return getattr(b_, x.upper() + "_B2")[p_]
                    return getattr(b_, name)

                def __setattr__(self, name, val):
                    setattr(self._b, name, val)

            def qview(d, par):
                return QView(sets[d], par)

            def nb(q):
                q.bi += 1
                i = q.banks[q.bi % 4]
                return i

            def v4(ap512):
                return ap512.rearrange("p (a b) -> p a b", a=4)

            def bc4(ap_p4):
                return ap_p4.unsqueeze(2).to_broadcast([ap_p4.shape[0], 4, 128])

            def quad_pre(t, d, with_out, par):
                q = qview(d, par)
                c0 = d * 4
                tsl = slice(t * 128, (t + 1) * 128)
                grp = tok_group(t)
                KB = [rawbuf("K", h, grp) for h in range(4)]
                VB = [rawbuf("V", h, grp) for h in range(4)]
                QB = [rawbuf("Q", h, grp) for h in range(4)] if with_out else []
                i = nb(q)
                bv = psbf(i)
                for h in range(4):
                    fw.tr(bv[:, h * 128:(h + 1) * 128], rawK[:, h, tsl], ident_b[:], [KB[h], IDB], [PSB[i]])
                fw.tt("dve", q.kbg[:], v4(bv[:, 0:512]), bc4(bgt[:, t, c0:c0 + 4]), ALU.mult, [PSB[i], BGT], [q.KBG_])
                fw.tt("dve", q.kdec[:], v4(bv[:, 0:512]), bc4(ekd[:, t, c0:c0 + 4]), ALU.mult, [PSB[i], EKD], [q.KDEC_])
                yield
                i = nb(q)
                bv = psbf(i)
                for h in range(4):
                    fw.tr(bv[:, h * 128:(h + 1) * 128], rawV[:, h, tsl], ident_b[:], [VB[h], IDB], [PSB[i]])
                fw.tt("dve", q.vb[:], v4(bv[:, 0:512]), bc4(beta[:, t, c0:c0 + 4]), ALU.mult, [PSB[i], BETA], [q.VB_])
                yield
                fw.tt("dve", q.gM[:], mcum[d][:, 0:128].unsqueeze(1).to_broadcast([128, 4, 128]), bc4(gg[:, t, c0:c0 + 4]),
                      ALU.mult, [CM, GG], [q.GM_])
                gMf = q.gM[:].rearrange("p a b -> p (a b)")
                i = nb(q)
                fw.mm(psb[i][:, :], ones_f[:], gMf, True, False, [ONF, q.GM_], [PSB[i]])
                fw.mm(psb[i][:, :], ident_b[:], maskb4[:, 2 * d + 1, :], False, True, [IDB, MASKB4], [PSB[i]])
                for h in range(4):
                    fw.act(q.Dstr[:, h, :], psb[i][:, h * 128:(h + 1) * 128], AF.Exp, [PSB[i], GCPL], [q.DSTR_],
                           bias=gcpl[:, t, c0 + h:c0 + h + 1], scale=-1.0)
                yield
                if with_out:
                    i = nb(q)
                    fw.mm(psb[i][:, :], ones_f[:], gMf, True, False, [ONF, q.GM_], [PSB[i]])
                    fw.mm(psb[i][:, :], ident_b[:], maskb4[:, 2 * d, :], False, True, [IDB, MASKB4], [PSB[i]])
                    for h in range(4):
                        fw.act(q.Dinc[:, h, :], psb[i][:, h * 128:(h + 1) * 128], AF.Exp, [PSB[i], NGCN], [q.DINC_],
                               bias=ngcn[:, t, c0 + h:c0 + h + 1], scale=1.0)
                    yield
                i = nb(q)
                for h in range(4):
                    fw.mm(psb[i][:, h * 128:(h + 1) * 128], rawK[:, h, tsl], rawK[:, h, tsl], h == 0, True, [KB[h]], [PSB[i]])
                fw.stt("dve", q.B0[:], v4(psb[i][:, :]), -1.0, q.Dstr[:], ALU.mult, ALU.mult, [PSB[i], q.DSTR_], [q.B0_])
                yield
                if with_out:
                    i = nb(q)
                    for h in range(4):
                        fw.mm(psb[i][:, h * 128:(h + 1) * 128], rawK[:, h, tsl], rawQ[:, h, tsl], h == 0, True,
                              [KB[h], QB[h]], [PSB[i]])
                    fw.tt("dve", q.AT[:], v4(psb[i][:, :]), q.Dinc[:], ALU.mult, [PSB[i], q.DINC_], [q.AT_])
                    yield
                AP = [q.AP0, q.AP1]
                APA = [q.APA0_, q.APA1_]
                APP = [q.APP0_, q.APP1_]
                Bb = [(q.B0, q.B0_), (q.B1, q.B1_)]
                i = nb(q)
                bv = psbf(i)
                for h in range(4):
                    fw.tr(bv[:, h * 128:(h + 1) * 128], q.B0[:, h, :], ident_b[:], [q.B0_, IDB], [PSB[i]])
                fw.cp("act", AP[0][:, :, 0:128], v4(bv[:, 0:512]), [PSB[i]], [APA[0]])
                fw.tt("dve", AP[1][:, :, 128:256], v4(bv[:, 0:512]), identb4[:], ALU.add, [PSB[i], IDB4], [APP[1]])
                yield
                for j in range(1, 6):
                    cur, nxt = (j - 1) % 2, j % 2
                    Bc, BcB = Bb[(j - 1) % 2]
                    Bn, BnB = Bb[j % 2]
                    if j == 1:
                        i = nb(q)
                        for h in range(4):
                            fw.mm(psb[i][:, h * 128:(h + 1) * 128], Bc[:, h, :], AP[cur][:, h, 0:128], h == 0, True,
                                  [BcB, APA[cur]], [PSB[i]])
                        i2 = nb(q)
                        for h in range(4):
                            fw.mm(psb[i2][:, h * 128:(h + 1) * 128], AP[cur][:, h, 0:128], Bc[:, h, :], h == 0, True,
                                  [BcB, APA[cur]], [PSB[i2]])
                        fw.cp("act", AP[nxt][:, :, 0:128], v4(psb[i][:, :]), [PSB[i]], [APA[nxt]])
                        fw.cp("dve", Bn[:], v4(psb[i2][:, :]), [PSB[i2]], [BnB])
                        yield
                    elif j < 5:
                        ia, ib = nb(q), nb(q)
                        for h in range(4):
                            bk = ia if h < 2 else ib
                            hh = h % 2
                            fw.mm(psb[bk][:, hh * 256:(hh + 1) * 256], Bc[:, h, :], AP[cur][:, h, :], hh == 0, True,
                                  [BcB, APA[cur], APP[cur]], [PSB[bk]])
                        i2 = nb(q)
                        for h in range(4):
                            fw.mm(psb[i2][:, h * 128:(h + 1) * 128], AP[cur][:, h, 0:128], Bc[:, h, :], h == 0, True,
                                  [BcB, APA[cur]], [PSB[i2]])
                        for bk, h0 in ((ia, 0), (ib, 2)):
                            pv_ = psb[bk][:, :].rearrange("p (a b) -> p a b", a=2)
                            fw.cp("act", AP[nxt][:, h0:h0 + 2, 0:128], pv_[:, :, 0:128], [PSB[bk]], [APA[nxt]])
                            fw.tt("dve", AP[nxt][:, h0:h0 + 2, 128:256], AP[cur][:, h0:h0 + 2, 128:256], pv_[:, :, 128:256],
                                  ALU.add, [PSB[bk], APP[cur]], [APP[nxt]])
                        fw.cp("dve", Bn[:], v4(psb[i2][:, :]), [PSB[i2]], [BnB])
                        yield
                    else:
                        i = nb(q)
                        for h in range(4):
                            fw.mm(psb[i][:, h * 128:(h + 1) * 128], Bc[:, h, :], AP[cur][:, h, 128:256], h == 0, True,
                                  [BcB, APP[cur]], [PSB[i]])
                        i2 = nb(q)
                        for h in range(4):
                            fw.mm(psb[i2][:, h * 128:(h + 1) * 128], AP[cur][:, h, 0:128], Bc[:, h, :], h == 0, True,
                                  [BcB, APA[cur]], [PSB[i2]])
                        fw.tt("dve", AP[nxt][:, :, 128:256], AP[cur][:, :, 128:256], v4(psb[i][:, :]), ALU.add,
                              [PSB[i], APP[cur]], [APP[nxt]])
                        fw.cp("act", Bn[:], v4(psb[i2][:, :]), [PSB[i2]], [BnB])
                        yield
                B5, B5B = Bb[1]
                i = nb(q)
                for h in range(4):
                    fw.mm(psb[i][:, h * 128:(h + 1) * 128], B5[:, h, :], AP[1][:, h, 128:256], h == 0, True, [B5B, APP[1]], [PSB[i]])
                fw.tt("dve", AP[0][:, :, 128:256], AP[1][:, :, 128:256], v4(psb[i][:, :]), ALU.add, [PSB[i], APP[1]], [APP[0]])
                yield
                Ptf = AP[0]
                PTF_ = APP[0]
                i = nb(q)
                for h in range(4):
                    fw.mm(psb[i][:, h * 128:(h + 1) * 128], Ptf[:, h, 128:256], q.vb[:, h, :], h == 0, True, [PTF_, q.VB_], [PSB[i]])
                fw.cp("act", q.u[:], v4(psb[i][:, :]), [PSB[i]], [q.U_])
                i = nb(q)
                for h in range(4):
                    fw.mm(psb[i][:, h * 128:(h + 1) * 128], q.kbg[:, h, :], Ptf[:, h, 128:256], h == 0, True, [PTF_, q.KBG_], [PSB[i]])
                fw.cp("dve", q.wT[:], v4(psb[i][:, :]), [PSB[i]], [q.WT_])
                yield

            def quad_steps(t, d, with_out, par):
                q = qview(d, par)
                c0 = d * 4
                tsl = slice(t * 128, (t + 1) * 128)
                grp = tok_group(t)
                KB = [rawbuf("K", h, grp) for h in range(4)]
                VB = [rawbuf("V", h, grp) for h in range(4)]
                QB = [rawbuf("Q", h, grp) for h in range(4)] if with_out else []
                for c in ((0, 1) if d == 0 else (1, 0)):
                    R = slice(64 * c, 64 * c + 64)
                    i = nb(q)
                    for h in range(4):
                        fw.mm(psb[i][:, h * 128:(h + 1) * 128], q.wT[:, h, :], q.Sbf[:, h, :], h == 0, True, [q.WT_, q.SBF_], [PSB[i]])
                    fw.tt("dve", q.vnew[R, :, :], q.u[R, :, :], v4(psb[i][R, :]), ALU.subtract, [PSB[i], q.U_], [q.VNEW_])
                    yield
                    if with_out:
                        i = nb(q)
                        for h in range(4):
                            fw.mm(psb[i][:, h * 128:(h + 1) * 128], rawQ[:, h, tsl], q.Sbf[:, h, :], h == 0, True,
                                  [QB[h], q.SBF_], [PSB[i]])
                        fw.tt("dve", q.tmp[R, :, :], v4(psb[i][R, :]), bc4(egc[R, t, c0:c0 + 4]), ALU.mult, [PSB[i], EGC], [q.TMP_])
                        i = nb(q)
                        for h in range(4):
                            fw.mm(psb[i][:, h * 128:(h + 1) * 128], q.AT[R, h, :], q.vnew[R, h, :], h == 0, True,
                                  [q.AT_, q.VNEW_], [PSB[i]])
                        fw.tt("dve", q.tmp[R, :, :], q.tmp[R, :, :], v4(psb[i][R, :]), ALU.add, [PSB[i], q.TMP_], [q.TMP_])
                        fw.tt("pool", Oacc[R, t, :], Oacc[R, t, :], q.tmp[R, :, :].rearrange("p a b -> p (a b)"), ALU.add,
                              [q.TMP_, OACC[t]], [OACC[t]])
                        yield
                    i = nb(q)
                    for h in range(4):
                        fw.mm(psb[i][:, h * 128:(h + 1) * 128], q.kdec[R, h, :], q.vnew[R, h, :], h == 0, True,
                              [q.KDEC_, q.VNEW_], [PSB[i]])
                    dlv = dl[:, t, :].rearrange("p (a b) -> p a b", b=2)[:, c0:c0 + 4, c]
                    fw.tt("dve", q.S[:], q.S[:], bc4(dlv), ALU.mult, [q.S_, DL], [q.S_])
                    fw.tt("dve", q.S[:], q.S[:], v4(psb[i][:, :]), ALU.add, [PSB[i], q.S_], [q.S_])
                    fw.cp("act", q.Sbf[:], q.S[:], [q.S_], [q.SBF_])
                    yield

            def chain(tiles, d):
                pre = quad_pre(tiles[0], d, tiles[0] < NT_EXT, 0)
                yield from pre
                for k, t in enumerate(tiles):
                    gens = [quad_steps(t, d, t < NT_EXT, k % 2)]
                    if k + 1 < len(tiles):
                        gens.append(quad_pre(tiles[k + 1], d, tiles[k + 1] < NT_EXT, (k + 1) % 2))
                    while gens:
                        for g_ in list(gens):
                            try:
                                next(g_)
                                yield
                            except StopIteration:
                                gens.remove(g_)

            nq = int(os.environ.get("GDN_NQ", "999"))
            chF = chain(([32, 33] + list(range(0, NT_EXT)))[:nq], 0)
            chB = chain(([33, 32] + list(range(31, -1, -1)))[:nq], 1)
            alive = [chF, chB]
            if os.environ.get("GDN_ONLY"):
                alive = [chF] if os.environ["GDN_ONLY"] == "F" else [chB]
            nst = 0
            while alive:
                for g_ in list(alive):
                    try:
                        next(g_)
                        nst += 1
                        chk("qs%d" % nst)
                    except StopIteration:
                        alive.remove(g_)
            fw.barrier()
            if stop_after == 4:
                dump("Oacc", Oacc[:], [128, NT_EXT, 512], OACC[0], BF16)
                dump("S0", sets[0].S[:], [128, 4, 128], sets[0].S_)
                dump("S1", sets[1].S[:], [128, 4, 128], sets[1].S_)
                fw.finish()
                return nc, dbg_outs
        gdn.close()

        att = ExitStack()
        st.enter_context(att)
        KTa = sb(att, "KTa", [128, TOK_ALL], BF16); KTA = [Buf("kta%d" % g) for g in range(9)]
        Va = sb(att, "Va", [128, NT_ALL, 2, 65], BF16); VA = [Buf("va%d" % t) for t in range(NT_ALL)]
        QTa = sb(att, "QTa", [128, 4, TOK_EXT], BF16); QTA = [Buf("qta%d" % t) for t in range(NT_EXT)]
        zs = sb(att, "zs", [128, NT_EXT, 512], BF16); ZS = [Buf("zs%d" % t) for t in range(NT_EXT)]
        fw.op("pool", lambda e: e.memset(Va[:], 1.0), [], VA)
        with ExitStack() as p2:
            rope_sb = sb(p2, "rope_sb", [128, NT_LAT, 64], F32); ROPE = Buf("rope")
            fw.ld(rope_sb[:], rope_cs.rearrange("(t p) c -> p t c", p=128), [ROPE])
            wa = sb(p2, "wa", [128, 8, 1280], BF16); WA = Buf("wa")
            load_weight_bf16(p2, "wa", w_in_a.rearrange("(k p) n -> p k n", p=128), 1280, wa, WA)
            nctx = make_norm(p2, "b")
            hT = [sb(p2, "hTb%d" % i, [128, 8, 512], BF16) for i in range(2)]
            HT = [Buf("hTb0"), Buf("hTb1")]
            qsq_ = [sb(p2, "qsq%d" % i, [128, 640], F32) for i in range(2)]; QSQ_ = [Buf("qsq0"), Buf("qsq1")]
            qst_ = [sb(p2, "qst%d" % i, [128, 32], F32) for i in range(2)]; QST_ = [Buf("qst0"), Buf("qst1")]
            qn_ = [sb(p2, "qn%d" % i, [128, 640], F32) for i in range(2)]; QN_ = [Buf("qn0"), Buf("qn1")]
            rt_ = [[sb(p2, "rt%d_%d" % (k, i), [128, 10, 32], F32) for i in range(4)] for k in range(2)]
            RT_ = [[Buf("rt%d_%d" % (k, i)) for i in range(4)] for k in range(2)]
            qr_ = [sb(p2, "qr%d" % i, [128, 640], BF16) for i in range(2)]; QR_ = [Buf("qr0"), Buf("qr1")]
            kg2 = sb(p2, "kg2", [128, 640], F32); KG2 = Buf("kg2")
            fw.cp("dve", kg2[:, 0:512], gq8[:], [GQ8], [KG2])
            for hh in range(2):
                fw.cp("dve", kg2[:, 512 + hh * 64:576 + hh * 64], kg_bc, [VEC], [KG2])
            allg = groups_ext + groups_oth

            def prep2(gi):
                grp = allg[gi]
                load_norm_group(nctx, grp)
                transpose_mod(nctx, len(grp), lambda kc, T, s=gi % 2: hT[s][:, kc, 0:T], HT[gi % 2],
                              16 if grp[0] >= NT_LAT else 0)
            prep2(0)
            tiles2 = [(gi, i_, t) for gi, grp in enumerate(allg) for i_, t in enumerate(grp)]

            def mm2(k):
                gi, i_, t = tiles2[k]
                s = gi % 2
                bq, bkv, bz = (0, 1, 2) if k % 2 == 0 else (3, 4, 5)
                lt = hT[s][:, :, i_ * 128:(i_ + 1) * 128]
                is_ext = t < NT_EXT
                if is_ext:
                    for kc in range(8):
                        fw.mm(psb[bq][:, :], lt[:, kc, :], wa[:, kc, 0:512], kc == 0, kc == 7, [WA, HT[s]], [PSB[bq]])
                    for kc in range(8):
                        fw.mm(psb[bz][:, :], lt[:, kc, :], wa[:, kc, 768:1280], kc == 0, kc == 7, [WA, HT[s]], [PSB[bz]])
                for kc in range(8):
                    fw.mm(psb[bkv][:, 0:256], lt[:, kc, :], wa[:, kc, 512:768], kc == 0, kc == 7, [WA, HT[s]], [PSB[bkv]])

            def post2(k):
                gi, i_, t = tiles2[k]
                bq, bkv, bz = (0, 1, 2) if k % 2 == 0 else (3, 4, 5)
                is_ext = t < NT_EXT
                is_ctx = t >= NT_LAT
                kk = k % 2
                qsq, QSQ, qst, QST, qn, QN, rt, RT, qr, QR = (qsq_[kk], QSQ_[kk], qst_[kk], QST_[kk], qn_[kk], QN_[kk],
                                                              rt_[kk], RT_[kk], qr_[kk], QR_[kk])
                if True:
                    hoff = 0 if is_ext else 8
                    if is_ext:
                        fw.act(zs[:, t, :], psb[bz][:, :], AF.Silu, [PSB[bz]], [ZS[t]])
                    fw.cp("act", Va[:, t, :, 0:64], psb[bkv][:, 128:256].rearrange("p (a b) -> p a b", a=2), [PSB[bkv]], [VA[t]])
                    if is_ext:
                        fw.cp("dve", qn[:, 0:512], psb[bq][:, :], [PSB[bq]], [QN])
                    fw.cp("dve", qn[:, 512:640], psb[bkv][:, 0:128], [PSB[bkv]], [QN])
                    yield
                    c_lo, c_hi = hoff * 64, 640
                    fw.tt("dve", qsq[:, c_lo:c_hi], qn[:, c_lo:c_hi], qn[:, c_lo:c_hi], ALU.mult, [QN], [QSQ])
                    yield
                    fw.op("dve", lambda e, hoff=hoff: e.tensor_reduce(
                        out=qst[:, hoff:10], in_=qsq[:, hoff * 64:640].rearrange("p (a b) -> p a b", b=64),
                        axis=AX.X, op=ALU.add), [QSQ], [QST])
                    yield
                    fw.act(qst[:, 10 + hoff:20], qst[:, hoff:10], AF.Sqrt, [QST, CST], [QST], bias=eps_ap, scale=1.0 / 64)
                    yield
                    fw.op("dve", lambda e, hoff=hoff: e.reciprocal(out=qst[:, 20 + hoff:30], in_=qst[:, 10 + hoff:20]), [QST], [QST])
                    yield
                    n_h = 10 - hoff
                    qv = qn[:, c_lo:c_hi].rearrange("p (a b) -> p a b", b=64)
                    fw.tt("dve", qv, qv, qst[:, 20 + hoff:30].unsqueeze(2).to_broadcast([128, n_h, 64]), ALU.mult, [QN, QST], [QN])
                    yield
                    fw.tt("dve", qn[:, c_lo:c_hi], qn[:, c_lo:c_hi], kg2[:, c_lo:c_hi], ALU.mult, [QN, KG2], [QN])
                    yield
                    qrv = qr[:, c_lo:c_hi].rearrange("p (a b) -> p a b", b=64)
                    if not is_ctx:
                        cosb = rope_sb[:, t, 0:32].unsqueeze(1).to_broadcast([128, n_h, 32])
                        sinb = rope_sb[:, t, 32:64].unsqueeze(1).to_broadcast([128, n_h, 32])
                        x1_, x2_ = qv[:, :, 0:32], qv[:, :, 32:64]
                        r0, r1, r2, r3 = (rt[j][:, hoff:10, :] for j in range(4))
                        fw.tt("dve", r0, x1_, cosb, ALU.mult, [QN, ROPE], [RT[0]])
                        fw.tt("dve", r1, x2_, sinb, ALU.mult, [QN, ROPE], [RT[1]])
                        yield
                        fw.tt("dve", r2, x2_, cosb, ALU.mult, [QN, ROPE], [RT[2]])
                        fw.tt("dve", r3, x1_, sinb, ALU.mult, [QN, ROPE], [RT[3]])
                        yield
                        fw.tt("dve", qrv[:, :, 0:32], r0, r1, ALU.subtract, [RT[0], RT[1]], [QR])
                        fw.tt("dve", qrv[:, :, 32:64], r2, r3, ALU.add, [RT[2], RT[3]], [QR])
                        yield
                    else:
                        fw.cp("dve", qr[:, c_lo:c_hi], qn[:, c_lo:c_hi], [QN], [QR])
                    pt, PT = next_pst()
                    fw.tr(pt[:, 0:128], qr[:, 512:640], ident_b[:], [QR, IDB], [PT])
                    if is_ext:
                        for j in range(4):
                            fw.tr(pt[:, 128 + j * 128:256 + j * 128], qr[:, j * 128:(j + 1) * 128], ident_b[:], [QR, IDB], [PT])
                    fw.cp("act", KTa[:, t * 128:(t + 1) * 128], pt[:, 0:128], [PT], [KTA[t // 4]])
                    if is_ext:
                        fw.cp("dve", QTa[:, :, t * 128:(t + 1) * 128], pt[:, 128:640].rearrange("p (a b) -> p a b", a=4),
                              [PT], [QTA[t]])

            prepped = {0}

            def ensure_prep(k):
                gi = tiles2[k][0]
                for g_ in range(gi + 2):
                    if g_ < len(allg) and g_ not in prepped and g_ <= gi + 1:
                        prep2(g_)
                        prepped.add(g_)
            ensure_prep(0)
            mm2(0)
            mm2(1)
            nt2 = len(tiles2)
            for p_ in range(0, nt2, 2):
                gens = [post2(k) for k in (p_, p_ + 1) if k < nt2]
                for g_ in gens:
                    next(g_)
                for k in (p_ + 2, p_ + 3):
                    if k < nt2:
                        ensure_prep(k)
                        mm2(k)
                alive = list(gens)
                while alive:
                    for g_ in list(alive):
                        try:
                            next(g_)
                        except StopIteration:
                            alive.remove(g_)
            fw.barrier()
        wo = sb(att, "wo", [128, 8, 1024], BF16); WO = Buf("wo")
        with ExitStack() as p2w:
            load_weight_bf16(p2w, "wo", w_out.rearrange("(k p) n -> p k n", p=128), 1024, wo, WO)
        if stop_after == 5:
            dump("KTa", KTa[:], [128, TOK_ALL], KTA[0], BF16)
            dump("QTa", QTa[:], [128, 4, TOK_EXT], QTA[0], BF16)
            dump("Va", Va[:], [128, NT_ALL, 2, 65], VA[0], BF16)
            dump("zs", zs[:], [128, NT_EXT, 512], ZS[0], BF16)
            fw.finish()
            return nc, dbg_outs

        with ExitStack() as p3:
            pT = [sb(p3, "pT%d" % i, [128, 512], BF16) for i in range(4)]
            PTB = [Buf("pT%d" % i) for i in range(4)]
            oa = sb(p3, "oa", [128, 512], F32); OA = Buf("oa")
            oT = [sb(p3, "oT%d" % i, [65, 512], F32) for i in range(2)]
            OTB = [Buf("oT0"), Buf("oT1")]
            og = sb(p3, "og", [128, 512], F32); OG = Buf("og")
            ost = sb(p3, "ost", [128, 32], F32); OST = Buf("ost")
            ojunk = sb(p3, "ojunk", [128, 512], BF16); OJ = Buf("ojunk")
            mix = sb(p3, "mix", [128, 1024], BF16); MIX = Buf("mix")
            mixT = sb(p3, "mixT", [128, 8, 128], BF16); MIXT = Buf("mixT")
            xres = [sb(p3, "xres%d" % i, [128, D], F32) for i in range(2)]
            XRES = [Buf("xres0"), Buf("xres1")]
            x1t = [sb(p3, "x1t%d" % i, [128, D], F32) for i in range(2)]
            X1T = [Buf("x1t0"), Buf("x1t1")]
            nctx = make_norm(p3, "c")
            h2stage = sb(p3, "h2stage", [128, 8, 128], BF16); H2S = Buf("h2stage")
            h2d = nc.dram_tensor("h2d", [128, 8, NT_OWN * 128 + 128], BF16, kind="Internal").ap()
            ob = (6, 7)
            obT = (4, 5)
            items = [(qt, kt, g) for qt in range(NT_EXT) for kt in range(NT_ALL) for g in range(2)]
            LOOK = 3

            def emit_qk(n):
                qt, kt, g = items[n]
                bi = n % 4
                P0, P1 = 64 * g, 64 * g + 64
                fw.mm(psb[bi][:, :], KTa[P0:P1, kt * 128:(kt + 1) * 128], QTa[P0:P1, :, qt * 128:(qt + 1) * 128],
                      True, True, [KTA[kt // 4], QTA[qt]], [PSB[bi]])

            def emit_pv(n):
                qt, kt, g = items[n]
                bi = n % 4
                pi = n % 4
                fw.act(pT[pi][:], psb[bi][:, :], AF.Exp, [PSB[bi]], [PTB[pi]])
                fw.mm(psb[obT[g]][0:65, :], Va[:, kt, g, :], pT[pi][:], kt == 0, kt == NT_ALL - 1,
                      [PTB[pi], VA[kt]], [PSB[obT[g]]])
            def epilogue(qt):
                for g in range(2):
                    fw.cpa(oT[g][:], psb[obT[g]][0:65, :], [PSB[obT[g]]], [OTB[g]])
                yield
                yield
                for g in range(2):
                    for j in range(4):
                        fw.tr(psb[ob[g]][:, j * 65:(j + 1) * 65], oT[g][:, j * 128:(j + 1) * 128], ident_f[0:65, 0:65],
                              [OTB[g], CM], [PSB[ob[g]]])
                    yield
                for g in range(2):
                    ov = psb[ob[g]][:, 0:260].rearrange("p (a b) -> p a b", b=65)
                    fw.op("dve", lambda e, ov=ov, g=g: e.reciprocal(out=ost[:, g * 4:g * 4 + 4], in_=ov[:, :, 64]), [PSB[ob[g]]], [OST])
                    fw.tt("dve", oa[:, g * 256:(g + 1) * 256].rearrange("p (a b) -> p a b", b=64), ov[:, :, 0:64],
                          ost[:, g * 4:g * 4 + 4].unsqueeze(2).to_broadcast([128, 4, 64]), ALU.mult, [PSB[ob[g]], OST], [OA])
                yield
                fw.act(ojunk[:], oa[:], AF.Square, [OA], [OJ, OST], accum=ost[:, 8:9])
                yield
                fw.act(ost[:, 9:10], ost[:, 8:9], AF.Sqrt, [OST, CST], [OST], bias=eps_ap, scale=1.0 / 512)
                yield
                fw.op("dve", lambda e: e.reciprocal(out=ost[:, 10:11], in_=ost[:, 9:10]), [OST], [OST])
                yield
                fw.stt("dve", mix[:, 0:512], oa[:], ost[:, 10:11], aog_bc, ALU.mult, ALU.mult, [OA, OST, VEC], [MIX])
                yield
                fw.tt("pool", og[:], Oacc[:, qt, :], Oacc[:, qt, :], ALU.mult, [OACC[qt]], [OG])
                yield
                fw.op("dve", lambda e: e.tensor_reduce(out=ost[:, 12:16], in_=og[:].rearrange("p (a b) -> p a b", b=128),
                                                       axis=AX.X, op=ALU.add), [OG], [OST])
                yield
                fw.act(ost[:, 16:20], ost[:, 12:16], AF.Sqrt, [OST, CST], [OST], bias=cst[:, 2:3], scale=1.0 / 128)
                yield
                fw.op("dve", lambda e: e.reciprocal(out=ost[:, 20:24], in_=ost[:, 16:20]), [OST], [OST])
                yield
                ogv = og[:].rearrange("p (a b) -> p a b", b=128)
                fw.tt("dve", ogv, Oacc[:, qt, :].rearrange("p (a b) -> p a b", b=128),
                      ost[:, 20:24].unsqueeze(2).to_broadcast([128, 4, 128]), ALU.mult, [OACC[qt], OST], [OG])
                yield
                fw.tt("pool", ogv, ogv, gng_bc.unsqueeze(1).to_broadcast([128, 4, 128]), ALU.mult, [OG, VEC], [OG])
                yield
                fw.tt("dve", mix[:, 512:1024], og[:], zs[:, qt, :], ALU.mult, [OG, ZS[qt]], [MIX])
                yield
                pt, PT = next_pst()
                for kc in range(8):
                    fw.tr(pt[:, kc * 128:(kc + 1) * 128], mix[:, kc * 128:(kc + 1) * 128], ident_b[:], [MIX, IDB], [PT])
                yield
                fw.cpa(mixT[:], pt[:, 0:1024].rearrange("p (a b) -> p a b", a=8), [PT], [MIXT])
                yield
                s_ = qt % 2
                fw.ld(xres[s_][:], xs[qt * 128:(qt + 1) * 128, :], [XRES[s_]])
                for half in range(2):
                    bi = 6 + half
                    for kc in range(8):
                        fw.mm(psb[bi][:, :], mixT[:, kc, :], wo[:, kc, half * 512:(half + 1) * 512], kc == 0, kc == 7,
                              [MIXT, WO], [PSB[bi]])
                    yield
                    fw.tt("dve", x1t[s_][:, half * 512:(half + 1) * 512], psb[bi][:, :], G12[:, 0, half * 512:(half + 1) * 512],
                          ALU.mult, [PSB[bi], G12B], [X1T[s_]])
                    yield
                    fw.tt("pool", x1t[s_][:, half * 512:(half + 1) * 512], x1t[s_][:, half * 512:(half + 1) * 512],
                          xres[s_][:, half * 512:(half + 1) * 512], ALU.add, [X1T[s_], XRES[s_]], [X1T[s_]])
                if qt < NT_OWN:
                    fw.stor(x1s[qt * 128:(qt + 1) * 128, :], x1t[s_][:], [X1T[s_]])
                yield
                norm_rows(nctx, x1t[s_][:], X1T[s_], 0)
                yield
                transpose_mod(nctx, 1, lambda kc, T: h2stage[:, kc, 0:T], H2S, 32)
                fw.stor(h2d[:, :, qt * 128:(qt + 1) * 128], h2stage[:], [H2S])

            pend = [None]

            def step_pending():
                if pend[0] is not None:
                    try:
                        next(pend[0])
                    except StopIteration:
                        pend[0] = None

            def flush_pending():
                while pend[0] is not None:
                    step_pending()
            emit_qk(0)
            emit_qk(1)
            for n in range(len(items)):
                if n % 2 == 0 and n + 2 < len(items):
                    emit_qk(n + 2)
                    emit_qk(n + 3)
                emit_pv(n)
                step_pending()
                qt, kt, g = items[n]
                if kt == NT_ALL - 1 and g == 1:
                    flush_pending()
                    pend[0] = epilogue(qt)
                    step_pending()
            flush_pending()
            fw.barrier()
        att.close()
        mixer.close()

        with ExitStack() as p4:
            NTOK = NT_OWN * 128
            actT = sb(p4, "actT", [128, 22, NTOK], BF16); ACTT = [Buf("actT%d" % c) for c in range(22)]
            with ExitStack() as p4a:
                h2T = sb(p4a, "h2T", [128, 8, NTOK + 128], BF16); H2T = Buf("h2T")
                for kc in range(8):
                    fw.ld(h2T[:, kc, :], h2d[:, kc, :], [H2T])
                HALF = NTOK // 2
                ub = [sb(p4a, "ubuf%d" % i, [128, HALF + 4], F32) for i in range(2)]
                UB = [Buf("ubuf0"), Buf("ubuf1")]
                ucb = [sb(p4a, "uc%d" % i, [128, HALF], F32) for i in range(2)]
                UC = [Buf("uc0"), Buf("uc1")]
                sg = sb(p4a, "sg", [128, NTOK], BF16); SG = [Buf("sg0"), Buf("sg1")]
                wst = [sb(p4a, "wust%d" % i, [128, 8, 256], F32) for i in range(2)]
                WST = [Buf("wust0"), Buf("wust1")]
                wu = [sb(p4a, "wu%d" % i, [128, 8, 256], BF16) for i in range(2)]
                WU = [Buf("wu0"), Buf("wu1")]
                fw.op("pool", lambda e: e.memset(ub[0][:, 0:1], 0.0), [], [UB[0]])
                wupv = w_up.rearrange("(k p) n -> p k n", p=128)
                bcnt = 0

                def load_w(c):
                    s_ = c % 2
                    fw.ld(wst[s_][:, :, 0:128], wupv[:, :, c * 128:(c + 1) * 128], [WST[s_]])
                    fw.ld(wst[s_][:, :, 128:256], wupv[:, :, (22 + c) * 128:(23 + c) * 128], [WST[s_]])
                    fw.cp("pool", wu[s_][:], wst[s_][:], [WST[s_]], [WU[s_]])
                load_w(0)
                for c in range(22):
                    s_ = c % 2
                    if c + 1 < 22:
                        load_w(c + 1)
                    for part in range(2):
                        ch = c + 22 * part
                        w0, w1, w2, bb = (fcw_sb[:, ch * 4 + j:ch * 4 + j + 1] for j in range(4))
                        for hf in range(2):
                            tok_lo, col_lo, ntk = (0, 1, HALF + 1) if hf == 0 else (HALF - 1, 0, HALF + 2)
                            for o in range(0, ntk, 512):
                                n = min(512, ntk - o)
                                bi = bcnt % 6
                                bcnt += 1
                                for kc in range(8):
                                    fw.mm(psb[bi][:, 0:n], wu[s_][:, kc, part * 128:(part + 1) * 128],
                                          h2T[:, kc, tok_lo + o:tok_lo + o + n], kc == 0, kc == 7, [WU[s_], H2T], [PSB[bi]])
                                fw.cp("act", ub[hf][:, col_lo + o:col_lo + o + n], psb[bi][:, 0:n], [PSB[bi]], [UB[hf]])
                                j0 = max(0, 1 - (col_lo + o))
                                j1 = min(n, HALF + 1 - (col_lo + o))
                                if j1 > j0:
                                    fw.act(ucb[hf][:, col_lo + o + j0 - 1:col_lo + o + j1 - 1], psb[bi][:, j0:j1], AF.Identity,
                                           [PSB[bi], FCW], [UC[hf]], bias=bb, scale=w1)
                            fw.stt("dve", ucb[hf][:], ub[hf][:, 0:HALF], w0, ucb[hf][:], ALU.mult, ALU.add, [UB[hf], FCW, UC[hf]], [UC[hf]])
                            fw.stt("dve", ucb[hf][:], ub[hf][:, 2:HALF + 2], w2, ucb[hf][:], ALU.mult, ALU.add, [UB[hf], FCW, UC[hf]], [UC[hf]])
                            hs = slice(hf * HALF, (hf + 1) * HALF)
                            if part == 0:
                                fw.act(sg[:, hs], ucb[hf][:], AF.Silu, [UC[hf]], [SG[hf]])
                            else:
                                fw.tt("pool", actT[:, c, hs], ucb[hf][:], sg[:, hs], ALU.mult, [UC[hf], SG[hf]], [ACTT[c]])
                fw.barrier()
            if stop_after == 7:
                dump("actT", actT[:], [128, 22, NTOK], ACTT[0], BF16)
                fw.finish()
                return nc, dbg_outs
            wd = sb(p4, "wd", [128, 22, 1024], BF16); WD = Buf("wd")
            load_weight_bf16(p4, "wd", w_down.rearrange("(c p) n -> p c n", p=128), 1024, wd, WD, piece=128)
            x1r = [sb(p4, "x1r%d" % i, [128, D], F32) for i in range(2)]
            X1R = [Buf("x1r0"), Buf("x1r1")]
            x2 = [sb(p4, "x2_%d" % i, [128, D], F32) for i in range(2)]
            X2 = [Buf("x2_0"), Buf("x2_1")]
            fj = sb(p4, "fjunk", [128, D], BF16); FJ = Buf("fjunk")
            fst = sb(p4, "fst", [128, 8], F32); FST = Buf("fst")
            def mm_down(t):
                s_ = t % 2
                fw.ld(x1r[s_][:], x1s[t * 128:(t + 1) * 128, :], [X1R[s_]])
                for half in range(2):
                    bi = 2 * s_ + half
                    for c in range(22):
                        fw.mm(psb[bi][:, :], actT[:, c, t * 128:(t + 1) * 128], wd[:, c, half * 512:(half + 1) * 512],
                              c == 0, c == 21, [ACTT[c], WD], [PSB[bi]])

            def post_down(t):
                s_ = t % 2
                for half in range(2):
                    bi = 2 * s_ + half
                    hs = slice(half * 512, (half + 1) * 512)
                    fw.tt("dve", x2[s_][:, hs], psb[bi][:, :], G12[:, 1, hs], ALU.mult, [PSB[bi], G12B], [X2[s_]])
                    fw.tt("pool", x2[s_][:, hs], x2[s_][:, hs], x1r[s_][:, hs], ALU.add, [X2[s_], X1R[s_]], [X2[s_]])
                fw.act(fj[:], x2[s_][:], AF.Square, [X2[s_]], [FJ, FST], accum=fst[:, 4 * s_:4 * s_ + 1])
                fw.act(fst[:, 4 * s_ + 1:4 * s_ + 2], fst[:, 4 * s_:4 * s_ + 1], AF.Sqrt, [FST, CST], [FST], bias=eps_ap, scale=1.0 / D)
                fw.op("dve", lambda e: e.reciprocal(out=fst[:, 4 * s_ + 2:4 * s_ + 3], in_=fst[:, 4 * s_ + 1:4 * s_ + 2]), [FST], [FST])
                fw.stt("dve", x2[s_][:], x2[s_][:], fst[:, 4 * s_ + 2:4 * s_ + 3], fng_bc, ALU.mult, ALU.mult, [X2[s_], FST, VEC], [X2[s_]])
                fw.stor(y[t * 128:(t + 1) * 128, :], x2[s_][:], [X2[s_]])
            mm_down(0)
            for t in range(NT_OWN):
                if t + 1 < NT_OWN:
                    mm_down(t + 1)
                post_down(t)
            fw.barrier()
        fw.finish()
        return nc, dbg_outs


def _prep_inputs(inputs):
    f = np.float32
    x = np.asarray(inputs["x"], f)
    c = np.asarray(inputs["c"], f)
    ctx = np.asarray(inputs["ctx"], f)
    c_ctx = np.asarray(inputs["c_ctx"], f)
    w_in = np.asarray(inputs["w_in"], f)[0]

    def col(v):
        return np.ascontiguousarray(v.reshape(-1, 128).T)

    q_a = w_in[:, 0:512].reshape(D, 8, 64)
    order = [0, 4, 1, 5, 2, 6, 3, 7]
    q_perm = q_a[:, order, :].reshape(D, 512)
    k_a = w_in[:, 512:640]
    v_a = w_in[:, 640:768]
    gq = w_in[:, 768:1280]
    gk = w_in[:, 1280:1792]
    gv = w_in[:, 1792:2304]
    z = w_in[:, 2304:2816]
    a_f, a_b, b_f, b_b = (w_in[:, 2816 + 4 * i:2820 + 4 * i] for i in range(4))
    w_in_a = np.ascontiguousarray(np.concatenate([q_perm, k_a, v_a, z], axis=1))
    convq = np.asarray(inputs["conv_qkv_w"], f)[0]
    ffw = np.asarray(inputs["ffn_conv_w"], f)[0]
    ffb = np.asarray(inputs["ffn_conv_b"], f)[0]

    idx = np.arange(128)
    same = (idx[:, None] // 64) == (idx[None, :] // 64)
    ident = np.eye(128, dtype=f)
    ind = np.stack([(idx < 64), (idx >= 64)], axis=1).astype(f)
    mF = np.concatenate([(same & (idx[:, None] <= idx[None, :])).astype(f), ind], axis=1)
    mB = np.concatenate([(same & (idx[:, None] >= idx[None, :])).astype(f), ind], axis=1)
    blk = same.astype(f)
    negF = np.where(same & (idx[:, None] <= idx[None, :]), 0.0, -BIG).astype(f)
    posF = np.where(same & (idx[None, :] < idx[:, None]), 0.0, BIG).astype(f)
    negB = np.where(same & (idx[:, None] >= idx[None, :]), 0.0, -BIG).astype(f)
    posB = np.where(same & (idx[None, :] > idx[:, None]), 0.0, BIG).astype(f)
    cmat = np.ascontiguousarray(np.concatenate([ident, mF, mB, blk, negF, posF, negB, posB], axis=1))

    rows = SEQ // 64
    row = np.repeat(np.arange(rows, dtype=f), 64)
    colp = np.tile(np.arange(64, dtype=f), rows)
    inv_freq = (10000.0 ** (-np.arange(16, dtype=f) / 16)).astype(f)
    ang = np.concatenate([row[:, None] * inv_freq, colp[:, None] * inv_freq], axis=-1).astype(f)
    rope = np.concatenate([np.cos(ang), np.sin(ang)], axis=1).astype(f)

    shared = dict(
        w_mod=np.ascontiguousarray(np.asarray(inputs["w_mod"], f)[0]),
        bmod_col=col(np.asarray(inputs["b_mod"], f)[0]),
        bmod_row=np.ascontiguousarray(np.asarray(inputs["b_mod"], f)[0][None, :]),
        ng_col=np.ascontiguousarray(np.concatenate([col(np.asarray(inputs["norm1_g"], f)[0]),
                                                    col(np.asarray(inputs["norm2_g"], f)[0])], axis=1)),
        w_in_a=w_in_a,
        w_out=np.ascontiguousarray(np.asarray(inputs["w_out"], f)[0]),
        w_up=np.ascontiguousarray(np.asarray(inputs["w_up"], f)[0]),
        w_down=np.ascontiguousarray(np.asarray(inputs["w_down"], f)[0]),
        cmat=cmat,
    )
    in_maps = []
    for r in range(8):
        b, flip = r // 2, (r % 2 == 1)
        m = dict(shared)
        xb, cb, rp = x[b], ctx[b], rope
        if flip:
            xb, cb, rp = xb[::-1], cb[::-1], rp[::-1]
        m["xs"] = np.ascontiguousarray(xb)
        m["cs"] = np.ascontiguousarray(cb)
        m["rope_cs"] = np.ascontiguousarray(rp)
        m["ccol"] = np.ascontiguousarray(np.concatenate([col(c[b]), col(c_ctx)], axis=1))
        aF, aB, bF, bB = (a_b, a_f, b_b, b_f) if flip else (a_f, a_b, b_f, b_b)
        m["w_in_g"] = np.ascontiguousarray(np.concatenate([gq, gk, gv, aF, aB, bF, bB], axis=1))
        taps = [2, 1, 0] if flip else [0, 1, 2]
        cwq = convq[taps]
        m["convw"] = np.ascontiguousarray(cwq.reshape(3, 12, 128).transpose(2, 1, 0).reshape(128, 36))
        fw_ = ffw[taps].reshape(3, NFC, 128)
        fb_ = ffb.reshape(1, NFC, 128)
        m["fcw"] = np.ascontiguousarray(np.concatenate([fw_, fb_], axis=0).transpose(2, 1, 0).reshape(128, NFC * 4))
        al = [np.asarray(inputs[k], f)[0] for k in ("a_log_f", "a_log_b", "dt_bias_f", "dt_bias_b")]
        if flip:
            al = [al[1], al[0], al[3], al[2]]
        m["vecs"] = np.ascontiguousarray(np.concatenate([
            np.asarray(inputs["q_norm_g"], f)[0], np.asarray(inputs["k_norm_g"], f)[0],
            np.asarray(inputs["attn_out_g"], f)[0], np.asarray(inputs["gdn_norm_g"], f)[0],
            al[0], al[1], al[2], al[3], np.asarray(inputs["final_norm_g"], f)])[None, :])
        in_maps.append(m)
    return in_maps


def kernel(**inputs):
    in_maps = _prep_inputs(inputs)
    if os.environ.get("KSTOP"):
        nc, _ = _build(dbg=True, stop_after=int(os.environ["KSTOP"]))
        run_bass_kernel_spmd(nc, in_maps, core_ids=list(range(8)))
        return np.zeros((4, SEQ, D), np.float32)
    nc, _ = _build()
    res = run_bass_kernel_spmd(nc, in_maps, core_ids=list(range(8)))
    out = np.empty((4, SEQ, D), np.float32)
    for r in range(8):
        yb = np.asarray(res.results[r]["y"], np.float32)
        b = r // 2
        if r % 2 == 0:
            out[b, 0:2048] = yb
        else:
            out[b, 2048:4096] = yb[::-1]
    return out
```

```python
import os
from contextlib import ExitStack
import numpy as np
import concourse.bass as bass
import concourse.mybir as mybir
from concourse.bass_utils import run_bass_kernel_spmd

F32 = mybir.dt.float32
BF16 = mybir.dt.bfloat16
AF = mybir.ActivationFunctionType
ALU = mybir.AluOpType
AX = mybir.AxisListType

D = 1024
SEQ = 4096
CTX = 256
NT_LAT = 32
NT_ALL = 34
NT_EXT = 17
NT_OWN = 16
TOK_ALL = NT_ALL * 128
TOK_EXT = NT_EXT * 128
DFF = 2816
NFC = 44
EPS = 1e-6
BIG = 30000.0


class Buf:
    __slots__ = ("name", "last_w", "readers", "dsem", "dcount", "excl")

    def __init__(self, name="b", excl=False):
        self.name = name
        self.excl = excl
        self.last_w = None
        self.readers = []
        self.dsem = None
        self.dcount = 0


class _Eng:
    def __init__(self, name):
        self.name = name
        self.count = 0
        self.waited = {}
        self.ops = []
        self.is_pe = name == "pe"


class FW:
    def __init__(self, nc, stack):
        self.nc = nc
        self.stack = stack
        self.engs = {n: _Eng(n) for n in ("pe", "act", "dve", "pool", "sp")}
        self.sems = {}
        for n in self.engs:
            self.sems[n] = stack.enter_context(nc.semaphore("s_" + n))
        self.nd = 0
        self._dma_tot = {}
        self.free_dsems = []
        self.rr = 0

    def _dma_sem(self, b):
        if b.dsem is None:
            key = "d%d" % self.nd
            self.nd += 1
            self.sems[key] = self.stack.enter_context(self.nc.semaphore("s_" + key))
            b.dsem = key
        return b.dsem

    def _deps(self, eng, reads, writes):
        deps = {}

        def add(ev):
            if ev is None:
                return
            k, v = ev
            if eng.is_pe and k == "pe":
                return
            if deps.get(k, 0) < v:
                deps[k] = v
        for b in reads:
            add(b.last_w)
            if b.excl:
                for r in b.readers:
                    if r[0] != eng.name:
                        add(r)
        for b in writes:
            add(b.last_w)
            for r in b.readers:
                add(r)
        waits = []
        for k, v in deps.items():
            if eng.waited.get(k, 0) < v:
                eng.waited[k] = v
                waits.append((k, v))
        return waits

    def op(self, engname, fn, reads=(), writes=()):
        eng = self.engs[engname]
        waits = self._deps(eng, reads, writes)
        eng.count += 1
        ev = (engname, eng.count)
        eng.ops.append((waits, fn, (engname, 1)))
        for b in reads:
            b.readers.append(ev)
        for b in writes:
            b.last_w = ev
            b.readers = []
        return ev

    def dma(self, fn, reads=(), writes=(), q="sp", track=None):
        eng = self.engs[q]
        waits = self._deps(eng, reads, writes)
        tb = track if track is not None else (writes[0] if writes else reads[0])
        key = self._dma_sem(tb)
        tb.dcount += 16
        ev = (key, tb.dcount)
        self._dma_tot[key] = tb.dcount
        eng.ops.append((waits, fn, (key, 16)))
        for b in reads:
            b.readers.append(ev)
        for b in writes:
            b.last_w = ev
            b.readers = []
        return ev

    def barrier(self):
        targets = {n: e.count for n, e in self.engs.items() if e.count > 0}
        for n, e in self.engs.items():
            waits = []
            for k, v in list(targets.items()) + list(self._dma_tot.items()):
                if k == n and e.is_pe:
                    continue
                if e.waited.get(k, 0) < v:
                    e.waited[k] = v
                    waits.append((k, v))
            if waits:
                e.ops.append((waits, None, None))

    def finish(self):
        self.barrier()
        nc = self.nc
        sems = self.sems

        def run(e, obj):
            for waits, fn, inc in e.ops:
                for k, v in waits:
                    obj.wait_ge(sems[k], v)
                if fn is not None:
                    ins = fn(obj)
                    ins.then_inc(sems[inc[0]], inc[1])

        with nc.Block() as block:
            @block.tensor
            def _(o):
                run(self.engs["pe"], o)

            @block.scalar
            def _(o):
                run(self.engs["act"], o)

            @block.vector
            def _(o):
                run(self.engs["dve"], o)

            @block.gpsimd
            def _(o):
                run(self.engs["pool"], o)

            @block.sync
            def _(o):
                run(self.engs["sp"], o)

    def mm(self, out, lhsT, rhs, start=True, stop=True, r=(), w=()):
        return self.op("pe", lambda e: e.matmul(out, lhsT=lhsT, rhs=rhs, start=start, stop=stop), r, w)

    def tr(self, out, in_, ident, r=(), w=()):
        return self.op("pe", lambda e: e.transpose(out=out, in_=in_, identity=ident), r, w)

    def act(self, out, in_, func, r=(), w=(), bias=None, scale=None, accum=None):
        kw = {}
        if bias is not None:
            kw["bias"] = bias
        if scale is not None:
            kw["scale"] = scale
        if accum is not None:
            kw["accum_out"] = accum
        return self.op("act", lambda e: e.activation(out=out, in_=in_, func=func, **kw), r, w)

    def ts(self, eng, out, in0, s1, s2, op0, op1=None, r=(), w=()):
        if op1 is None:
            return self.op(eng, lambda e: e.tensor_scalar(out=out, in0=in0, scalar1=s1, scalar2=None, op0=op0), r, w)
        return self.op(eng, lambda e: e.tensor_scalar(out=out, in0=in0, scalar1=s1, scalar2=s2, op0=op0, op1=op1), r, w)

    def tt(self, eng, out, in0, in1, op, r=(), w=()):
        return self.op(eng, lambda e: e.tensor_tensor(out=out, in0=in0, in1=in1, op=op), r, w)

    def stt(self, eng, out, in0, scalar, in1, op0, op1, r=(), w=()):
        return self.op(eng, lambda e: e.scalar_tensor_tensor(out=out, in0=in0, scalar=scalar, in1=in1, op0=op0, op1=op1), r, w)

    def cp(self, eng, out, in_, r=(), w=()):
        if eng == "act":
            return self.op("act", lambda e: e.copy(out=out, in_=in_), r, w)
        return self.op(eng, lambda e: e.tensor_copy(out=out, in_=in_), r, w)

    def cpa(self, out, in_, r=(), w=()):
        self.rr += 1
        return self.cp("dve" if self.rr % 2 else "act", out, in_, r, w)

    def ld(self, out, in_, w, q="sp", r=()):
        return self.dma(lambda e: e.dma_start(out=out, in_=in_), reads=r, writes=w, q=q)

    def stor(self, out, in_, r, track=None):
        return self.dma(lambda e: e.dma_start(out=out, in_=in_), reads=r, writes=(), track=track)


class _Stop(Exception):
    pass


def _build(dbg=False, stop_after=None):
    nc = bass.Bass("TRN2", target_bir_lowering=False)
    dbg_outs = {}
    try:
        return _build_inner(nc, dbg, stop_after, dbg_outs)
    except _Stop:
        return nc, dbg_outs


def _build_inner(nc, dbg, stop_after, dbg_outs):

    def din(name, shape):
        return nc.dram_tensor(name, list(shape), F32, kind="ExternalInput").ap()

    xs = din("xs", [SEQ, D])
    cs = din("cs", [CTX, D])
    ccol = din("ccol", [128, 16])
    w_mod = din("w_mod", [D, 6 * D])
    bmod_col = din("bmod_col", [128, 48])
    bmod_row = din("bmod_row", [1, 6 * D])
    ng_col = din("ng_col", [128, 16])
    w_in_g = din("w_in_g", [D, 1552])
    w_in_a = din("w_in_a", [D, 1280])
    convw = din("convw", [128, 36])
    rope_cs = din("rope_cs", [SEQ, 64])
    vecs = din("vecs", [1, 128 + 512 + 128 + 16 + 1024])
    w_out = din("w_out", [D, D])
    w_up = din("w_up", [D, 2 * DFF])
    fcw = din("fcw", [128, NFC * 4])
    w_down = din("w_down", [DFF, D])
    cmat = din("cmat", [128, 8 * 128 + 4])
    if stop_after is None:
        y = nc.dram_tensor("y", [NT_OWN * 128, D], F32, kind="ExternalOutput").ap()
        x1s = nc.dram_tensor("x1s", [NT_OWN * 128, D], F32, kind="Internal").ap()

    with ExitStack() as st:
        fw = FW(nc, st)

        def chk(tag):
            if os.environ.get("DBGSTOP") == tag:
                fw.finish()
                raise _Stop()

        def sb(stack, name, shape, dt):
            return stack.enter_context(nc.sbuf_tensor(name, list(shape), dt))

        psb = [st.enter_context(nc.psum_tensor("psb%d" % i, [128, 512], F32)) for i in range(8)]
        PSB = [Buf("psb%d" % i, excl=True) for i in range(8)]

        def psbf(i):
            return psb[i][:, :].bitcast(BF16)
        pst_i = [0]

        def next_pst():
            pst_i[0] += 1
            i = 6 + pst_i[0] % 2
            return psbf(i), PSB[i]

        cm = sb(st, "cm", [128, 8 * 128 + 4], F32); CM = Buf("cm")
        fw.ld(cm[:], cmat[:, :], [CM])
        ident_f = cm[:, 0:128]
        mcum = [cm[:, 128:258], cm[:, 258:388]]
        blk = cm[:, 388:516]
        negm = [cm[:, 516:644], cm[:, 772:900]]
        posm = [cm[:, 644:772], cm[:, 900:1028]]
        ident_b = sb(st, "ident_b", [128, 128], BF16); IDB = Buf("idb")
        fw.cp("dve", ident_b[:], ident_f, [CM], [IDB])
        maskb = sb(st, "maskb", [128, 4, 128], BF16); MASKB = Buf("maskb")
        fw.cp("dve", maskb[:].rearrange("p a b -> p (a b)"), cm[:, 516:1028], [CM], [MASKB])
        ones_b = sb(st, "ones_b", [128, 128], BF16); ONB = Buf("onb")
        fw.op("pool", lambda e: e.memset(ones_b[:], 1.0), [], [ONB])
        ones_f = sb(st, "ones_f", [128, 128], F32); ONF = Buf("onf")
        fw.op("pool", lambda e: e.memset(ones_f[:], 1.0), [], [ONF])
        cst = sb(st, "cst", [128, 8], F32); CST = Buf("cst")
        fw.op("pool", lambda e: e.memset(cst[:, 0:1], EPS), [], [CST])
        fw.op("pool", lambda e: e.memset(cst[:, 1:2], 1.0), [], [CST])
        fw.op("pool", lambda e: e.memset(cst[:, 2:3], EPS * 128.0), [], [CST])
        eps_ap = cst[:, 0:1]

        vb_ = sb(st, "vecs_bc", [128, 1808], F32); VEC = Buf("vecs")
        fw.ld(vb_[:], vecs[0:1, :].broadcast_to([128, 1808]), [VEC])
        qg_bc = vb_[:, 0:64]
        kg_bc = vb_[:, 64:128]
        aog_bc = vb_[:, 128:640]
        gng_bc = vb_[:, 640:768]
        alogdt_bc = vb_[:, 768:784]
        fng_bc = vb_[:, 784:1808]
        gq8 = sb(st, "gq8", [128, 512], F32); GQ8 = Buf("gq8")
        for hh in range(8):
            fw.ts("dve", gq8[:, hh * 64:(hh + 1) * 64], qg_bc, 0.125, None, ALU.mult, None, [VEC], [GQ8])
        cw = sb(st, "convw_sb", [128, 36], F32); CW = Buf("cw")
        fw.ld(cw[:], convw[:, :], [CW])
        fcw_sb = sb(st, "fcw_sb", [128, NFC * 4], F32); FCW = Buf("fcw")
        fw.ld(fcw_sb[:], fcw[:, :], [FCW])
        ngc = sb(st, "ngc", [128, 16], F32); NGC = Buf("ngc")
        fw.ld(ngc[:], ng_col[:, :], [NGC])
        G12 = sb(st, "G12", [128, 2, 1024], F32); G12B = Buf("G12")
        modv = sb(st, "modv", [128, 48], F32); MODV = Buf("modv")

        def dump(name, ap, shape, buf, dt=F32):
            if not dbg:
                return
            t = nc.dram_tensor("dbg_" + name, list(shape), dt, kind="ExternalOutput").ap()
            dbg_outs[name] = t
            fw.stor(t, ap, [buf])

        if stop_after == -1:
            dump("gq8", gq8[:], [128, 512], GQ8)
            fw.finish()
            return nc, dbg_outs
        with ExitStack() as p0:
            sc = sb(p0, "sc", [128, 16], F32); SC = Buf("sc")
            fw.ld(sc[:], ccol[:, :], [SC])
            fw.act(sc[:], sc[:], AF.Silu, [SC], [SC])
            sc2 = sb(p0, "sc2", [128, 8, 2], F32); SC2 = Buf("sc2")
            fw.cp("dve", sc2[:, :, 0], sc[:, 0:8], [SC], [SC2])
            fw.cp("dve", sc2[:, :, 1], sc[:, 8:16], [SC], [SC2])
            scbc = sb(p0, "scbc", [128, 8, 128], F32); SCBC = Buf("scbc")
            for k in range(8):
                fw.ts("dve", scbc[:, k, :], ones_f[:], sc[:, k:k + 1], None, ALU.mult, None, [SC, ONF], [SCBC])
            bmc = sb(p0, "bmc", [128, 48], F32); BMC = Buf("bmc")
            fw.ld(bmc[:], bmod_col[:, :], [BMC])
            bg = sb(p0, "bgate", [128, 2, 1024], F32); BG = Buf("bgate")
            fw.ld(bg[:, 0, :], bmod_row[0:1, 2048:3072].broadcast_to([128, 1024]), [BG])
            fw.ld(bg[:, 1, :], bmod_row[0:1, 5120:6144].broadcast_to([128, 1024]), [BG])
            mcol = sb(p0, "mcol", [128, 48, 2], F32); MCOL = Buf("mcol")
            wm = [sb(p0, "wm%d" % i, [128, 8, 512], F32) for i in range(2)]
            WM = [Buf("wm0"), Buf("wm1")]
            wmv = w_mod.rearrange("(k p) n -> p k n", p=128)
            for jb in range(12):
                s = jb % 2
                fw.ld(wm[s][:], wmv[:, :, jb * 512:(jb + 1) * 512], [WM[s]])
                if jb in (4, 5, 10, 11):
                    gi = 0 if jb < 6 else 1
                    half = jb % 2 if jb < 6 else (jb - 10)
                    pb, PB = psb[0], PSB[0]
                    for k in range(8):
                        fw.mm(pb[:, :], scbc[:, k, :], wm[s][:, k, :], k == 0, k == 7, [SCBC, WM[s]], [PB])
                    fw.tt("dve", G12[:, gi, half * 512:(half + 1) * 512], pb[:, :], bg[:, gi, half * 512:(half + 1) * 512],
                          ALU.add, [PB, BG], [G12B])
                else:
                    pb, PB = psb[1], PSB[1]
                    for cc in range(4):
                        for k in range(8):
                            fw.mm(pb[:, cc * 2:cc * 2 + 2], wm[s][:, k, cc * 128:(cc + 1) * 128], sc2[:, k, :],
                                  k == 0, k == 7, [SC2, WM[s]], [PB])
                    for col in range(2):
                        fw.tt("dve", mcol[:, jb * 4:jb * 4 + 4, col], pb[:, col:8:2], bmc[:, jb * 4:jb * 4 + 4],
                              ALU.add, [PB, BMC], [MCOL])
            tmp8 = sb(p0, "tmp8", [128, 8], F32); T8 = Buf("t8")
            fw.ts("dve", tmp8[:], mcol[:, 8:16, 0], 1.0, None, ALU.add, None, [MCOL], [T8])
            fw.tt("dve", modv[:, 0:8], tmp8[:], ngc[:, 0:8], ALU.mult, [T8, NGC], [MODV])
            fw.cp("dve", modv[:, 8:16], mcol[:, 0:8, 0], [MCOL], [MODV])
            fw.ts("dve", tmp8[:], mcol[:, 8:16, 1], 1.0, None, ALU.add, None, [MCOL], [T8])
            fw.tt("dve", modv[:, 16:24], tmp8[:], ngc[:, 0:8], ALU.mult, [T8, NGC], [MODV])
            fw.cp("dve", modv[:, 24:32], mcol[:, 0:8, 1], [MCOL], [MODV])
            fw.ts("dve", tmp8[:], mcol[:, 32:40, 0], 1.0, None, ALU.add, None, [MCOL], [T8])
            fw.tt("dve", modv[:, 32:40], tmp8[:], ngc[:, 8:16], ALU.mult, [T8, NGC], [MODV])
            fw.cp("dve", modv[:, 40:48], mcol[:, 24:32, 0], [MCOL], [MODV])
            dump("modv", modv[:], [128, 48], MODV)
            dump("G12", G12[:], [128, 2, 1024], G12B)
            fw.barrier()
        if stop_after == 0:
            fw.finish()
            return nc, dbg_outs

        def tile_src(t):
            if t < NT_LAT:
                return xs[t * 128:(t + 1) * 128, :]
            return cs[(t - NT_LAT) * 128:(t - NT_LAT + 1) * 128, :]

        class NormCtx:
            pass

        def make_norm(stack, tag):
            n = NormCtx()
            n.xt = [sb(stack, "xt%s%d" % (tag, i), [128, D], F32) for i in range(2)]
            n.XT = [Buf("xt%d" % i) for i in range(2)]
            n.junk = sb(stack, "junk" + tag, [128, D], BF16); n.JUNK = Buf("junk")
            n.stt = sb(stack, "nst" + tag, [128, 4], F32); n.ST = Buf("nst")
            n.xn = [sb(stack, "xn%s%d" % (tag, i), [128, D], BF16) for i in range(4)]
            n.XN = [Buf("xn%d" % i) for i in range(4)]
            n.i = 0
            return n

        def norm_rows(n, src_ap, src_buf, slot):
            fw.act(n.junk[:], src_ap, AF.Square, [src_buf], [n.JUNK, n.ST], accum=n.stt[:, 0:1])
            fw.act(n.stt[:, 1:2], n.stt[:, 0:1], AF.Ln, [n.ST, CST], [n.ST], bias=eps_ap, scale=1.0 / D)
            fw.act(n.stt[:, 2:3], n.stt[:, 1:2], AF.Exp, [n.ST], [n.ST], scale=-0.5)
            fw.ts("dve", n.xn[slot][:], src_ap, n.stt[:, 2:3], None, ALU.mult, None, [src_buf, n.ST], [n.XN[slot]])

        def transpose_mod(n, ntile, hT_ap_fn, HT, acol0, ncols_last=128):
            for kc in range(8):
                pt, PT = next_pst()
                for i in range(ntile):
                    fw.tr(pt[:, i * 128:(i + 1) * 128], n.xn[i][:, kc * 128:(kc + 1) * 128], ident_b[:],
                          [n.XN[i], IDB], [PT])
                T = (ntile - 1) * 128 + ncols_last
                chk("tm_tr")
                fw.ts("dve", hT_ap_fn(kc, T), pt[:, 0:T], modv[:, acol0 + kc:acol0 + kc + 1],
                      modv[:, acol0 + 8 + kc:acol0 + 9 + kc], ALU.mult, ALU.add, [PT, MODV], [HT])
                chk("tm_ev%d" % kc)

        def load_norm_group(n, tiles):
            for i, t in enumerate(tiles):
                s = n.i % 2
                n.i += 1
                fw.ld(n.xt[s][:], tile_src(t), [n.XT[s]])
                norm_rows(n, n.xt[s][:], n.XT[s], i)

        def load_weight_bf16(stack, name, src_view, ncols, dst, DST, piece=256):
            K = src_view.shape[1]
            with ExitStack() as ws:
                stg = [sb(ws, "%s_stg%d" % (name, i), [128, K, piece], F32) for i in range(2)]
                STG = [Buf("stg0"), Buf("stg1")]
                i = 0
                for c0 in range(0, ncols, piece):
                    c1 = min(ncols, c0 + piece)
                    s = i % 2
                    fw.ld(stg[s][:, :, 0:c1 - c0], src_view[:, :, c0:c1], [STG[s]])
                    eng = ("dve", "act")[i % 2]
                    fw.cp(eng, dst[:, :, c0:c1], stg[s][:, :, 0:c1 - c0], [STG[s]], [DST])
                    i += 1
                fw.barrier()

        groups_ext = [[0, 1, 2, 3], [4, 5, 6, 7], [8, 9, 10, 11], [12, 13, 14, 15], [16]]
        groups_oth = [[17, 18, 19], [20, 21, 22, 23], [24, 25, 26, 27], [28, 29, 30, 31], [32, 33]]

        mixer = ExitStack()
        st.enter_context(mixer)
        Oacc = sb(mixer, "Oacc", [128, NT_EXT, 512], BF16); OACC = [Buf("oacc%d" % t) for t in range(NT_EXT)]

        gdn = ExitStack()
        st.enter_context(gdn)
        rawK = sb(gdn, "rawK", [128, 4, TOK_ALL], BF16)
        rawV = sb(gdn, "rawV", [128, 4, TOK_ALL], BF16)
        rawQ = sb(gdn, "rawQ", [128, 4, TOK_EXT], BF16)
        RAW = {}
        ab = sb(gdn, "ab", [128, NT_ALL, 16], F32); AB = Buf("ab")

        def rawbuf(kind, h, grp):
            key = (kind, h, grp)
            if key not in RAW:
                RAW[key] = Buf("raw%s%d_%d" % (kind, h, grp))
            return RAW[key]

        def tok_group(t):
            return t // 4

        with ExitStack() as p1:
            wg = sb(p1, "wg", [128, 8, 1552], BF16); WG = Buf("wg")
            load_weight_bf16(p1, "wg", w_in_g.rearrange("(k p) n -> p k n", p=128), 1552, wg, WG)
            chk("wload")
            nctx = make_norm(p1, "a")
            hT = [sb(p1, "hT%d" % i, [128, 8, 512], BF16) for i in range(2)]
            HT = [Buf("hT0"), Buf("hT1")]
            SEG = 1024
            NSL = 2
            acc = [sb(p1, "cacc%d" % i, [128, SEG], F32) for i in range(NSL)]
            ACC = [Buf("cacc%d" % i) for i in range(NSL)]
            sq = [sb(p1, "csq%d" % i, [128, SEG], BF16) for i in range(NSL)]
            SQ = [Buf("csq%d" % i) for i in range(NSL)]
            rin = [sb(p1, "crin%d" % i, [128, 512], F32) for i in range(2)]
            RIN = [Buf("crin%d" % i) for i in range(2)]
            rci = [0]
            pending = []
            csi = [0]
            cprev = {}

            def conv_seg(ch, a, b, s0, s1):
                kind = "QKV"[ch // 4]
                h = ch % 4
                arr = (rawQ, rawK, rawV)[ch // 4]
                w0, w1, w2 = (cw[:, ch * 3 + j:ch * 3 + j + 1] for j in range(3))
                n = s1 - s0
                sl = csi[0] % NSL
                csi[0] += 1
                bufs = sorted({tok_group(t) for t in range(s0 // 128, (s1 + 127) // 128)} |
                              ({tok_group(s1 // 128)} if s1 < b else set()))
                RB = [rawbuf(kind, h, g) for g in bufs]
                fw.act(acc[sl][:, 0:n], arr[:, h, s0:s1], AF.Copy, RB + [CW], [ACC[sl]], scale=w1)
                if s0 > a:
                    pl = cprev[(ch, a)]
                    fw.stt("dve", acc[sl][:, 0:1], pl[0], w0, acc[sl][:, 0:1], ALU.mult, ALU.add,
                           [pl[1], CW, ACC[sl]], [ACC[sl]])
                fw.stt("dve", acc[sl][:, 1:n], arr[:, h, s0:s1 - 1], w0, acc[sl][:, 1:n], ALU.mult, ALU.add,
                       RB + [CW, ACC[sl]], [ACC[sl]])
                nr = n if s1 < b else n - 1
                fw.stt("dve", acc[sl][:, 0:nr], arr[:, h, s0 + 1:s0 + 1 + nr], w2, acc[sl][:, 0:nr], ALU.mult, ALU.add,
                       RB + [CW, ACC[sl]], [ACC[sl]])
                if s1 < b:
                    keep = sb(p1, "keep%d_%d" % (ch, s0), [128, 1], BF16)
                    KB = Buf("keep")
                    fw.cp("pool", keep[:], arr[:, h, s1 - 1:s1], RB, [KB])
                    cprev[(ch, a)] = (keep[:], KB)
                WB = [rawbuf(kind, h, g) for g in sorted({tok_group(t) for t in range(s0 // 128, (s1 + 127) // 128)})]

                def tail():
                    if kind == "V":
                        fw.act(arr[:, h, s0:s1], acc[sl][:, 0:n], AF.Silu, [ACC[sl]], WB)
                        return
                    fw.act(acc[sl][:, 0:n], acc[sl][:, 0:n], AF.Silu, [ACC[sl]], [ACC[sl]])
                    fw.tt("pool", sq[sl][:, 0:n], acc[sl][:, 0:n], acc[sl][:, 0:n], ALU.mult, [ACC[sl]], [SQ[sl]])
                    for c0 in range(0, n, 512):
                        c1 = min(n, c0 + 512)
                        rci[0] += 1
                        pb, PB = psb[4 + rci[0] % 2], PSB[4 + rci[0] % 2]
                        r_, RN = rin[rci[0] % 2], RIN[rci[0] % 2]
                        fw.mm(pb[:, 0:c1 - c0], ones_b[:], sq[sl][:, c0:c1], True, True, [ONB, SQ[sl]], [PB])
                        fw.act(r_[:, 0:c1 - c0], pb[:, 0:c1 - c0], AF.Ln, [PB, CST], [RN], bias=eps_ap, scale=1.0)
                        fw.act(r_[:, 0:c1 - c0], r_[:, 0:c1 - c0], AF.Exp, [RN], [RN], scale=-0.5)
                        fw.tt("dve", arr[:, h, s0 + c0:s0 + c1], acc[sl][:, c0:c1], r_[:, 0:c1 - c0], ALU.mult,
                              [ACC[sl], RN], WB)
                pending.append(tail)
                while len(pending) > 1:
                    pending.pop(0)()

            csegs = []
            for ch in range(12):
                rngs = [(0, TOK_EXT)] if ch < 4 else [(0, SEQ), (SEQ, TOK_ALL)]
                for (a_, b_) in rngs:
                    for s0 in range(a_, b_, SEG):
                        s1 = min(b_, s0 + SEG)
                        need = None if a_ == SEQ else min(b_, s1 + 1)
                        csegs.append((need, ch, a_, b_, s0, s1))
            cdone = set()
            cav = [0, False]

            def emit_ready_convs(avail_lat, ctx_done, limit=None):
                k = 0
                for i_, (need, ch, a_, b_, s0, s1) in enumerate(csegs):
                    if i_ in cdone:
                        continue
                    ok = ctx_done if need is None else need <= avail_lat
                    if ok:
                        conv_seg(ch, a_, b_, s0, s1)
                        cdone.add(i_)
                        k += 1
                        if limit is not None and k >= limit:
                            return

            allg = groups_ext + groups_oth

            def prep1(gi):
                grp = allg[gi]
                load_norm_group(nctx, grp)
                transpose_mod(nctx, len(grp), lambda kc, T, s=gi % 2: hT[s][:, kc, 0:T], HT[gi % 2],
                              16 if grp[0] >= NT_LAT else 0)
            prep1(0)
            for gi, grp in enumerate(allg):
                is_ext = grp[0] < NT_EXT
                is_ctx = grp[0] >= NT_LAT
                s = gi % 2
                T = len(grp) * 128
                tok0 = grp[0] * 128
                chunks = list(range(12)) if is_ext else list(range(4, 12))
                for ci, ch in enumerate(chunks):
                    if ci == 2 and gi + 1 < len(allg):
                        prep1(gi + 1)
                    if gi > 0:
                        emit_ready_convs(cav[0], cav[1], limit=2)
                    pb, PB = psb[ci % 4], PSB[ci % 4]
                    for kc in range(8):
                        fw.mm(pb[:, 0:T], wg[:, kc, ch * 128:(ch + 1) * 128], hT[s][:, kc, 0:T], kc == 0, kc == 7,
                              [WG, HT[s]], [PB])
                    kind = "QKV"[ch // 4]
                    dst = (rawQ, rawK, rawV)[ch // 4]
                    fw.cpa(dst[:, ch % 4, tok0:tok0 + T], pb[:, 0:T], [PB], [rawbuf(kind, ch % 4, tok_group(grp[0]))])
                for i, t in enumerate(grp):
                    pb, PB = psb[4 + (i % 2)], PSB[4 + (i % 2)]
                    for kc in range(8):
                        fw.mm(pb[:, 0:16], hT[s][:, kc, i * 128:(i + 1) * 128], wg[:, kc, 1536:1552], kc == 0, kc == 7,
                              [WG, HT[s]], [PB])
                    fw.cp("act", ab[:, t, :], pb[:, 0:16], [PB], [AB])
                chk("grp0")
                cav[0] = (grp[-1] + 1) * 128 if grp[0] < NT_LAT else SEQ
                cav[1] = grp[0] >= NT_LAT
            emit_ready_convs(SEQ, True)
            while pending:
                pending.pop(0)()
            assert len(cdone) == len(csegs)
            fw.barrier()
        if stop_after == 1:
            dump("rawK", rawK[:], [128, 4, TOK_ALL], rawbuf("K", 0, 0), BF16)
            dump("ab", ab[:], [128, NT_ALL, 16], AB)
            fw.finish()
            return nc, dbg_outs

        if stop_after == 2:
            dump("KT", rawK[:], [128, 4, TOK_ALL], rawbuf("K", 0, 0), BF16)
            dump("QT", rawQ[:], [128, 4, TOK_EXT], rawbuf("Q", 0, 0), BF16)
            dump("VT", rawV[:], [128, 4, TOK_ALL], rawbuf("V", 0, 0), BF16)
            dump("ab", ab[:], [128, NT_ALL, 16], AB)
            fw.finish()
            return nc, dbg_outs

        def a3(name, n, stack=gdn):
            return sb(stack, name, [128, NT_ALL, n], F32)
        gg = a3("gg", 8); GG = Buf("gg")
        beta = a3("beta", 8); BETA = Buf("beta")
        egc = a3("egc", 8); EGC = Buf("egc")
        ekd = a3("ekd", 8); EKD = Buf("ekd")
        bgt = a3("bgt", 8); BGT = Buf("bgt")
        gcpl = a3("gcpl", 8); GCPL = Buf("gcpl")
        ngcn = a3("ngcn", 8); NGCN = Buf("ngcn")
        dl = a3("dl", 16); DL = Buf("dl")
        with ExitStack() as pg:
            t1 = a3("t1", 8, pg); T1 = Buf("t1")
            lnb = a3("lnb", 8, pg); LNB = Buf("lnb")
            gcs = a3("gcs", 32, pg); GCS = Buf("gcs")
            ealog = sb(pg, "ealog", [128, 8], F32); EAL = Buf("ealog")
            gI = [sb(pg, "gI%d" % i, [128, 8, 2], F32) for i in range(2)]
            GI = [Buf("gI0"), Buf("gI1")]
            fw.tt("dve", t1[:], ab[:, :, 0:8], alogdt_bc[:, 8:16].unsqueeze(1).to_broadcast([128, NT_ALL, 8]), ALU.add,
                  [AB, VEC], [T1])
            fw.act(t1[:], t1[:], AF.Exp, [T1], [T1])
            fw.act(t1[:], t1[:], AF.Ln, [T1, CST], [T1], bias=cst[:, 1:2], scale=1.0)
            fw.act(ealog[:], alogdt_bc[:, 0:8], AF.Exp, [VEC], [EAL])
            fw.stt("dve", gg[:], t1[:], -1.0, ealog[:].unsqueeze(1).to_broadcast([128, NT_ALL, 8]), ALU.mult, ALU.mult,
                   [T1, EAL], [GG])
            fw.act(beta[:], ab[:, :, 8:16], AF.Sigmoid, [AB], [BETA])
            fw.act(lnb[:], beta[:], AF.Ln, [BETA], [LNB])
            for t in range(NT_ALL):
                k = t % 2
                pb, PB = psb[k], PSB[k]
                for c in range(2):
                    fw.ts("dve", gI[k][:, :, c], gg[:, t, :], mcum[0][:, 128 + c:129 + c], None, ALU.mult, None,
                          [GG, CM], [GI[k]])
                fw.mm(pb[:, 0:4], mcum[0][:, 0:128], gg[:, t, 0:4], True, True, [CM, GG], [PB])
                fw.mm(pb[:, 4:8], mcum[1][:, 0:128], gg[:, t, 4:8], False, True, [CM, GG], [PB])
                fw.mm(pb[:, 8:16], blk, gg[:, t, 0:8], False, True, [CM, GG], [PB])
                fw.mm(pb[:, 16:32], ones_f[:], gI[k][:].rearrange("p a b -> p (a b)"), False, True, [ONF, GI[k]], [PB])
                fw.cp("act", gcs[:, t, :], pb[:, 0:32], [PB], [GCS])
            fw.act(egc[:], gcs[:, :, 0:8], AF.Exp, [GCS], [EGC])
            fw.tt("dve", t1[:], gcs[:, :, 8:16], gcs[:, :, 0:8], ALU.subtract, [GCS], [T1])
            fw.act(ekd[:], t1[:], AF.Exp, [T1], [EKD])
            fw.tt("dve", bgt[:], beta[:], egc[:], ALU.mult, [BETA, EGC], [BGT])
            fw.tt("dve", gcpl[:], gcs[:, :, 0:8], lnb[:], ALU.add, [GCS, LNB], [GCPL])
            fw.ts("dve", ngcn[:], gcs[:, :, 0:8], -1.0, None, ALU.mult, None, [GCS], [NGCN])
            fw.act(dl[:], gcs[:, :, 16:32], AF.Exp, [GCS], [DL])
            fw.barrier()
        if stop_after == 3:
            dump("gg", gg[:], [128, NT_ALL, 8], GG)
            dump("beta", beta[:], [128, NT_ALL, 8], BETA)
            fw.finish()
            return nc, dbg_outs

        fw.op("pool", lambda e: e.memset(Oacc[:], 0.0), [], OACC)
        with ExitStack() as ps_:
            maskb4 = sb(ps_, "maskb4", [128, 4, 512], BF16); MASKB4 = Buf("maskb4")
            for ty in range(4):
                for h in range(4):
                    fw.cp("pool", maskb4[:, ty, h * 128:(h + 1) * 128], maskb[:, ty, :], [MASKB], [MASKB4])
            identb4 = sb(ps_, "identb4", [128, 4, 128], BF16); IDB4 = Buf("idb4")
            for h in range(4):
                fw.cp("dve", identb4[:, h, :], ident_b[:], [IDB], [IDB4])

            class QS:
                pass
            DBL = ("kbg", "kdec", "vb", "AT", "wT", "u")
            sets = []
            for d in range(2):
                q = QS()
                for nm, dt_ in (("kbg", BF16), ("kdec", BF16), ("vb", BF16), ("gM", F32), ("Dstr", BF16), ("Dinc", BF16),
                                ("B0", BF16), ("B1", BF16), ("AT", BF16),
                                ("u", F32), ("wT", BF16), ("vnew", BF16), ("tmp", F32), ("S", F32), ("Sbf", BF16)):
                    if nm in DBL:
                        setattr(q, nm + "_2", [sb(ps_, "q%d_%s_%d" % (d, nm, i), [128, 4, 128], dt_) for i in range(2)])
                        setattr(q, nm.upper() + "_B2", [Buf("q%d_%s_%d" % (d, nm, i)) for i in range(2)])
                    else:
                        setattr(q, nm, sb(ps_, "q%d_%s" % (d, nm), [128, 4, 128], dt_))
                        setattr(q, nm.upper() + "_", Buf("q%d_%s" % (d, nm)))
                q.AP0 = sb(ps_, "q%d_AP0" % d, [128, 4, 256], BF16)
                q.AP1 = sb(ps_, "q%d_AP1" % d, [128, 4, 256], BF16)
                q.APA0_, q.APA1_, q.APP0_, q.APP1_ = Buf("apa0"), Buf("apa1"), Buf("app0"), Buf("app1")
                fw.op("pool", lambda e, q=q: e.memset(q.S[:], 0.0), [], [q.S_])
                fw.op("pool", lambda e, q=q: e.memset(q.Sbf[:], 0.0), [], [q.SBF_])
                q.banks = [0, 1, 2, 3] if d == 0 else [4, 5, 6, 7]
                q.bi = 0
                sets.append(q)

            class QView:
                def __init__(self, base, par):
                    object.__setattr__(self, "_b", base)
                    object.__setattr__(self, "_p", par)

                def __getattr__(self, name):
                    b_, p_ = self._b, self._p
                    if name in DBL:
                        return getattr(b_, name + "_2")[p_]
                    if name.endswith("_") and name[:-1].lower() in [x.lower() for x in DBL] and name[:-1].isupper():
                        for x in DBL:
                            if x.upper() == name[:-1]:
                                return getattr(b_, x.upper() + "_B2")[p_]
                    return getattr(b_, name)

                def __setattr__(self, name, val):
                    setattr(self._b, name, val)

            def qview(d, par):
                return QView(sets[d], par)

            def nb(q):
                q.bi += 1
                i = q.banks[q.bi % 4]
                return i

            def v4(ap512):
                return ap512.rearrange("p (a b) -> p a b", a=4)

            def bc4(ap_p4):
                return ap_p4.unsqueeze(2).to_broadcast([ap_p4.shape[0], 4, 128])

            def quad_pre(t, d, with_out, par):
                q = qview(d, par)
                c0 = d * 4
                tsl = slice(t * 128, (t + 1) * 128)
                grp = tok_group(t)
                KB = [rawbuf("K", h, grp) for h in range(4)]
                VB = [rawbuf("V", h, grp) for h in range(4)]
                QB = [rawbuf("Q", h, grp) for h in range(4)] if with_out else []
                i = nb(q)
                bv = psbf(i)
                for h in range(4):
                    fw.tr(bv[:, h * 128:(h + 1) * 128], rawK[:, h, tsl], ident_b[:], [KB[h], IDB], [PSB[i]])
                fw.tt("dve", q.kbg[:], v4(bv[:, 0:512]), bc4(bgt[:, t, c0:c0 + 4]), ALU.mult, [PSB[i], BGT], [q.KBG_])
                fw.tt("dve", q.kdec[:], v4(bv[:, 0:512]), bc4(ekd[:, t, c0:c0 + 4]), ALU.mult, [PSB[i], EKD], [q.KDEC_])
                yield
                i = nb(q)
                bv = psbf(i)
                for h in range(4):
                    fw.tr(bv[:, h * 128:(h + 1) * 128], rawV[:, h, tsl], ident_b[:], [VB[h], IDB], [PSB[i]])
                fw.tt("dve", q.vb[:], v4(bv[:, 0:512]), bc4(beta[:, t, c0:c0 + 4]), ALU.mult, [PSB[i], BETA], [q.VB_])
                yield
                fw.tt("dve", q.gM[:], mcum[d][:, 0:128].unsqueeze(1).to_broadcast([128, 4, 128]), bc4(gg[:, t, c0:c0 + 4]),
                      ALU.mult, [CM, GG], [q.GM_])
                gMf = q.gM[:].rearrange("p a b -> p (a b)")
                i = nb(q)
                fw.mm(psb[i][:, :], ones_f[:], gMf, True, False, [ONF, q.GM_], [PSB[i]])
                fw.mm(psb[i][:, :], ident_b[:], maskb4[:, 2 * d + 1, :], False, True, [IDB, MASKB4], [PSB[i]])
                for h in range(4):
                    fw.act(q.Dstr[:, h, :], psb[i][:, h * 128:(h + 1) * 128], AF.Exp, [PSB[i], GCPL], [q.DSTR_],
                           bias=gcpl[:, t, c0 + h:c0 + h + 1], scale=-1.0)
                yield
                if with_out:
                    i = nb(q)
                    fw.mm(psb[i][:, :], ones_f[:], gMf, True, False, [ONF, q.GM_], [PSB[i]])
                    fw.mm(psb[i][:, :], ident_b[:], maskb4[:, 2 * d, :], False, True, [IDB, MASKB4], [PSB[i]])
                    for h in range(4):
                        fw.act(q.Dinc[:, h, :], psb[i][:, h * 128:(h + 1) * 128], AF.Exp, [PSB[i], NGCN], [q.DINC_],
                               bias=ngcn[:, t, c0 + h:c0 + h + 1], scale=1.0)
                    yield
                i = nb(q)
                for h in range(4):
                    fw.mm(psb[i][:, h * 128:(h + 1) * 128], rawK[:, h, tsl], rawK[:, h, tsl], h == 0, True, [KB[h]], [PSB[i]])
                fw.stt("dve", q.B0[:], v4(psb[i][:, :]), -1.0, q.Dstr[:], ALU.mult, ALU.mult, [PSB[i], q.DSTR_], [q.B0_])
                yield
                if with_out:
                    i = nb(q)
                    for h in range(4):
                        fw.mm(psb[i][:, h * 128:(h + 1) * 128], rawK[:, h, tsl], rawQ[:, h, tsl], h == 0, True,
                              [KB[h], QB[h]], [PSB[i]])
                    fw.tt("dve", q.AT[:], v4(psb[i][:, :]), q.Dinc[:], ALU.mult, [PSB[i], q.DINC_], [q.AT_])
                    yield
                AP = [q.AP0, q.AP1]
                APA = [q.APA0_, q.APA1_]
                APP = [q.APP0_, q.APP1_]
                Bb = [(q.B0, q.B0_), (q.B1, q.B1_)]
                i = nb(q)
                bv = psbf(i)
                for h in range(4):
                    fw.tr(bv[:, h * 128:(h + 1) * 128], q.B0[:, h, :], ident_b[:], [q.B0_, IDB], [PSB[i]])
                fw.cp("act", AP[0][:, :, 0:128], v4(bv[:, 0:512]), [PSB[i]], [APA[0]])
                fw.tt("dve", AP[1][:, :, 128:256], v4(bv[:, 0:512]), identb4[:], ALU.add, [PSB[i], IDB4], [APP[1]])
                yield
                for j in range(1, 6):
                    cur, nxt = (j - 1) % 2, j % 2
                    Bc, BcB = Bb[(j - 1) % 2]
                    Bn, BnB = Bb[j % 2]
                    if j == 1:
                        i = nb(q)
                        for h in range(4):
                            fw.mm(psb[i][:, h * 128:(h + 1) * 128], Bc[:, h, :], AP[cur][:, h, 0:128], h == 0, True,
                                  [BcB, APA[cur]], [PSB[i]])
                        i2 = nb(q)
                        for h in range(4):
                            fw.mm(psb[i2][:, h * 128:(h + 1) * 128], AP[cur][:, h, 0:128], Bc[:, h, :], h == 0, True,
                                  [BcB, APA[cur]], [PSB[i2]])
                        fw.cp("act", AP[nxt][:, :, 0:128], v4(psb[i][:, :]), [PSB[i]], [APA[nxt]])
                        fw.cp("dve", Bn[:], v4(psb[i2][:, :]), [PSB[i2]], [BnB])
                        yield
                    elif j < 5:
                        ia, ib = nb(q), nb(q)
                        for h in range(4):
                            bk = ia if h < 2 else ib
                            hh = h % 2
                            fw.mm(psb[bk][:, hh * 256:(hh + 1) * 256], Bc[:, h, :], AP[cur][:, h, :], hh == 0, True,
                                  [BcB, APA[cur], APP[cur]], [PSB[bk]])
                        i2 = nb(q)
                        for h in range(4):
                            fw.mm(psb[i2][:, h * 128:(h + 1) * 128], AP[cur][:, h, 0:128], Bc[:, h, :], h == 0, True,
                                  [BcB, APA[cur]], [PSB[i2]])
                        for bk, h0 in ((ia, 0), (ib, 2)):
                            pv_ = psb[bk][:, :].rearrange("p (a b) -> p a b", a=2)
                            fw.cp("act", AP[nxt][:, h0:h0 + 2, 0:128], pv_[:, :, 0:128], [PSB[bk]], [APA[nxt]])
                            fw.tt("dve", AP[nxt][:, h0:h0 + 2, 128:256], AP[cur][:, h0:h0 + 2, 128:256], pv_[:, :, 128:256],
                                  ALU.add, [PSB[bk], APP[cur]], [APP[nxt]])
                        fw.cp("dve", Bn[:], v4(psb[i2][:, :]), [PSB[i2]], [BnB])
                        yield
                    else:
                        i = nb(q)
                        for h in range(4):
                            fw.mm(psb[i][:, h * 128:(h + 1) * 128], Bc[:, h, :], AP[cur][:, h, 128:256], h == 0, True,
                                  [BcB, APP[cur]], [PSB[i]])
                        i2 = nb(q)
                        for h in range(4):
                            fw.mm(psb[i2][:, h * 128:(h + 1) * 128], AP[cur][:, h, 0:128], Bc[:, h, :], h == 0, True,
                                  [BcB, APA[cur]], [PSB[i2]])
                        fw.tt("dve", AP[nxt][:, :, 128:256], AP[cur][:, :, 128:256], v4(psb[i][:, :]), ALU.add,
                              [PSB[i], APP[cur]], [APP[nxt]])
                        fw.cp("act", Bn[:], v4(psb[i2][:, :]), [PSB[i2]], [BnB])
                        yield
                B5, B5B = Bb[1]
                i = nb(q)
                for h in range(4):
                    fw.mm(psb[i][:, h * 128:(h + 1) * 128], B5[:, h, :], AP[1][:, h, 128:256], h == 0, True, [B5B, APP[1]], [PSB[i]])
                fw.tt("dve", AP[0][:, :, 128:256], AP[1][:, :, 128:256], v4(psb[i][:, :]), ALU.add, [PSB[i], APP[1]], [APP[0]])
                yield
                Ptf = AP[0]
                PTF_ = APP[0]
                i = nb(q)
                for h in range(4):
                    fw.mm(psb[i][:, h * 128:(h + 1) * 128], Ptf[:, h, 128:256], q.vb[:, h, :], h == 0, True, [PTF_, q.VB_], [PSB[i]])
                fw.cp("act", q.u[:], v4(psb[i][:, :]), [PSB[i]], [q.U_])
                i = nb(q)
                for h in range(4):
                    fw.mm(psb[i][:, h * 128:(h + 1) * 128], q.kbg[:, h, :], Ptf[:, h, 128:256], h == 0, True, [PTF_, q.KBG_], [PSB[i]])
                fw.cp("dve", q.wT[:], v4(psb[i][:, :]), [PSB[i]], [q.WT_])
                yield

            def quad_steps(t, d, with_out, par):
                q = qview(d, par)
                c0 = d * 4
                tsl = slice(t * 128, (t + 1) * 128)
                grp = tok_group(t)
                KB = [rawbuf("K", h, grp) for h in range(4)]
                VB = [rawbuf("V", h, grp) for h in range(4)]
                QB = [rawbuf("Q", h, grp) for h in range(4)] if with_out else []
                for c in ((0, 1) if d == 0 else (1, 0)):
                    R = slice(64 * c, 64 * c + 64)
                    i = nb(q)
                    for h in range(4):
                        fw.mm(psb[i][:, h * 128:(h + 1) * 128], q.wT[:, h, :], q.Sbf[:, h, :], h == 0, True, [q.WT_, q.SBF_], [PSB[i]])
                    fw.tt("dve", q.vnew[R, :, :], q.u[R, :, :], v4(psb[i][R, :]), ALU.subtract, [PSB[i], q.U_], [q.VNEW_])
                    yield
                    if with_out:
                        i = nb(q)
                        for h in range(4):
                            fw.mm(psb[i][:, h * 128:(h + 1) * 128], rawQ[:, h, tsl], q.Sbf[:, h, :], h == 0, True,
                                  [QB[h], q.SBF_], [PSB[i]])
                        fw.tt("dve", q.tmp[R, :, :], v4(psb[i][R, :]), bc4(egc[R, t, c0:c0 + 4]), ALU.mult, [PSB[i], EGC], [q.TMP_])
                        i = nb(q)
                        for h in range(4):
                            fw.mm(psb[i][:, h * 128:(h + 1) * 128], q.AT[R, h, :], q.vnew[R, h, :], h == 0, True,
                                  [q.AT_, q.VNEW_], [PSB[i]])
                        fw.tt("dve", q.tmp[R, :, :], q.tmp[R, :, :], v4(psb[i][R, :]), ALU.add, [PSB[i], q.TMP_], [q.TMP_])
                        fw.tt("pool", Oacc[R, t, :], Oacc[R, t, :], q.tmp[R, :, :].rearrange("p a b -> p (a b)"), ALU.add,
                              [q.TMP_, OACC[t]], [OACC[t]])
                        yield
                    i = nb(q)
                    for h in range(4):
                        fw.mm(psb[i][:, h * 128:(h + 1) * 128], q.kdec[R, h, :], q.vnew[R, h, :], h == 0, True,
                              [q.KDEC_, q.VNEW_], [PSB[i]])
                    dlv = dl[:, t, :].rearrange("p (a b) -> p a b", b=2)[:, c0:c0 + 4, c]
                    fw.tt("dve", q.S[:], q.S[:], bc4(dlv), ALU.mult, [q.S_, DL], [q.S_])
                    fw.tt("dve", q.S[:], q.S[:], v4(psb[i][:, :]), ALU.add, [PSB[i], q.S_], [q.S_])
                    fw.cp("act", q.Sbf[:], q.S[:], [q.S_], [q.SBF_])
                    yield

            def chain(tiles, d):
                pre = quad_pre(tiles[0], d, tiles[0] < NT_EXT, 0)
                yield from pre
                for k, t in enumerate(tiles):
                    gens = [quad_steps(t, d, t < NT_EXT, k % 2)]
                    if k + 1 < len(tiles):
                        gens.append(quad_pre(tiles[k + 1], d, tiles[k + 1] < NT_EXT, (k + 1) % 2))
                    while gens:
                        for g_ in list(gens):
                            try:
                                next(g_)
                                yield
                            except StopIteration:
                                gens.remove(g_)

            nq = int(os.environ.get("GDN_NQ", "999"))
            chF = chain(([32, 33] + list(range(0, NT_EXT)))[:nq], 0)
            chB = chain(([33, 32] + list(range(31, -1, -1)))[:nq], 1)
            alive = [chF, chB]
            if os.environ.get("GDN_ONLY"):
                alive = [chF] if os.environ["GDN_ONLY"] == "F" else [chB]
            nst = 0
            while alive:
                for g_ in list(alive):
                    try:
                        next(g_)
                        nst += 1
                        chk("qs%d" % nst)
                    except StopIteration:
                        alive.remove(g_)
            fw.barrier()
            if stop_after == 4:
                dump("Oacc", Oacc[:], [128, NT_EXT, 512], OACC[0], BF16)
                dump("S0", sets[0].S[:], [128, 4, 128], sets[0].S_)
                dump("S1", sets[1].S[:], [128, 4, 128], sets[1].S_)
                fw.finish()
                return nc, dbg_outs
        gdn.close()

        att = ExitStack()
        st.enter_context(att)
        KTa = sb(att, "KTa", [128, TOK_ALL], BF16); KTA = [Buf("kta%d" % g) for g in range(9)]
        Va = sb(att, "Va", [128, NT_ALL, 2, 65], BF16); VA = [Buf("va%d" % t) for t in range(NT_ALL)]
        QTa = sb(att, "QTa", [128, 4, TOK_EXT], BF16); QTA = [Buf("qta%d" % t) for t in range(NT_EXT)]
        zs = sb(att, "zs", [128, NT_EXT, 512], BF16); ZS = [Buf("zs%d" % t) for t in range(NT_EXT)]
        fw.op("pool", lambda e: e.memset(Va[:], 1.0), [], VA)
        with ExitStack() as p2:
            rope_sb = sb(p2, "rope_sb", [128, NT_LAT, 64], F32); ROPE = Buf("rope")
            fw.ld(rope_sb[:], rope_cs.rearrange("(t p) c -> p t c", p=128), [ROPE])
            wa = sb(p2, "wa", [128, 8, 1280], BF16); WA = Buf("wa")
            load_weight_bf16(p2, "wa", w_in_a.rearrange("(k p) n -> p k n", p=128), 1280, wa, WA)
            nctx = make_norm(p2, "b")
            hT = [sb(p2, "hTb%d" % i, [128, 8, 512], BF16) for i in range(2)]
            HT = [Buf("hTb0"), Buf("hTb1")]
            qsq_ = [sb(p2, "qsq%d" % i, [128, 640], F32) for i in range(2)]; QSQ_ = [Buf("qsq0"), Buf("qsq1")]
            qst_ = [sb(p2, "qst%d" % i, [128, 32], F32) for i in range(2)]; QST_ = [Buf("qst0"), Buf("qst1")]
            qn_ = [sb(p2, "qn%d" % i, [128, 640], F32) for i in range(2)]; QN_ = [Buf("qn0"), Buf("qn1")]
            rt_ = [[sb(p2, "rt%d_%d" % (k, i), [128, 10, 32], F32) for i in range(4)] for k in range(2)]
            RT_ = [[Buf("rt%d_%d" % (k, i)) for i in range(4)] for k in range(2)]
            qr_ = [sb(p2, "qr%d" % i, [128, 640], BF16) for i in range(2)]; QR_ = [Buf("qr0"), Buf("qr1")]
            kg2 = sb(p2, "kg2", [128, 640], F32); KG2 = Buf("kg2")
            fw.cp("dve", kg2[:, 0:512], gq8[:], [GQ8], [KG2])
            for hh in range(2):
                fw.cp("dve", kg2[:, 512 + hh * 64:576 + hh * 64], kg_bc, [VEC], [KG2])
            allg = groups_ext + groups_oth

            def prep2(gi):
                grp = allg[gi]
                load_norm_group(nctx, grp)
                transpose_mod(nctx, len(grp), lambda kc, T, s=gi % 2: hT[s][:, kc, 0:T], HT[gi % 2],
                              16 if grp[0] >= NT_LAT else 0)
            prep2(0)
            tiles2 = [(gi, i_, t) for gi, grp in enumerate(allg) for i_, t in enumerate(grp)]

            def mm2(k):
                gi, i_, t = tiles2[k]
                s = gi % 2
                bq, bkv, bz = (0, 1, 2) if k % 2 == 0 else (3, 4, 5)
                lt = hT[s][:, :, i_ * 128:(i_ + 1) * 128]
                is_ext = t < NT_EXT
                if is_ext:
                    for kc in range(8):
                        fw.mm(psb[bq][:, :], lt[:, kc, :], wa[:, kc, 0:512], kc == 0, kc == 7, [WA, HT[s]], [PSB[bq]])
                    for kc in range(8):
                        fw.mm(psb[bz][:, :], lt[:, kc, :], wa[:, kc, 768:1280], kc == 0, kc == 7, [WA, HT[s]], [PSB[bz]])
                for kc in range(8):
                    fw.mm(psb[bkv][:, 0:256], lt[:, kc, :], wa[:, kc, 512:768], kc == 0, kc == 7, [WA, HT[s]], [PSB[bkv]])

            def post2(k):
                gi, i_, t = tiles2[k]
                bq, bkv, bz = (0, 1, 2) if k % 2 == 0 else (3, 4, 5)
                is_ext = t < NT_EXT
                is_ctx = t >= NT_LAT
                kk = k % 2
                qsq, QSQ, qst, QST, qn, QN, rt, RT, qr, QR = (qsq_[kk], QSQ_[kk], qst_[kk], QST_[kk], qn_[kk], QN_[kk],
                                                              rt_[kk], RT_[kk], qr_[kk], QR_[kk])
                if True:
                    hoff = 0 if is_ext else 8
                    if is_ext:
                        fw.act(zs[:, t, :], psb[bz][:, :], AF.Silu, [PSB[bz]], [ZS[t]])
                    fw.cp("act", Va[:, t, :, 0:64], psb[bkv][:, 128:256].rearrange("p (a b) -> p a b", a=2), [PSB[bkv]], [VA[t]])
                    if is_ext:
                        fw.cp("dve", qn[:, 0:512], psb[bq][:, :], [PSB[bq]], [QN])
                    fw.cp("dve", qn[:, 512:640], psb[bkv][:, 0:128], [PSB[bkv]], [QN])
                    yield
                    c_lo, c_hi = hoff * 64, 640
                    fw.tt("dve", qsq[:, c_lo:c_hi], qn[:, c_lo:c_hi], qn[:, c_lo:c_hi], ALU.mult, [QN], [QSQ])
                    yield
                    fw.op("dve", lambda e, hoff=hoff: e.tensor_reduce(
                        out=qst[:, hoff:10], in_=qsq[:, hoff * 64:640].rearrange("p (a b) -> p a b", b=64),
                        axis=AX.X, op=ALU.add), [QSQ], [QST])
                    yield
                    fw.act(qst[:, 10 + hoff:20], qst[:, hoff:10], AF.Ln, [QST, CST], [QST], bias=eps_ap, scale=1.0 / 64)
                    yield
                    fw.act(qst[:, 20 + hoff:30], qst[:, 10 + hoff:20], AF.Exp, [QST], [QST], scale=-0.5)
                    yield
                    n_h = 10 - hoff
                    qv = qn[:, c_lo:c_hi].rearrange("p (a b) -> p a b", b=64)
                    fw.tt("dve", qv, qv, qst[:, 20 + hoff:30].unsqueeze(2).to_broadcast([128, n_h, 64]), ALU.mult, [QN, QST], [QN])
                    yield
                    fw.tt("dve", qn[:, c_lo:c_hi], qn[:, c_lo:c_hi], kg2[:, c_lo:c_hi], ALU.mult, [QN, KG2], [QN])
                    yield
                    qrv = qr[:, c_lo:c_hi].rearrange("p (a b) -> p a b", b=64)
                    if not is_ctx:
                        cosb = rope_sb[:, t, 0:32].unsqueeze(1).to_broadcast([128, n_h, 32])
                        sinb = rope_sb[:, t, 32:64].unsqueeze(1).to_broadcast([128, n_h, 32])
                        x1_, x2_ = qv[:, :, 0:32], qv[:, :, 32:64]
                        r0, r1, r2, r3 = (rt[j][:, hoff:10, :] for j in range(4))
                        fw.tt("dve", r0, x1_, cosb, ALU.mult, [QN, ROPE], [RT[0]])
                        fw.tt("dve", r1, x2_, sinb, ALU.mult, [QN, ROPE], [RT[1]])
                        yield
                        fw.tt("dve", r2, x2_, cosb, ALU.mult, [QN, ROPE], [RT[2]])
                        fw.tt("dve", r3, x1_, sinb, ALU.mult, [QN, ROPE], [RT[3]])
                        yield
                        fw.tt("dve", qrv[:, :, 0:32], r0, r1, ALU.subtract, [RT[0], RT[1]], [QR])
                        fw.tt("dve", qrv[:, :, 32:64], r2, r3, ALU.add, [RT[2], RT[3]], [QR])
                        yield
                    else:
                        fw.cp("dve", qr[:, c_lo:c_hi], qn[:, c_lo:c_hi], [QN], [QR])
                    pt, PT = next_pst()
                    fw.tr(pt[:, 0:128], qr[:, 512:640], ident_b[:], [QR, IDB], [PT])
                    if is_ext:
                        for j in range(4):
                            fw.tr(pt[:, 128 + j * 128:256 + j * 128], qr[:, j * 128:(j + 1) * 128], ident_b[:], [QR, IDB], [PT])
                    fw.cp("act", KTa[:, t * 128:(t + 1) * 128], pt[:, 0:128], [PT], [KTA[t // 4]])
                    if is_ext:
                        fw.cp("dve", QTa[:, :, t * 128:(t + 1) * 128], pt[:, 128:640].rearrange("p (a b) -> p a b", a=4),
                              [PT], [QTA[t]])

            prepped = {0}

            def ensure_prep(k):
                gi = tiles2[k][0]
                for g_ in range(gi + 2):
                    if g_ < len(allg) and g_ not in prepped and g_ <= gi + 1:
                        prep2(g_)
                        prepped.add(g_)
            ensure_prep(0)
            mm2(0)
            mm2(1)
            nt2 = len(tiles2)
            for p_ in range(0, nt2, 2):
                gens = [post2(k) for k in (p_, p_ + 1) if k < nt2]
                for g_ in gens:
                    next(g_)
                for k in (p_ + 2, p_ + 3):
                    if k < nt2:
                        ensure_prep(k)
                        mm2(k)
                alive = list(gens)
                while alive:
                    for g_ in list(alive):
                        try:
                            next(g_)
                        except StopIteration:
                            alive.remove(g_)
            fw.barrier()
        wo = sb(att, "wo", [128, 8, 1024], BF16); WO = Buf("wo")
        with ExitStack() as p2w:
            load_weight_bf16(p2w, "wo", w_out.rearrange("(k p) n -> p k n", p=128), 1024, wo, WO)
        if stop_after == 5:
            dump("KTa", KTa[:], [128, TOK_ALL], KTA[0], BF16)
            dump("QTa", QTa[:], [128, 4, TOK_EXT], QTA[0], BF16)
            dump("Va", Va[:], [128, NT_ALL, 2, 65], VA[0], BF16)
            dump("zs", zs[:], [128, NT_EXT, 512], ZS[0], BF16)
            fw.finish()
            return nc, dbg_outs

        with ExitStack() as p3:
            pT = [sb(p3, "pT%d" % i, [128, 512], BF16) for i in range(4)]
            PTB = [Buf("pT%d" % i) for i in range(4)]
            oa = sb(p3, "oa", [128, 512], F32); OA = Buf("oa")
            oT = [sb(p3, "oT%d" % i, [65, 512], F32) for i in range(2)]
            OTB = [Buf("oT0"), Buf("oT1")]
            og = sb(p3, "og", [128, 512], F32); OG = Buf("og")
            ost = sb(p3, "ost", [128, 32], F32); OST = Buf("ost")
            ojunk = sb(p3, "ojunk", [128, 512], BF16); OJ = Buf("ojunk")
            mix = sb(p3, "mix", [128, 1024], BF16); MIX = Buf("mix")
            mixT = sb(p3, "mixT", [128, 8, 128], BF16); MIXT = Buf("mixT")
            xres = [sb(p3, "xres%d" % i, [128, D], F32) for i in range(2)]
            XRES = [Buf("xres0"), Buf("xres1")]
            x1t = [sb(p3, "x1t%d" % i, [128, D], F32) for i in range(2)]
            X1T = [Buf("x1t0"), Buf("x1t1")]
            nctx = make_norm(p3, "c")
            h2stage = sb(p3, "h2stage", [128, 8, 128], BF16); H2S = Buf("h2stage")
            h2d = nc.dram_tensor("h2d", [128, 8, NT_OWN * 128 + 128], BF16, kind="Internal").ap()
            ob = (6, 7)
            obT = (4, 5)
            items = [(qt, kt, g) for qt in range(NT_EXT) for kt in range(NT_ALL) for g in range(2)]
            LOOK = 3

            def emit_qk(n):
                qt, kt, g = items[n]
                bi = n % 4
                P0, P1 = 64 * g, 64 * g + 64
                fw.mm(psb[bi][:, :], KTa[P0:P1, kt * 128:(kt + 1) * 128], QTa[P0:P1, :, qt * 128:(qt + 1) * 128],
                      True, True, [KTA[kt // 4], QTA[qt]], [PSB[bi]])

            def emit_pv(n):
                qt, kt, g = items[n]
                bi = n % 4
                pi = n % 4
                fw.act(pT[pi][:], psb[bi][:, :], AF.Exp, [PSB[bi]], [PTB[pi]])
                fw.mm(psb[obT[g]][0:65, :], Va[:, kt, g, :], pT[pi][:], kt == 0, kt == NT_ALL - 1,
                      [PTB[pi], VA[kt]], [PSB[obT[g]]])
            def epilogue(qt):
                for g in range(2):
                    fw.cpa(oT[g][:], psb[obT[g]][0:65, :], [PSB[obT[g]]], [OTB[g]])
                yield
                yield
                for g in range(2):
                    for j in range(4):
                        fw.tr(psb[ob[g]][:, j * 65:(j + 1) * 65], oT[g][:, j * 128:(j + 1) * 128], ident_f[0:65, 0:65],
                              [OTB[g], CM], [PSB[ob[g]]])
                    yield
                for g in range(2):
                    ov = psb[ob[g]][:, 0:260].rearrange("p (a b) -> p a b", b=65)
                    fw.op("dve", lambda e, ov=ov, g=g: e.reciprocal(out=ost[:, g * 4:g * 4 + 4], in_=ov[:, :, 64]), [PSB[ob[g]]], [OST])
                    fw.tt("dve", oa[:, g * 256:(g + 1) * 256].rearrange("p (a b) -> p a b", b=64), ov[:, :, 0:64],
                          ost[:, g * 4:g * 4 + 4].unsqueeze(2).to_broadcast([128, 4, 64]), ALU.mult, [PSB[ob[g]], OST], [OA])
                yield
                fw.act(ojunk[:], oa[:], AF.Square, [OA], [OJ, OST], accum=ost[:, 8:9])
                yield
                fw.act(ost[:, 9:10], ost[:, 8:9], AF.Ln, [OST, CST], [OST], bias=eps_ap, scale=1.0 / 512)
                yield
                fw.act(ost[:, 10:11], ost[:, 9:10], AF.Exp, [OST], [OST], scale=-0.5)
                yield
                fw.stt("dve", mix[:, 0:512], oa[:], ost[:, 10:11], aog_bc, ALU.mult, ALU.mult, [OA, OST, VEC], [MIX])
                yield
                fw.tt("pool", og[:], Oacc[:, qt, :], Oacc[:, qt, :], ALU.mult, [OACC[qt]], [OG])
                yield
                fw.op("dve", lambda e: e.tensor_reduce(out=ost[:, 12:16], in_=og[:].rearrange("p (a b) -> p a b", b=128),
                                                       axis=AX.X, op=ALU.add), [OG], [OST])
                yield
                fw.act(ost[:, 16:20], ost[:, 12:16], AF.Ln, [OST, CST], [OST], bias=cst[:, 2:3], scale=1.0 / 128)
                yield
                fw.act(ost[:, 20:24], ost[:, 16:20], AF.Exp, [OST], [OST], scale=-0.5)
                yield
                ogv = og[:].rearrange("p (a b) -> p a b", b=128)
                fw.tt("dve", ogv, Oacc[:, qt, :].rearrange("p (a b) -> p a b", b=128),
                      ost[:, 20:24].unsqueeze(2).to_broadcast([128, 4, 128]), ALU.mult, [OACC[qt], OST], [OG])
                yield
                fw.tt("pool", ogv, ogv, gng_bc.unsqueeze(1).to_broadcast([128, 4, 128]), ALU.mult, [OG, VEC], [OG])
                yield
                fw.tt("dve", mix[:, 512:1024], og[:], zs[:, qt, :], ALU.mult, [OG, ZS[qt]], [MIX])
                yield
                pt, PT = next_pst()
                for kc in range(8):
                    fw.tr(pt[:, kc * 128:(kc + 1) * 128], mix[:, kc * 128:(kc + 1) * 128], ident_b[:], [MIX, IDB], [PT])
                yield
                fw.cpa(mixT[:], pt[:, 0:1024].rearrange("p (a b) -> p a b", a=8), [PT], [MIXT])
                yield
                s_ = qt % 2
                fw.ld(xres[s_][:], xs[qt * 128:(qt + 1) * 128, :], [XRES[s_]])
                for half in range(2):
                    bi = 6 + half
                    for kc in range(8):
                        fw.mm(psb[bi][:, :], mixT[:, kc, :], wo[:, kc, half * 512:(half + 1) * 512], kc == 0, kc == 7,
                              [MIXT, WO], [PSB[bi]])
                    yield
                    fw.tt("dve", x1t[s_][:, half * 512:(half + 1) * 512], psb[bi][:, :], G12[:, 0, half * 512:(half + 1) * 512],
                          ALU.mult, [PSB[bi], G12B], [X1T[s_]])
                    yield
                    fw.tt("pool", x1t[s_][:, half * 512:(half + 1) * 512], x1t[s_][:, half * 512:(half + 1) * 512],
                          xres[s_][:, half * 512:(half + 1) * 512], ALU.add, [X1T[s_], XRES[s_]], [X1T[s_]])
                if qt < NT_OWN:
                    fw.stor(x1s[qt * 128:(qt + 1) * 128, :], x1t[s_][:], [X1T[s_]])
                yield
                norm_rows(nctx, x1t[s_][:], X1T[s_], 0)
                yield
                transpose_mod(nctx, 1, lambda kc, T: h2stage[:, kc, 0:T], H2S, 32)
                fw.stor(h2d[:, :, qt * 128:(qt + 1) * 128], h2stage[:], [H2S])

            pend = [None]

            def step_pending():
                if pend[0] is not None:
                    try:
                        next(pend[0])
                    except StopIteration:
                        pend[0] = None

            def flush_pending():
                while pend[0] is not None:
                    step_pending()
            emit_qk(0)
            emit_qk(1)
            for n in range(len(items)):
                if n % 2 == 0 and n + 2 < len(items):
                    emit_qk(n + 2)
                    emit_qk(n + 3)
                emit_pv(n)
                step_pending()
                qt, kt, g = items[n]
                if kt == NT_ALL - 1 and g == 1:
                    flush_pending()
                    pend[0] = epilogue(qt)
                    step_pending()
            flush_pending()
            fw.barrier()
        att.close()
        mixer.close()

        with ExitStack() as p4:
            NTOK = NT_OWN * 128
            actT = sb(p4, "actT", [128, 22, NTOK], BF16); ACTT = [Buf("actT%d" % c) for c in range(22)]
            with ExitStack() as p4a:
                h2T = sb(p4a, "h2T", [128, 8, NTOK + 128], BF16); H2T = Buf("h2T")
                for kc in range(8):
                    fw.ld(h2T[:, kc, :], h2d[:, kc, :], [H2T])
                HALF = NTOK // 2
                ub = [sb(p4a, "ubuf%d" % i, [128, HALF + 4], F32) for i in range(2)]
                UB = [Buf("ubuf0"), Buf("ubuf1")]
                ucb = [sb(p4a, "uc%d" % i, [128, HALF], F32) for i in range(2)]
                UC = [Buf("uc0"), Buf("uc1")]
                sg = sb(p4a, "sg", [128, NTOK], BF16); SG = [Buf("sg0"), Buf("sg1")]
                wst = [sb(p4a, "wust%d" % i, [128, 8, 256], F32) for i in range(2)]
                WST = [Buf("wust0"), Buf("wust1")]
                wu = [sb(p4a, "wu%d" % i, [128, 8, 256], BF16) for i in range(2)]
                WU = [Buf("wu0"), Buf("wu1")]
                fw.op("pool", lambda e: e.memset(ub[0][:, 0:1], 0.0), [], [UB[0]])
                wupv = w_up.rearrange("(k p) n -> p k n", p=128)
                bcnt = 0

                def load_w(c):
                    s_ = c % 2
                    fw.ld(wst[s_][:, :, 0:128], wupv[:, :, c * 128:(c + 1) * 128], [WST[s_]])
                    fw.ld(wst[s_][:, :, 128:256], wupv[:, :, (22 + c) * 128:(23 + c) * 128], [WST[s_]])
                    fw.cp("pool", wu[s_][:], wst[s_][:], [WST[s_]], [WU[s_]])
                load_w(0)
                for c in range(22):
                    s_ = c % 2
                    if c + 1 < 22:
                        load_w(c + 1)
                    for part in range(2):
                        ch = c + 22 * part
                        w0, w1, w2, bb = (fcw_sb[:, ch * 4 + j:ch * 4 + j + 1] for j in range(4))
                        for hf in range(2):
                            tok_lo, col_lo, ntk = (0, 1, HALF + 1) if hf == 0 else (HALF - 1, 0, HALF + 2)
                            for o in range(0, ntk, 512):
                                n = min(512, ntk - o)
                                bi = bcnt % 6
                                bcnt += 1
                                for kc in range(8):
                                    fw.mm(psb[bi][:, 0:n], wu[s_][:, kc, part * 128:(part + 1) * 128],
                                          h2T[:, kc, tok_lo + o:tok_lo + o + n], kc == 0, kc == 7, [WU[s_], H2T], [PSB[bi]])
                                fw.cp("act", ub[hf][:, col_lo + o:col_lo + o + n], psb[bi][:, 0:n], [PSB[bi]], [UB[hf]])
                                j0 = max(0, 1 - (col_lo + o))
                                j1 = min(n, HALF + 1 - (col_lo + o))
                                if j1 > j0:
                                    fw.act(ucb[hf][:, col_lo + o + j0 - 1:col_lo + o + j1 - 1], psb[bi][:, j0:j1], AF.Identity,
                                           [PSB[bi], FCW], [UC[hf]], bias=bb, scale=w1)
                            fw.stt("dve", ucb[hf][:], ub[hf][:, 0:HALF], w0, ucb[hf][:], ALU.mult, ALU.add, [UB[hf], FCW, UC[hf]], [UC[hf]])
                            fw.stt("dve", ucb[hf][:], ub[hf][:, 2:HALF + 2], w2, ucb[hf][:], ALU.mult, ALU.add, [UB[hf], FCW, UC[hf]], [UC[hf]])
                            hs = slice(hf * HALF, (hf + 1) * HALF)
                            if part == 0:
                                fw.act(sg[:, hs], ucb[hf][:], AF.Silu, [UC[hf]], [SG[hf]])
                            else:
                                fw.tt("pool", actT[:, c, hs], ucb[hf][:], sg[:, hs], ALU.mult, [UC[hf], SG[hf]], [ACTT[c]])
                fw.barrier()
            if stop_after == 7:
                dump("actT", actT[:], [128, 22, NTOK], ACTT[0], BF16)
                fw.finish()
                return nc, dbg_outs
            wd = sb(p4, "wd", [128, 22, 1024], BF16); WD = Buf("wd")
            load_weight_bf16(p4, "wd", w_down.rearrange("(c p) n -> p c n", p=128), 1024, wd, WD, piece=128)
            x1r = [sb(p4, "x1r%d" % i, [128, D], F32) for i in range(2)]
            X1R = [Buf("x1r0"), Buf("x1r1")]
            x2 = [sb(p4, "x2_%d" % i, [128, D], F32) for i in range(2)]
            X2 = [Buf("x2_0"), Buf("x2_1")]
            fj = sb(p4, "fjunk", [128, D], BF16); FJ = Buf("fjunk")
            fst = sb(p4, "fst", [128, 8], F32); FST = Buf("fst")
            def mm_down(t):
                s_ = t % 2
                fw.ld(x1r[s_][:], x1s[t * 128:(t + 1) * 128, :], [X1R[s_]])
                for half in range(2):
                    bi = 2 * s_ + half
                    for c in range(22):
                        fw.mm(psb[bi][:, :], actT[:, c, t * 128:(t + 1) * 128], wd[:, c, half * 512:(half + 1) * 512],
                              c == 0, c == 21, [ACTT[c], WD], [PSB[bi]])

            def post_down(t):
                s_ = t % 2
                for half in range(2):
                    bi = 2 * s_ + half
                    hs = slice(half * 512, (half + 1) * 512)
                    fw.tt("dve", x2[s_][:, hs], psb[bi][:, :], G12[:, 1, hs], ALU.mult, [PSB[bi], G12B], [X2[s_]])
                    fw.tt("pool", x2[s_][:, hs], x2[s_][:, hs], x1r[s_][:, hs], ALU.add, [X2[s_], X1R[s_]], [X2[s_]])
                fw.act(fj[:], x2[s_][:], AF.Square, [X2[s_]], [FJ, FST], accum=fst[:, 4 * s_:4 * s_ + 1])
                fw.act(fst[:, 4 * s_ + 1:4 * s_ + 2], fst[:, 4 * s_:4 * s_ + 1], AF.Ln, [FST, CST], [FST], bias=eps_ap, scale=1.0 / D)
                fw.act(fst[:, 4 * s_ + 2:4 * s_ + 3], fst[:, 4 * s_ + 1:4 * s_ + 2], AF.Exp, [FST], [FST], scale=-0.5)
                fw.stt("dve", x2[s_][:], x2[s_][:], fst[:, 4 * s_ + 2:4 * s_ + 3], fng_bc, ALU.mult, ALU.mult, [X2[s_], FST, VEC], [X2[s_]])
                fw.stor(y[t * 128:(t + 1) * 128, :], x2[s_][:], [X2[s_]])
            mm_down(0)
            for t in range(NT_OWN):
                if t + 1 < NT_OWN:
                    mm_down(t + 1)
                post_down(t)
            fw.barrier()
        fw.finish()
        return nc, dbg_outs


def _prep_inputs(inputs):
    f = np.float32
    x = np.asarray(inputs["x"], f)
    c = np.asarray(inputs["c"], f)
    ctx = np.asarray(inputs["ctx"], f)
    c_ctx = np.asarray(inputs["c_ctx"], f)
    w_in = np.asarray(inputs["w_in"], f)[0]

    def col(v):
        return np.ascontiguousarray(v.reshape(-1, 128).T)

    q_a = w_in[:, 0:512].reshape(D, 8, 64)
    order = [0, 4, 1, 5, 2, 6, 3, 7]
    q_perm = q_a[:, order, :].reshape(D, 512)
    k_a = w_in[:, 512:640]
    v_a = w_in[:, 640:768]
    gq = w_in[:, 768:1280]
    gk = w_in[:, 1280:1792]
    gv = w_in[:, 1792:2304]
    z = w_in[:, 2304:2816]
    a_f, a_b, b_f, b_b = (w_in[:, 2816 + 4 * i:2820 + 4 * i] for i in range(4))
    w_in_a = np.ascontiguousarray(np.concatenate([q_perm, k_a, v_a, z], axis=1))
    convq = np.asarray(inputs["conv_qkv_w"], f)[0]
    ffw = np.asarray(inputs["ffn_conv_w"], f)[0]
    ffb = np.asarray(inputs["ffn_conv_b"], f)[0]

    idx = np.arange(128)
    same = (idx[:, None] // 64) == (idx[None, :] // 64)
    ident = np.eye(128, dtype=f)
    ind = np.stack([(idx < 64), (idx >= 64)], axis=1).astype(f)
    mF = np.concatenate([(same & (idx[:, None] <= idx[None, :])).astype(f), ind], axis=1)
    mB = np.concatenate([(same & (idx[:, None] >= idx[None, :])).astype(f), ind], axis=1)
    blk = same.astype(f)
    negF = np.where(same & (idx[:, None] <= idx[None, :]), 0.0, -BIG).astype(f)
    posF = np.where(same & (idx[None, :] < idx[:, None]), 0.0, BIG).astype(f)
    negB = np.where(same & (idx[:, None] >= idx[None, :]), 0.0, -BIG).astype(f)
    posB = np.where(same & (idx[None, :] > idx[:, None]), 0.0, BIG).astype(f)
    cmat = np.ascontiguousarray(np.concatenate([ident, mF, mB, blk, negF, posF, negB, posB], axis=1))

    rows = SEQ // 64
    row = np.repeat(np.arange(rows, dtype=f), 64)
    colp = np.tile(np.arange(64, dtype=f), rows)
    inv_freq = (10000.0 ** (-np.arange(16, dtype=f) / 16)).astype(f)
    ang = np.concatenate([row[:, None] * inv_freq, colp[:, None] * inv_freq], axis=-1).astype(f)
    rope = np.concatenate([np.cos(ang), np.sin(ang)], axis=1).astype(f)

    shared = dict(
        w_mod=np.ascontiguousarray(np.asarray(inputs["w_mod"], f)[0]),
        bmod_col=col(np.asarray(inputs["b_mod"], f)[0]),
        bmod_row=np.ascontiguousarray(np.asarray(inputs["b_mod"], f)[0][None, :]),
        ng_col=np.ascontiguousarray(np.concatenate([col(np.asarray(inputs["norm1_g"], f)[0]),
                                                    col(np.asarray(inputs["norm2_g"], f)[0])], axis=1)),
        w_in_a=w_in_a,
        w_out=np.ascontiguousarray(np.asarray(inputs["w_out"], f)[0]),
        w_up=np.ascontiguousarray(np.asarray(inputs["w_up"], f)[0]),
        w_down=np.ascontiguousarray(np.asarray(inputs["w_down"], f)[0]),
        cmat=cmat,
    )
    in_maps = []
    for r in range(8):
        b, flip = r // 2, (r % 2 == 1)
        m = dict(shared)
        xb, cb, rp = x[b], ctx[b], rope
        if flip:
            xb, cb, rp = xb[::-1], cb[::-1], rp[::-1]
        m["xs"] = np.ascontiguousarray(xb)
        m["cs"] = np.ascontiguousarray(cb)
        m["rope_cs"] = np.ascontiguousarray(rp)
        m["ccol"] = np.ascontiguousarray(np.concatenate([col(c[b]), col(c_ctx)], axis=1))
        aF, aB, bF, bB = (a_b, a_f, b_b, b_f) if flip else (a_f, a_b, b_f, b_b)
        m["w_in_g"] = np.ascontiguousarray(np.concatenate([gq, gk, gv, aF, aB, bF, bB], axis=1))
        taps = [2, 1, 0] if flip else [0, 1, 2]
        cwq = convq[taps]
        m["convw"] = np.ascontiguousarray(cwq.reshape(3, 12, 128).transpose(2, 1, 0).reshape(128, 36))
        fw_ = ffw[taps].reshape(3, NFC, 128)
        fb_ = ffb.reshape(1, NFC, 128)
        m["fcw"] = np.ascontiguousarray(np.concatenate([fw_, fb_], axis=0).transpose(2, 1, 0).reshape(128, NFC * 4))
        al = [np.asarray(inputs[k], f)[0] for k in ("a_log_f", "a_log_b", "dt_bias_f", "dt_bias_b")]
        if flip:
            al = [al[1], al[0], al[3], al[2]]
        m["vecs"] = np.ascontiguousarray(np.concatenate([
            np.asarray(inputs["q_norm_g"], f)[0], np.asarray(inputs["k_norm_g"], f)[0],
            np.asarray(inputs["attn_out_g"], f)[0], np.asarray(inputs["gdn_norm_g"], f)[0],
            al[0], al[1], al[2], al[3], np.asarray(inputs["final_norm_g"], f)])[None, :])
        in_maps.append(m)
    return in_maps


def kernel(**inputs):
    in_maps = _prep_inputs(inputs)
    if os.environ.get("KSTOP"):
        nc, _ = _build(dbg=True, stop_after=int(os.environ["KSTOP"]))
        run_bass_kernel_spmd(nc, in_maps, core_ids=list(range(8)))
        return np.zeros((4, SEQ, D), np.float32)
    nc, _ = _build()
    res = run_bass_kernel_spmd(nc, in_maps, core_ids=list(range(8)))
    out = np.empty((4, SEQ, D), np.float32)
    for r in range(8):
        yb = np.asarray(res.results[r]["y"], np.float32)
        b = r // 2
        if r % 2 == 0:
            out[b, 0:2048] = yb
        else:
            out[b, 2048:4096] = yb[::-1]
    return out
```

```python
import os
from contextlib import ExitStack
import numpy as np
import concourse.bass as bass
import concourse.mybir as mybir
from concourse.bass_utils import run_bass_kernel_spmd

F32 = mybir.dt.float32
BF16 = mybir.dt.bfloat16
AF = mybir.ActivationFunctionType
ALU = mybir.AluOpType
AX = mybir.AxisListType

D = 1024
SEQ = 4096
CTX = 256
NT_LAT = 32
NT_ALL = 34
NT_EXT = 17
NT_OWN = 16
TOK_ALL = NT_ALL * 128
TOK_EXT = NT_EXT * 128
DFF = 2816
NFC = 44
EPS = 1e-6
BIG = 30000.0


class Buf:
    __slots__ = ("name", "last_w", "readers", "dsem", "dcount", "excl")

    def __init__(self, name="b", excl=False):
        self.name = name
        self.excl = excl
        self.last_w = None
        self.readers = []
        self.dsem = None
        self.dcount = 0


class _Eng:
    def __init__(self, name):
        self.name = name
        self.count = 0
        self.waited = {}
        self.ops = []
        self.is_pe = name == "pe"


class FW:
    def __init__(self, nc, stack):
        self.nc = nc
        self.stack = stack
        self.engs = {n: _Eng(n) for n in ("pe", "act", "dve", "pool", "sp")}
        self.sems = {}
        for n in self.engs:
            self.sems[n] = stack.enter_context(nc.semaphore("s_" + n))
        self.nd = 0
        self._dma_tot = {}
        self.free_dsems = []
        self.rr = 0

    def _dma_sem(self, b):
        if b.dsem is None:
            key = "d%d" % self.nd
            self.nd += 1
            self.sems[key] = self.stack.enter_context(self.nc.semaphore("s_" + key))
            b.dsem = key
        return b.dsem

    def _deps(self, eng, reads, writes):
        deps = {}

        def add(ev):
            if ev is None:
                return
            k, v = ev
            if eng.is_pe and k == "pe":
                return
            if deps.get(k, 0) < v:
                deps[k] = v
        for b in reads:
            add(b.last_w)
            if b.excl:
                for r in b.readers:
                    if r[0] != eng.name:
                        add(r)
        for b in writes:
            add(b.last_w)
            for r in b.readers:
                add(r)
        waits = []
        for k, v in deps.items():
            if eng.waited.get(k, 0) < v:
                eng.waited[k] = v
                waits.append((k, v))
        return waits

    def op(self, engname, fn, reads=(), writes=()):
        eng = self.engs[engname]
        waits = self._deps(eng, reads, writes)
        eng.count += 1
        ev = (engname, eng.count)
        eng.ops.append((waits, fn, (engname, 1)))
        for b in reads:
            b.readers.append(ev)
        for b in writes:
            b.last_w = ev
            b.readers = []
        return ev

    def dma(self, fn, reads=(), writes=(), q="sp", track=None):
        eng = self.engs[q]
        waits = self._deps(eng, reads, writes)
        tb = track if track is not None else (writes[0] if writes else reads[0])
        key = self._dma_sem(tb)
        tb.dcount += 16
        ev = (key, tb.dcount)
        self._dma_tot[key] = tb.dcount
        eng.ops.append((waits, fn, (key, 16)))
        for b in reads:
            b.readers.append(ev)
        for b in writes:
            b.last_w = ev
            b.readers = []
        return ev

    def barrier(self):
        targets = {n: e.count for n, e in self.engs.items() if e.count > 0}
        for n, e in self.engs.items():
            waits = []
            for k, v in list(targets.items()) + list(self._dma_tot.items()):
                if k == n and e.is_pe:
                    continue
                if e.waited.get(k, 0) < v:
                    e.waited[k] = v
                    waits.append((k, v))
            if waits:
                e.ops.append((waits, None, None))

    def finish(self):
        self.barrier()
        nc = self.nc
        sems = self.sems

        def run(e, obj):
            for waits, fn, inc in e.ops:
                for k, v in waits:
                    obj.wait_ge(sems[k], v)
                if fn is not None:
                    ins = fn(obj)
                    ins.then_inc(sems[inc[0]], inc[1])

        with nc.Block() as block:
            @block.tensor
            def _(o):
                run(self.engs["pe"], o)

            @block.scalar
            def _(o):
                run(self.engs["act"], o)

            @block.vector
            def _(o):
                run(self.engs["dve"], o)

            @block.gpsimd
            def _(o):
                run(self.engs["pool"], o)

            @block.sync
            def _(o):
                run(self.engs["sp"], o)

    def mm(self, out, lhsT, rhs, start=True, stop=True, r=(), w=()):
        return self.op("pe", lambda e: e.matmul(out, lhsT=lhsT, rhs=rhs, start=start, stop=stop), r, w)

    def tr(self, out, in_, ident, r=(), w=()):
        return self.op("pe", lambda e: e.transpose(out=out, in_=in_, identity=ident), r, w)

    def act(self, out, in_, func, r=(), w=(), bias=None, scale=None, accum=None):
        kw = {}
        if bias is not None:
            kw["bias"] = bias
        if scale is not None:
            kw["scale"] = scale
        if accum is not None:
            kw["accum_out"] = accum
        return self.op("act", lambda e: e.activation(out=out, in_=in_, func=func, **kw), r, w)

    def ts(self, eng, out, in0, s1, s2, op0, op1=None, r=(), w=()):
        if op1 is None:
            return self.op(eng, lambda e: e.tensor_scalar(out=out, in0=in0, scalar1=s1, scalar2=None, op0=op0), r, w)
        return self.op(eng, lambda e: e.tensor_scalar(out=out, in0=in0, scalar1=s1, scalar2=s2, op0=op0, op1=op1), r, w)

    def tt(self, eng, out, in0, in1, op, r=(), w=()):
        return self.op(eng, lambda e: e.tensor_tensor(out=out, in0=in0, in1=in1, op=op), r, w)

    def stt(self, eng, out, in0, scalar, in1, op0, op1, r=(), w=()):
        return self.op(eng, lambda e: e.scalar_tensor_tensor(out=out, in0=in0, scalar=scalar, in1=in1, op0=op0, op1=op1), r, w)

    def cp(self, eng, out, in_, r=(), w=()):
        if eng == "act":
            return self.op("act", lambda e: e.copy(out=out, in_=in_), r, w)
        return self.op(eng, lambda e: e.tensor_copy(out=out, in_=in_), r, w)

    def cpa(self, out, in_, r=(), w=()):
        self.rr += 1
        return self.cp("dve" if self.rr % 2 else "act", out, in_, r, w)

    def ld(self, out, in_, w, q="sp", r=()):
        return self.dma(lambda e: e.dma_start(out=out, in_=in_), reads=r, writes=w, q=q)

    def stor(self, out, in_, r, track=None):
        return self.dma(lambda e: e.dma_start(out=out, in_=in_), reads=r, writes=(), track=track)


class _Stop(Exception):
    pass


def _build(dbg=False, stop_after=None):
    nc = bass.Bass("TRN2", target_bir_lowering=False)
    dbg_outs = {}
    try:
        return _build_inner(nc, dbg, stop_after, dbg_outs)
    except _Stop:
        return nc, dbg_outs


def _build_inner(nc, dbg, stop_after, dbg_outs):

    def din(name, shape):
        return nc.dram_tensor(name, list(shape), F32, kind="ExternalInput").ap()

    xs = din("xs", [SEQ, D])
    cs = din("cs", [CTX, D])
    ccol = din("ccol", [128, 16])
    w_mod = din("w_mod", [D, 6 * D])
    bmod_col = din("bmod_col", [128, 48])
    bmod_row = din("bmod_row", [1, 6 * D])
    ng_col = din("ng_col", [128, 16])
    w_in_g = din("w_in_g", [D, 1552])
    w_in_a = din("w_in_a", [D, 1280])
    convw = din("convw", [128, 36])
    rope_cs = din("rope_cs", [SEQ, 64])
    vecs = din("vecs", [1, 128 + 512 + 128 + 16 + 1024])
    w_out = din("w_out", [D, D])
    w_up = din("w_up", [D, 2 * DFF])
    fcw = din("fcw", [128, NFC * 4])
    w_down = din("w_down", [DFF, D])
    cmat = din("cmat", [128, 8 * 128 + 4])
    if stop_after is None:
        y = nc.dram_tensor("y", [NT_OWN * 128, D], F32, kind="ExternalOutput").ap()
        x1s = nc.dram_tensor("x1s", [NT_OWN * 128, D], F32, kind="Internal").ap()

    with ExitStack() as st:
        fw = FW(nc, st)

        def chk(tag):
            if os.environ.get("DBGSTOP") == tag:
                fw.finish()
                raise _Stop()

        def sb(stack, name, shape, dt):
            return stack.enter_context(nc.sbuf_tensor(name, list(shape), dt))

        psb = [st.enter_context(nc.psum_tensor("psb%d" % i, [128, 512], F32)) for i in range(8)]
        PSB = [Buf("psb%d" % i, excl=True) for i in range(8)]

        def psbf(i):
            return psb[i][:, :].bitcast(BF16)
        pst_i = [0]

        def next_pst():
            pst_i[0] += 1
            i = 6 + pst_i[0] % 2
            return psbf(i), PSB[i]

        cm = sb(st, "cm", [128, 8 * 128 + 4], F32); CM = Buf("cm")
        fw.ld(cm[:], cmat[:, :], [CM])
        ident_f = cm[:, 0:128]
        mcum = [cm[:, 128:258], cm[:, 258:388]]
        blk = cm[:, 388:516]
        negm = [cm[:, 516:644], cm[:, 772:900]]
        posm = [cm[:, 644:772], cm[:, 900:1028]]
        ident_b = sb(st, "ident_b", [128, 128], BF16); IDB = Buf("idb")
        fw.cp("dve", ident_b[:], ident_f, [CM], [IDB])
        maskb = sb(st, "maskb", [128, 4, 128], BF16); MASKB = Buf("maskb")
        fw.cp("dve", maskb[:].rearrange("p a b -> p (a b)"), cm[:, 516:1028], [CM], [MASKB])
        ones_b = sb(st, "ones_b", [128, 128], BF16); ONB = Buf("onb")
        fw.op("pool", lambda e: e.memset(ones_b[:], 1.0), [], [ONB])
        ones_f = sb(st, "ones_f", [128, 128], F32); ONF = Buf("onf")
        fw.op("pool", lambda e: e.memset(ones_f[:], 1.0), [], [ONF])
        cst = sb(st, "cst", [128, 8], F32); CST = Buf("cst")
        fw.op("pool", lambda e: e.memset(cst[:, 0:1], EPS), [], [CST])
        fw.op("pool", lambda e: e.memset(cst[:, 1:2], 1.0), [], [CST])
        fw.op("pool", lambda e: e.memset(cst[:, 2:3], EPS * 128.0), [], [CST])
        eps_ap = cst[:, 0:1]

        vb_ = sb(st, "vecs_bc", [128, 1808], F32); VEC = Buf("vecs")
        fw.ld(vb_[:], vecs[0:1, :].broadcast_to([128, 1808]), [VEC])
        qg_bc = vb_[:, 0:64]
        kg_bc = vb_[:, 64:128]
        aog_bc = vb_[:, 128:640]
        gng_bc = vb_[:, 640:768]
        alogdt_bc = vb_[:, 768:784]
        fng_bc = vb_[:, 784:1808]
        gq8 = sb(st, "gq8", [128, 512], F32); GQ8 = Buf("gq8")
        for hh in range(8):
            fw.ts("dve", gq8[:, hh * 64:(hh + 1) * 64], qg_bc, 0.125, None, ALU.mult, None, [VEC], [GQ8])
        cw = sb(st, "convw_sb", [128, 36], F32); CW = Buf("cw")
        fw.ld(cw[:], convw[:, :], [CW])
        fcw_sb = sb(st, "fcw_sb", [128, NFC * 4], F32); FCW = Buf("fcw")
        fw.ld(fcw_sb[:], fcw[:, :], [FCW])
        ngc = sb(st, "ngc", [128, 16], F32); NGC = Buf("ngc")
        fw.ld(ngc[:], ng_col[:, :], [NGC])
        G12 = sb(st, "G12", [128, 2, 1024], F32); G12B = Buf("G12")
        modv = sb(st, "modv", [128, 48], F32); MODV = Buf("modv")

        def dump(name, ap, shape, buf, dt=F32):
            if not dbg:
                return
            t = nc.dram_tensor("dbg_" + name, list(shape), dt, kind="ExternalOutput").ap()
            dbg_outs[name] = t
            fw.stor(t, ap, [buf])

        if stop_after == -1:
            dump("gq8", gq8[:], [128, 512], GQ8)
            fw.finish()
            return nc, dbg_outs
        with ExitStack() as p0:
            sc = sb(p0, "sc", [128, 16], F32); SC = Buf("sc")
            fw.ld(sc[:], ccol[:, :], [SC])
            fw.act(sc[:], sc[:], AF.Silu, [SC], [SC])
            sc2 = sb(p0, "sc2", [128, 8, 2], F32); SC2 = Buf("sc2")
            fw.cp("dve", sc2[:, :, 0], sc[:, 0:8], [SC], [SC2])
            fw.cp("dve", sc2[:, :, 1], sc[:, 8:16], [SC], [SC2])
            scbc = sb(p0, "scbc", [128, 8, 128], F32); SCBC = Buf("scbc")
            for k in range(8):
                fw.ts("dve", scbc[:, k, :], ones_f[:], sc[:, k:k + 1], None, ALU.mult, None, [SC, ONF], [SCBC])
            bmc = sb(p0, "bmc", [128, 48], F32); BMC = Buf("bmc")
            fw.ld(bmc[:], bmod_col[:, :], [BMC])
            bg = sb(p0, "bgate", [128, 2, 1024], F32); BG = Buf("bgate")
            fw.ld(bg[:, 0, :], bmod_row[0:1, 2048:3072].broadcast_to([128, 1024]), [BG])
            fw.ld(bg[:, 1, :], bmod_row[0:1, 5120:6144].broadcast_to([128, 1024]), [BG])
            mcol = sb(p0, "mcol", [128, 48, 2], F32); MCOL = Buf("mcol")
            wm = [sb(p0, "wm%d" % i, [128, 8, 512], F32) for i in range(2)]
            WM = [Buf("wm0"), Buf("wm1")]
            wmv = w_mod.rearrange("(k p) n -> p k n", p=128)
            for jb in range(12):
                s = jb % 2
                fw.ld(wm[s][:], wmv[:, :, jb * 512:(jb + 1) * 512], [WM[s]])
                if jb in (4, 5, 10, 11):
                    gi = 0 if jb < 6 else 1
                    half = jb % 2 if jb < 6 else (jb - 10)
                    pb, PB = psb[0], PSB[0]
                    for k in range(8):
                        fw.mm(pb[:, :], scbc[:, k, :], wm[s][:, k, :], k == 0, k == 7, [SCBC, WM[s]], [PB])
                    fw.tt("dve", G12[:, gi, half * 512:(half + 1) * 512], pb[:, :], bg[:, gi, half * 512:(half + 1) * 512],
                          ALU.add, [PB, BG], [G12B])
                else:
                    pb, PB = psb[1], PSB[1]
                    for cc in range(4):
                        for k in range(8):
                            fw.mm(pb[:, cc * 2:cc * 2 + 2], wm[s][:, k, cc * 128:(cc + 1) * 128], sc2[:, k, :],
                                  k == 0, k == 7, [SC2, WM[s]], [PB])
                    for col in range(2):
                        fw.tt("dve", mcol[:, jb * 4:jb * 4 + 4, col], pb[:, col:8:2], bmc[:, jb * 4:jb * 4 + 4],
                              ALU.add, [PB, BMC], [MCOL])
            tmp8 = sb(p0, "tmp8", [128, 8], F32); T8 = Buf("t8")
            fw.ts("dve", tmp8[:], mcol[:, 8:16, 0], 1.0, None, ALU.add, None, [MCOL], [T8])
            fw.tt("dve", modv[:, 0:8], tmp8[:], ngc[:, 0:8], ALU.mult, [T8, NGC], [MODV])
            fw.cp("dve", modv[:, 8:16], mcol[:, 0:8, 0], [MCOL], [MODV])
            fw.ts("dve", tmp8[:], mcol[:, 8:16, 1], 1.0, None, ALU.add, None, [MCOL], [T8])
            fw.tt("dve", modv[:, 16:24], tmp8[:], ngc[:, 0:8], ALU.mult, [T8, NGC], [MODV])
            fw.cp("dve", modv[:, 24:32], mcol[:, 0:8, 1], [MCOL], [MODV])
            fw.ts("dve", tmp8[:], mcol[:, 32:40, 0], 1.0, None, ALU.add, None, [MCOL], [T8])
            fw.tt("dve", modv[:, 32:40], tmp8[:], ngc[:, 8:16], ALU.mult, [T8, NGC], [MODV])
            fw.cp("dve", modv[:, 40:48], mcol[:, 24:32, 0], [MCOL], [MODV])
            dump("modv", modv[:], [128, 48], MODV)
            dump("G12", G12[:], [128, 2, 1024], G12B)
            fw.barrier()
        if stop_after == 0:
            fw.finish()
            return nc, dbg_outs

        def tile_src(t):
            if t < NT_LAT:
                return xs[t * 128:(t + 1) * 128, :]
            return cs[(t - NT_LAT) * 128:(t - NT_LAT + 1) * 128, :]

        class NormCtx:
            pass

        def make_norm(stack, tag):
            n = NormCtx()
            n.xt = [sb(stack, "xt%s%d" % (tag, i), [128, D], F32) for i in range(2)]
            n.XT = [Buf("xt%d" % i) for i in range(2)]
            n.junk = sb(stack, "junk" + tag, [128, D], BF16); n.JUNK = Buf("junk")
            n.stt = sb(stack, "nst" + tag, [128, 4], F32); n.ST = Buf("nst")
            n.xn = [sb(stack, "xn%s%d" % (tag, i), [128, D], BF16) for i in range(4)]
            n.XN = [Buf("xn%d" % i) for i in range(4)]
            n.i = 0
            return n

        def norm_rows(n, src_ap, src_buf, slot):
            fw.act(n.junk[:], src_ap, AF.Square, [src_buf], [n.JUNK, n.ST], accum=n.stt[:, 0:1])
            fw.act(n.stt[:, 1:2], n.stt[:, 0:1], AF.Ln, [n.ST, CST], [n.ST], bias=eps_ap, scale=1.0 / D)
            fw.act(n.stt[:, 2:3], n.stt[:, 1:2], AF.Exp, [n.ST], [n.ST], scale=-0.5)
            fw.ts("dve", n.xn[slot][:], src_ap, n.stt[:, 2:3], None, ALU.mult, None, [src_buf, n.ST], [n.XN[slot]])

        def transpose_mod(n, ntile, hT_ap_fn, HT, acol0, ncols_last=128):
            for kc in range(8):
                pt, PT = next_pst()
                for i in range(ntile):
                    fw.tr(pt[:, i * 128:(i + 1) * 128], n.xn[i][:, kc * 128:(kc + 1) * 128], ident_b[:],
                          [n.XN[i], IDB], [PT])
                T = (ntile - 1) * 128 + ncols_last
                chk("tm_tr")
                fw.ts("dve", hT_ap_fn(kc, T), pt[:, 0:T], modv[:, acol0 + kc:acol0 + kc + 1],
                      modv[:, acol0 + 8 + kc:acol0 + 9 + kc], ALU.mult, ALU.add, [PT, MODV], [HT])
                chk("tm_ev%d" % kc)

        def load_norm_group(n, tiles):
            for i, t in enumerate(tiles):
                s = n.i % 2
                n.i += 1
                fw.ld(n.xt[s][:], tile_src(t), [n.XT[s]])
                norm_rows(n, n.xt[s][:], n.XT[s], i)

        def load_weight_bf16(stack, name, src_view, ncols, dst, DST, piece=256):
            K = src_view.shape[1]
            with ExitStack() as ws:
                stg = [sb(ws, "%s_stg%d" % (name, i), [128, K, piece], F32) for i in range(2)]
                STG = [Buf("stg0"), Buf("stg1")]
                i = 0
                for c0 in range(0, ncols, piece):
                    c1 = min(ncols, c0 + piece)
                    s = i % 2
                    fw.ld(stg[s][:, :, 0:c1 - c0], src_view[:, :, c0:c1], [STG[s]])
                    eng = ("dve", "act")[i % 2]
                    fw.cp(eng, dst[:, :, c0:c1], stg[s][:, :, 0:c1 - c0], [STG[s]], [DST])
                    i += 1
                fw.barrier()

        groups_ext = [[0, 1, 2, 3], [4, 5, 6, 7], [8, 9, 10, 11], [12, 13, 14, 15], [16]]
        groups_oth = [[17, 18, 19], [20, 21, 22, 23], [24, 25, 26, 27], [28, 29, 30, 31], [32, 33]]

        mixer = ExitStack()
        st.enter_context(mixer)
        Oacc = sb(mixer, "Oacc", [128, NT_EXT, 512], BF16); OACC = [Buf("oacc%d" % t) for t in range(NT_EXT)]

        gdn = ExitStack()
        st.enter_context(gdn)
        rawK = sb(gdn, "rawK", [128, 4, TOK_ALL], BF16)
        rawV = sb(gdn, "rawV", [128, 4, TOK_ALL], BF16)
        rawQ = sb(gdn, "rawQ", [128, 4, TOK_EXT], BF16)
        RAW = {}
        ab = sb(gdn, "ab", [128, NT_ALL, 16], F32); AB = Buf("ab")

        def rawbuf(kind, h, grp):
            key = (kind, h, grp)
            if key not in RAW:
                RAW[key] = Buf("raw%s%d_%d" % (kind, h, grp))
            return RAW[key]

        def tok_group(t):
            return t // 4

        with ExitStack() as p1:
            wg = sb(p1, "wg", [128, 8, 1552], BF16); WG = Buf("wg")
            load_weight_bf16(p1, "wg", w_in_g.rearrange("(k p) n -> p k n", p=128), 1552, wg, WG)
            chk("wload")
            nctx = make_norm(p1, "a")
            hT = [sb(p1, "hT%d" % i, [128, 8, 512], BF16) for i in range(2)]
            HT = [Buf("hT0"), Buf("hT1")]
            SEG = 1024
            NSL = 2
            acc = [sb(p1, "cacc%d" % i, [128, SEG], F32) for i in range(NSL)]
            ACC = [Buf("cacc%d" % i) for i in range(NSL)]
            sq = [sb(p1, "csq%d" % i, [128, SEG], BF16) for i in range(NSL)]
            SQ = [Buf("csq%d" % i) for i in range(NSL)]
            rin = [sb(p1, "crin%d" % i, [128, 512], F32) for i in range(2)]
            RIN = [Buf("crin%d" % i) for i in range(2)]
            rci = [0]
            pending = []
            csi = [0]
            cprev = {}

            def conv_seg(ch, a, b, s0, s1):
                kind = "QKV"[ch // 4]
                h = ch % 4
                arr = (rawQ, rawK, rawV)[ch // 4]
                w0, w1, w2 = (cw[:, ch * 3 + j:ch * 3 + j + 1] for j in range(3))
                n = s1 - s0
                sl = csi[0] % NSL
                csi[0] += 1
                bufs = sorted({tok_group(t) for t in range(s0 // 128, (s1 + 127) // 128)} |
                              ({tok_group(s1 // 128)} if s1 < b else set()))
                RB = [rawbuf(kind, h, g) for g in bufs]
                fw.act(acc[sl][:, 0:n], arr[:, h, s0:s1], AF.Copy, RB + [CW], [ACC[sl]], scale=w1)
                if s0 > a:
                    pl = cprev[(ch, a)]
                    fw.stt("dve", acc[sl][:, 0:1], pl[0], w0, acc[sl][:, 0:1], ALU.mult, ALU.add,
                           [pl[1], CW, ACC[sl]], [ACC[sl]])
                fw.stt("dve", acc[sl][:, 1:n], arr[:, h, s0:s1 - 1], w0, acc[sl][:, 1:n], ALU.mult, ALU.add,
                       RB + [CW, ACC[sl]], [ACC[sl]])
                nr = n if s1 < b else n - 1
                fw.stt("dve", acc[sl][:, 0:nr], arr[:, h, s0 + 1:s0 + 1 + nr], w2, acc[sl][:, 0:nr], ALU.mult, ALU.add,
                       RB + [CW, ACC[sl]], [ACC[sl]])
                if s1 < b:
                    keep = sb(p1, "keep%d_%d" % (ch, s0), [128, 1], BF16)
                    KB = Buf("keep")
                    fw.cp("pool", keep[:], arr[:, h, s1 - 1:s1], RB, [KB])
                    cprev[(ch, a)] = (keep[:], KB)
                WB = [rawbuf(kind, h, g) for g in sorted({tok_group(t) for t in range(s0 // 128, (s1 + 127) // 128)})]

                def tail_a():
                    if kind == "V":
                        fw.act(arr[:, h, s0:s1], acc[sl][:, 0:n], AF.Silu, [ACC[sl]], WB)
                        return
                    fw.act(acc[sl][:, 0:n], acc[sl][:, 0:n], AF.Silu, [ACC[sl]], [ACC[sl]])
                    fw.tt("pool", sq[sl][:, 0:n], acc[sl][:, 0:n], acc[sl][:, 0:n], ALU.mult, [ACC[sl]], [SQ[sl]])

                def tail_b():
                    if kind == "V":
                        return
                    for c0 in range(0, n, 512):
                        c1 = min(n, c0 + 512)
                        rci[0] += 1
                        pb, PB = psb[4 + rci[0] % 2], PSB[4 + rci[0] % 2]
                        r_, RN = rin[rci[0] % 2], RIN[rci[0] % 2]
                        fw.mm(pb[:, 0:c1 - c0], ones_b[:], sq[sl][:, c0:c1], True, True, [ONB, SQ[sl]], [PB])
                        fw.act(r_[:, 0:c1 - c0], pb[:, 0:c1 - c0], AF.Ln, [PB, CST], [RN], bias=eps_ap, scale=1.0)
                        fw.act(r_[:, 0:c1 - c0], r_[:, 0:c1 - c0], AF.Exp, [RN], [RN], scale=-0.5)
                        fw.tt("dve", arr[:, h, s0 + c0:s0 + c1], acc[sl][:, c0:c1], r_[:, 0:c1 - c0], ALU.mult,
                              [ACC[sl], RN], WB)
                pending.append((tail_a, tail_b))
                if len(pending) >= 2:
                    flush_tails()

            def flush_tails():
                for ta, _ in pending:
                    ta()
                for _, tb in pending:
                    tb()
                del pending[:]

            csegs = []
            for ch in range(12):
                rngs = [(0, TOK_EXT)] if ch < 4 else [(0, SEQ), (SEQ, TOK_ALL)]
                for (a_, b_) in rngs:
                    for s0 in range(a_, b_, SEG):
                        s1 = min(b_, s0 + SEG)
                        need = None if a_ == SEQ else min(b_, s1 + 1)
                        csegs.append((need, ch, a_, b_, s0, s1))
            cdone = set()
            cav = [0, False]

            def emit_ready_convs(avail_lat, ctx_done, limit=None):
                k = 0
                for i_, (need, ch, a_, b_, s0, s1) in enumerate(csegs):
                    if i_ in cdone:
                        continue
                    ok = ctx_done if need is None else need <= avail_lat
                    if ok:
                        conv_seg(ch, a_, b_, s0, s1)
                        cdone.add(i_)
                        k += 1
                        if limit is not None and k >= limit:
                            return

            allg = groups_ext + groups_oth

            def prep1(gi):
                grp = allg[gi]
                load_norm_group(nctx, grp)
                transpose_mod(nctx, len(grp), lambda kc, T, s=gi % 2: hT[s][:, kc, 0:T], HT[gi % 2],
                              16 if grp[0] >= NT_LAT else 0)
            prep1(0)
            for gi, grp in enumerate(allg):
                is_ext = grp[0] < NT_EXT
                is_ctx = grp[0] >= NT_LAT
                s = gi % 2
                T = len(grp) * 128
                tok0 = grp[0] * 128
                chunks = list(range(12)) if is_ext else list(range(4, 12))
                for ci, ch in enumerate(chunks):
                    if ci == 2 and gi + 1 < len(allg):
                        prep1(gi + 1)
                    if gi > 0:
                        emit_ready_convs(cav[0], cav[1], limit=2)
                    pb, PB = psb[ci % 4], PSB[ci % 4]
                    for kc in range(8):
                        fw.mm(pb[:, 0:T], wg[:, kc, ch * 128:(ch + 1) * 128], hT[s][:, kc, 0:T], kc == 0, kc == 7,
                              [WG, HT[s]], [PB])
                    kind = "QKV"[ch // 4]
                    dst = (rawQ, rawK, rawV)[ch // 4]
                    fw.cpa(dst[:, ch % 4, tok0:tok0 + T], pb[:, 0:T], [PB], [rawbuf(kind, ch % 4, tok_group(grp[0]))])
                for i, t in enumerate(grp):
                    pb, PB = psb[4 + (i % 2)], PSB[4 + (i % 2)]
                    for kc in range(8):
                        fw.mm(pb[:, 0:16], hT[s][:, kc, i * 128:(i + 1) * 128], wg[:, kc, 1536:1552], kc == 0, kc == 7,
                              [WG, HT[s]], [PB])
                    fw.cp("act", ab[:, t, :], pb[:, 0:16], [PB], [AB])
                chk("grp0")
                cav[0] = (grp[-1] + 1) * 128 if grp[0] < NT_LAT else SEQ
                cav[1] = grp[0] >= NT_LAT
            emit_ready_convs(SEQ, True)
            flush_tails()
            assert len(cdone) == len(csegs)
            fw.barrier()
        if stop_after == 1:
            dump("rawK", rawK[:], [128, 4, TOK_ALL], rawbuf("K", 0, 0), BF16)
            dump("ab", ab[:], [128, NT_ALL, 16], AB)
            fw.finish()
            return nc, dbg_outs

        if stop_after == 2:
            dump("KT", rawK[:], [128, 4, TOK_ALL], rawbuf("K", 0, 0), BF16)
            dump("QT", rawQ[:], [128, 4, TOK_EXT], rawbuf("Q", 0, 0), BF16)
            dump("VT", rawV[:], [128, 4, TOK_ALL], rawbuf("V", 0, 0), BF16)
            dump("ab", ab[:], [128, NT_ALL, 16], AB)
            fw.finish()
            return nc, dbg_outs

        def a3(name, n, stack=gdn):
            return sb(stack, name, [128, NT_ALL, n], F32)
        gg = a3("gg", 8); GG = Buf("gg")
        beta = a3("beta", 8); BETA = Buf("beta")
        egc = a3("egc", 8); EGC = Buf("egc")
        ekd = a3("ekd", 8); EKD = Buf("ekd")
        bgt = a3("bgt", 8); BGT = Buf("bgt")
        gcpl = a3("gcpl", 8); GCPL = Buf("gcpl")
        ngcn = a3("ngcn", 8); NGCN = Buf("ngcn")
        dl = a3("dl", 16); DL = Buf("dl")
        with ExitStack() as pg:
            t1 = a3("t1", 8, pg); T1 = Buf("t1")
            lnb = a3("lnb", 8, pg); LNB = Buf("lnb")
            gcs = a3("gcs", 32, pg); GCS = Buf("gcs")
            ealog = sb(pg, "ealog", [128, 8], F32); EAL = Buf("ealog")
            gI = [sb(pg, "gI%d" % i, [128, 8, 2], F32) for i in range(2)]
            GI = [Buf("gI0"), Buf("gI1")]
            fw.tt("dve", t1[:], ab[:, :, 0:8], alogdt_bc[:, 8:16].unsqueeze(1).to_broadcast([128, NT_ALL, 8]), ALU.add,
                  [AB, VEC], [T1])
            fw.act(t1[:], t1[:], AF.Exp, [T1], [T1])
            fw.act(t1[:], t1[:], AF.Ln, [T1, CST], [T1], bias=cst[:, 1:2], scale=1.0)
            fw.act(ealog[:], alogdt_bc[:, 0:8], AF.Exp, [VEC], [EAL])
            fw.stt("dve", gg[:], t1[:], -1.0, ealog[:].unsqueeze(1).to_broadcast([128, NT_ALL, 8]), ALU.mult, ALU.mult,
                   [T1, EAL], [GG])
            fw.act(beta[:], ab[:, :, 8:16], AF.Sigmoid, [AB], [BETA])
            fw.act(lnb[:], beta[:], AF.Ln, [BETA], [LNB])
            for t in range(NT_ALL):
                k = t % 2
                pb, PB = psb[k], PSB[k]
                for c in range(2):
                    fw.ts("dve", gI[k][:, :, c], gg[:, t, :], mcum[0][:, 128 + c:129 + c], None, ALU.mult, None,
                          [GG, CM], [GI[k]])
                fw.mm(pb[:, 0:4], mcum[0][:, 0:128], gg[:, t, 0:4], True, True, [CM, GG], [PB])
                fw.mm(pb[:, 4:8], mcum[1][:, 0:128], gg[:, t, 4:8], False, True, [CM, GG], [PB])
                fw.mm(pb[:, 8:16], blk, gg[:, t, 0:8], False, True, [CM, GG], [PB])
                fw.mm(pb[:, 16:32], ones_f[:], gI[k][:].rearrange("p a b -> p (a b)"), False, True, [ONF, GI[k]], [PB])
                fw.cp("act", gcs[:, t, :], pb[:, 0:32], [PB], [GCS])
            fw.act(egc[:], gcs[:, :, 0:8], AF.Exp, [GCS], [EGC])
            fw.tt("dve", t1[:], gcs[:, :, 8:16], gcs[:, :, 0:8], ALU.subtract, [GCS], [T1])
            fw.act(ekd[:], t1[:], AF.Exp, [T1], [EKD])
            fw.tt("dve", bgt[:], beta[:], egc[:], ALU.mult, [BETA, EGC], [BGT])
            fw.tt("dve", gcpl[:], gcs[:, :, 0:8], lnb[:], ALU.add, [GCS, LNB], [GCPL])
            fw.ts("dve", ngcn[:], gcs[:, :, 0:8], -1.0, None, ALU.mult, None, [GCS], [NGCN])
            fw.act(dl[:], gcs[:, :, 16:32], AF.Exp, [GCS], [DL])
            fw.barrier()
        if stop_after == 3:
            dump("gg", gg[:], [128, NT_ALL, 8], GG)
            dump("beta", beta[:], [128, NT_ALL, 8], BETA)
            fw.finish()
            return nc, dbg_outs

        fw.op("pool", lambda e: e.memset(Oacc[:], 0.0), [], OACC)
        with ExitStack() as ps_:
            maskb4 = sb(ps_, "maskb4", [128, 4, 512], BF16); MASKB4 = Buf("maskb4")
            for ty in range(4):
                for h in range(4):
                    fw.cp("pool", maskb4[:, ty, h * 128:(h + 1) * 128], maskb[:, ty, :], [MASKB], [MASKB4])
            identb4 = sb(ps_, "identb4", [128, 4, 128], BF16); IDB4 = Buf("idb4")
            for h in range(4):
                fw.cp("dve", identb4[:, h, :], ident_b[:], [IDB], [IDB4])

            class QS:
                pass
            DBL = ("kbg", "kdec", "vb", "AT", "wT", "u")
            sets = []
            for d in range(2):
                q = QS()
                for nm, dt_ in (("kbg", BF16), ("kdec", BF16), ("vb", BF16), ("gM", F32), ("Dstr", BF16), ("Dinc", BF16),
                                ("B0", BF16), ("B1", BF16), ("AT", BF16),
                                ("u", F32), ("wT", BF16), ("vnew", BF16), ("tmp", F32), ("S", F32), ("Sbf", BF16)):
                    if nm in DBL:
                        setattr(q, nm + "_2", [sb(ps_, "q%d_%s_%d" % (d, nm, i), [128, 4, 128], dt_) for i in range(2)])
                        setattr(q, nm.upper() + "_B2", [Buf("q%d_%s_%d" % (d, nm, i)) for i in range(2)])
                    else:
                        setattr(q, nm, sb(ps_, "q%d_%s" % (d, nm), [128, 4, 128], dt_))
                        setattr(q, nm.upper() + "_", Buf("q%d_%s" % (d, nm)))
                q.AP0 = sb(ps_, "q%d_AP0" % d, [128, 4, 256], BF16)
                q.AP1 = sb(ps_, "q%d_AP1" % d, [128, 4, 256], BF16)
                q.APA0_, q.APA1_, q.APP0_, q.APP1_ = Buf("apa0"), Buf("apa1"), Buf("app0"), Buf("app1")
                fw.op("pool", lambda e, q=q: e.memset(q.S[:], 0.0), [], [q.S_])
                fw.op("pool", lambda e, q=q: e.memset(q.Sbf[:], 0.0), [], [q.SBF_])
                q.banks = [0, 1, 2, 3] if d == 0 else [4, 5, 6, 7]
                q.bi = 0
                sets.append(q)

            class QView:
                def __init__(self, base, par):
                    object.__setattr__(self, "_b", base)
                    object.__setattr__(self, "_p", par)

                def __getattr__(self, name):
                    b_, p_ = self._b, self._p
                    if name in DBL:
                        return getattr(b_, name + "_2")[p_]
                    if name.endswith("_") and name[:-1].lower() in [x.lower() for x in DBL] and name[:-1].isupper():
                        for x in DBL:
                            if x.upper() == name[:-1]:
                                return getattr(b_, x.upper() + "_B2")[p_]
                    return getattr(b_, name)

                def __setattr__(self, name, val):
                    setattr(self._b, name, val)

            def qview(d, par):
                return QView(sets[d], par)

            def nb(q):
                q.bi += 1
                i = q.banks[q.bi % 4]
                return i

            def v4(ap512):
                return ap512.rearrange("p (a b) -> p a b", a=4)

            def bc4(ap_p4):
                return ap_p4.unsqueeze(2).to_broadcast([ap_p4.shape[0], 4, 128])

            def quad_pre(t, d, with_out, par):
                q = qview(d, par)
                c0 = d * 4
                tsl = slice(t * 128, (t + 1) * 128)
                grp = tok_group(t)
                KB = [rawbuf("K", h, grp) for h in range(4)]
                VB = [rawbuf("V", h, grp) for h in range(4)]
                QB = [rawbuf("Q", h, grp) for h in range(4)] if with_out else []
                i = nb(q)
                bv = psbf(i)
                for h in range(4):
                    fw.tr(bv[:, h * 128:(h + 1) * 128], rawK[:, h, tsl], ident_b[:], [KB[h], IDB], [PSB[i]])
                fw.tt("dve", q.kbg[:], v4(bv[:, 0:512]), bc4(bgt[:, t, c0:c0 + 4]), ALU.mult, [PSB[i], BGT], [q.KBG_])
                fw.tt("dve", q.kdec[:], v4(bv[:, 0:512]), bc4(ekd[:, t, c0:c0 + 4]), ALU.mult, [PSB[i], EKD], [q.KDEC_])
                yield
                i = nb(q)
                bv = psbf(i)
                for h in range(4):
                    fw.tr(bv[:, h * 128:(h + 1) * 128], rawV[:, h, tsl], ident_b[:], [VB[h], IDB], [PSB[i]])
                fw.tt("dve", q.vb[:], v4(bv[:, 0:512]), bc4(beta[:, t, c0:c0 + 4]), ALU.mult, [PSB[i], BETA], [q.VB_])
                yield
                fw.tt("dve", q.gM[:], mcum[d][:, 0:128].unsqueeze(1).to_broadcast([128, 4, 128]), bc4(gg[:, t, c0:c0 + 4]),
                      ALU.mult, [CM, GG], [q.GM_])
                gMf = q.gM[:].rearrange("p a b -> p (a b)")
                i = nb(q)
                fw.mm(psb[i][:, :], ones_f[:], gMf, True, False, [ONF, q.GM_], [PSB[i]])
                fw.mm(psb[i][:, :], ident_b[:], maskb4[:, 2 * d + 1, :], False, True, [IDB, MASKB4], [PSB[i]])
                for h in range(4):
                    fw.act(q.Dstr[:, h, :], psb[i][:, h * 128:(h + 1) * 128], AF.Exp, [PSB[i], GCPL], [q.DSTR_],
                           bias=gcpl[:, t, c0 + h:c0 + h + 1], scale=-1.0)
                yield
                if with_out:
                    i = nb(q)
                    fw.mm(psb[i][:, :], ones_f[:], gMf, True, False, [ONF, q.GM_], [PSB[i]])
                    fw.mm(psb[i][:, :], ident_b[:], maskb4[:, 2 * d, :], False, True, [IDB, MASKB4], [PSB[i]])
                    for h in range(4):
                        fw.act(q.Dinc[:, h, :], psb[i][:, h * 128:(h + 1) * 128], AF.Exp, [PSB[i], NGCN], [q.DINC_],
                               bias=ngcn[:, t, c0 + h:c0 + h + 1], scale=1.0)
                    yield
                i = nb(q)
                for h in range(4):
                    fw.mm(psb[i][:, h * 128:(h + 1) * 128], rawK[:, h, tsl], rawK[:, h, tsl], h == 0, True, [KB[h]], [PSB[i]])
                fw.stt("dve", q.B0[:], v4(psb[i][:, :]), -1.0, q.Dstr[:], ALU.mult, ALU.mult, [PSB[i], q.DSTR_], [q.B0_])
                yield
                if with_out:
                    i = nb(q)
                    for h in range(4):
                        fw.mm(psb[i][:, h * 128:(h + 1) * 128], rawK[:, h, tsl], rawQ[:, h, tsl], h == 0, True,
                              [KB[h], QB[h]], [PSB[i]])
                    fw.tt("dve", q.AT[:], v4(psb[i][:, :]), q.Dinc[:], ALU.mult, [PSB[i], q.DINC_], [q.AT_])
                    yield
                AP = [q.AP0, q.AP1]
                APA = [q.APA0_, q.APA1_]
                APP = [q.APP0_, q.APP1_]
                Bb = [(q.B0, q.B0_), (q.B1, q.B1_)]
                i = nb(q)
                bv = psbf(i)
                for h in range(4):
                    fw.tr(bv[:, h * 128:(h + 1) * 128], q.B0[:, h, :], ident_b[:], [q.B0_, IDB], [PSB[i]])
                fw.cp("act", AP[0][:, :, 0:128], v4(bv[:, 0:512]), [PSB[i]], [APA[0]])
                fw.tt("dve", AP[1][:, :, 128:256], v4(bv[:, 0:512]), identb4[:], ALU.add, [PSB[i], IDB4], [APP[1]])
                yield
                for j in range(1, 6):
                    cur, nxt = (j - 1) % 2, j % 2
                    Bc, BcB = Bb[(j - 1) % 2]
                    Bn, BnB = Bb[j % 2]
                    if j == 1:
                        i = nb(q)
                        for h in range(4):
                            fw.mm(psb[i][:, h * 128:(h + 1) * 128], Bc[:, h, :], AP[cur][:, h, 0:128], h == 0, True,
                                  [BcB, APA[cur]], [PSB[i]])
                        i2 = nb(q)
                        for h in range(4):
                            fw.mm(psb[i2][:, h * 128:(h + 1) * 128], AP[cur][:, h, 0:128], Bc[:, h, :], h == 0, True,
                                  [BcB, APA[cur]], [PSB[i2]])
                        fw.cp("act", AP[nxt][:, :, 0:128], v4(psb[i][:, :]), [PSB[i]], [APA[nxt]])
                        fw.cp("dve", Bn[:], v4(psb[i2][:, :]), [PSB[i2]], [BnB])
                        yield
                    elif j < 5:
                        ia, ib = nb(q), nb(q)
                        for h in range(4):
                            bk = ia if h < 2 else ib
                            hh = h % 2
                            fw.mm(psb[bk][:, hh * 256:(hh + 1) * 256], Bc[:, h, :], AP[cur][:, h, :], hh == 0, True,
                                  [BcB, APA[cur], APP[cur]], [PSB[bk]])
                        i2 = nb(q)
                        for h in range(4):
                            fw.mm(psb[i2][:, h * 128:(h + 1) * 128], AP[cur][:, h, 0:128], Bc[:, h, :], h == 0, True,
                                  [BcB, APA[cur]], [PSB[i2]])
                        for bk, h0 in ((ia, 0), (ib, 2)):
                            pv_ = psb[bk][:, :].rearrange("p (a b) -> p a b", a=2)
                            fw.cp("act", AP[nxt][:, h0:h0 + 2, 0:128], pv_[:, :, 0:128], [PSB[bk]], [APA[nxt]])
                            fw.tt("dve", AP[nxt][:, h0:h0 + 2, 128:256], AP[cur][:, h0:h0 + 2, 128:256], pv_[:, :, 128:256],
                                  ALU.add, [PSB[bk], APP[cur]], [APP[nxt]])
                        fw.cp("dve", Bn[:], v4(psb[i2][:, :]), [PSB[i2]], [BnB])
                        yield
                    else:
                        i = nb(q)
                        for h in range(4):
                            fw.mm(psb[i][:, h * 128:(h + 1) * 128], Bc[:, h, :], AP[cur][:, h, 128:256], h == 0, True,
                                  [BcB, APP[cur]], [PSB[i]])
                        i2 = nb(q)
                        for h in range(4):
                            fw.mm(psb[i2][:, h * 128:(h + 1) * 128], AP[cur][:, h, 0:128], Bc[:, h, :], h == 0, True,
                                  [BcB, APA[cur]], [PSB[i2]])
                        fw.tt("dve", AP[nxt][:, :, 128:256], AP[cur][:, :, 128:256], v4(psb[i][:, :]), ALU.add,
                              [PSB[i], APP[cur]], [APP[nxt]])
                        fw.cp("act", Bn[:], v4(psb[i2][:, :]), [PSB[i2]], [BnB])
                        yield
                B5, B5B = Bb[1]
                i = nb(q)
                for h in range(4):
                    fw.mm(psb[i][:, h * 128:(h + 1) * 128], B5[:, h, :], AP[1][:, h, 128:256], h == 0, True, [B5B, APP[1]], [PSB[i]])
                fw.tt("dve", AP[0][:, :, 128:256], AP[1][:, :, 128:256], v4(psb[i][:, :]), ALU.add, [PSB[i], APP[1]], [APP[0]])
                yield
                Ptf = AP[0]
                PTF_ = APP[0]
                i = nb(q)
                for h in range(4):
                    fw.mm(psb[i][:, h * 128:(h + 1) * 128], Ptf[:, h, 128:256], q.vb[:, h, :], h == 0, True, [PTF_, q.VB_], [PSB[i]])
                fw.cp("act", q.u[:], v4(psb[i][:, :]), [PSB[i]], [q.U_])
                i = nb(q)
                for h in range(4):
                    fw.mm(psb[i][:, h * 128:(h + 1) * 128], q.kbg[:, h, :], Ptf[:, h, 128:256], h == 0, True, [PTF_, q.KBG_], [PSB[i]])
                fw.cp("dve", q.wT[:], v4(psb[i][:, :]), [PSB[i]], [q.WT_])
                yield

            def quad_steps(t, d, with_out, par):
                q = qview(d, par)
                c0 = d * 4
                tsl = slice(t * 128, (t + 1) * 128)
                grp = tok_group(t)
                KB = [rawbuf("K", h, grp) for h in range(4)]
                VB = [rawbuf("V", h, grp) for h in range(4)]
                QB = [rawbuf("Q", h, grp) for h in range(4)] if with_out else []
                for c in ((0, 1) if d == 0 else (1, 0)):
                    R = slice(64 * c, 64 * c + 64)
                    i = nb(q)
                    for h in range(4):
                        fw.mm(psb[i][:, h * 128:(h + 1) * 128], q.wT[:, h, :], q.Sbf[:, h, :], h == 0, True, [q.WT_, q.SBF_], [PSB[i]])
                    fw.tt("dve", q.vnew[R, :, :], q.u[R, :, :], v4(psb[i][R, :]), ALU.subtract, [PSB[i], q.U_], [q.VNEW_])
                    yield
                    if with_out:
                        i = nb(q)
                        for h in range(4):
                            fw.mm(psb[i][:, h * 128:(h + 1) * 128], rawQ[:, h, tsl], q.Sbf[:, h, :], h == 0, True,
                                  [QB[h], q.SBF_], [PSB[i]])
                        fw.tt("dve", q.tmp[R, :, :], v4(psb[i][R, :]), bc4(egc[R, t, c0:c0 + 4]), ALU.mult, [PSB[i], EGC], [q.TMP_])
                        i = nb(q)
                        for h in range(4):
                            fw.mm(psb[i][:, h * 128:(h + 1) * 128], q.AT[R, h, :], q.vnew[R, h, :], h == 0, True,
                                  [q.AT_, q.VNEW_], [PSB[i]])
                        fw.tt("dve", q.tmp[R, :, :], q.tmp[R, :, :], v4(psb[i][R, :]), ALU.add, [PSB[i], q.TMP_], [q.TMP_])
                        fw.tt("pool", Oacc[R, t, :], Oacc[R, t, :], q.tmp[R, :, :].rearrange("p a b -> p (a b)"), ALU.add,
                              [q.TMP_, OACC[t]], [OACC[t]])
                        yield
                    i = nb(q)
                    for h in range(4):
                        fw.mm(psb[i][:, h * 128:(h + 1) * 128], q.kdec[R, h, :], q.vnew[R, h, :], h == 0, True,
                              [q.KDEC_, q.VNEW_], [PSB[i]])
                    dlv = dl[:, t, :].rearrange("p (a b) -> p a b", b=2)[:, c0:c0 + 4, c]
                    fw.tt("dve", q.S[:], q.S[:], bc4(dlv), ALU.mult, [q.S_, DL], [q.S_])
                    fw.tt("dve", q.S[:], q.S[:], v4(psb[i][:, :]), ALU.add, [PSB[i], q.S_], [q.S_])
                    fw.cp("act", q.Sbf[:], q.S[:], [q.S_], [q.SBF_])
                    yield

            def chain(tiles, d):
                pre = quad_pre(tiles[0], d, tiles[0] < NT_EXT, 0)
                yield from pre
                for k, t in enumerate(tiles):
                    gens = [quad_steps(t, d, t < NT_EXT, k % 2)]
                    if k + 1 < len(tiles):
                        gens.append(quad_pre(tiles[k + 1], d, tiles[k + 1] < NT_EXT, (k + 1) % 2))
                    while gens:
                        for g_ in list(gens):
                            try:
                                next(g_)
                                yield
                            except StopIteration:
                                gens.remove(g_)

            nq = int(os.environ.get("GDN_NQ", "999"))
            chF = chain(([32, 33] + list(range(0, NT_EXT)))[:nq], 0)
            chB = chain(([33, 32] + list(range(31, -1, -1)))[:nq], 1)
            alive = [chF, chB]
            if os.environ.get("GDN_ONLY"):
                alive = [chF] if os.environ["GDN_ONLY"] == "F" else [chB]
            nst = 0
            while alive:
                for g_ in list(alive):
                    try:
                        next(g_)
                        nst += 1
                        chk("qs%d" % nst)
                    except StopIteration:
                        alive.remove(g_)
            fw.barrier()
            if stop_after == 4:
                dump("Oacc", Oacc[:], [128, NT_EXT, 512], OACC[0], BF16)
                dump("S0", sets[0].S[:], [128, 4, 128], sets[0].S_)
                dump("S1", sets[1].S[:], [128, 4, 128], sets[1].S_)
                fw.finish()
                return nc, dbg_outs
        gdn.close()

        att = ExitStack()
        st.enter_context(att)
        KTa = sb(att, "KTa", [128, TOK_ALL], BF16); KTA = [Buf("kta%d" % g) for g in range(9)]
        Va = sb(att, "Va", [128, NT_ALL, 2, 65], BF16); VA = [Buf("va%d" % t) for t in range(NT_ALL)]
        QTa = sb(att, "QTa", [128, 4, TOK_EXT], BF16); QTA = [Buf("qta%d" % t) for t in range(NT_EXT)]
        zs = sb(att, "zs", [128, NT_EXT, 512], BF16); ZS = [Buf("zs%d" % t) for t in range(NT_EXT)]
        fw.op("pool", lambda e: e.memset(Va[:], 1.0), [], VA)
        with ExitStack() as p2:
            rope_sb = sb(p2, "rope_sb", [128, NT_LAT, 64], F32); ROPE = Buf("rope")
            fw.ld(rope_sb[:], rope_cs.rearrange("(t p) c -> p t c", p=128), [ROPE])
            wa = sb(p2, "wa", [128, 8, 1280], BF16); WA = Buf("wa")
            load_weight_bf16(p2, "wa", w_in_a.rearrange("(k p) n -> p k n", p=128), 1280, wa, WA)
            nctx = make_norm(p2, "b")
            hT = [sb(p2, "hTb%d" % i, [128, 8, 512], BF16) for i in range(2)]
            HT = [Buf("hTb0"), Buf("hTb1")]
            qsq_ = [sb(p2, "qsq%d" % i, [128, 640], F32) for i in range(2)]; QSQ_ = [Buf("qsq0"), Buf("qsq1")]
            qst_ = [sb(p2, "qst%d" % i, [128, 32], F32) for i in range(2)]; QST_ = [Buf("qst0"), Buf("qst1")]
            qn_ = [sb(p2, "qn%d" % i, [128, 640], F32) for i in range(2)]; QN_ = [Buf("qn0"), Buf("qn1")]
            rt_ = [[sb(p2, "rt%d_%d" % (k, i), [128, 10, 32], F32) for i in range(4)] for k in range(2)]
            RT_ = [[Buf("rt%d_%d" % (k, i)) for i in range(4)] for k in range(2)]
            qr_ = [sb(p2, "qr%d" % i, [128, 640], BF16) for i in range(2)]; QR_ = [Buf("qr0"), Buf("qr1")]
            kg2 = sb(p2, "kg2", [128, 640], F32); KG2 = Buf("kg2")
            fw.cp("dve", kg2[:, 0:512], gq8[:], [GQ8], [KG2])
            for hh in range(2):
                fw.cp("dve", kg2[:, 512 + hh * 64:576 + hh * 64], kg_bc, [VEC], [KG2])
            allg = groups_ext + groups_oth

            def prep2(gi):
                grp = allg[gi]
                load_norm_group(nctx, grp)
                transpose_mod(nctx, len(grp), lambda kc, T, s=gi % 2: hT[s][:, kc, 0:T], HT[gi % 2],
                              16 if grp[0] >= NT_LAT else 0)
            prep2(0)
            tiles2 = [(gi, i_, t) for gi, grp in enumerate(allg) for i_, t in enumerate(grp)]

            def mm2(k):
                gi, i_, t = tiles2[k]
                s = gi % 2
                bq, bkv, bz = (0, 1, 2) if k % 2 == 0 else (3, 4, 5)
                lt = hT[s][:, :, i_ * 128:(i_ + 1) * 128]
                is_ext = t < NT_EXT
                if is_ext:
                    for kc in range(8):
                        fw.mm(psb[bq][:, :], lt[:, kc, :], wa[:, kc, 0:512], kc == 0, kc == 7, [WA, HT[s]], [PSB[bq]])
                    for kc in range(8):
                        fw.mm(psb[bz][:, :], lt[:, kc, :], wa[:, kc, 768:1280], kc == 0, kc == 7, [WA, HT[s]], [PSB[bz]])
                for kc in range(8):
                    fw.mm(psb[bkv][:, 0:256], lt[:, kc, :], wa[:, kc, 512:768], kc == 0, kc == 7, [WA, HT[s]], [PSB[bkv]])

            def post2(k):
                gi, i_, t = tiles2[k]
                bq, bkv, bz = (0, 1, 2) if k % 2 == 0 else (3, 4, 5)
                is_ext = t < NT_EXT
                is_ctx = t >= NT_LAT
                kk = k % 2
                qsq, QSQ, qst, QST, qn, QN, rt, RT, qr, QR = (qsq_[kk], QSQ_[kk], qst_[kk], QST_[kk], qn_[kk], QN_[kk],
                                                              rt_[kk], RT_[kk], qr_[kk], QR_[kk])
                if True:
                    hoff = 0 if is_ext else 8
                    if is_ext:
                        fw.act(zs[:, t, :], psb[bz][:, :], AF.Silu, [PSB[bz]], [ZS[t]])
                    fw.cp("act", Va[:, t, :, 0:64], psb[bkv][:, 128:256].rearrange("p (a b) -> p a b", a=2), [PSB[bkv]], [VA[t]])
                    if is_ext:
                        fw.cp("dve", qn[:, 0:512], psb[bq][:, :], [PSB[bq]], [QN])
                    fw.cp("dve", qn[:, 512:640], psb[bkv][:, 0:128], [PSB[bkv]], [QN])
                    yield
                    c_lo, c_hi = hoff * 64, 640
                    fw.tt("dve", qsq[:, c_lo:c_hi], qn[:, c_lo:c_hi], qn[:, c_lo:c_hi], ALU.mult, [QN], [QSQ])
                    yield
                    fw.op("dve", lambda e, hoff=hoff: e.tensor_reduce(
                        out=qst[:, hoff:10], in_=qsq[:, hoff * 64:640].rearrange("p (a b) -> p a b", b=64),
                        axis=AX.X, op=ALU.add), [QSQ], [QST])
                    yield
                    fw.act(qst[:, 10 + hoff:20], qst[:, hoff:10], AF.Ln, [QST, CST], [QST], bias=eps_ap, scale=1.0 / 64)
                    yield
                    fw.act(qst[:, 20 + hoff:30], qst[:, 10 + hoff:20], AF.Exp, [QST], [QST], scale=-0.5)
                    yield
                    n_h = 10 - hoff
                    qv = qn[:, c_lo:c_hi].rearrange("p (a b) -> p a b", b=64)
                    fw.tt("dve", qv, qv, qst[:, 20 + hoff:30].unsqueeze(2).to_broadcast([128, n_h, 64]), ALU.mult, [QN, QST], [QN])
                    yield
                    fw.tt("dve", qn[:, c_lo:c_hi], qn[:, c_lo:c_hi], kg2[:, c_lo:c_hi], ALU.mult, [QN, KG2], [QN])
                    yield
                    qrv = qr[:, c_lo:c_hi].rearrange("p (a b) -> p a b", b=64)
                    if not is_ctx:
                        cosb = rope_sb[:, t, 0:32].unsqueeze(1).to_broadcast([128, n_h, 32])
                        sinb = rope_sb[:, t, 32:64].unsqueeze(1).to_broadcast([128, n_h, 32])
                        x1_, x2_ = qv[:, :, 0:32], qv[:, :, 32:64]
                        r0, r1, r2, r3 = (rt[j][:, hoff:10, :] for j in range(4))
                        fw.tt("dve", r0, x1_, cosb, ALU.mult, [QN, ROPE], [RT[0]])
                        fw.tt("dve", r1, x2_, sinb, ALU.mult, [QN, ROPE], [RT[1]])
                        yield
                        fw.tt("dve", r2, x2_, cosb, ALU.mult, [QN, ROPE], [RT[2]])
                        fw.tt("dve", r3, x1_, sinb, ALU.mult, [QN, ROPE], [RT[3]])
                        yield
                        fw.tt("dve", qrv[:, :, 0:32], r0, r1, ALU.subtract, [RT[0], RT[1]], [QR])
                        fw.tt("dve", qrv[:, :, 32:64], r2, r3, ALU.add, [RT[2], RT[3]], [QR])
                        yield
                    else:
                        fw.cp("dve", qr[:, c_lo:c_hi], qn[:, c_lo:c_hi], [QN], [QR])
                    pt, PT = next_pst()
                    fw.tr(pt[:, 0:128], qr[:, 512:640], ident_b[:], [QR, IDB], [PT])
                    if is_ext:
                        for j in range(4):
                            fw.tr(pt[:, 128 + j * 128:256 + j * 128], qr[:, j * 128:(j + 1) * 128], ident_b[:], [QR, IDB], [PT])
                    fw.cp("act", KTa[:, t * 128:(t + 1) * 128], pt[:, 0:128], [PT], [KTA[t // 4]])
                    if is_ext:
                        fw.cp("dve", QTa[:, :, t * 128:(t + 1) * 128], pt[:, 128:640].rearrange("p (a b) -> p a b", a=4),
                              [PT], [QTA[t]])

            prepped = {0}

            def ensure_prep(k):
                gi = tiles2[k][0]
                for g_ in range(gi + 2):
                    if g_ < len(allg) and g_ not in prepped and g_ <= gi + 1:
                        prep2(g_)
                        prepped.add(g_)
            ensure_prep(0)
            mm2(0)
            mm2(1)
            nt2 = len(tiles2)
            for p_ in range(0, nt2, 2):
                gens = [post2(k) for k in (p_, p_ + 1) if k < nt2]
                for g_ in gens:
                    next(g_)
                for k in (p_ + 2, p_ + 3):
                    if k < nt2:
                        ensure_prep(k)
                        mm2(k)
                alive = list(gens)
                while alive:
                    for g_ in list(alive):
                        try:
                            next(g_)
                        except StopIteration:
                            alive.remove(g_)
            fw.barrier()
        wo = sb(att, "wo", [128, 8, 1024], BF16); WO = Buf("wo")
        with ExitStack() as p2w:
            load_weight_bf16(p2w, "wo", w_out.rearrange("(k p) n -> p k n", p=128), 1024, wo, WO)
        if stop_after == 5:
            dump("KTa", KTa[:], [128, TOK_ALL], KTA[0], BF16)
            dump("QTa", QTa[:], [128, 4, TOK_EXT], QTA[0], BF16)
            dump("Va", Va[:], [128, NT_ALL, 2, 65], VA[0], BF16)
            dump("zs", zs[:], [128, NT_EXT, 512], ZS[0], BF16)
            fw.finish()
            return nc, dbg_outs

        with ExitStack() as p3:
            pT = [sb(p3, "pT%d" % i, [128, 512], BF16) for i in range(4)]
            PTB = [Buf("pT%d" % i) for i in range(4)]
            oa = sb(p3, "oa", [128, 512], F32); OA = Buf("oa")
            oT = [sb(p3, "oT%d" % i, [65, 512], F32) for i in range(2)]
            OTB = [Buf("oT0"), Buf("oT1")]
            og = sb(p3, "og", [128, 512], F32); OG = Buf("og")
            ost = sb(p3, "ost", [128, 32], F32); OST = Buf("ost")
            ojunk = sb(p3, "ojunk", [128, 512], BF16); OJ = Buf("ojunk")
            mix = sb(p3, "mix", [128, 1024], BF16); MIX = Buf("mix")
            mixT = sb(p3, "mixT", [128, 8, 128], BF16); MIXT = Buf("mixT")
            xres = [sb(p3, "xres%d" % i, [128, D], F32) for i in range(2)]
            XRES = [Buf("xres0"), Buf("xres1")]
            x1t = [sb(p3, "x1t%d" % i, [128, D], F32) for i in range(2)]
            X1T = [Buf("x1t0"), Buf("x1t1")]
            nctx = make_norm(p3, "c")
            h2stage = sb(p3, "h2stage", [128, 8, 128], BF16); H2S = Buf("h2stage")
            h2d = nc.dram_tensor("h2d", [128, 8, NT_OWN * 128 + 128], BF16, kind="Internal").ap()
            ob = (6, 7)
            obT = (4, 5)
            items = [(qt, kt, g) for qt in range(NT_EXT) for kt in range(NT_ALL) for g in range(2)]
            LOOK = 3

            def emit_qk(n):
                qt, kt, g = items[n]
                bi = n % 4
                P0, P1 = 64 * g, 64 * g + 64
                fw.mm(psb[bi][:, :], KTa[P0:P1, kt * 128:(kt + 1) * 128], QTa[P0:P1, :, qt * 128:(qt + 1) * 128],
                      True, True, [KTA[kt // 4], QTA[qt]], [PSB[bi]])

            def emit_pv(n):
                qt, kt, g = items[n]
                bi = n % 4
                pi = n % 4
                fw.act(pT[pi][:], psb[bi][:, :], AF.Exp, [PSB[bi]], [PTB[pi]])
                fw.mm(psb[obT[g]][0:65, :], Va[:, kt, g, :], pT[pi][:], kt == 0, kt == NT_ALL - 1,
                      [PTB[pi], VA[kt]], [PSB[obT[g]]])
            def epilogue(qt):
                for g in range(2):
                    fw.cpa(oT[g][:], psb[obT[g]][0:65, :], [PSB[obT[g]]], [OTB[g]])
                yield
                yield
                for g in range(2):
                    for j in range(4):
                        fw.tr(psb[ob[g]][:, j * 65:(j + 1) * 65], oT[g][:, j * 128:(j + 1) * 128], ident_f[0:65, 0:65],
                              [OTB[g], CM], [PSB[ob[g]]])
                    yield
                for g in range(2):
                    ov = psb[ob[g]][:, 0:260].rearrange("p (a b) -> p a b", b=65)
                    fw.op("dve", lambda e, ov=ov, g=g: e.reciprocal(out=ost[:, g * 4:g * 4 + 4], in_=ov[:, :, 64]), [PSB[ob[g]]], [OST])
                    fw.tt("dve", oa[:, g * 256:(g + 1) * 256].rearrange("p (a b) -> p a b", b=64), ov[:, :, 0:64],
                          ost[:, g * 4:g * 4 + 4].unsqueeze(2).to_broadcast([128, 4, 64]), ALU.mult, [PSB[ob[g]], OST], [OA])
                yield
                fw.act(ojunk[:], oa[:], AF.Square, [OA], [OJ, OST], accum=ost[:, 8:9])
                yield
                fw.act(ost[:, 9:10], ost[:, 8:9], AF.Ln, [OST, CST], [OST], bias=eps_ap, scale=1.0 / 512)
                yield
                fw.act(ost[:, 10:11], ost[:, 9:10], AF.Exp, [OST], [OST], scale=-0.5)
                yield
                fw.stt("dve", mix[:, 0:512], oa[:], ost[:, 10:11], aog_bc, ALU.mult, ALU.mult, [OA, OST, VEC], [MIX])
                yield
                fw.tt("pool", og[:], Oacc[:, qt, :], Oacc[:, qt, :], ALU.mult, [OACC[qt]], [OG])
                yield
                fw.op("dve", lambda e: e.tensor_reduce(out=ost[:, 12:16], in_=og[:].rearrange("p (a b) -> p a b", b=128),
                                                       axis=AX.X, op=ALU.add), [OG], [OST])
                yield
                fw.act(ost[:, 16:20], ost[:, 12:16], AF.Ln, [OST, CST], [OST], bias=cst[:, 2:3], scale=1.0 / 128)
                yield
                fw.act(ost[:, 20:24], ost[:, 16:20], AF.Exp, [OST], [OST], scale=-0.5)
                yield
                ogv = og[:].rearrange("p (a b) -> p a b", b=128)
                fw.tt("dve", ogv, Oacc[:, qt, :].rearrange("p (a b) -> p a b", b=128),
                      ost[:, 20:24].unsqueeze(2).to_broadcast([128, 4, 128]), ALU.mult, [OACC[qt], OST], [OG])
                yield
                fw.tt("pool", ogv, ogv, gng_bc.unsqueeze(1).to_broadcast([128, 4, 128]), ALU.mult, [OG, VEC], [OG])
                yield
                fw.tt("dve", mix[:, 512:1024], og[:], zs[:, qt, :], ALU.mult, [OG, ZS[qt]], [MIX])
                yield
                pt, PT = next_pst()
                for kc in range(8):
                    fw.tr(pt[:, kc * 128:(kc + 1) * 128], mix[:, kc * 128:(kc + 1) * 128], ident_b[:], [MIX, IDB], [PT])
                yield
                fw.cpa(mixT[:], pt[:, 0:1024].rearrange("p (a b) -> p a b", a=8), [PT], [MIXT])
                yield
                s_ = qt % 2
                fw.ld(xres[s_][:], xs[qt * 128:(qt + 1) * 128, :], [XRES[s_]])
                for half in range(2):
                    bi = 6 + half
                    for kc in range(8):
                        fw.mm(psb[bi][:, :], mixT[:, kc, :], wo[:, kc, half * 512:(half + 1) * 512], kc == 0, kc == 7,
                              [MIXT, WO], [PSB[bi]])
                    yield
                    fw.tt("dve", x1t[s_][:, half * 512:(half + 1) * 512], psb[bi][:, :], G12[:, 0, half * 512:(half + 1) * 512],
                          ALU.mult, [PSB[bi], G12B], [X1T[s_]])
                    yield
                    fw.tt("pool", x1t[s_][:, half * 512:(half + 1) * 512], x1t[s_][:, half * 512:(half + 1) * 512],
                          xres[s_][:, half * 512:(half + 1) * 512], ALU.add, [X1T[s_], XRES[s_]], [X1T[s_]])
                if qt < NT_OWN:
                    fw.stor(x1s[qt * 128:(qt + 1) * 128, :], x1t[s_][:], [X1T[s_]])
                yield
                norm_rows(nctx, x1t[s_][:], X1T[s_], 0)
                yield
                transpose_mod(nctx, 1, lambda kc, T: h2stage[:, kc, 0:T], H2S, 32)
                fw.stor(h2d[:, :, qt * 128:(qt + 1) * 128], h2stage[:], [H2S])

            pend = [None]

            def step_pending():
                if pend[0] is not None:
                    try:
                        next(pend[0])
                    except StopIteration:
                        pend[0] = None

            def flush_pending():
                while pend[0] is not None:
                    step_pending()
            emit_qk(0)
            emit_qk(1)
            for n in range(len(items)):
                if n % 2 == 0 and n + 2 < len(items):
                    emit_qk(n + 2)
                    emit_qk(n + 3)
                emit_pv(n)
                step_pending()
                qt, kt, g = items[n]
                if kt == NT_ALL - 1 and g == 1:
                    flush_pending()
                    pend[0] = epilogue(qt)
                    step_pending()
            flush_pending()
            fw.barrier()
        att.close()
        mixer.close()

        with ExitStack() as p4:
            NTOK = NT_OWN * 128
            actT = sb(p4, "actT", [128, 22, NTOK], BF16); ACTT = [Buf("actT%d" % c) for c in range(22)]
            with ExitStack() as p4a:
                h2T = sb(p4a, "h2T", [128, 8, NTOK + 128], BF16); H2T = Buf("h2T")
                for kc in range(8):
                    fw.ld(h2T[:, kc, :], h2d[:, kc, :], [H2T])
                HALF = NTOK // 2
                ub = [sb(p4a, "ubuf%d" % i, [128, HALF + 4], F32) for i in range(2)]
                UB = [Buf("ubuf0"), Buf("ubuf1")]
                ucb = [sb(p4a, "uc%d" % i, [128, HALF], F32) for i in range(2)]
                UC = [Buf("uc0"), Buf("uc1")]
                sg = sb(p4a, "sg", [128, NTOK], BF16); SG = [Buf("sg0"), Buf("sg1")]
                wst = [sb(p4a, "wust%d" % i, [128, 8, 256], F32) for i in range(2)]
                WST = [Buf("wust0"), Buf("wust1")]
                wu = [sb(p4a, "wu%d" % i, [128, 8, 256], BF16) for i in range(2)]
                WU = [Buf("wu0"), Buf("wu1")]
                fw.op("pool", lambda e: e.memset(ub[0][:, 0:1], 0.0), [], [UB[0]])
                wupv = w_up.rearrange("(k p) n -> p k n", p=128)
                bcnt = 0

                def load_w(c):
                    s_ = c % 2
                    fw.ld(wst[s_][:, :, 0:128], wupv[:, :, c * 128:(c + 1) * 128], [WST[s_]])
                    fw.ld(wst[s_][:, :, 128:256], wupv[:, :, (22 + c) * 128:(23 + c) * 128], [WST[s_]])
                    fw.cp("pool", wu[s_][:], wst[s_][:], [WST[s_]], [WU[s_]])
                load_w(0)
                for c in range(22):
                    s_ = c % 2
                    if c + 1 < 22:
                        load_w(c + 1)
                    for part in range(2):
                        ch = c + 22 * part
                        w0, w1, w2, bb = (fcw_sb[:, ch * 4 + j:ch * 4 + j + 1] for j in range(4))
                        for hf in range(2):
                            tok_lo, col_lo, ntk = (0, 1, HALF + 1) if hf == 0 else (HALF - 1, 0, HALF + 2)
                            for o in range(0, ntk, 512):
                                n = min(512, ntk - o)
                                bi = bcnt % 6
                                bcnt += 1
                                for kc in range(8):
                                    fw.mm(psb[bi][:, 0:n], wu[s_][:, kc, part * 128:(part + 1) * 128],
                                          h2T[:, kc, tok_lo + o:tok_lo + o + n], kc == 0, kc == 7, [WU[s_], H2T], [PSB[bi]])
                                fw.cp("act", ub[hf][:, col_lo + o:col_lo + o + n], psb[bi][:, 0:n], [PSB[bi]], [UB[hf]])
                                j0 = max(0, 1 - (col_lo + o))
                                j1 = min(n, HALF + 1 - (col_lo + o))
                                if j1 > j0:
                                    fw.act(ucb[hf][:, col_lo + o + j0 - 1:col_lo + o + j1 - 1], psb[bi][:, j0:j1], AF.Identity,
                                           [PSB[bi], FCW], [UC[hf]], bias=bb, scale=w1)
                            fw.stt("dve", ucb[hf][:], ub[hf][:, 0:HALF], w0, ucb[hf][:], ALU.mult, ALU.add, [UB[hf], FCW, UC[hf]], [UC[hf]])
                            fw.stt("dve", ucb[hf][:], ub[hf][:, 2:HALF + 2], w2, ucb[hf][:], ALU.mult, ALU.add, [UB[hf], FCW, UC[hf]], [UC[hf]])
                            hs = slice(hf * HALF, (hf + 1) * HALF)
                            if part == 0:
                                fw.act(sg[:, hs], ucb[hf][:], AF.Silu, [UC[hf]], [SG[hf]])
                            else:
                                fw.tt("pool", actT[:, c, hs], ucb[hf][:], sg[:, hs], ALU.mult, [UC[hf], SG[hf]], [ACTT[c]])
                fw.barrier()
            if stop_after == 7:
                dump("actT", actT[:], [128, 22, NTOK], ACTT[0], BF16)
                fw.finish()
                return nc, dbg_outs
            wd = sb(p4, "wd", [128, 22, 1024], BF16); WD = Buf("wd")
            load_weight_bf16(p4, "wd", w_down.rearrange("(c p) n -> p c n", p=128), 1024, wd, WD, piece=128)
            x1r = [sb(p4, "x1r%d" % i, [128, D], F32) for i in range(2)]
            X1R = [Buf("x1r0"), Buf("x1r1")]
            x2 = [sb(p4, "x2_%d" % i, [128, D], F32) for i in range(2)]
            X2 = [Buf("x2_0"), Buf("x2_1")]
            fj = sb(p4, "fjunk", [128, D], BF16); FJ = Buf("fjunk")
            fst = sb(p4, "fst", [128, 8], F32); FST = Buf("fst")
            def mm_down(t):
                s_ = t % 2
                fw.ld(x1r[s_][:], x1s[t * 128:(t + 1) * 128, :], [X1R[s_]])
                for half in range(2):
                    bi = 2 * s_ + half
                    for c in range(22):
                        fw.mm(psb[bi][:, :], actT[:, c, t * 128:(t + 1) * 128], wd[:, c, half * 512:(half + 1) * 512],
                              c == 0, c == 21, [ACTT[c], WD], [PSB[bi]])

            def post_down(t):
                s_ = t % 2
                for half in range(2):
                    bi = 2 * s_ + half
                    hs = slice(half * 512, (half + 1) * 512)
                    fw.tt("dve", x2[s_][:, hs], psb[bi][:, :], G12[:, 1, hs], ALU.mult, [PSB[bi], G12B], [X2[s_]])
                    fw.tt("pool", x2[s_][:, hs], x2[s_][:, hs], x1r[s_][:, hs], ALU.add, [X2[s_], X1R[s_]], [X2[s_]])
                fw.act(fj[:], x2[s_][:], AF.Square, [X2[s_]], [FJ, FST], accum=fst[:, 4 * s_:4 * s_ + 1])
                fw.act(fst[:, 4 * s_ + 1:4 * s_ + 2], fst[:, 4 * s_:4 * s_ + 1], AF.Ln, [FST, CST], [FST], bias=eps_ap, scale=1.0 / D)
                fw.act(fst[:, 4 * s_ + 2:4 * s_ + 3], fst[:, 4 * s_ + 1:4 * s_ + 2], AF.Exp, [FST], [FST], scale=-0.5)
                fw.stt("dve", x2[s_][:], x2[s_][:], fst[:, 4 * s_ + 2:4 * s_ + 3], fng_bc, ALU.mult, ALU.mult, [X2[s_], FST, VEC], [X2[s_]])
                fw.stor(y[t * 128:(t + 1) * 128, :], x2[s_][:], [X2[s_]])
            mm_down(0)
            for t in range(NT_OWN):
                if t + 1 < NT_OWN:
                    mm_down(t + 1)
                post_down(t)
            fw.barrier()
        fw.finish()
        return nc, dbg_outs


def _prep_inputs(inputs):
    f = np.float32
    x = np.asarray(inputs["x"], f)
    c = np.asarray(inputs["c"], f)
    ctx = np.asarray(inputs["ctx"], f)
    c_ctx = np.asarray(inputs["c_ctx"], f)
    w_in = np.asarray(inputs["w_in"], f)[0]

    def col(v):
        return np.ascontiguousarray(v.reshape(-1, 128).T)

    q_a = w_in[:, 0:512].reshape(D, 8, 64)
    order = [0, 4, 1, 5, 2, 6, 3, 7]
    q_perm = q_a[:, order, :].reshape(D, 512)
    k_a = w_in[:, 512:640]
    v_a = w_in[:, 640:768]
    gq = w_in[:, 768:1280]
    gk = w_in[:, 1280:1792]
    gv = w_in[:, 1792:2304]
    z = w_in[:, 2304:2816]
    a_f, a_b, b_f, b_b = (w_in[:, 2816 + 4 * i:2820 + 4 * i] for i in range(4))
    w_in_a = np.ascontiguousarray(np.concatenate([q_perm, k_a, v_a, z], axis=1))
    convq = np.asarray(inputs["conv_qkv_w"], f)[0]
    ffw = np.asarray(inputs["ffn_conv_w"], f)[0]
    ffb = np.asarray(inputs["ffn_conv_b"], f)[0]

    idx = np.arange(128)
    same = (idx[:, None] // 64) == (idx[None, :] // 64)
    ident = np.eye(128, dtype=f)
    ind = np.stack([(idx < 64), (idx >= 64)], axis=1).astype(f)
    mF = np.concatenate([(same & (idx[:, None] <= idx[None, :])).astype(f), ind], axis=1)
    mB = np.concatenate([(same & (idx[:, None] >= idx[None, :])).astype(f), ind], axis=1)
    blk = same.astype(f)
    negF = np.where(same & (idx[:, None] <= idx[None, :]), 0.0, -BIG).astype(f)
    posF = np.where(same & (idx[None, :] < idx[:, None]), 0.0, BIG).astype(f)
    negB = np.where(same & (idx[:, None] >= idx[None, :]), 0.0, -BIG).astype(f)
    posB = np.where(same & (idx[None, :] > idx[:, None]), 0.0, BIG).astype(f)
    cmat = np.ascontiguousarray(np.concatenate([ident, mF, mB, blk, negF, posF, negB, posB], axis=1))

    rows = SEQ // 64
    row = np.repeat(np.arange(rows, dtype=f), 64)
    colp = np.tile(np.arange(64, dtype=f), rows)
    inv_freq = (10000.0 ** (-np.arange(16, dtype=f) / 16)).astype(f)
    ang = np.concatenate([row[:, None] * inv_freq, colp[:, None] * inv_freq], axis=-1).astype(f)
    rope = np.concatenate([np.cos(ang), np.sin(ang)], axis=1).astype(f)

    shared = dict(
        w_mod=np.ascontiguousarray(np.asarray(inputs["w_mod"], f)[0]),
        bmod_col=col(np.asarray(inputs["b_mod"], f)[0]),
        bmod_row=np.ascontiguousarray(np.asarray(inputs["b_mod"], f)[0][None, :]),
        ng_col=np.ascontiguousarray(np.concatenate([col(np.asarray(inputs["norm1_g"], f)[0]),
                                                    col(np.asarray(inputs["norm2_g"], f)[0])], axis=1)),
        w_in_a=w_in_a,
        w_out=np.ascontiguousarray(np.asarray(inputs["w_out"], f)[0]),
        w_up=np.ascontiguousarray(np.asarray(inputs["w_up"], f)[0]),
        w_down=np.ascontiguousarray(np.asarray(inputs["w_down"], f)[0]),
        cmat=cmat,
    )
    in_maps = []
    for r in range(8):
        b, flip = r // 2, (r % 2 == 1)
        m = dict(shared)
        xb, cb, rp = x[b], ctx[b], rope
        if flip:
            xb, cb, rp = xb[::-1], cb[::-1], rp[::-1]
        m["xs"] = np.ascontiguousarray(xb)
        m["cs"] = np.ascontiguousarray(cb)
        m["rope_cs"] = np.ascontiguousarray(rp)
        m["ccol"] = np.ascontiguousarray(np.concatenate([col(c[b]), col(c_ctx)], axis=1))
        aF, aB, bF, bB = (a_b, a_f, b_b, b_f) if flip else (a_f, a_b, b_f, b_b)
        m["w_in_g"] = np.ascontiguousarray(np.concatenate([gq, gk, gv, aF, aB, bF, bB], axis=1))
        taps = [2, 1, 0] if flip else [0, 1, 2]
        cwq = convq[taps]
        m["convw"] = np.ascontiguousarray(cwq.reshape(3, 12, 128).transpose(2, 1, 0).reshape(128, 36))
        fw_ = ffw[taps].reshape(3, NFC, 128)
        fb_ = ffb.reshape(1, NFC, 128)
        m["fcw"] = np.ascontiguousarray(np.concatenate([fw_, fb_], axis=0).transpose(2, 1, 0).reshape(128, NFC * 4))
        al = [np.asarray(inputs[k], f)[0] for k in ("a_log_f", "a_log_b", "dt_bias_f", "dt_bias_b")]
        if flip:
            al = [al[1], al[0], al[3], al[2]]
        m["vecs"] = np.ascontiguousarray(np.concatenate([
            np.asarray(inputs["q_norm_g"], f)[0], np.asarray(inputs["k_norm_g"], f)[0],
            np.asarray(inputs["attn_out_g"], f)[0], np.asarray(inputs["gdn_norm_g"], f)[0],
            al[0], al[1], al[2], al[3], np.asarray(inputs["final_norm_g"], f)])[None, :])
        in_maps.append(m)
    return in_maps


def kernel(**inputs):
    in_maps = _prep_inputs(inputs)
    if os.environ.get("KSTOP"):
        nc, _ = _build(dbg=True, stop_after=int(os.environ["KSTOP"]))
        run_bass_kernel_spmd(nc, in_maps, core_ids=list(range(8)))
        return np.zeros((4, SEQ, D), np.float32)
    nc, _ = _build()
    res = run_bass_kernel_spmd(nc, in_maps, core_ids=list(range(8)))
    out = np.empty((4, SEQ, D), np.float32)
    for r in range(8):
        yb = np.asarray(res.results[r]["y"], np.float32)
        b = r // 2
        if r % 2 == 0:
            out[b, 0:2048] = yb
        else:
            out[b, 2048:4096] = yb[::-1]
    return out
```

```python
import os
from contextlib import ExitStack
import numpy as np
import concourse.bass as bass
import concourse.mybir as mybir
from concourse.bass_utils import run_bass_kernel_spmd

F32 = mybir.dt.float32
BF16 = mybir.dt.bfloat16
AF = mybir.ActivationFunctionType
ALU = mybir.AluOpType
AX = mybir.AxisListType

D = 1024
SEQ = 4096
CTX = 256
NT_LAT = 32
NT_ALL = 34
NT_EXT = 17
NT_OWN = 16
TOK_ALL = NT_ALL * 128
TOK_EXT = NT_EXT * 128
DFF = 2816
NFC = 44
EPS = 1e-6
BIG = 30000.0


class Buf:
    __slots__ = ("name", "last_w", "readers", "dsem", "dcount", "excl")

    def __init__(self, name="b", excl=False):
        self.name = name
        self.excl = excl
        self.last_w = None
        self.readers = []
        self.dsem = None
        self.dcount = 0


class _Eng:
    def __init__(self, name):
        self.name = name
        self.count = 0
        self.waited = {}
        self.ops = []
        self.is_pe = name == "pe"


class FW:
    def __init__(self, nc, stack):
        self.nc = nc
        self.stack = stack
        self.engs = {n: _Eng(n) for n in ("pe", "act", "dve", "pool", "sp")}
        self.sems = {}
        for n in self.engs:
            self.sems[n] = stack.enter_context(nc.semaphore("s_" + n))
        self.nd = 0
        self._dma_tot = {}
        self.free_dsems = []
        self.rr = 0

    def _dma_sem(self, b):
        if b.dsem is None:
            key = "d%d" % self.nd
            self.nd += 1
            self.sems[key] = self.stack.enter_context(self.nc.semaphore("s_" + key))
            b.dsem = key
        return b.dsem

    def _deps(self, eng, reads, writes):
        deps = {}

        def add(ev):
            if ev is None:
                return
            k, v = ev
            if eng.is_pe and k == "pe":
                return
            if deps.get(k, 0) < v:
                deps[k] = v
        for b in reads:
            add(b.last_w)
            if b.excl:
                for r in b.readers:
                    if r[0] != eng.name:
                        add(r)
        for b in writes:
            add(b.last_w)
            for r in b.readers:
                add(r)
        waits = []
        for k, v in deps.items():
            if eng.waited.get(k, 0) < v:
                eng.waited[k] = v
                waits.append((k, v))
        return waits

    def op(self, engname, fn, reads=(), writes=()):
        eng = self.engs[engname]
        waits = self._deps(eng, reads, writes)
        eng.count += 1
        ev = (engname, eng.count)
        eng.ops.append((waits, fn, (engname, 1)))
        for b in reads:
            b.readers.append(ev)
        for b in writes:
            b.last_w = ev
            b.readers = []
        return ev

    def dma(self, fn, reads=(), writes=(), q="sp", track=None):
        eng = self.engs[q]
        waits = self._deps(eng, reads, writes)
        tb = track if track is not None else (writes[0] if writes else reads[0])
        key = self._dma_sem(tb)
        tb.dcount += 16
        ev = (key, tb.dcount)
        self._dma_tot[key] = tb.dcount
        eng.ops.append((waits, fn, (key, 16)))
        for b in reads:
            b.readers.append(ev)
        for b in writes:
            b.last_w = ev
            b.readers = []
        return ev

    def barrier(self):
        targets = {n: e.count for n, e in self.engs.items() if e.count > 0}
        for n, e in self.engs.items():
            waits = []
            for k, v in list(targets.items()) + list(self._dma_tot.items()):
                if k == n and e.is_pe:
                    continue
                if e.waited.get(k, 0) < v:
                    e.waited[k] = v
                    waits.append((k, v))
            if waits:
                e.ops.append((waits, None, None))

    def finish(self):
        self.barrier()
        nc = self.nc
        sems = self.sems

        def run(e, obj):
            for waits, fn, inc in e.ops:
                for k, v in waits:
                    obj.wait_ge(sems[k], v)
                if fn is not None:
                    ins = fn(obj)
                    ins.then_inc(sems[inc[0]], inc[1])

        with nc.Block() as block:
            @block.tensor
            def _(o):
                run(self.engs["pe"], o)

            @block.scalar
            def _(o):
                run(self.engs["act"], o)

            @block.vector
            def _(o):
                run(self.engs["dve"], o)

            @block.gpsimd
            def _(o):
                run(self.engs["pool"], o)

            @block.sync
            def _(o):
                run(self.engs["sp"], o)

    def mm(self, out, lhsT, rhs, start=True, stop=True, r=(), w=()):
        return self.op("pe", lambda e: e.matmul(out, lhsT=lhsT, rhs=rhs, start=start, stop=stop), r, w)

    def tr(self, out, in_, ident, r=(), w=()):
        return self.op("pe", lambda e: e.transpose(out=out, in_=in_, identity=ident), r, w)

    def act(self, out, in_, func, r=(), w=(), bias=None, scale=None, accum=None):
        kw = {}
        if bias is not None:
            kw["bias"] = bias
        if scale is not None:
            kw["scale"] = scale
        if accum is not None:
            kw["accum_out"] = accum
        return self.op("act", lambda e: e.activation(out=out, in_=in_, func=func, **kw), r, w)

    def ts(self, eng, out, in0, s1, s2, op0, op1=None, r=(), w=()):
        if op1 is None:
            return self.op(eng, lambda e: e.tensor_scalar(out=out, in0=in0, scalar1=s1, scalar2=None, op0=op0), r, w)
        return self.op(eng, lambda e: e.tensor_scalar(out=out, in0=in0, scalar1=s1, scalar2=s2, op0=op0, op1=op1), r, w)

    def tt(self, eng, out, in0, in1, op, r=(), w=()):
        return self.op(eng, lambda e: e.tensor_tensor(out=out, in0=in0, in1=in1, op=op), r, w)

    def stt(self, eng, out, in0, scalar, in1, op0, op1, r=(), w=()):
        return self.op(eng, lambda e: e.scalar_tensor_tensor(out=out, in0=in0, scalar=scalar, in1=in1, op0=op0, op1=op1), r, w)

    def cp(self, eng, out, in_, r=(), w=()):
        if eng == "act":
            return self.op("act", lambda e: e.copy(out=out, in_=in_), r, w)
        return self.op(eng, lambda e: e.tensor_copy(out=out, in_=in_), r, w)

    def cpa(self, out, in_, r=(), w=()):
        self.rr += 1
        return self.cp("dve" if self.rr % 2 else "act", out, in_, r, w)

    def ld(self, out, in_, w, q="sp", r=()):
        return self.dma(lambda e: e.dma_start(out=out, in_=in_), reads=r, writes=w, q=q)

    def stor(self, out, in_, r, track=None):
        return self.dma(lambda e: e.dma_start(out=out, in_=in_), reads=r, writes=(), track=track)


class _Stop(Exception):
    pass


def _build(dbg=False, stop_after=None):
    nc = bass.Bass("TRN2", target_bir_lowering=False)
    dbg_outs = {}
    try:
        return _build_inner(nc, dbg, stop_after, dbg_outs)
    except _Stop:
        return nc, dbg_outs


def _build_inner(nc, dbg, stop_after, dbg_outs):

    def din(name, shape):
        return nc.dram_tensor(name, list(shape), F32, kind="ExternalInput").ap()

    xs = din("xs", [SEQ, D])
    cs = din("cs", [CTX, D])
    ccol = din("ccol", [128, 16])
    w_mod = din("w_mod", [D, 6 * D])
    bmod_col = din("bmod_col", [128, 48])
    bmod_row = din("bmod_row", [1, 6 * D])
    ng_col = din("ng_col", [128, 16])
    w_in_g = din("w_in_g", [D, 1552])
    w_in_a = din("w_in_a", [D, 1280])
    convw = din("convw", [128, 36])
    rope_cs = din("rope_cs", [SEQ, 64])
    vecs = din("vecs", [1, 128 + 512 + 128 + 16 + 1024])
    w_out = din("w_out", [D, D])
    w_up = din("w_up", [D, 2 * DFF])
    fcw = din("fcw", [128, NFC * 4])
    w_down = din("w_down", [DFF, D])
    cmat = din("cmat", [128, 8 * 128 + 4])
    if stop_after is None:
        y = nc.dram_tensor("y", [NT_OWN * 128, D], F32, kind="ExternalOutput").ap()
        x1s = nc.dram_tensor("x1s", [NT_OWN * 128, D], F32, kind="Internal").ap()

    with ExitStack() as st:
        fw = FW(nc, st)

        def chk(tag):
            if os.environ.get("DBGSTOP") == tag:
                fw.finish()
                raise _Stop()

        def sb(stack, name, shape, dt):
            return stack.enter_context(nc.sbuf_tensor(name, list(shape), dt))

        psb = [st.enter_context(nc.psum_tensor("psb%d" % i, [128, 512], F32)) for i in range(8)]
        PSB = [Buf("psb%d" % i, excl=True) for i in range(8)]

        def psbf(i):
            return psb[i][:, :].bitcast(BF16)
        pst_i = [0]

        def next_pst():
            pst_i[0] += 1
            i = 6 + pst_i[0] % 2
            return psbf(i), PSB[i]

        cm = sb(st, "cm", [128, 8 * 128 + 4], F32); CM = Buf("cm")
        fw.ld(cm[:], cmat[:, :], [CM])
        ident_f = cm[:, 0:128]
        mcum = [cm[:, 128:258], cm[:, 258:388]]
        blk = cm[:, 388:516]
        negm = [cm[:, 516:644], cm[:, 772:900]]
        posm = [cm[:, 644:772], cm[:, 900:1028]]
        ident_b = sb(st, "ident_b", [128, 128], BF16); IDB = Buf("idb")
        fw.cp("dve", ident_b[:], ident_f, [CM], [IDB])
        maskb = sb(st, "maskb", [128, 4, 128], BF16); MASKB = Buf("maskb")
        fw.cp("dve", maskb[:].rearrange("p a b -> p (a b)"), cm[:, 516:1028], [CM], [MASKB])
        ones_b = sb(st, "ones_b", [128, 128], BF16); ONB = Buf("onb")
        fw.op("pool", lambda e: e.memset(ones_b[:], 1.0), [], [ONB])
        ones_f = sb(st, "ones_f", [128, 128], F32); ONF = Buf("onf")
        fw.op("pool", lambda e: e.memset(ones_f[:], 1.0), [], [ONF])
        cst = sb(st, "cst", [128, 8], F32); CST = Buf("cst")
        fw.op("pool", lambda e: e.memset(cst[:, 0:1], EPS), [], [CST])
        fw.op("pool", lambda e: e.memset(cst[:, 1:2], 1.0), [], [CST])
        fw.op("pool", lambda e: e.memset(cst[:, 2:3], EPS * 128.0), [], [CST])
        eps_ap = cst[:, 0:1]

        vb_ = sb(st, "vecs_bc", [128, 1808], F32); VEC = Buf("vecs")
        fw.ld(vb_[:], vecs[0:1, :].broadcast_to([128, 1808]), [VEC])
        qg_bc = vb_[:, 0:64]
        kg_bc = vb_[:, 64:128]
        aog_bc = vb_[:, 128:640]
        gng_bc = vb_[:, 640:768]
        alogdt_bc = vb_[:, 768:784]
        fng_bc = vb_[:, 784:1808]
        gq8 = sb(st, "gq8", [128, 512], F32); GQ8 = Buf("gq8")
        for hh in range(8):
            fw.ts("dve", gq8[:, hh * 64:(hh + 1) * 64], qg_bc, 0.125, None, ALU.mult, None, [VEC], [GQ8])
        cw = sb(st, "convw_sb", [128, 36], F32); CW = Buf("cw")
        fw.ld(cw[:], convw[:, :], [CW])
        fcw_sb = sb(st, "fcw_sb", [128, NFC * 4], F32); FCW = Buf("fcw")
        fw.ld(fcw_sb[:], fcw[:, :], [FCW])
        ngc = sb(st, "ngc", [128, 16], F32); NGC = Buf("ngc")
        fw.ld(ngc[:], ng_col[:, :], [NGC])
        G12 = sb(st, "G12", [128, 2, 1024], F32); G12B = Buf("G12")
        modv = sb(st, "modv", [128, 48], F32); MODV = Buf("modv")

        def dump(name, ap, shape, buf, dt=F32):
            if not dbg:
                return
            t = nc.dram_tensor("dbg_" + name, list(shape), dt, kind="ExternalOutput").ap()
            dbg_outs[name] = t
            fw.stor(t, ap, [buf])

        if stop_after == -1:
            dump("gq8", gq8[:], [128, 512], GQ8)
            fw.finish()
            return nc, dbg_outs
        with ExitStack() as p0:
            sc = sb(p0, "sc", [128, 16], F32); SC = Buf("sc")
            fw.ld(sc[:], ccol[:, :], [SC])
            fw.act(sc[:], sc[:], AF.Silu, [SC], [SC])
            sc2 = sb(p0, "sc2", [128, 8, 2], F32); SC2 = Buf("sc2")
            fw.cp("dve", sc2[:, :, 0], sc[:, 0:8], [SC], [SC2])
            fw.cp("dve", sc2[:, :, 1], sc[:, 8:16], [SC], [SC2])
            scbc = sb(p0, "scbc", [128, 8, 128], F32); SCBC = Buf("scbc")
            for k in range(8):
                fw.ts("dve", scbc[:, k, :], ones_f[:], sc[:, k:k + 1], None, ALU.mult, None, [SC, ONF], [SCBC])
            bmc = sb(p0, "bmc", [128, 48], F32); BMC = Buf("bmc")
            fw.ld(bmc[:], bmod_col[:, :], [BMC])
            bg = sb(p0, "bgate", [128, 2, 1024], F32); BG = Buf("bgate")
            fw.ld(bg[:, 0, :], bmod_row[0:1, 2048:3072].broadcast_to([128, 1024]), [BG])
            fw.ld(bg[:, 1, :], bmod_row[0:1, 5120:6144].broadcast_to([128, 1024]), [BG])
            mcol = sb(p0, "mcol", [128, 48, 2], F32); MCOL = Buf("mcol")
            wm = [sb(p0, "wm%d" % i, [128, 8, 512], F32) for i in range(2)]
            WM = [Buf("wm0"), Buf("wm1")]
            wmv = w_mod.rearrange("(k p) n -> p k n", p=128)
            for jb in range(12):
                s = jb % 2
                fw.ld(wm[s][:], wmv[:, :, jb * 512:(jb + 1) * 512], [WM[s]])
                if jb in (4, 5, 10, 11):
                    gi = 0 if jb < 6 else 1
                    half = jb % 2 if jb < 6 else (jb - 10)
                    pb, PB = psb[0], PSB[0]
                    for k in range(8):
                        fw.mm(pb[:, :], scbc[:, k, :], wm[s][:, k, :], k == 0, k == 7, [SCBC, WM[s]], [PB])
                    fw.tt("dve", G12[:, gi, half * 512:(half + 1) * 512], pb[:, :], bg[:, gi, half * 512:(half + 1) * 512],
                          ALU.add, [PB, BG], [G12B])
                else:
                    pb, PB = psb[1], PSB[1]
                    for cc in range(4):
                        for k in range(8):
                            fw.mm(pb[:, cc * 2:cc * 2 + 2], wm[s][:, k, cc * 128:(cc + 1) * 128], sc2[:, k, :],
                                  k == 0, k == 7, [SC2, WM[s]], [PB])
                    for col in range(2):
                        fw.tt("dve", mcol[:, jb * 4:jb * 4 + 4, col], pb[:, col:8:2], bmc[:, jb * 4:jb * 4 + 4],
                              ALU.add, [PB, BMC], [MCOL])
            tmp8 = sb(p0, "tmp8", [128, 8], F32); T8 = Buf("t8")
            fw.ts("dve", tmp8[:], mcol[:, 8:16, 0], 1.0, None, ALU.add, None, [MCOL], [T8])
            fw.tt("dve", modv[:, 0:8], tmp8[:], ngc[:, 0:8], ALU.mult, [T8, NGC], [MODV])
            fw.cp("dve", modv[:, 8:16], mcol[:, 0:8, 0], [MCOL], [MODV])
            fw.ts("dve", tmp8[:], mcol[:, 8:16, 1], 1.0, None, ALU.add, None, [MCOL], [T8])
            fw.tt("dve", modv[:, 16:24], tmp8[:], ngc[:, 0:8], ALU.mult, [T8, NGC], [MODV])
            fw.cp("dve", modv[:, 24:32], mcol[:, 0:8, 1], [MCOL], [MODV])
            fw.ts("dve", tmp8[:], mcol[:, 32:40, 0], 1.0, None, ALU.add, None, [MCOL], [T8])
            fw.tt("dve", modv[:, 32:40], tmp8[:], ngc[:, 8:16], ALU.mult, [T8, NGC], [MODV])
            fw.cp("dve", modv[:, 40:48], mcol[:, 24:32, 0], [MCOL], [MODV])
            dump("modv", modv[:], [128, 48], MODV)
            dump("G12", G12[:], [128, 2, 1024], G12B)
            fw.barrier()
        if stop_after == 0:
            fw.finish()
            return nc, dbg_outs

        def tile_src(t):
            if t < NT_LAT:
                return xs[t * 128:(t + 1) * 128, :]
            return cs[(t - NT_LAT) * 128:(t - NT_LAT + 1) * 128, :]

        class NormCtx:
            pass

        def make_norm(stack, tag):
            n = NormCtx()
            n.xt = [sb(stack, "xt%s%d" % (tag, i), [128, D], F32) for i in range(2)]
            n.XT = [Buf("xt%d" % i) for i in range(2)]
            n.junk = sb(stack, "junk" + tag, [128, D], BF16); n.JUNK = Buf("junk")
            n.stt = sb(stack, "nst" + tag, [128, 4], F32); n.ST = Buf("nst")
            n.xn = [sb(stack, "xn%s%d" % (tag, i), [128, D], BF16) for i in range(4)]
            n.XN = [Buf("xn%d" % i) for i in range(4)]
            n.i = 0
            return n

        def norm_rows(n, src_ap, src_buf, slot):
            fw.act(n.junk[:], src_ap, AF.Square, [src_buf], [n.JUNK, n.ST], accum=n.stt[:, 0:1])
            fw.act(n.stt[:, 1:2], n.stt[:, 0:1], AF.Ln, [n.ST, CST], [n.ST], bias=eps_ap, scale=1.0 / D)
            fw.act(n.stt[:, 2:3], n.stt[:, 1:2], AF.Exp, [n.ST], [n.ST], scale=-0.5)
            fw.ts("dve", n.xn[slot][:], src_ap, n.stt[:, 2:3], None, ALU.mult, None, [src_buf, n.ST], [n.XN[slot]])

        def transpose_mod(n, ntile, hT_ap_fn, HT, acol0, ncols_last=128):
            for kc in range(8):
                pt, PT = next_pst()
                for i in range(ntile):
                    fw.tr(pt[:, i * 128:(i + 1) * 128], n.xn[i][:, kc * 128:(kc + 1) * 128], ident_b[:],
                          [n.XN[i], IDB], [PT])
                T = (ntile - 1) * 128 + ncols_last
                chk("tm_tr")
                fw.ts("dve", hT_ap_fn(kc, T), pt[:, 0:T], modv[:, acol0 + kc:acol0 + kc + 1],
                      modv[:, acol0 + 8 + kc:acol0 + 9 + kc], ALU.mult, ALU.add, [PT, MODV], [HT])
                chk("tm_ev%d" % kc)

        def load_norm_group(n, tiles):
            for i, t in enumerate(tiles):
                s = n.i % 2
                n.i += 1
                fw.ld(n.xt[s][:], tile_src(t), [n.XT[s]])
                norm_rows(n, n.xt[s][:], n.XT[s], i)

        def load_weight_bf16(stack, name, src_view, ncols, dst, DST, piece=256):
            K = src_view.shape[1]
            with ExitStack() as ws:
                stg = [sb(ws, "%s_stg%d" % (name, i), [128, K, piece], F32) for i in range(2)]
                STG = [Buf("stg0"), Buf("stg1")]
                i = 0
                for c0 in range(0, ncols, piece):
                    c1 = min(ncols, c0 + piece)
                    s = i % 2
                    fw.ld(stg[s][:, :, 0:c1 - c0], src_view[:, :, c0:c1], [STG[s]])
                    eng = ("dve", "act")[i % 2]
                    fw.cp(eng, dst[:, :, c0:c1], stg[s][:, :, 0:c1 - c0], [STG[s]], [DST])
                    i += 1
                fw.barrier()

        groups_ext = [[0, 1, 2, 3], [4, 5, 6, 7], [8, 9, 10, 11], [12, 13, 14, 15], [16]]
        groups_oth = [[17, 18, 19], [20, 21, 22, 23], [24, 25, 26, 27], [28, 29, 30, 31], [32, 33]]

        mixer = ExitStack()
        st.enter_context(mixer)
        Oacc = sb(mixer, "Oacc", [128, NT_EXT, 512], BF16); OACC = [Buf("oacc%d" % t) for t in range(NT_EXT)]

        gdn = ExitStack()
        st.enter_context(gdn)
        rawK = sb(gdn, "rawK", [128, 4, TOK_ALL], BF16)
        rawV = sb(gdn, "rawV", [128, 4, TOK_ALL], BF16)
        rawQ = sb(gdn, "rawQ", [128, 4, TOK_EXT], BF16)
        RAW = {}
        ab = sb(gdn, "ab", [128, NT_ALL, 16], F32); AB = Buf("ab")

        def rawbuf(kind, h, grp):
            key = (kind, h, grp)
            if key not in RAW:
                RAW[key] = Buf("raw%s%d_%d" % (kind, h, grp))
            return RAW[key]

        def tok_group(t):
            return t // 4

        with ExitStack() as p1:
            wg = sb(p1, "wg", [128, 8, 1552], BF16); WG = Buf("wg")
            load_weight_bf16(p1, "wg", w_in_g.rearrange("(k p) n -> p k n", p=128), 1552, wg, WG)
            chk("wload")
            nctx = make_norm(p1, "a")
            hT = [sb(p1, "hT%d" % i, [128, 8, 512], BF16) for i in range(2)]
            HT = [Buf("hT0"), Buf("hT1")]
            SEG = 1024
            NSL = 2
            acc = [sb(p1, "cacc%d" % i, [128, SEG], F32) for i in range(NSL)]
            ACC = [Buf("cacc%d" % i) for i in range(NSL)]
            sq = [sb(p1, "csq%d" % i, [128, SEG], BF16) for i in range(NSL)]
            SQ = [Buf("csq%d" % i) for i in range(NSL)]
            rin = [sb(p1, "crin%d" % i, [128, 512], F32) for i in range(2)]
            RIN = [Buf("crin%d" % i) for i in range(2)]
            rci = [0]
            pending = []
            csi = [0]
            cprev = {}

            def conv_seg(ch, a, b, s0, s1):
                kind = "QKV"[ch // 4]
                h = ch % 4
                arr = (rawQ, rawK, rawV)[ch // 4]
                w0, w1, w2 = (cw[:, ch * 3 + j:ch * 3 + j + 1] for j in range(3))
                n = s1 - s0
                sl = csi[0] % NSL
                csi[0] += 1
                bufs = sorted({tok_group(t) for t in range(s0 // 128, (s1 + 127) // 128)} |
                              ({tok_group(s1 // 128)} if s1 < b else set()))
                RB = [rawbuf(kind, h, g) for g in bufs]
                fw.act(acc[sl][:, 0:n], arr[:, h, s0:s1], AF.Copy, RB + [CW], [ACC[sl]], scale=w1)
                if s0 > a:
                    pl = cprev[(ch, a)]
                    fw.stt("dve", acc[sl][:, 0:1], pl[0], w0, acc[sl][:, 0:1], ALU.mult, ALU.add,
                           [pl[1], CW, ACC[sl]], [ACC[sl]])
                fw.stt("dve", acc[sl][:, 1:n], arr[:, h, s0:s1 - 1], w0, acc[sl][:, 1:n], ALU.mult, ALU.add,
                       RB + [CW, ACC[sl]], [ACC[sl]])
                nr = n if s1 < b else n - 1
                fw.stt("dve", acc[sl][:, 0:nr], arr[:, h, s0 + 1:s0 + 1 + nr], w2, acc[sl][:, 0:nr], ALU.mult, ALU.add,
                       RB + [CW, ACC[sl]], [ACC[sl]])
                if s1 < b:
                    keep = sb(p1, "keep%d_%d" % (ch, s0), [128, 1], BF16)
                    KB = Buf("keep")
                    fw.cp("pool", keep[:], arr[:, h, s1 - 1:s1], RB, [KB])
                    cprev[(ch, a)] = (keep[:], KB)
                WB = [rawbuf(kind, h, g) for g in sorted({tok_group(t) for t in range(s0 // 128, (s1 + 127) // 128)})]

                def tail_a():
                    if kind == "V":
                        fw.act(arr[:, h, s0:s1], acc[sl][:, 0:n], AF.Silu, [ACC[sl]], WB)
                        return
                    fw.act(acc[sl][:, 0:n], acc[sl][:, 0:n], AF.Silu, [ACC[sl]], [ACC[sl]])
                    fw.tt("pool", sq[sl][:, 0:n], acc[sl][:, 0:n], acc[sl][:, 0:n], ALU.mult, [ACC[sl]], [SQ[sl]])

                def tail_b():
                    if kind == "V":
                        return
                    for c0 in range(0, n, 512):
                        c1 = min(n, c0 + 512)
                        rci[0] += 1
                        pb, PB = psb[4 + rci[0] % 2], PSB[4 + rci[0] % 2]
                        r_, RN = rin[rci[0] % 2], RIN[rci[0] % 2]
                        fw.mm(pb[:, 0:c1 - c0], ones_b[:], sq[sl][:, c0:c1], True, True, [ONB, SQ[sl]], [PB])
                        fw.act(r_[:, 0:c1 - c0], pb[:, 0:c1 - c0], AF.Ln, [PB, CST], [RN], bias=eps_ap, scale=1.0)
                        fw.act(r_[:, 0:c1 - c0], r_[:, 0:c1 - c0], AF.Exp, [RN], [RN], scale=-0.5)
                        fw.tt("dve", arr[:, h, s0 + c0:s0 + c1], acc[sl][:, c0:c1], r_[:, 0:c1 - c0], ALU.mult,
                              [ACC[sl], RN], WB)
                pending.append((tail_a, tail_b))
                if len(pending) >= 2:
                    flush_tails()

            def flush_tails():
                for ta, _ in pending:
                    ta()
                for _, tb in pending:
                    tb()
                del pending[:]

            csegs = []
            for ch in range(12):
                rngs = [(0, TOK_EXT)] if ch < 4 else [(0, SEQ), (SEQ, TOK_ALL)]
                for (a_, b_) in rngs:
                    for s0 in range(a_, b_, SEG):
                        s1 = min(b_, s0 + SEG)
                        need = None if a_ == SEQ else min(b_, s1 + 1)
                        csegs.append((need, ch, a_, b_, s0, s1))
            cdone = set()
            cav = [0, False]

            def emit_ready_convs(avail_lat, ctx_done, limit=None):
                k = 0
                for i_, (need, ch, a_, b_, s0, s1) in enumerate(csegs):
                    if i_ in cdone:
                        continue
                    ok = ctx_done if need is None else need <= avail_lat
                    if ok:
                        conv_seg(ch, a_, b_, s0, s1)
                        cdone.add(i_)
                        k += 1
                        if limit is not None and k >= limit:
                            return

            allg = groups_ext + groups_oth

            def prep1(gi):
                grp = allg[gi]
                load_norm_group(nctx, grp)
                transpose_mod(nctx, len(grp), lambda kc, T, s=gi % 2: hT[s][:, kc, 0:T], HT[gi % 2],
                              16 if grp[0] >= NT_LAT else 0)
            prep1(0)
            for gi, grp in enumerate(allg):
                is_ext = grp[0] < NT_EXT
                is_ctx = grp[0] >= NT_LAT
                s = gi % 2
                T = len(grp) * 128
                tok0 = grp[0] * 128
                chunks = list(range(12)) if is_ext else list(range(4, 12))
                for ci, ch in enumerate(chunks):
                    if ci == 2 and gi + 1 < len(allg):
                        prep1(gi + 1)
                    if gi > 0:
                        emit_ready_convs(cav[0], cav[1], limit=2)
                    pb, PB = psb[ci % 4], PSB[ci % 4]
                    for kc in range(8):
                        fw.mm(pb[:, 0:T], wg[:, kc, ch * 128:(ch + 1) * 128], hT[s][:, kc, 0:T], kc == 0, kc == 7,
                              [WG, HT[s]], [PB])
                    kind = "QKV"[ch // 4]
                    dst = (rawQ, rawK, rawV)[ch // 4]
                    fw.cpa(dst[:, ch % 4, tok0:tok0 + T], pb[:, 0:T], [PB], [rawbuf(kind, ch % 4, tok_group(grp[0]))])
                for i, t in enumerate(grp):
                    pb, PB = psb[4 + (i % 2)], PSB[4 + (i % 2)]
                    for kc in range(8):
                        fw.mm(pb[:, 0:16], hT[s][:, kc, i * 128:(i + 1) * 128], wg[:, kc, 1536:1552], kc == 0, kc == 7,
                              [WG, HT[s]], [PB])
                    fw.cp("act", ab[:, t, :], pb[:, 0:16], [PB], [AB])
                chk("grp0")
                cav[0] = (grp[-1] + 1) * 128 if grp[0] < NT_LAT else SEQ
                cav[1] = grp[0] >= NT_LAT
            emit_ready_convs(SEQ, True)
            flush_tails()
            assert len(cdone) == len(csegs)
            fw.barrier()
        if stop_after == 1:
            dump("rawK", rawK[:], [128, 4, TOK_ALL], rawbuf("K", 0, 0), BF16)
            dump("ab", ab[:], [128, NT_ALL, 16], AB)
            fw.finish()
            return nc, dbg_outs

        if stop_after == 2:
            dump("KT", rawK[:], [128, 4, TOK_ALL], rawbuf("K", 0, 0), BF16)
            dump("QT", rawQ[:], [128, 4, TOK_EXT], rawbuf("Q", 0, 0), BF16)
            dump("VT", rawV[:], [128, 4, TOK_ALL], rawbuf("V", 0, 0), BF16)
            dump("ab", ab[:], [128, NT_ALL, 16], AB)
            fw.finish()
            return nc, dbg_outs

        def a3(name, n, stack=gdn):
            return sb(stack, name, [128, NT_ALL, n], F32)
        gg = a3("gg", 8); GG = Buf("gg")
        beta = a3("beta", 8); BETA = Buf("beta")
        egc = a3("egc", 8); EGC = Buf("egc")
        ekd = a3("ekd", 8); EKD = Buf("ekd")
        bgt = a3("bgt", 8); BGT = Buf("bgt")
        gcpl = a3("gcpl", 8); GCPL = Buf("gcpl")
        ngcn = a3("ngcn", 8); NGCN = Buf("ngcn")
        dl = a3("dl", 16); DL = Buf("dl")
        with ExitStack() as pg:
            t1 = a3("t1", 8, pg); T1 = Buf("t1")
            lnb = a3("lnb", 8, pg); LNB = Buf("lnb")
            gcs = a3("gcs", 32, pg); GCS = Buf("gcs")
            ealog = sb(pg, "ealog", [128, 8], F32); EAL = Buf("ealog")
            gI = [sb(pg, "gI%d" % i, [128, 8, 2], F32) for i in range(2)]
            GI = [Buf("gI0"), Buf("gI1")]
            fw.tt("dve", t1[:], ab[:, :, 0:8], alogdt_bc[:, 8:16].unsqueeze(1).to_broadcast([128, NT_ALL, 8]), ALU.add,
                  [AB, VEC], [T1])
            fw.act(t1[:], t1[:], AF.Exp, [T1], [T1])
            fw.act(t1[:], t1[:], AF.Ln, [T1, CST], [T1], bias=cst[:, 1:2], scale=1.0)
            fw.act(ealog[:], alogdt_bc[:, 0:8], AF.Exp, [VEC], [EAL])
            fw.stt("dve", gg[:], t1[:], -1.0, ealog[:].unsqueeze(1).to_broadcast([128, NT_ALL, 8]), ALU.mult, ALU.mult,
                   [T1, EAL], [GG])
            fw.act(beta[:], ab[:, :, 8:16], AF.Sigmoid, [AB], [BETA])
            fw.act(lnb[:], beta[:], AF.Ln, [BETA], [LNB])
            for t in range(NT_ALL):
                k = t % 2
                pb, PB = psb[k], PSB[k]
                for c in range(2):
                    fw.ts("dve", gI[k][:, :, c], gg[:, t, :], mcum[0][:, 128 + c:129 + c], None, ALU.mult, None,
                          [GG, CM], [GI[k]])
                fw.mm(pb[:, 0:4], mcum[0][:, 0:128], gg[:, t, 0:4], True, False, [CM, GG], [PB])
                fw.mm(pb[:, 4:8], mcum[1][:, 0:128], gg[:, t, 4:8], False, False, [CM, GG], [PB])
                fw.mm(pb[:, 8:16], blk, gg[:, t, 0:8], False, False, [CM, GG], [PB])
                fw.mm(pb[:, 16:32], ones_f[:], gI[k][:].rearrange("p a b -> p (a b)"), False, True, [ONF, GI[k]], [PB])
                fw.cp("act", gcs[:, t, :], pb[:, 0:32], [PB], [GCS])
            fw.act(egc[:], gcs[:, :, 0:8], AF.Exp, [GCS], [EGC])
            fw.tt("dve", t1[:], gcs[:, :, 8:16], gcs[:, :, 0:8], ALU.subtract, [GCS], [T1])
            fw.act(ekd[:], t1[:], AF.Exp, [T1], [EKD])
            fw.tt("dve", bgt[:], beta[:], egc[:], ALU.mult, [BETA, EGC], [BGT])
            fw.tt("dve", gcpl[:], gcs[:, :, 0:8], lnb[:], ALU.add, [GCS, LNB], [GCPL])
            fw.ts("dve", ngcn[:], gcs[:, :, 0:8], -1.0, None, ALU.mult, None, [GCS], [NGCN])
            fw.act(dl[:], gcs[:, :, 16:32], AF.Exp, [GCS], [DL])
            fw.barrier()
        if stop_after == 3:
            dump("gg", gg[:], [128, NT_ALL, 8], GG)
            dump("beta", beta[:], [128, NT_ALL, 8], BETA)
            fw.finish()
            return nc, dbg_outs

        fw.op("pool", lambda e: e.memset(Oacc[:], 0.0), [], OACC)
        with ExitStack() as ps_:
            maskb4 = sb(ps_, "maskb4", [128, 4, 512], BF16); MASKB4 = Buf("maskb4")
            for ty in range(4):
                for h in range(4):
                    fw.cp("pool", maskb4[:, ty, h * 128:(h + 1) * 128], maskb[:, ty, :], [MASKB], [MASKB4])
            identb4 = sb(ps_, "identb4", [128, 4, 128], BF16); IDB4 = Buf("idb4")
            for h in range(4):
                fw.cp("dve", identb4[:, h, :], ident_b[:], [IDB], [IDB4])

            class QS:
                pass
            DBL = ("kbg", "kdec", "vb", "AT", "wT", "u")
            sets = []
            for d in range(2):
                q = QS()
                for nm, dt_ in (("kbg", BF16), ("kdec", BF16), ("vb", BF16), ("gM", F32), ("Dstr", BF16), ("Dinc", BF16),
                                ("B0", BF16), ("B1", BF16), ("AT", BF16),
                                ("u", F32), ("wT", BF16), ("vnew", BF16), ("tmp", F32), ("S", F32), ("Sbf", BF16)):
                    if nm in DBL:
                        setattr(q, nm + "_2", [sb(ps_, "q%d_%s_%d" % (d, nm, i), [128, 4, 128], dt_) for i in range(2)])
                        setattr(q, nm.upper() + "_B2", [Buf("q%d_%s_%d" % (d, nm, i)) for i in range(2)])
                    else:
                        setattr(q, nm, sb(ps_, "q%d_%s" % (d, nm), [128, 4, 128], dt_))
                        setattr(q, nm.upper() + "_", Buf("q%d_%s" % (d, nm)))
                q.AP0 = sb(ps_, "q%d_AP0" % d, [128, 4, 256], BF16)
                q.AP1 = sb(ps_, "q%d_AP1" % d, [128, 4, 256], BF16)
                q.APA0_, q.APA1_, q.APP0_, q.APP1_ = Buf("apa0"), Buf("apa1"), Buf("app0"), Buf("app1")
                fw.op("pool", lambda e, q=q: e.memset(q.S[:], 0.0), [], [q.S_])
                fw.op("pool", lambda e, q=q: e.memset(q.Sbf[:], 0.0), [], [q.SBF_])
                q.banks = [0, 1, 2, 3] if d == 0 else [4, 5, 6, 7]
                q.bi = 0
                sets.append(q)

            class QView:
                def __init__(self, base, par):
                    object.__setattr__(self, "_b", base)
                    object.__setattr__(self, "_p", par)

                def __getattr__(self, name):
                    b_, p_ = self._b, self._p
                    if name in DBL:
                        return getattr(b_, name + "_2")[p_]
                    if name.endswith("_") and name[:-1].lower() in [x.lower() for x in DBL] and name[:-1].isupper():
                        for x in DBL:
                            if x.upper() == name[:-1]:
                                return getattr(b_, x.upper() + "_B2")[p_]
                    return getattr(b_, name)

                def __setattr__(self, name, val):
                    setattr(self._b, name, val)

            def qview(d, par):
                return QView(sets[d], par)

            def nb(q):
                q.bi += 1
                i = q.banks[q.bi % 4]
                return i

            def v4(ap512):
                return ap512.rearrange("p (a b) -> p a b", a=4)

            def bc4(ap_p4):
                return ap_p4.unsqueeze(2).to_broadcast([ap_p4.shape[0], 4, 128])

            def quad_pre(t, d, with_out, par):
                q = qview(d, par)
                c0 = d * 4
                tsl = slice(t * 128, (t + 1) * 128)
                grp = tok_group(t)
                KB = [rawbuf("K", h, grp) for h in range(4)]
                VB = [rawbuf("V", h, grp) for h in range(4)]
                QB = [rawbuf("Q", h, grp) for h in range(4)] if with_out else []
                i = nb(q)
                bv = psbf(i)
                for h in range(4):
                    fw.tr(bv[:, h * 128:(h + 1) * 128], rawK[:, h, tsl], ident_b[:], [KB[h], IDB], [PSB[i]])
                fw.tt("dve", q.kbg[:], v4(bv[:, 0:512]), bc4(bgt[:, t, c0:c0 + 4]), ALU.mult, [PSB[i], BGT], [q.KBG_])
                fw.tt("dve", q.kdec[:], v4(bv[:, 0:512]), bc4(ekd[:, t, c0:c0 + 4]), ALU.mult, [PSB[i], EKD], [q.KDEC_])
                yield
                i = nb(q)
                bv = psbf(i)
                for h in range(4):
                    fw.tr(bv[:, h * 128:(h + 1) * 128], rawV[:, h, tsl], ident_b[:], [VB[h], IDB], [PSB[i]])
                fw.tt("dve", q.vb[:], v4(bv[:, 0:512]), bc4(beta[:, t, c0:c0 + 4]), ALU.mult, [PSB[i], BETA], [q.VB_])
                yield
                fw.tt("dve", q.gM[:], mcum[d][:, 0:128].unsqueeze(1).to_broadcast([128, 4, 128]), bc4(gg[:, t, c0:c0 + 4]),
                      ALU.mult, [CM, GG], [q.GM_])
                gMf = q.gM[:].rearrange("p a b -> p (a b)")
                i = nb(q)
                fw.mm(psb[i][:, :], ones_f[:], gMf, True, False, [ONF, q.GM_], [PSB[i]])
                fw.mm(psb[i][:, :], ident_b[:], maskb4[:, 2 * d + 1, :], False, True, [IDB, MASKB4], [PSB[i]])
                for h in range(4):
                    fw.act(q.Dstr[:, h, :], psb[i][:, h * 128:(h + 1) * 128], AF.Exp, [PSB[i], GCPL], [q.DSTR_],
                           bias=gcpl[:, t, c0 + h:c0 + h + 1], scale=-1.0)
                yield
                if with_out:
                    i = nb(q)
                    fw.mm(psb[i][:, :], ones_f[:], gMf, True, False, [ONF, q.GM_], [PSB[i]])
                    fw.mm(psb[i][:, :], ident_b[:], maskb4[:, 2 * d, :], False, True, [IDB, MASKB4], [PSB[i]])
                    for h in range(4):
                        fw.act(q.Dinc[:, h, :], psb[i][:, h * 128:(h + 1) * 128], AF.Exp, [PSB[i], NGCN], [q.DINC_],
                               bias=ngcn[:, t, c0 + h:c0 + h + 1], scale=1.0)
                    yield
                i = nb(q)
                for h in range(4):
                    fw.mm(psb[i][:, h * 128:(h + 1) * 128], rawK[:, h, tsl], rawK[:, h, tsl], h == 0, h == 3, [KB[h]], [PSB[i]])
                fw.stt("dve", q.B0[:], v4(psb[i][:, :]), -1.0, q.Dstr[:], ALU.mult, ALU.mult, [PSB[i], q.DSTR_], [q.B0_])
                yield
                if with_out:
                    i = nb(q)
                    for h in range(4):
                        fw.mm(psb[i][:, h * 128:(h + 1) * 128], rawK[:, h, tsl], rawQ[:, h, tsl], h == 0, h == 3,
                              [KB[h], QB[h]], [PSB[i]])
                    fw.tt("dve", q.AT[:], v4(psb[i][:, :]), q.Dinc[:], ALU.mult, [PSB[i], q.DINC_], [q.AT_])
                    yield
                AP = [q.AP0, q.AP1]
                APA = [q.APA0_, q.APA1_]
                APP = [q.APP0_, q.APP1_]
                Bb = [(q.B0, q.B0_), (q.B1, q.B1_)]
                i = nb(q)
                bv = psbf(i)
                for h in range(4):
                    fw.tr(bv[:, h * 128:(h + 1) * 128], q.B0[:, h, :], ident_b[:], [q.B0_, IDB], [PSB[i]])
                fw.cp("act", AP[0][:, :, 0:128], v4(bv[:, 0:512]), [PSB[i]], [APA[0]])
                fw.tt("dve", AP[1][:, :, 128:256], v4(bv[:, 0:512]), identb4[:], ALU.add, [PSB[i], IDB4], [APP[1]])
                yield
                for j in range(1, 6):
                    cur, nxt = (j - 1) % 2, j % 2
                    Bc, BcB = Bb[(j - 1) % 2]
                    Bn, BnB = Bb[j % 2]
                    if j == 1:
                        i = nb(q)
                        for h in range(4):
                            fw.mm(psb[i][:, h * 128:(h + 1) * 128], Bc[:, h, :], AP[cur][:, h, 0:128], h == 0, h == 3,
                                  [BcB, APA[cur]], [PSB[i]])
                        i2 = nb(q)
                        for h in range(4):
                            fw.mm(psb[i2][:, h * 128:(h + 1) * 128], AP[cur][:, h, 0:128], Bc[:, h, :], h == 0, h == 3,
                                  [BcB, APA[cur]], [PSB[i2]])
                        fw.cp("act", AP[nxt][:, :, 0:128], v4(psb[i][:, :]), [PSB[i]], [APA[nxt]])
                        fw.cp("dve", Bn[:], v4(psb[i2][:, :]), [PSB[i2]], [BnB])
                        yield
                    elif j < 5:
                        ia, ib = nb(q), nb(q)
                        for h in range(4):
                            bk = ia if h < 2 else ib
                            hh = h % 2
                            fw.mm(psb[bk][:, hh * 256:(hh + 1) * 256], Bc[:, h, :], AP[cur][:, h, :], hh == 0, h == 3,
                                  [BcB, APA[cur], APP[cur]], [PSB[bk]])
                        i2 = nb(q)
                        for h in range(4):
                            fw.mm(psb[i2][:, h * 128:(h + 1) * 128], AP[cur][:, h, 0:128], Bc[:, h, :], h == 0, h == 3,
                                  [BcB, APA[cur]], [PSB[i2]])
                        for bk, h0 in ((ia, 0), (ib, 2)):
                            pv_ = psb[bk][:, :].rearrange("p (a b) -> p a b", a=2)
                            fw.cp("act", AP[nxt][:, h0:h0 + 2, 0:128], pv_[:, :, 0:128], [PSB[bk]], [APA[nxt]])
                            fw.tt("dve", AP[nxt][:, h0:h0 + 2, 128:256], AP[cur][:, h0:h0 + 2, 128:256], pv_[:, :, 128:256],
                                  ALU.add, [PSB[bk], APP[cur]], [APP[nxt]])
                        fw.cp("dve", Bn[:], v4(psb[i2][:, :]), [PSB[i2]], [BnB])
                        yield
                    else:
                        i = nb(q)
                        for h in range(4):
                            fw.mm(psb[i][:, h * 128:(h + 1) * 128], Bc[:, h, :], AP[cur][:, h, 128:256], h == 0, h == 3,
                                  [BcB, APP[cur]], [PSB[i]])
                        i2 = nb(q)
                        for h in range(4):
                            fw.mm(psb[i2][:, h * 128:(h + 1) * 128], AP[cur][:, h, 0:128], Bc[:, h, :], h == 0, h == 3,
                                  [BcB, APA[cur]], [PSB[i2]])
                        fw.tt("dve", AP[nxt][:, :, 128:256], AP[cur][:, :, 128:256], v4(psb[i][:, :]), ALU.add,
                              [PSB[i], APP[cur]], [APP[nxt]])
                        fw.cp("act", Bn[:], v4(psb[i2][:, :]), [PSB[i2]], [BnB])
                        yield
                B5, B5B = Bb[1]
                i = nb(q)
                for h in range(4):
                    fw.mm(psb[i][:, h * 128:(h + 1) * 128], B5[:, h, :], AP[1][:, h, 128:256], h == 0, h == 3, [B5B, APP[1]], [PSB[i]])
                fw.tt("dve", AP[0][:, :, 128:256], AP[1][:, :, 128:256], v4(psb[i][:, :]), ALU.add, [PSB[i], APP[1]], [APP[0]])
                yield
                Ptf = AP[0]
                PTF_ = APP[0]
                i = nb(q)
                for h in range(4):
                    fw.mm(psb[i][:, h * 128:(h + 1) * 128], Ptf[:, h, 128:256], q.vb[:, h, :], h == 0, h == 3, [PTF_, q.VB_], [PSB[i]])
                fw.cp("act", q.u[:], v4(psb[i][:, :]), [PSB[i]], [q.U_])
                i = nb(q)
                for h in range(4):
                    fw.mm(psb[i][:, h * 128:(h + 1) * 128], q.kbg[:, h, :], Ptf[:, h, 128:256], h == 0, h == 3, [PTF_, q.KBG_], [PSB[i]])
                fw.cp("dve", q.wT[:], v4(psb[i][:, :]), [PSB[i]], [q.WT_])
                yield

            def quad_steps(t, d, with_out, par):
                q = qview(d, par)
                c0 = d * 4
                tsl = slice(t * 128, (t + 1) * 128)
                grp = tok_group(t)
                KB = [rawbuf("K", h, grp) for h in range(4)]
                VB = [rawbuf("V", h, grp) for h in range(4)]
                QB = [rawbuf("Q", h, grp) for h in range(4)] if with_out else []
                for c in ((0, 1) if d == 0 else (1, 0)):
                    R = slice(64 * c, 64 * c + 64)
                    i = nb(q)
                    for h in range(4):
                        fw.mm(psb[i][:, h * 128:(h + 1) * 128], q.wT[:, h, :], q.Sbf[:, h, :], h == 0, h == 3, [q.WT_, q.SBF_], [PSB[i]])
                    fw.tt("dve", q.vnew[R, :, :], q.u[R, :, :], v4(psb[i][R, :]), ALU.subtract, [PSB[i], q.U_], [q.VNEW_])
                    yield
                    if with_out:
                        i = nb(q)
                        for h in range(4):
                            fw.mm(psb[i][:, h * 128:(h + 1) * 128], rawQ[:, h, tsl], q.Sbf[:, h, :], h == 0, h == 3,
                                  [QB[h], q.SBF_], [PSB[i]])
                        fw.tt("dve", q.tmp[R, :, :], v4(psb[i][R, :]), bc4(egc[R, t, c0:c0 + 4]), ALU.mult, [PSB[i], EGC], [q.TMP_])
                        i = nb(q)
                        for h in range(4):
                            fw.mm(psb[i][:, h * 128:(h + 1) * 128], q.AT[R, h, :], q.vnew[R, h, :], h == 0, h == 3,
                                  [q.AT_, q.VNEW_], [PSB[i]])
                        fw.tt("dve", q.tmp[R, :, :], q.tmp[R, :, :], v4(psb[i][R, :]), ALU.add, [PSB[i], q.TMP_], [q.TMP_])
                        fw.tt("pool", Oacc[R, t, :], Oacc[R, t, :], q.tmp[R, :, :].rearrange("p a b -> p (a b)"), ALU.add,
                              [q.TMP_, OACC[t]], [OACC[t]])
                        yield
                    i = nb(q)
                    for h in range(4):
                        fw.mm(psb[i][:, h * 128:(h + 1) * 128], q.kdec[R, h, :], q.vnew[R, h, :], h == 0, h == 3,
                              [q.KDEC_, q.VNEW_], [PSB[i]])
                    dlv = dl[:, t, :].rearrange("p (a b) -> p a b", b=2)[:, c0:c0 + 4, c]
                    fw.tt("dve", q.S[:], q.S[:], bc4(dlv), ALU.mult, [q.S_, DL], [q.S_])
                    fw.tt("dve", q.S[:], q.S[:], v4(psb[i][:, :]), ALU.add, [PSB[i], q.S_], [q.S_])
                    fw.cp("act", q.Sbf[:], q.S[:], [q.S_], [q.SBF_])
                    yield

            def chain(tiles, d):
                pre = quad_pre(tiles[0], d, tiles[0] < NT_EXT, 0)
                yield from pre
                for k, t in enumerate(tiles):
                    gens = [quad_steps(t, d, t < NT_EXT, k % 2)]
                    if k + 1 < len(tiles):
                        gens.append(quad_pre(tiles[k + 1], d, tiles[k + 1] < NT_EXT, (k + 1) % 2))
                    while gens:
                        for g_ in list(gens):
                            try:
                                next(g_)
                                yield
                            except StopIteration:
                                gens.remove(g_)

            nq = int(os.environ.get("GDN_NQ", "999"))
            chF = chain(([32, 33] + list(range(0, NT_EXT)))[:nq], 0)
            chB = chain(([33, 32] + list(range(31, -1, -1)))[:nq], 1)
            alive = [chF, chB]
            if os.environ.get("GDN_ONLY"):
                alive = [chF] if os.environ["GDN_ONLY"] == "F" else [chB]
            nst = 0
            while alive:
                for g_ in list(alive):
                    try:
                        next(g_)
                        nst += 1
                        chk("qs%d" % nst)
                    except StopIteration:
                        alive.remove(g_)
            fw.barrier()
            if stop_after == 4:
                dump("Oacc", Oacc[:], [128, NT_EXT, 512], OACC[0], BF16)
                dump("S0", sets[0].S[:], [128, 4, 128], sets[0].S_)
                dump("S1", sets[1].S[:], [128, 4, 128], sets[1].S_)
                fw.finish()
                return nc, dbg_outs
        gdn.close()

        att = ExitStack()
        st.enter_context(att)
        KTa = sb(att, "KTa", [128, TOK_ALL], BF16); KTA = [Buf("kta%d" % g) for g in range(9)]
        Va = sb(att, "Va", [128, NT_ALL, 2, 65], BF16); VA = [Buf("va%d" % t) for t in range(NT_ALL)]
        QTa = sb(att, "QTa", [128, 4, TOK_EXT], BF16); QTA = [Buf("qta%d" % t) for t in range(NT_EXT)]
        zs = sb(att, "zs", [128, NT_EXT, 512], BF16); ZS = [Buf("zs%d" % t) for t in range(NT_EXT)]
        fw.op("pool", lambda e: e.memset(Va[:], 1.0), [], VA)
        with ExitStack() as p2:
            rope_sb = sb(p2, "rope_sb", [128, NT_LAT, 64], F32); ROPE = Buf("rope")
            fw.ld(rope_sb[:], rope_cs.rearrange("(t p) c -> p t c", p=128), [ROPE])
            wa = sb(p2, "wa", [128, 8, 1280], BF16); WA = Buf("wa")
            load_weight_bf16(p2, "wa", w_in_a.rearrange("(k p) n -> p k n", p=128), 1280, wa, WA)
            nctx = make_norm(p2, "b")
            hT = [sb(p2, "hTb%d" % i, [128, 8, 512], BF16) for i in range(2)]
            HT = [Buf("hTb0"), Buf("hTb1")]
            qsq_ = [sb(p2, "qsq%d" % i, [128, 640], F32) for i in range(2)]; QSQ_ = [Buf("qsq0"), Buf("qsq1")]
            qst_ = [sb(p2, "qst%d" % i, [128, 32], F32) for i in range(2)]; QST_ = [Buf("qst0"), Buf("qst1")]
            qn_ = [sb(p2, "qn%d" % i, [128, 640], F32) for i in range(2)]; QN_ = [Buf("qn0"), Buf("qn1")]
            rt_ = [[sb(p2, "rt%d_%d" % (k, i), [128, 10, 32], F32) for i in range(4)] for k in range(2)]
            RT_ = [[Buf("rt%d_%d" % (k, i)) for i in range(4)] for k in range(2)]
            qr_ = [sb(p2, "qr%d" % i, [128, 640], BF16) for i in range(2)]; QR_ = [Buf("qr0"), Buf("qr1")]
            kg2 = sb(p2, "kg2", [128, 640], F32); KG2 = Buf("kg2")
            fw.cp("dve", kg2[:, 0:512], gq8[:], [GQ8], [KG2])
            for hh in range(2):
                fw.cp("dve", kg2[:, 512 + hh * 64:576 + hh * 64], kg_bc, [VEC], [KG2])
            allg = groups_ext + groups_oth

            def prep2(gi):
                grp = allg[gi]
                load_norm_group(nctx, grp)
                transpose_mod(nctx, len(grp), lambda kc, T, s=gi % 2: hT[s][:, kc, 0:T], HT[gi % 2],
                              16 if grp[0] >= NT_LAT else 0)
            prep2(0)
            tiles2 = [(gi, i_, t) for gi, grp in enumerate(allg) for i_, t in enumerate(grp)]

            def mm2(k):
                gi, i_, t = tiles2[k]
                s = gi % 2
                bq, bkv, bz = (0, 1, 2) if k % 2 == 0 else (3, 4, 5)
                lt = hT[s][:, :, i_ * 128:(i_ + 1) * 128]
                is_ext = t < NT_EXT
                if is_ext:
                    for kc in range(8):
                        fw.mm(psb[bq][:, :], lt[:, kc, :], wa[:, kc, 0:512], kc == 0, kc == 7, [WA, HT[s]], [PSB[bq]])
                    for kc in range(8):
                        fw.mm(psb[bz][:, :], lt[:, kc, :], wa[:, kc, 768:1280], kc == 0, kc == 7, [WA, HT[s]], [PSB[bz]])
                for kc in range(8):
                    fw.mm(psb[bkv][:, 0:256], lt[:, kc, :], wa[:, kc, 512:768], kc == 0, kc == 7, [WA, HT[s]], [PSB[bkv]])

            def post2(k):
                gi, i_, t = tiles2[k]
                bq, bkv, bz = (0, 1, 2) if k % 2 == 0 else (3, 4, 5)
                is_ext = t < NT_EXT
                is_ctx = t >= NT_LAT
                kk = k % 2
                qsq, QSQ, qst, QST, qn, QN, rt, RT, qr, QR = (qsq_[kk], QSQ_[kk], qst_[kk], QST_[kk], qn_[kk], QN_[kk],
                                                              rt_[kk], RT_[kk], qr_[kk], QR_[kk])
                if True:
                    hoff = 0 if is_ext else 8
                    if is_ext:
                        fw.cp("act", zs[:, t, :], psb[bz][:, :], [PSB[bz]], [ZS[t]])
                    fw.cp("act", Va[:, t, :, 0:64], psb[bkv][:, 128:256].rearrange("p (a b) -> p a b", a=2), [PSB[bkv]], [VA[t]])
                    if is_ext:
                        fw.cp("dve", qn[:, 0:512], psb[bq][:, :], [PSB[bq]], [QN])
                    fw.cp("dve", qn[:, 512:640], psb[bkv][:, 0:128], [PSB[bkv]], [QN])
                    yield
                    c_lo, c_hi = hoff * 64, 640
                    fw.tt("dve", qsq[:, c_lo:c_hi], qn[:, c_lo:c_hi], qn[:, c_lo:c_hi], ALU.mult, [QN], [QSQ])
                    yield
                    fw.op("dve", lambda e, hoff=hoff: e.tensor_reduce(
                        out=qst[:, hoff:10], in_=qsq[:, hoff * 64:640].rearrange("p (a b) -> p a b", b=64),
                        axis=AX.X, op=ALU.add), [QSQ], [QST])
                    yield
                    fw.act(qst[:, 10 + hoff:20], qst[:, hoff:10], AF.Ln, [QST, CST], [QST], bias=eps_ap, scale=1.0 / 64)
                    yield
                    fw.act(qst[:, 20 + hoff:30], qst[:, 10 + hoff:20], AF.Exp, [QST], [QST], scale=-0.5)
                    yield
                    n_h = 10 - hoff
                    qv = qn[:, c_lo:c_hi].rearrange("p (a b) -> p a b", b=64)
                    fw.tt("dve", qv, qv, qst[:, 20 + hoff:30].unsqueeze(2).to_broadcast([128, n_h, 64]), ALU.mult, [QN, QST], [QN])
                    yield
                    fw.tt("dve", qn[:, c_lo:c_hi], qn[:, c_lo:c_hi], kg2[:, c_lo:c_hi], ALU.mult, [QN, KG2], [QN])
                    yield
                    qrv = qr[:, c_lo:c_hi].rearrange("p (a b) -> p a b", b=64)
                    if not is_ctx:
                        cosb = rope_sb[:, t, 0:32].unsqueeze(1).to_broadcast([128, n_h, 32])
                        sinb = rope_sb[:, t, 32:64].unsqueeze(1).to_broadcast([128, n_h, 32])
                        x1_, x2_ = qv[:, :, 0:32], qv[:, :, 32:64]
                        r0, r1, r2, r3 = (rt[j][:, hoff:10, :] for j in range(4))
                        fw.tt("dve", r0, x1_, cosb, ALU.mult, [QN, ROPE], [RT[0]])
                        fw.tt("dve", r1, x2_, sinb, ALU.mult, [QN, ROPE], [RT[1]])
                        yield
                        fw.tt("dve", r2, x2_, cosb, ALU.mult, [QN, ROPE], [RT[2]])
                        fw.tt("dve", r3, x1_, sinb, ALU.mult, [QN, ROPE], [RT[3]])
                        yield
                        fw.tt("dve", qrv[:, :, 0:32], r0, r1, ALU.subtract, [RT[0], RT[1]], [QR])
                        fw.tt("dve", qrv[:, :, 32:64], r2, r3, ALU.add, [RT[2], RT[3]], [QR])
                        yield
                    else:
                        fw.cp("dve", qr[:, c_lo:c_hi], qn[:, c_lo:c_hi], [QN], [QR])
                    pt, PT = next_pst()
                    fw.tr(pt[:, 0:128], qr[:, 512:640], ident_b[:], [QR, IDB], [PT])
                    if is_ext:
                        for j in range(4):
                            fw.tr(pt[:, 128 + j * 128:256 + j * 128], qr[:, j * 128:(j + 1) * 128], ident_b[:], [QR, IDB], [PT])
                    fw.cp("act", KTa[:, t * 128:(t + 1) * 128], pt[:, 0:128], [PT], [KTA[t // 4]])
                    if is_ext:
                        fw.cp("dve", QTa[:, :, t * 128:(t + 1) * 128], pt[:, 128:640].rearrange("p (a b) -> p a b", a=4),
                              [PT], [QTA[t]])

            prepped = {0}

            def ensure_prep(k):
                gi = tiles2[k][0]
                for g_ in range(gi + 2):
                    if g_ < len(allg) and g_ not in prepped and g_ <= gi + 1:
                        prep2(g_)
                        prepped.add(g_)
            ensure_prep(0)
            mm2(0)
            mm2(1)
            nt2 = len(tiles2)
            for p_ in range(0, nt2, 2):
                gens = [post2(k) for k in (p_, p_ + 1) if k < nt2]
                for g_ in gens:
                    next(g_)
                for k in (p_ + 2, p_ + 3):
                    if k < nt2:
                        ensure_prep(k)
                        mm2(k)
                alive = list(gens)
                while alive:
                    for g_ in list(alive):
                        try:
                            next(g_)
                        except StopIteration:
                            alive.remove(g_)
            for t in range(NT_EXT):
                fw.act(zs[:, t, :], zs[:, t, :], AF.Silu, [ZS[t]], [ZS[t]])
            fw.barrier()
        wo = sb(att, "wo", [128, 8, 1024], BF16); WO = Buf("wo")
        with ExitStack() as p2w:
            load_weight_bf16(p2w, "wo", w_out.rearrange("(k p) n -> p k n", p=128), 1024, wo, WO)
        if stop_after == 5:
            dump("KTa", KTa[:], [128, TOK_ALL], KTA[0], BF16)
            dump("QTa", QTa[:], [128, 4, TOK_EXT], QTA[0], BF16)
            dump("Va", Va[:], [128, NT_ALL, 2, 65], VA[0], BF16)
            dump("zs", zs[:], [128, NT_EXT, 512], ZS[0], BF16)
            fw.finish()
            return nc, dbg_outs

        with ExitStack() as p3:
            pT = [sb(p3, "pT%d" % i, [128, 512], BF16) for i in range(4)]
            PTB = [Buf("pT%d" % i) for i in range(4)]
            oa = sb(p3, "oa", [128, 512], F32); OA = Buf("oa")
            oT = [sb(p3, "oT%d" % i, [65, 512], F32) for i in range(2)]
            OTB = [Buf("oT0"), Buf("oT1")]
            og = sb(p3, "og", [128, 512], F32); OG = Buf("og")
            ost = sb(p3, "ost", [128, 32], F32); OST = Buf("ost")
            ojunk = sb(p3, "ojunk", [128, 512], BF16); OJ = Buf("ojunk")
            mix = sb(p3, "mix", [128, 1024], BF16); MIX = Buf("mix")
            mixT = sb(p3, "mixT", [128, 8, 128], BF16); MIXT = Buf("mixT")
            xres = [sb(p3, "xres%d" % i, [128, D], F32) for i in range(2)]
            XRES = [Buf("xres0"), Buf("xres1")]
            x1t = [sb(p3, "x1t%d" % i, [128, D], F32) for i in range(2)]
            X1T = [Buf("x1t0"), Buf("x1t1")]
            nctx = make_norm(p3, "c")
            h2stage = sb(p3, "h2stage", [128, 8, 128], BF16); H2S = Buf("h2stage")
            h2d = nc.dram_tensor("h2d", [128, 8, NT_OWN * 128 + 128], BF16, kind="Internal").ap()
            ob = (6, 7)
            obT = (4, 5)
            items = [(qt, kt, g) for qt in range(NT_EXT) for kt in range(NT_ALL) for g in range(2)]
            LOOK = 3

            def emit_qk(n):
                qt, kt, g = items[n]
                bi = n % 4
                P0, P1 = 64 * g, 64 * g + 64
                fw.mm(psb[bi][:, :], KTa[P0:P1, kt * 128:(kt + 1) * 128], QTa[P0:P1, :, qt * 128:(qt + 1) * 128],
                      True, True, [KTA[kt // 4], QTA[qt]], [PSB[bi]])

            def emit_pv(n):
                qt, kt, g = items[n]
                bi = n % 4
                pi = n % 4
                fw.act(pT[pi][:], psb[bi][:, :], AF.Exp, [PSB[bi]], [PTB[pi]])
                fw.mm(psb[obT[g]][0:65, :], Va[:, kt, g, :], pT[pi][:], kt == 0, kt == NT_ALL - 1,
                      [PTB[pi], VA[kt]], [PSB[obT[g]]])
            def epilogue(qt):
                for g in range(2):
                    fw.cpa(oT[g][:], psb[obT[g]][0:65, :], [PSB[obT[g]]], [OTB[g]])
                yield
                yield
                for g in range(2):
                    for j in range(4):
                        fw.tr(psb[ob[g]][:, j * 65:(j + 1) * 65], oT[g][:, j * 128:(j + 1) * 128], ident_f[0:65, 0:65],
                              [OTB[g], CM], [PSB[ob[g]]])
                    yield
                for g in range(2):
                    ov = psb[ob[g]][:, 0:260].rearrange("p (a b) -> p a b", b=65)
                    fw.op("dve", lambda e, ov=ov, g=g: e.reciprocal(out=ost[:, g * 4:g * 4 + 4], in_=ov[:, :, 64]), [PSB[ob[g]]], [OST])
                    fw.tt("dve", oa[:, g * 256:(g + 1) * 256].rearrange("p (a b) -> p a b", b=64), ov[:, :, 0:64],
                          ost[:, g * 4:g * 4 + 4].unsqueeze(2).to_broadcast([128, 4, 64]), ALU.mult, [PSB[ob[g]], OST], [OA])
                yield
                fw.act(ojunk[:], oa[:], AF.Square, [OA], [OJ, OST], accum=ost[:, 8:9])
                yield
                fw.act(ost[:, 9:10], ost[:, 8:9], AF.Ln, [OST, CST], [OST], bias=eps_ap, scale=1.0 / 512)
                yield
                fw.act(ost[:, 10:11], ost[:, 9:10], AF.Exp, [OST], [OST], scale=-0.5)
                yield
                fw.stt("dve", mix[:, 0:512], oa[:], ost[:, 10:11], aog_bc, ALU.mult, ALU.mult, [OA, OST, VEC], [MIX])
                yield
                fw.tt("pool", og[:], Oacc[:, qt, :], Oacc[:, qt, :], ALU.mult, [OACC[qt]], [OG])
                yield
                fw.op("dve", lambda e: e.tensor_reduce(out=ost[:, 12:16], in_=og[:].rearrange("p (a b) -> p a b", b=128),
                                                       axis=AX.X, op=ALU.add), [OG], [OST])
                yield
                fw.act(ost[:, 16:20], ost[:, 12:16], AF.Ln, [OST, CST], [OST], bias=cst[:, 2:3], scale=1.0 / 128)
                yield
                fw.act(ost[:, 20:24], ost[:, 16:20], AF.Exp, [OST], [OST], scale=-0.5)
                yield
                ogv = og[:].rearrange("p (a b) -> p a b", b=128)
                fw.tt("dve", ogv, Oacc[:, qt, :].rearrange("p (a b) -> p a b", b=128),
                      ost[:, 20:24].unsqueeze(2).to_broadcast([128, 4, 128]), ALU.mult, [OACC[qt], OST], [OG])
                yield
                fw.tt("pool", ogv, ogv, gng_bc.unsqueeze(1).to_broadcast([128, 4, 128]), ALU.mult, [OG, VEC], [OG])
                yield
                fw.tt("dve", mix[:, 512:1024], og[:], zs[:, qt, :], ALU.mult, [OG, ZS[qt]], [MIX])
                yield
                pt, PT = next_pst()
                for kc in range(8):
                    fw.tr(pt[:, kc * 128:(kc + 1) * 128], mix[:, kc * 128:(kc + 1) * 128], ident_b[:], [MIX, IDB], [PT])
                yield
                fw.cpa(mixT[:], pt[:, 0:1024].rearrange("p (a b) -> p a b", a=8), [PT], [MIXT])
                yield
                s_ = qt % 2
                fw.ld(xres[s_][:], xs[qt * 128:(qt + 1) * 128, :], [XRES[s_]])
                for half in range(2):
                    bi = 6 + half
                    for kc in range(8):
                        fw.mm(psb[bi][:, :], mixT[:, kc, :], wo[:, kc, half * 512:(half + 1) * 512], kc == 0, kc == 7,
                              [MIXT, WO], [PSB[bi]])
                    yield
                    fw.tt("dve", x1t[s_][:, half * 512:(half + 1) * 512], psb[bi][:, :], G12[:, 0, half * 512:(half + 1) * 512],
                          ALU.mult, [PSB[bi], G12B], [X1T[s_]])
                    yield
                    fw.tt("pool", x1t[s_][:, half * 512:(half + 1) * 512], x1t[s_][:, half * 512:(half + 1) * 512],
                          xres[s_][:, half * 512:(half + 1) * 512], ALU.add, [X1T[s_], XRES[s_]], [X1T[s_]])
                if qt < NT_OWN:
                    fw.stor(x1s[qt * 128:(qt + 1) * 128, :], x1t[s_][:], [X1T[s_]])
                yield
                norm_rows(nctx, x1t[s_][:], X1T[s_], 0)
                yield
                transpose_mod(nctx, 1, lambda kc, T: h2stage[:, kc, 0:T], H2S, 32)
                fw.stor(h2d[:, :, qt * 128:(qt + 1) * 128], h2stage[:], [H2S])

            pend = [None]

            def step_pending():
                if pend[0] is not None:
                    try:
                        next(pend[0])
                    except StopIteration:
                        pend[0] = None

            def flush_pending():
                while pend[0] is not None:
                    step_pending()
            emit_qk(0)
            emit_qk(1)
            for n in range(len(items)):
                if n % 2 == 0 and n + 2 < len(items):
                    emit_qk(n + 2)
                    emit_qk(n + 3)
                emit_pv(n)
                step_pending()
                qt, kt, g = items[n]
                if kt == NT_ALL - 1 and g == 1:
                    flush_pending()
                    pend[0] = epilogue(qt)
                    step_pending()
            flush_pending()
            fw.barrier()
        att.close()
        mixer.close()

        with ExitStack() as p4:
            NTOK = NT_OWN * 128
            actT = sb(p4, "actT", [128, 22, NTOK], BF16); ACTT = [Buf("actT%d" % c) for c in range(22)]
            with ExitStack() as p4a:
                h2T = sb(p4a, "h2T", [128, 8, NTOK + 128], BF16); H2T = Buf("h2T")
                for kc in range(8):
                    fw.ld(h2T[:, kc, :], h2d[:, kc, :], [H2T])
                HALF = NTOK // 2
                ub = [sb(p4a, "ubuf%d" % i, [128, HALF + 4], F32) for i in range(2)]
                UB = [Buf("ubuf0"), Buf("ubuf1")]
                ucb = [sb(p4a, "uc%d" % i, [128, HALF], F32) for i in range(2)]
                UC = [Buf("uc0"), Buf("uc1")]
                sg = sb(p4a, "sg", [128, NTOK], BF16); SG = [Buf("sg0"), Buf("sg1")]
                wst = [sb(p4a, "wust%d" % i, [128, 8, 256], F32) for i in range(2)]
                WST = [Buf("wust0"), Buf("wust1")]
                wu = [sb(p4a, "wu%d" % i, [128, 8, 256], BF16) for i in range(2)]
                WU = [Buf("wu0"), Buf("wu1")]
                fw.op("pool", lambda e: e.memset(ub[0][:, 0:1], 0.0), [], [UB[0]])
                wupv = w_up.rearrange("(k p) n -> p k n", p=128)
                bcnt = 0

                def load_w(c):
                    s_ = c % 2
                    fw.ld(wst[s_][:, :, 0:128], wupv[:, :, c * 128:(c + 1) * 128], [WST[s_]])
                    fw.ld(wst[s_][:, :, 128:256], wupv[:, :, (22 + c) * 128:(23 + c) * 128], [WST[s_]])
                    fw.cp("pool", wu[s_][:], wst[s_][:], [WST[s_]], [WU[s_]])
                load_w(0)
                for c in range(22):
                    s_ = c % 2
                    if c + 1 < 22:
                        load_w(c + 1)
                    for part in range(2):
                        ch = c + 22 * part
                        w0, w1, w2, bb = (fcw_sb[:, ch * 4 + j:ch * 4 + j + 1] for j in range(4))
                        for hf in range(2):
                            tok_lo, col_lo, ntk = (0, 1, HALF + 1) if hf == 0 else (HALF - 1, 0, HALF + 2)
                            for o in range(0, ntk, 512):
                                n = min(512, ntk - o)
                                bi = bcnt % 6
                                bcnt += 1
                                for kc in range(8):
                                    fw.mm(psb[bi][:, 0:n], wu[s_][:, kc, part * 128:(part + 1) * 128],
                                          h2T[:, kc, tok_lo + o:tok_lo + o + n], kc == 0, kc == 7, [WU[s_], H2T], [PSB[bi]])
                                fw.cp("act", ub[hf][:, col_lo + o:col_lo + o + n], psb[bi][:, 0:n], [PSB[bi]], [UB[hf]])
                                j0 = max(0, 1 - (col_lo + o))
                                j1 = min(n, HALF + 1 - (col_lo + o))
                                if j1 > j0:
                                    fw.act(ucb[hf][:, col_lo + o + j0 - 1:col_lo + o + j1 - 1], psb[bi][:, j0:j1], AF.Identity,
                                           [PSB[bi], FCW], [UC[hf]], bias=bb, scale=w1)
                            fw.stt("dve", ucb[hf][:], ub[hf][:, 0:HALF], w0, ucb[hf][:], ALU.mult, ALU.add, [UB[hf], FCW, UC[hf]], [UC[hf]])
                            fw.stt("dve", ucb[hf][:], ub[hf][:, 2:HALF + 2], w2, ucb[hf][:], ALU.mult, ALU.add, [UB[hf], FCW, UC[hf]], [UC[hf]])
                            hs = slice(hf * HALF, (hf + 1) * HALF)
                            if part == 0:
                                fw.act(sg[:, hs], ucb[hf][:], AF.Silu, [UC[hf]], [SG[hf]])
                            else:
                                fw.tt("pool", actT[:, c, hs], ucb[hf][:], sg[:, hs], ALU.mult, [UC[hf], SG[hf]], [ACTT[c]])
                fw.barrier()
            if stop_after == 7:
                dump("actT", actT[:], [128, 22, NTOK], ACTT[0], BF16)
                fw.finish()
                return nc, dbg_outs
            wd = sb(p4, "wd", [128, 22, 1024], BF16); WD = Buf("wd")
            load_weight_bf16(p4, "wd", w_down.rearrange("(c p) n -> p c n", p=128), 1024, wd, WD, piece=128)
            x1r = [sb(p4, "x1r%d" % i, [128, D], F32) for i in range(2)]
            X1R = [Buf("x1r0"), Buf("x1r1")]
            x2 = [sb(p4, "x2_%d" % i, [128, D], F32) for i in range(2)]
            X2 = [Buf("x2_0"), Buf("x2_1")]
            fj = sb(p4, "fjunk", [128, D], BF16); FJ = Buf("fjunk")
            fst = sb(p4, "fst", [128, 8], F32); FST = Buf("fst")
            def mm_down(t):
                s_ = t % 2
                fw.ld(x1r[s_][:], x1s[t * 128:(t + 1) * 128, :], [X1R[s_]])
                for half in range(2):
                    bi = 2 * s_ + half
                    for c in range(22):
                        fw.mm(psb[bi][:, :], actT[:, c, t * 128:(t + 1) * 128], wd[:, c, half * 512:(half + 1) * 512],
                              c == 0, c == 21, [ACTT[c], WD], [PSB[bi]])

            def post_down(t):
                s_ = t % 2
                for half in range(2):
                    bi = 2 * s_ + half
                    hs = slice(half * 512, (half + 1) * 512)
                    fw.tt("dve", x2[s_][:, hs], psb[bi][:, :], G12[:, 1, hs], ALU.mult, [PSB[bi], G12B], [X2[s_]])
                    fw.tt("pool", x2[s_][:, hs], x2[s_][:, hs], x1r[s_][:, hs], ALU.add, [X2[s_], X1R[s_]], [X2[s_]])
                fw.act(fj[:], x2[s_][:], AF.Square, [X2[s_]], [FJ, FST], accum=fst[:, 4 * s_:4 * s_ + 1])
                fw.act(fst[:, 4 * s_ + 1:4 * s_ + 2], fst[:, 4 * s_:4 * s_ + 1], AF.Ln, [FST, CST], [FST], bias=eps_ap, scale=1.0 / D)
                fw.act(fst[:, 4 * s_ + 2:4 * s_ + 3], fst[:, 4 * s_ + 1:4 * s_ + 2], AF.Exp, [FST], [FST], scale=-0.5)
                fw.stt("dve", x2[s_][:], x2[s_][:], fst[:, 4 * s_ + 2:4 * s_ + 3], fng_bc, ALU.mult, ALU.mult, [X2[s_], FST, VEC], [X2[s_]])
                fw.stor(y[t * 128:(t + 1) * 128, :], x2[s_][:], [X2[s_]])
            mm_down(0)
            for t in range(NT_OWN):
                if t + 1 < NT_OWN:
                    mm_down(t + 1)
                post_down(t)
            fw.barrier()
        fw.finish()
        return nc, dbg_outs


def _prep_inputs(inputs):
    f = np.float32
    x = np.asarray(inputs["x"], f)
    c = np.asarray(inputs["c"], f)
    ctx = np.asarray(inputs["ctx"], f)
    c_ctx = np.asarray(inputs["c_ctx"], f)
    w_in = np.asarray(inputs["w_in"], f)[0]

    def col(v):
        return np.ascontiguousarray(v.reshape(-1, 128).T)

    q_a = w_in[:, 0:512].reshape(D, 8, 64)
    order = [0, 4, 1, 5, 2, 6, 3, 7]
    q_perm = q_a[:, order, :].reshape(D, 512)
    k_a = w_in[:, 512:640]
    v_a = w_in[:, 640:768]
    gq = w_in[:, 768:1280]
    gk = w_in[:, 1280:1792]
    gv = w_in[:, 1792:2304]
    z = w_in[:, 2304:2816]
    a_f, a_b, b_f, b_b = (w_in[:, 2816 + 4 * i:2820 + 4 * i] for i in range(4))
    w_in_a = np.ascontiguousarray(np.concatenate([q_perm, k_a, v_a, z], axis=1))
    convq = np.asarray(inputs["conv_qkv_w"], f)[0]
    ffw = np.asarray(inputs["ffn_conv_w"], f)[0]
    ffb = np.asarray(inputs["ffn_conv_b"], f)[0]

    idx = np.arange(128)
    same = (idx[:, None] // 64) == (idx[None, :] // 64)
    ident = np.eye(128, dtype=f)
    ind = np.stack([(idx < 64), (idx >= 64)], axis=1).astype(f)
    mF = np.concatenate([(same & (idx[:, None] <= idx[None, :])).astype(f), ind], axis=1)
    mB = np.concatenate([(same & (idx[:, None] >= idx[None, :])).astype(f), ind], axis=1)
    blk = same.astype(f)
    negF = np.where(same & (idx[:, None] <= idx[None, :]), 0.0, -BIG).astype(f)
    posF = np.where(same & (idx[None, :] < idx[:, None]), 0.0, BIG).astype(f)
    negB = np.where(same & (idx[:, None] >= idx[None, :]), 0.0, -BIG).astype(f)
    posB = np.where(same & (idx[None, :] > idx[:, None]), 0.0, BIG).astype(f)
    cmat = np.ascontiguousarray(np.concatenate([ident, mF, mB, blk, negF, posF, negB, posB], axis=1))

    rows = SEQ // 64
    row = np.repeat(np.arange(rows, dtype=f), 64)
    colp = np.tile(np.arange(64, dtype=f), rows)
    inv_freq = (10000.0 ** (-np.arange(16, dtype=f) / 16)).astype(f)
    ang = np.concatenate([row[:, None] * inv_freq, colp[:, None] * inv_freq], axis=-1).astype(f)
    rope = np.concatenate([np.cos(ang), np.sin(ang)], axis=1).astype(f)

    shared = dict(
        w_mod=np.ascontiguousarray(np.asarray(inputs["w_mod"], f)[0]),
        bmod_col=col(np.asarray(inputs["b_mod"], f)[0]),
        bmod_row=np.ascontiguousarray(np.asarray(inputs["b_mod"], f)[0][None, :]),
        ng_col=np.ascontiguousarray(np.concatenate([col(np.asarray(inputs["norm1_g"], f)[0]),
                                                    col(np.asarray(inputs["norm2_g"], f)[0])], axis=1)),
        w_in_a=w_in_a,
        w_out=np.ascontiguousarray(np.asarray(inputs["w_out"], f)[0]),
        w_up=np.ascontiguousarray(np.asarray(inputs["w_up"], f)[0]),
        w_down=np.ascontiguousarray(np.asarray(inputs["w_down"], f)[0]),
        cmat=cmat,
    )
    in_maps = []
    for r in range(8):
        b, flip = r // 2, (r % 2 == 1)
        m = dict(shared)
        xb, cb, rp = x[b], ctx[b], rope
        if flip:
            xb, cb, rp = xb[::-1], cb[::-1], rp[::-1]
        m["xs"] = np.ascontiguousarray(xb)
        m["cs"] = np.ascontiguousarray(cb)
        m["rope_cs"] = np.ascontiguousarray(rp)
        m["ccol"] = np.ascontiguousarray(np.concatenate([col(c[b]), col(c_ctx)], axis=1))
        aF, aB, bF, bB = (a_b, a_f, b_b, b_f) if flip else (a_f, a_b, b_f, b_b)
        m["w_in_g"] = np.ascontiguousarray(np.concatenate([gq, gk, gv, aF, aB, bF, bB], axis=1))
        taps = [2, 1, 0] if flip else [0, 1, 2]
        cwq = convq[taps]
        m["convw"] = np.ascontiguousarray(cwq.reshape(3, 12, 128).transpose(2, 1, 0).reshape(128, 36))
        fw_ = ffw[taps].reshape(3, NFC, 128)
        fb_ = ffb.reshape(1, NFC, 128)
        m["fcw"] = np.ascontiguousarray(np.concatenate([fw_, fb_], axis=0).transpose(2, 1, 0).reshape(128, NFC * 4))
        al = [np.asarray(inputs[k], f)[0] for k in ("a_log_f", "a_log_b", "dt_bias_f", "dt_bias_b")]
        if flip:
            al = [al[1], al[0], al[3], al[2]]
        m["vecs"] = np.ascontiguousarray(np.concatenate([
            np.asarray(inputs["q_norm_g"], f)[0], np.asarray(inputs["k_norm_g"], f)[0],
            np.asarray(inputs["attn_out_g"], f)[0], np.asarray(inputs["gdn_norm_g"], f)[0],
            al[0], al[1], al[2], al[3], np.asarray(inputs["final_norm_g"], f)])[None, :])
        in_maps.append(m)
    return in_maps


def kernel(**inputs):
    in_maps = _prep_inputs(inputs)
    if os.environ.get("KSTOP"):
        nc, _ = _build(dbg=True, stop_after=int(os.environ["KSTOP"]))
        run_bass_kernel_spmd(nc, in_maps, core_ids=list(range(8)))
        return np.zeros((4, SEQ, D), np.float32)
    nc, _ = _build()
    res = run_bass_kernel_spmd(nc, in_maps, core_ids=list(range(8)))
    out = np.empty((4, SEQ, D), np.float32)
    for r in range(8):
        yb = np.asarray(res.results[r]["y"], np.float32)
        b = r // 2
        if r % 2 == 0:
            out[b, 0:2048] = yb
        else:
            out[b, 2048:4096] = yb[::-1]
    return out
```
